# Optimizing a Trainium2 kernel written in Bass

```python
import jax
import jax.numpy as jnp
from jax import lax
import numpy as np

D_MODEL = 2048
BATCH = 4
SEQ = 2048
DEPTH = 2
DEC_BATCH = 32
DEC_SEQ = 1
PAST_LEN = 16384
PAGE_SIZE = 128

EXPAND = 2
WIDTH = EXPAND * D_MODEL
HEAD_DIM = 64
N_HEADS_A = WIDTH // HEAD_DIM
N_Q_HEADS = WIDTH // HEAD_DIM
N_KV_HEADS = 8
GQA_GROUP = N_Q_HEADS // N_KV_HEADS
WINDOW = 128
BLOCK = 128
ROPE_DIM = HEAD_DIM // 4
ROPE_THETA = 500000.0
N_META = 16
N_A_LAYERS = DEPTH // 2
N_B_LAYERS = DEPTH - N_A_LAYERS
LORA_W = max(32, int(round(1.8 * D_MODEL ** 0.5 / 32)) * 32)
LORA_A = max(32, int(round(1.8 * D_MODEL ** 0.5 / 32)) * 32)
RMS_EPS = 1e-6
GN_EPS = 64e-5

kernel_name = 'rwkv7_yoco_swa_sink_decoder_step'


def rmsnorm(x, g):
    xf = x.astype(jnp.float32)
    y = xf * lax.rsqrt(jnp.mean(xf * xf, axis=-1, keepdims=True) + RMS_EPS)
    return (y * g.astype(jnp.float32)).astype(x.dtype)


def rotary(x, pos):
    half = ROPE_DIM // 2
    inv_freq = ROPE_THETA ** (-jnp.arange(half, dtype=jnp.float32) * 2.0 / ROPE_DIM)
    ang = pos.astype(jnp.float32)[:, None] * inv_freq[None, :]
    cos = jnp.cos(ang)[:, None, :]
    sin = jnp.sin(ang)[:, None, :]
    xr = x[..., :ROPE_DIM].astype(jnp.float32)
    x1, x2 = xr[..., :half], xr[..., half:]
    rot = jnp.concatenate([x1 * cos - x2 * sin, x2 * cos + x1 * sin], axis=-1)
    return jnp.concatenate([rot.astype(x.dtype), x[..., ROPE_DIM:]], axis=-1)


def rwkv7_step(S, inp):
    r, d, k, v, kk, b = inp
    sa = jnp.einsum('bhij,bhj->bhi', S, kk)
    S = S * d[:, :, None, :] - sa[..., None] * b[:, :, None, :] + v[..., None] * k[:, :, None, :]
    return S, jnp.einsum('bhij,bhj->bhi', S, r)


def rwkv7_layer(xn, shift_prev, S0, mu, w_rkvz, w0, w1, w2, a0, a1, a2, k_k, k_a, r_k, gn_g, gn_b, w_out):
    f32 = jnp.float32
    B, T, _ = xn.shape
    dt = xn.dtype
    hn = (N_HEADS_A, HEAD_DIM)
    prev = jnp.concatenate([shift_prev[:, None, :].astype(dt), xn[:, :-1]], axis=1)
    xx = prev - xn
    xm = xn[:, :, None, :] + xx[:, :, None, :] * mu.astype(dt)
    rkvz = jnp.einsum('btpd,pde->btpe', xm[:, :, :4], w_rkvz)
    r, k, v, z = rkvz[:, :, 0], rkvz[:, :, 1], rkvz[:, :, 2], rkvz[:, :, 3]
    xw, xa = xm[:, :, 4], xm[:, :, 5]
    w_log = -jax.nn.softplus(-(w0 + jnp.tanh(xw @ w1) @ w2).astype(f32)) - 0.5
    decay = jnp.exp(-jnp.exp(w_log))
    a = jax.nn.sigmoid((a0 + (xa @ a1) @ a2).astype(f32))
    heads = lambda t: t.astype(f32).reshape(B, T, N_HEADS_A, HEAD_DIM)
    r_h, k_h, v_h, a_h, d_h = heads(r), heads(k), heads(v), heads(a), heads(decay)
    kk = k_h * k_k.astype(f32).reshape(hn)
    kk = kk / jnp.maximum(jnp.sqrt(jnp.sum(kk * kk, axis=-1, keepdims=True)), 1e-12)
    k_h = k_h * (1.0 + (a_h - 1.0) * k_a.astype(f32).reshape(hn))
    b_h = kk * a_h
    tm = lambda t: jnp.swapaxes(t, 0, 1)
    S_fin, y = lax.scan(rwkv7_step, S0.astype(f32),
                        (tm(r_h), tm(d_h), tm(k_h), tm(v_h), tm(kk), tm(b_h)))
    y = jnp.swapaxes(y, 0, 1)
    mean = jnp.mean(y, axis=-1, keepdims=True)
    var = jnp.mean(jnp.square(y - mean), axis=-1, keepdims=True)
    y = (y - mean) * lax.rsqrt(var + GN_EPS) * gn_g.astype(f32).reshape(hn) + gn_b.astype(f32).reshape(hn)
    y = y + jnp.sum(r_h * k_h * r_k.astype(f32), axis=-1, keepdims=True) * v_h
    y = y.reshape(B, T, WIDTH).astype(dt) * jax.nn.silu(z)
    return y @ w_out, S_fin.astype(S0.dtype), xn[:, -1]


def shared_kv(h, kv_norm, w_kv, pos):
    B, T, _ = h.shape
    kv = (rmsnorm(h, kv_norm) @ w_kv).reshape(B, T, 2, N_KV_HEADS, HEAD_DIM)
    return rotary(kv[:, :, 0], pos), kv[:, :, 1]


def sink_attend(q, k, v, valid, sinks):
    f32 = jnp.float32
    s = jnp.einsum('bqhgd,bkhd->bhgqk', q, k).astype(f32) * (HEAD_DIM ** -0.5)
    s = jnp.where(valid, s, -jnp.inf)
    sk = sinks.astype(f32)[None, :, :, None, None]
    m = jnp.maximum(jnp.max(s, axis=-1, keepdims=True), sk)
    p = jnp.exp(s - m)
    p = p / (jnp.sum(p, axis=-1, keepdims=True) + jnp.exp(sk - m))
    return jnp.einsum('bhgqk,bkhd->bqhgd', p.astype(v.dtype), v)


def banded_prompt_attention(q, k, v, sinks):
    B, L = q.shape[:2]
    lead = (-N_META) % BLOCK
    tail = (-(lead + L)) % BLOCK
    P = lead + L + tail
    nb = P // BLOCK
    padt = lambda t: jnp.pad(t, ((0, 0), (lead, tail)) + ((0, 0),) * (t.ndim - 2))
    qb = padt(q).reshape(B, nb, BLOCK, N_KV_HEADS, GQA_GROUP, HEAD_DIM)
    kb = padt(k).reshape(B, nb, BLOCK, N_KV_HEADS, HEAD_DIM)
    vb = padt(v).reshape(B, nb, BLOCK, N_KV_HEADS, HEAD_DIM)
    band = lambda t: jnp.concatenate([jnp.pad(t, ((0, 0), (1, 0), (0, 0), (0, 0), (0, 0)))[:, :-1], t], axis=2)
    kband, vband = band(kb), band(vb)
    qi = jnp.arange(BLOCK)
    kj = jnp.arange(2 * BLOCK) - BLOCK

    def one_block(args):
        q_blk, k_blk, v_blk, n = args
        qp = n * BLOCK + qi
        kp = n * BLOCK + kj
        diff = qp[:, None] - kp[None, :]
        valid = (kp[None, :] >= lead) & (diff >= 0) & (diff <= WINDOW)
        return sink_attend(q_blk, k_blk, v_blk, valid, sinks)

    out = lax.map(one_block, (jnp.moveaxis(qb, 1, 0), jnp.moveaxis(kband, 1, 0),
                              jnp.moveaxis(vband, 1, 0), jnp.arange(nb)))
    out = jnp.moveaxis(out, 0, 1).reshape(B, P, N_Q_HEADS, HEAD_DIM)[:, lead:lead + L]
    return out.reshape(B, L, WIDTH)


def setup_inputs(seed: int = 0) -> dict:
    key = jax.random.key(seed)
    ks = iter(jax.random.split(key, 40))
    f32 = jnp.float32
    nrm = lambda shape, scale: scale * jax.random.normal(next(ks), shape, f32)
    NA, NB, D, E = N_A_LAYERS, N_B_LAYERS, D_MODEL, WIDTH
    win = min(WINDOW, PAST_LEN)
    return {
        'x_prompt': nrm((BATCH, SEQ, D), 1.0),
        'x_sample': nrm((DEC_BATCH, DEC_SEQ, D), 1.0),
        'state_wkv': nrm((NA, DEC_BATCH, N_HEADS_A, HEAD_DIM, HEAD_DIM), 0.3),
        'state_shift': nrm((NA, DEC_BATCH, D), 1.0),
        'cache_k': nrm((DEC_BATCH, win, N_KV_HEADS, HEAD_DIM), 1.0),
        'cache_v': nrm((DEC_BATCH, win, N_KV_HEADS, HEAD_DIM), 1.0),
        'meta_tokens': nrm((N_META, D), 1.0),
        'a_norm': 1.0 + nrm((NA, D), 0.02),
        'a_mu': jax.random.uniform(next(ks), (NA, 6, D), f32),
        'a_w_rkvz': nrm((NA, 4, D, E), D ** -0.5),
        'a_w0': jax.random.uniform(next(ks), (NA, E), f32, minval=-4.0, maxval=1.0),
        'a_w1': nrm((NA, D, LORA_W), D ** -0.5),
        'a_w2': nrm((NA, LORA_W, E), 0.1 * LORA_W ** -0.5),
        'a_a0': nrm((NA, E), 0.1),
        'a_a1': nrm((NA, D, LORA_A), D ** -0.5),
        'a_a2': nrm((NA, LORA_A, E), 0.1 * LORA_A ** -0.5),
        'a_k_k': 0.85 + nrm((NA, E), 0.02),
        'a_k_a': 1.0 + nrm((NA, E), 0.02),
        'a_r_k': nrm((NA, N_HEADS_A, HEAD_DIM), 0.1),
        'a_gn_g': 1.0 + nrm((NA, E), 0.02),
        'a_gn_b': nrm((NA, E), 0.01),
        'a_w_out': nrm((NA, E, D), E ** -0.5),
        'kv_norm': 1.0 + nrm((D,), 0.02),
        'w_kv': nrm((D, 2 * N_KV_HEADS * HEAD_DIM), D ** -0.5),
        'b_norm': 1.0 + nrm((NB, D), 0.02),
        'b_w_qz': nrm((NB, D, 2 * E), D ** -0.5),
        'b_sinks': nrm((NB, N_Q_HEADS), 0.5),
        'b_w_o': nrm((NB, E, D), E ** -0.5),
        'final_norm': 1.0 + nrm((D,), 0.02),
    }


def reference(x_prompt, x_sample, state_wkv, state_shift, cache_k, cache_v, meta_tokens,
              a_norm, a_mu, a_w_rkvz, a_w0, a_w1, a_w2, a_a0, a_a1, a_a2, a_k_k, a_k_a, a_r_k,
              a_gn_g, a_gn_b, a_w_out, kv_norm, w_kv, b_norm, b_w_qz, b_sinks, b_w_o, final_norm):
    B = x_prompt.shape[0]
    L = x_prompt.shape[1] + N_META
    DB, S_dec = x_sample.shape[0], x_sample.shape[1]
    win = cache_k.shape[1]
    dt = x_prompt.dtype
    hp = jnp.concatenate([jnp.broadcast_to(meta_tokens.astype(dt)[None], (B, N_META, D_MODEL)), x_prompt], axis=1)
    hs = x_sample
    pos_p = jnp.arange(L)
    pos_s = PAST_LEN + jnp.arange(S_dec)
    k_pos = PAST_LEN - win + jnp.arange(win + S_dec)
    diff_s = pos_s[:, None] - k_pos[None, :]
    valid_s = (diff_s >= 0) & (diff_s <= WINDOW)
    zero_S = jnp.zeros((B, N_HEADS_A, HEAD_DIM, HEAD_DIM), state_wkv.dtype)
    zero_shift = jnp.zeros((B, D_MODEL), dt)
    p_wkv, p_shift, s_wkv, s_shift = [], [], [], []
    for i in range(DEPTH):
        if i < N_A_LAYERS:
            pa = (a_mu[i], a_w_rkvz[i], a_w0[i], a_w1[i], a_w2[i], a_a0[i], a_a1[i], a_a2[i],
                  a_k_k[i], a_k_a[i], a_r_k[i], a_gn_g[i], a_gn_b[i], a_w_out[i])
            o, S_new, sh = rwkv7_layer(rmsnorm(hp, a_norm[i]), zero_shift, zero_S, *pa)
            hp = hp + o
            p_wkv.append(S_new)
            p_shift.append(sh)
            o, S_new, sh = rwkv7_layer(rmsnorm(hs, a_norm[i]), state_shift[i], state_wkv[i], *pa)
            hs = hs + o
            s_wkv.append(S_new)
            s_shift.append(sh)
            if i == N_A_LAYERS - 1:
                kp, vp = shared_kv(hp, kv_norm, w_kv, pos_p)
                ksn, vsn = shared_kv(hs, kv_norm, w_kv, pos_s)
                k_all = jnp.concatenate([cache_k.astype(ksn.dtype), ksn], axis=1)
                v_all = jnp.concatenate([cache_v.astype(vsn.dtype), vsn], axis=1)
        else:
            j = i - N_A_LAYERS
            sinks = b_sinks[j].reshape(N_KV_HEADS, GQA_GROUP)
            q, z = jnp.split(rmsnorm(hp, b_norm[j]) @ b_w_qz[j], 2, axis=-1)
            q = rotary(q.reshape(B, L, N_Q_HEADS, HEAD_DIM), pos_p)
            att = banded_prompt_attention(q, kp, vp, sinks)
            hp = hp + (att * jax.nn.silu(z)) @ b_w_o[j]
            q, z = jnp.split(rmsnorm(hs, b_norm[j]) @ b_w_qz[j], 2, axis=-1)
            q = rotary(q.reshape(DB, S_dec, N_Q_HEADS, HEAD_DIM), pos_s)
            q = q.reshape(DB, S_dec, N_KV_HEADS, GQA_GROUP, HEAD_DIM)
            att = sink_attend(q, k_all, v_all, valid_s, sinks).reshape(DB, S_dec, WIDTH)
            hs = hs + (att * jax.nn.silu(z)) @ b_w_o[j]
    y_prompt = rmsnorm(hp, final_norm)[:, N_META:]
    y_sample = rmsnorm(hs, final_norm)
    p_state_wkv = jnp.stack(p_wkv)
    p_state_shift = jnp.stack(p_shift)
    p_cache_k = kp[:, -win:]
    p_cache_v = vp[:, -win:]
    s_state_wkv = jnp.stack(s_wkv)
    s_state_shift = jnp.stack(s_shift)
    s_cache_k = k_all[:, -win:]
    s_cache_v = v_all[:, -win:]
    return (y_prompt, y_sample, p_state_wkv, p_state_shift, p_cache_k, p_cache_v,
            s_state_wkv, s_state_shift, s_cache_k, s_cache_v)
```

```python
import numpy as np
from contextlib import ExitStack
import concourse.bass as bass
import concourse.mybir as mybir
from concourse.bass_utils import run_bass_kernel_spmd

F32 = mybir.dt.float32
BF16 = mybir.dt.bfloat16
AF = mybir.ActivationFunctionType
ALU = mybir.AluOpType
AX = mybir.AxisListType

COMPUTE = ('pe', 'act', 'dve', 'pool')
ALLENG = ('pe', 'act', 'dve', 'pool', 'sp')
SAME_ENGINE_SYNC = True
PSUM_KEYS = {'pC0', 'pC1', 'pTP', 'pPQ', 'pPA', 'pRX', 'pYS', 'ps', 'psg', 'psL', 'pSC', 'pAO'}


class Prog:
    def __init__(self, nc, stack):
        self.nc = nc
        self.stack = stack
        self.esem = {e: stack.enter_context(nc.semaphore("s_" + e)) for e in COMPUTE}
        self.ecnt = {e: 0 for e in COMPUTE}
        self.dsem = {}
        self.dcnt = {}
        self.waited = {e: {} for e in ALLENG}
        self.reset()

    def reset(self):
        self.ops = []
        self.lastw = {}
        self.readers = {}
        self.chain = {}

    max_ops = None

    def op(self, eng, fn, reads=(), writes=(), key=None):
        i = len(self.ops)
        if self.max_ops is not None and i >= self.max_ops:
            return -1
        pr = [r for r in reads if (r[0] if isinstance(r, tuple) else r) in PSUM_KEYS]
        if pr:
            reads = [r for r in reads if r not in pr]
            writes = list(writes) + [r for r in pr if r not in writes]
        deps = set()
        for r in reads:
            w = self.lastw.get(r)
            if w is not None:
                deps.add(w)
        for w_ in writes:
            w = self.lastw.get(w_)
            if w is not None:
                deps.add(w)
            deps.update(self.readers.get(w_, ()))
        if key is not None:
            prev = self.chain.get(key)
            if prev is not None:
                deps.add(prev)
            self.chain[key] = i
        self.ops.append(dict(eng=eng, fn=fn, deps=deps, key=key))
        for w_ in writes:
            self.lastw[w_] = i
            self.readers[w_] = []
        ws = set(writes)
        for r in reads:
            if r not in ws:
                self.readers.setdefault(r, []).append(i)
        return i

    def dma(self, q, out, in_, reads=(), writes=(), key=None, **kw):
        assert key is not None
        return self.op(q, lambda e: e.dma_start(out=out, in_=in_, **kw), reads, writes, key=key)

    def flush(self):
        nc = self.nc
        ops = self.ops
        if not ops:
            return
        needed = set()
        for o in ops:
            needed.update(o['deps'])
        lastop = {}
        for i, o in enumerate(ops):
            if o['key'] is None:
                lastop[o['eng']] = i
        needed.update(lastop.values())
        tgt = [None] * len(ops)
        for i, o in enumerate(ops):
            if o['key'] is not None:
                k = o['key']
                if k not in self.dsem:
                    self.dsem[k] = self.stack.enter_context(nc.semaphore("d_" + str(len(self.dsem))))
                    self.dcnt[k] = 0
                self.dcnt[k] += 16
                tgt[i] = (self.dsem[k], self.dcnt[k], ('d', k))
            elif i in needed:
                e = o['eng']
                self.ecnt[e] += 1
                tgt[i] = (self.esem[e], self.ecnt[e], ('e', e))
        per = {e: [] for e in ALLENG}
        for i, o in enumerate(ops):
            per[o['eng']].append(i)
        end_waits = []
        for e in COMPUTE:
            if e in lastop:
                end_waits.append(tgt[lastop[e]])
        for k in self.chain:
            end_waits.append((self.dsem[k], self.dcnt[k], ('d', k)))

        def run(ename, eobj):
            waited = self.waited[ename]
            for i in per[ename]:
                o = ops[i]
                need = {}
                for d in o['deps']:
                    od = ops[d]
                    if od['key'] is None and od['eng'] == ename:
                        if ename == 'pe' or not SAME_ENGINE_SYNC:
                            continue
                    sem, val, sid = tgt[d]
                    if need.get(sid, (None, 0))[1] < val:
                        need[sid] = (sem, val)
                for sid, (sem, val) in need.items():
                    if waited.get(sid, 0) < val:
                        eobj.wait_ge(sem, val)
                        waited[sid] = val
                ins = o['fn'](eobj)
                if tgt[i] is not None:
                    if o['key'] is not None:
                        ins.then_inc(tgt[i][0], 16)
                    else:
                        ins.then_inc(tgt[i][0], 1)
            for sem, val, sid in end_waits:
                if waited.get(sid, 0) < val:
                    eobj.wait_ge(sem, val)
                    waited[sid] = val

        with nc.Block() as block:
            @block.tensor
            def _(e):
                run('pe', e)

            @block.scalar
            def _(e):
                run('act', e)

            @block.vector
            def _(e):
                run('dve', e)

            @block.gpsimd
            def _(e):
                run('pool', e)

            @block.sync
            def _(e):
                run('sp', e)
        self.reset()


NT = 18
T = NT * 128
D = 2048
E = 4096
NS = 4
RMS_EPS = 1e-6


class Ctx:
    pass


_uid = [0]


def mk(nc, es, name, shape, dt, psum=False):
    _uid[0] += 1
    name = f"{name}_u{_uid[0]}"
    if psum:
        return es.enter_context(nc.psum_tensor(name, shape, dt))
    return es.enter_context(nc.sbuf_tensor(name, shape, dt))


def stage_A(P, nc, d):
    with ExitStack() as es:
        gA = mk(nc, es, "gA", [128, D], F32)
        muT = mk(nc, es, "muT", [128, 6, 16], F32)
        ident = mk(nc, es, "identA", [128, 128], F32)
        xc = [mk(nc, es, f"xc{i}", [128, D], F32) for i in range(2)]
        xp = [mk(nc, es, f"xp{i}", [128, D], F32) for i in range(2)]
        junk = mk(nc, es, "junkA", [128, D], F32)
        st = [mk(nc, es, f"stA{i}", [128, 4], F32) for i in range(2)]
        xnT = [mk(nc, es, f"xnT{i}", [128, 16, 128], F32) for i in range(2)]
        xxT = [mk(nc, es, f"xxT{i}", [128, 16, 128], F32) for i in range(2)]
        tmp = [mk(nc, es, f"tmpA{i}", [128, 16, 128], F32) for i in range(2)]
        xm = [mk(nc, es, f"xmA{i}", [128, 16, 128], BF16) for i in range(3)]
        ps = [mk(nc, es, f"psA{i}", [128, 512], F32, psum=True) for i in range(4)]

        P.dma('sp', gA[:], d['a_norm'][0].partition_broadcast(128), writes=['gA'], key='gA')
        P.dma('sp', muT[:], d['muT'], writes=['muT'], key='muT')
        P.dma('sp', ident[:], d['ident'], writes=['ident'], key='ident')
        ev = 0
        mi = 0
        for t in range(NT):
            b = t % 2
            P.dma('sp', xc[b][:], d['xin'][1 + 128 * t: 1 + 128 * t + 128, :], writes=[('xc', b)], key=('xc', b))
            if t < NT - 1:
                P.dma('sp', xp[b][:], d['xin'][128 * t: 128 * t + 128, :], writes=[('xp', b)], key=('xp', b))
            else:
                P.dma('sp', xp[b][:], d['sshift'], writes=[('xp', b)], key=('xp', b))
            P.op('pool', lambda e, b=b: e.memset(st[b][:, 0:2], 0.0), writes=[('st', b, 0), ('st', b, 1)])
            P.op('act', lambda e, b=b: e.activation(out=junk[:], in_=xc[b][:], func=AF.Square, accum_out=st[b][:, 0:1]),
                 reads=[('xc', b)], writes=['junk', ('st', b, 0)])
            if t < NT - 1:
                P.op('act', lambda e, b=b: e.activation(out=junk[:], in_=xp[b][:], func=AF.Square, accum_out=st[b][:, 1:2]),
                     reads=[('xp', b)], writes=['junk', ('st', b, 1)])
            nst = 2 if t < NT - 1 else 1
            P.op('dve', lambda e, b=b, n=nst: e.tensor_scalar(out=st[b][:, 2:2 + n], in0=st[b][:, 0:n], scalar1=1.0 / D, scalar2=RMS_EPS,
                                                              op0=ALU.mult, op1=ALU.add),
                 reads=[('st', b, 0), ('st', b, 1)], writes=[('st', b, 2)])
            P.op('act', lambda e, b=b, n=nst: e.sqrt(out=st[b][:, 2:2 + n], in_=st[b][:, 2:2 + n]),
                 reads=[('st', b, 2)], writes=[('st', b, 2)])
            P.op('dve', lambda e, b=b, n=nst: e.reciprocal(out=st[b][:, 2:2 + n], in_=st[b][:, 2:2 + n]),
                 reads=[('st', b, 2)], writes=[('st', b, 2)])
            P.op('dve', lambda e, b=b: e.scalar_tensor_tensor(out=xc[b][:], in0=xc[b][:], scalar=st[b][:, 2:3], in1=gA[:],
                                                              op0=ALU.mult, op1=ALU.mult),
                 reads=[('xc', b), ('st', b, 2), 'gA'], writes=[('xc', b)])
            if t < NT - 1:
                P.op('dve', lambda e, b=b: e.scalar_tensor_tensor(out=xp[b][:], in0=xp[b][:], scalar=st[b][:, 3:4], in1=gA[:],
                                                                  op0=ALU.mult, op1=ALU.mult),
                     reads=[('xp', b), ('st', b, 2), 'gA'], writes=[('xp', b)])
            if t == NT - 2:
                P.dma('sp', d['o_pshift'], xc[b][127:128, :], reads=[('xc', b)], key='o_pshift')
            if t == NT - 1:
                P.dma('sp', d['o_sshift'], xc[b][0:NS, :], reads=[('xc', b)], key='o_sshift')
            P.op('pool', lambda e, b=b: e.tensor_tensor(out=xp[b][:], in0=xp[b][:], in1=xc[b][:], op=ALU.subtract),
                 reads=[('xp', b), ('xc', b)], writes=[('xp', b)])
            for (src, srck, dst, dstk) in ((xc, 'xc', xnT, 'xnT'), (xp, 'xp', xxT, 'xxT')):
                for q in range(4):
                    pb = ev % 4
                    for j in range(4):
                        c = q * 4 + j
                        P.op('pe', lambda e, pb=pb, j=j, c=c, src=src, b=b: e.transpose(out=ps[pb][:, j * 128:(j + 1) * 128],
                                                                                        in_=src[b][:, c * 128:(c + 1) * 128], identity=ident[:]),
                             reads=[(srck, b), 'ident'], writes=[('ps', pb)])
                    eng = 'act' if ev % 2 == 0 else 'dve'
                    if eng == 'act':
                        P.op('act', lambda e, pb=pb, q=q, dst=dst, b=b: e.copy(out=dst[b][:, q * 4:(q + 1) * 4, :], in_=ps[pb][:].rearrange("p (a n) -> p a n", a=4)),
                             reads=[('ps', pb)], writes=[(dstk, b)])
                    else:
                        P.op('dve', lambda e, pb=pb, q=q, dst=dst, b=b: e.tensor_copy(out=dst[b][:, q * 4:(q + 1) * 4, :], in_=ps[pb][:].rearrange("p (a n) -> p a n", a=4)),
                             reads=[('ps', pb)], writes=[(dstk, b)])
                    ev += 1
            for p in range(6):
                m = mi % 3
                mi += 1
                e1 = 'dve' if p % 2 == 0 else 'pool'
                P.op(e1, lambda e, b=b, p=p: e.tensor_tensor(out=tmp[p % 2][:], in0=xxT[b][:], in1=muT[:, p, :].unsqueeze(2).to_broadcast([128, 16, 128]), op=ALU.mult),
                     reads=[('xxT', b), 'muT'], writes=[('tmp', p % 2)])
                P.op(e1, lambda e, b=b, p=p, m=m: e.tensor_tensor(out=xm[m][:], in0=tmp[p % 2][:], in1=xnT[b][:], op=ALU.add),
                     reads=[('tmp', p % 2), ('xnT', b)], writes=[('xm', m)])
                P.dma('sp', d['xmT'][p][:, :, t * 128:(t + 1) * 128], xm[m][:], reads=[('xm', m)], writes=[('xmT', p)], key=('xmst', m))
        P.flush()


def gemm_tokmajor(P, nc, es, actT_src, kc, w_src, ncols, evac, wkey, tiles=range(NT), act_res='actT', actT=None):
    wt = [mk(nc, es, f"wt_{wkey}{i}", [128, kc, 512], BF16) for i in range(2)]
    ps = [mk(nc, es, f"psg_{wkey}{i}", [128, 512], F32, psum=True) for i in range(4)]
    wv = w_src.rearrange("(c p) n -> p c n", p=128)
    cnt = 0
    for cb in range(ncols // 512):
        wb = cb % 2
        P.dma('pool', wt[wb][:], wv[:, :, cb * 512:(cb + 1) * 512], writes=[('wt', wkey, wb)], key=('wt', wkey, wb))
        for t in tiles:
            pb = cnt % 4
            cnt += 1
            for c in range(kc):
                P.op('pe', lambda e, pb=pb, c=c, t=t, wb=wb: e.matmul(ps[pb][:], lhsT=actT[:, c, t * 128:(t + 1) * 128], rhs=wt[wb][:, c, :],
                                                                      start=(c == 0), stop=(c == kc - 1)),
                     reads=[act_res, ('wt', wkey, wb)], writes=[('psg', wkey, pb)])
            evac(t, cb, ps[pb], ('psg', wkey, pb), cnt)


def stage_B(P, nc, d, projs=(0, 1, 2, 3)):
    for p in projs:
        with ExitStack() as es:
            actT = mk(nc, es, "actT", [128, 16, T], BF16)
            ob = [mk(nc, es, f"obB{i}", [128, 512], BF16) for i in range(4)]
            P.dma('sp', actT[:], d['xmT'][p], writes=['actT'], key='actT')

            def evac(t, cb, pst, pkey, cnt, p=p):
                o = cnt % 4
                if cnt % 2 == 0:
                    P.op('act', lambda e: e.copy(out=ob[o][:], in_=pst[:]), reads=[pkey], writes=[('ob', o)])
                else:
                    P.op('dve', lambda e: e.tensor_copy(out=ob[o][:], in_=pst[:]), reads=[pkey], writes=[('ob', o)])
                P.dma('sp', d['rkvz'][p][t * 128:(t + 1) * 128, cb * 512:(cb + 1) * 512], ob[o][:], reads=[('ob', o)], key=('obst', o))
            gemm_tokmajor(P, nc, es, None, 16, d['a_w_rkvz'][0, p], E, evac, f"B{p}", actT=actT)
            P.flush()


def tt(P, eng, out, in0, in1, op, reads, writes):
    P.op(eng, lambda e: e.tensor_tensor(out=out, in0=in0, in1=in1, op=op), reads, writes)


def ts(P, eng, out, in0, s1, s2, op0, op1, reads, writes):
    if s2 is None:
        P.op(eng, lambda e: e.tensor_scalar(out=out, in0=in0, scalar1=s1, scalar2=None, op0=op0), reads, writes)
    else:
        P.op(eng, lambda e: e.tensor_scalar(out=out, in0=in0, scalar1=s1, scalar2=s2, op0=op0, op1=op1), reads, writes)


def stt(P, eng, out, in0, scalar, in1, op0, op1, reads, writes):
    P.op(eng, lambda e: e.scalar_tensor_tensor(out=out, in0=in0, scalar=scalar, in1=in1, op0=op0, op1=op1), reads, writes)


def actf(P, out, in_, func, reads, writes, scale=1.0):
    P.op('act', lambda e: e.activation(out=out, in_=in_, func=func, scale=scale), reads, writes)


def cp(P, eng, out, in_, reads, writes):
    if eng == 'act':
        P.op('act', lambda e: e.copy(out=out, in_=in_), reads, writes)
    else:
        P.op(eng, lambda e: e.tensor_copy(out=out, in_=in_), reads, writes)


def mm(P, out, lhsT, rhs, start, stop, reads, writes):
    P.op('pe', lambda e: e.matmul(out, lhsT=lhsT, rhs=rhs, start=start, stop=stop), reads, writes)


def tr(P, out, in_, ident, reads, writes):
    P.op('pe', lambda e: e.transpose(out=out, in_=in_, identity=ident), reads, writes)


def red(P, eng, out, in_, reads, writes):
    P.op(eng, lambda e: e.reduce_sum(out=out, in_=in_, axis=AX.X), reads, writes)


def recip(P, out, in_, reads, writes):
    P.op('dve', lambda e: e.reciprocal(out=out, in_=in_), reads, writes)


GN_EPS = 64e-5
SROW0 = 17 * 128
ZROW0 = SROW0 + NS
NCH = 17 + NS
CH_LIST = list(range(NCH))
CB_LIST = list(range(8))


def load_rows(P, q, dst, src, ch, c0, c1, writes, key):
    if ch < 17:
        P.dma(q, dst[:], src[ch * 128:(ch + 1) * 128, c0:c1], writes=writes, key=key)
    else:
        s = ch - 17
        P.dma(q, dst[0:1], src[SROW0 + s:SROW0 + s + 1, c0:c1], writes=writes, key=key)
        P.dma(q, dst[1:65], src[ZROW0:ZROW0 + 64, c0:c1], writes=writes, key=key)
        P.dma(q, dst[64:128], src[ZROW0:ZROW0 + 64, c0:c1], writes=writes, key=key)


def stage_B_lora(P, nc, d):
    for which, (xi, w1n, w2n, outn, func) in enumerate(((4, 'a_w1', 'a_w2', 'wpre', AF.Tanh), (5, 'a_a1', 'a_a2', 'apre', AF.Copy))):
        with ExitStack() as es:
            actT = mk(nc, es, "actT", [128, 16, T], BF16)
            w1 = mk(nc, es, "w1", [128, 16, 96], BF16)
            w2 = mk(nc, es, "w2", [96, E], BF16)
            hT = mk(nc, es, "hT", [96, T], BF16)
            ob = [mk(nc, es, f"obL{i}", [128, 512], F32) for i in range(4)]
            ps = [mk(nc, es, f"psL{i}", [128, 512], F32, psum=True) for i in range(4)]
            P.dma('sp', actT[:], d['xmT'][xi], writes=['actT'], key='actT')
            P.dma('pool', w1[:], d[w1n][0].rearrange("(c p) n -> p c n", p=128), writes=['w1'], key='w1')
            P.dma('pool', w2[:], d[w2n][0], writes=['w2'], key='w2')
            cnt = 0
            for t in range(NT):
                pb = cnt % 4
                cnt += 1
                for c in range(16):
                    mm(P, ps[pb][0:96, 0:128], w1[:, c, :], actT[:, c, t * 128:(t + 1) * 128], c == 0, c == 15,
                       ['actT', 'w1'], [('psL', pb)])
                actf(P, hT[:, t * 128:(t + 1) * 128], ps[pb][0:96, 0:128], func, [('psL', pb)], [('hT', t)])
            for t in range(NT):
                for cb in range(8):
                    pb = cnt % 4
                    cnt += 1
                    mm(P, ps[pb][:], hT[:, t * 128:(t + 1) * 128], w2[:, cb * 512:(cb + 1) * 512], True, True,
                       [('hT', t), 'w2'], [('psL', pb)])
                    cp(P, 'act' if cnt % 2 else 'dve', ob[pb][:], ps[pb][:], [('psL', pb)], [('obL', pb)])
                    P.dma('sp', d[outn][t * 128:(t + 1) * 128, cb * 512:(cb + 1) * 512], ob[pb][:], reads=[('obL', pb)], key=('obLst', pb))
            P.flush()


def stage_CDE(P, nc, d):
    with ExitStack() as es:
        ident = mk(nc, es, "identF", [128, 128], F32)
        identb = mk(nc, es, "identB", [128, 128], BF16)
        tri = mk(nc, es, "tri", [128, 128], F32)
        ones = mk(nc, es, "ones", [128, 128], F32)
        onehot = mk(nc, es, "onehot", [128, 1], F32)
        lmask = mk(nc, es, "lmask", [128, 2], F32)
        mask4 = mk(nc, es, "mask4", [128, 512], F32)
        negsl = mk(nc, es, "negsl", [128, 128], F32)
        for nm, tl in (('ident', ident), ('tri', tri), ('ones', ones), ('onehot', onehot), ('lmask', lmask), ('mask4', mask4), ('negsl', negsl)):
            P.dma('sp', tl[:], d[nm], writes=[nm], key=nm)
        P.dma('pool', identb[:], d['ident'], writes=['identb'], key='identb')
        CONST = ['ident', 'tri', 'ones', 'onehot', 'lmask', 'mask4', 'negsl', 'identb']
        ST = mk(nc, es, "ST", [128, 32, 64], F32)
        STb = mk(nc, es, "STb", [128, 32, 64], BF16)
        SN = mk(nc, es, "SN", [64, 64, 64], F32)
        BON = mk(nc, es, "BON", [128, 64], F32)
        stmp = mk(nc, es, "stmp", [128, 64], F32)
        ygT = mk(nc, es, "ygT", [128, 32, 128], BF16)
        NB = 2
        PRM = [mk(nc, es, f"PRM{i}", [128, 7, 512], F32) for i in range(NB)]
        Rb = [mk(nc, es, f"Rb{i}", [128, 512], BF16) for i in range(NB)]
        Kb = [mk(nc, es, f"Kb{i}", [128, 512], BF16) for i in range(NB)]
        Vb = [mk(nc, es, f"Vb{i}", [128, 512], BF16) for i in range(NB)]
        Zb = [mk(nc, es, f"Zb{i}", [128, 512], BF16) for i in range(NB)]
        Wp = [mk(nc, es, f"Wp{i}", [128, 512], F32) for i in range(NB)]
        Ap = [mk(nc, es, f"Ap{i}", [128, 512], F32) for i in range(NB)]
        f32names = ['LD', 'KKf', 'KMf', 'Bf', 'SQ', 'T1', 'E1', 'E2', 'E3', 'E4', 'GT', 'Dinv']
        W = {n: mk(nc, es, n, [128, 512], F32) for n in f32names}
        sm = mk(nc, es, "sm", [128, 64], F32)
        TM = [mk(nc, es, f"TM{i}", [128, 4, 512], BF16) for i in range(NB)]
        KVb = [mk(nc, es, f"KVb{i}", [128, 512], BF16) for i in range(NB)]
        BVb = [mk(nc, es, f"BVb{i}", [128, 512], BF16) for i in range(NB)]
        FT = [mk(nc, es, f"FT{i}", [128, 4, 4, 128], BF16) for i in range(NB)]
        gCs = [mk(nc, es, f"gCs{i}", [128, 4], F32) for i in range(NB)]
        AK = mk(nc, es, "AK", [128, 4, 512], BF16)
        MT = mk(nc, es, "MT", [128, 4, 128], BF16)
        Rm = [mk(nc, es, f"Rm{i}", [128, 4, 128], BF16) for i in range(2)]
        PP = [mk(nc, es, f"PP{i}", [128, 4, 2, 128], BF16) for i in range(2)]
        Xb = mk(nc, es, "Xb", [128, 256], BF16)
        nSA = mk(nc, es, "nSA", [128, 256], BF16)
        Ycb = mk(nc, es, "Ycb", [128, 512], F32)
        EY = {n: mk(nc, es, n, [128, 512], F32) for n in ('Ysq', 'Yn', 'Sz')}
        YG = mk(nc, es, "YG", [128, 512], BF16)
        pC0 = mk(nc, es, "pC0", [128, 512], F32, psum=True)
        pC1 = mk(nc, es, "pC1", [128, 512], F32, psum=True)
        pTP = mk(nc, es, "pTP", [128, 1024], BF16, psum=True)
        pPQ = mk(nc, es, "pPQ", [128, 4, 2, 128], F32, psum=True)
        pPA = mk(nc, es, "pPA", [128, 512], F32, psum=True)
        pRX = mk(nc, es, "pRX", [128, 512], F32, psum=True)
        pYS = mk(nc, es, "pYS", [128, 512], F32, psum=True)

        def state_load(s):
            P.dma('sp', SN[:], d['swkv'][s].rearrange("h v k -> v h k"), writes=['SN'], key='SN')
            for g8 in range(4):
                for q in range(8):
                    gp = g8 * 8 + q
                    tr(P, pC0[:, q * 64:(q + 1) * 64], SN[:, 2 * gp:2 * gp + 2, :].rearrange("v a k -> v (a k)"), ident[0:64, 0:64],
                       ['SN', 'ident'], ['pC0'])
                cp(P, 'act', ST[:, g8 * 8:(g8 + 1) * 8, :], pC0[:].rearrange("p (a v) -> p a v", a=8), ['pC0'], [('ST', g8 * 8 + q) for q in range(8)])
                cp(P, 'dve', STb[:, g8 * 8:(g8 + 1) * 8, :], pC0[:].rearrange("p (a v) -> p a v", a=8), ['pC0'], [('STb', g8 * 8 + q) for q in range(8)])

        def state_save(dst):
            for g4 in range(8):
                for q in range(4):
                    gp = g4 * 4 + q
                    tr(P, pC0[0:64, q * 128:(q + 1) * 128], ST[:, gp, :], ident[:], [('ST', gp), 'ident'], ['pC0'])
                cp(P, 'act', SN[:, g4 * 8:(g4 + 1) * 8, :].rearrange("v a k -> v (a k)"), pC0[0:64, :], ['pC0'], ['SN'])
            P.dma('sp', dst.rearrange("h v k -> v h k"), SN[:], reads=['SN'], key='SNst')

        P.op('pool', lambda e: e.memset(ST[:], 0.0), writes=[('ST', g) for g in range(32)])
        P.op('pool', lambda e: e.memset(STb[:], 0.0), writes=[('STb', g) for g in range(32)])

        it = 0
        for ch in CH_LIST:
            lcol = 0 if ch < 17 else 1
            if ch >= 17:
                state_load(ch - 17)
            for cb in CB_LIST:
                b = it % NB
                it += 1
                c0, c1 = cb * 512, (cb + 1) * 512
                P.dma('sp', PRM[b][:], d['prm'][:, c0:c1].partition_broadcast(128), writes=[('PRM', b)], key=('PRM', b))
                load_rows(P, 'sp', Rb[b], d['rkvz'][0], ch, c0, c1, [('Rb', b)], ('Rb', b))
                load_rows(P, 'sp', Kb[b], d['rkvz'][1], ch, c0, c1, [('Kb', b)], ('Kb', b))
                load_rows(P, 'sp', Vb[b], d['rkvz'][2], ch, c0, c1, [('Vb', b)], ('Vb', b))
                load_rows(P, 'sp', Zb[b], d['rkvz'][3], ch, c0, c1, [('Zb', b)], ('Zb', b))
                load_rows(P, 'sp', Wp[b], d['wpre'], ch, c0, c1, [('Wp', b)], ('Wp', b))
                load_rows(P, 'sp', Ap[b], d['apre'], ch, c0, c1, [('Ap', b)], ('Ap', b))
                pass
                prm = lambda i, b=b: PRM[b][:, i, :]
                tt(P, 'dve', Wp[b][:], Wp[b][:], prm(0), ALU.add, [('Wp', b), ('PRM', b)], [('Wp', b)])
                actf(P, Wp[b][:], Wp[b][:], AF.Sigmoid, [('Wp', b)], [('Wp', b)])
                ts(P, 'dve', W['LD'][:], Wp[b][:], lmask[:, lcol:lcol + 1], None, ALU.mult, None, [('Wp', b), 'lmask'], ['LD'])
                tt(P, 'dve', Ap[b][:], Ap[b][:], prm(1), ALU.add, [('Ap', b), ('PRM', b)], [('Ap', b)])
                actf(P, Ap[b][:], Ap[b][:], AF.Sigmoid, [('Ap', b)], [('Ap', b)])
                tt(P, 'dve', W['KKf'][:], Kb[b][:], prm(2), ALU.mult, [('Kb', b), ('PRM', b)], ['KKf'])
                tt(P, 'pool', W['SQ'][:], W['KKf'][:], W['KKf'][:], ALU.mult, ['KKf'], ['SQ'])
                red(P, 'dve', sm[:, 0:8], W['SQ'][:].rearrange("p (h c) -> p h c", h=8), ['SQ'], [('sm', 0)])
                ts(P, 'dve', sm[:, 0:8], sm[:, 0:8], 1e-24, None, ALU.max, None, [('sm', 0)], [('sm', 0)])
                P.op('act', lambda e: e.sqrt(out=sm[:, 0:8], in_=sm[:, 0:8]), [('sm', 0)], [('sm', 0)])
                recip(P, sm[:, 0:8], sm[:, 0:8], [('sm', 0)], [('sm', 0)])
                tt(P, 'dve', W['KKf'][:].rearrange("p (h c) -> p h c", h=8), W['KKf'][:].rearrange("p (h c) -> p h c", h=8),
                   sm[:, 0:8].unsqueeze(2).to_broadcast([128, 8, 64]), ALU.mult, ['KKf', ('sm', 0)], ['KKf'])
                stt(P, 'dve', W['T1'][:], Ap[b][:], -1.0, prm(3), ALU.add, ALU.mult, [('Ap', b), ('PRM', b)], ['T1'])
                stt(P, 'dve', W['KMf'][:], W['T1'][:], 1.0, Kb[b][:], ALU.add, ALU.mult, ['T1', ('Kb', b)], ['KMf'])
                tt(P, 'pool', W['Bf'][:], W['KKf'][:], Ap[b][:], ALU.mult, ['KKf', ('Ap', b)], ['Bf'])
                tt(P, 'pool', W['T1'][:], Rb[b][:], W['KMf'][:], ALU.mult, [('Rb', b), 'KMf'], ['T1'])
                tt(P, 'pool', W['T1'][:], W['T1'][:], prm(4), ALU.mult, ['T1', ('PRM', b)], ['T1'])
                red(P, 'dve', BON[:, cb * 8:(cb + 1) * 8], W['T1'][:].rearrange("p (h c) -> p h c", h=8), ['T1'], [('BON', cb)])
                pass
                mm(P, pC0[:], tri[:], W['LD'][:], True, True, ['tri', 'LD'], ['pC0'])
                mm(P, pC1[:], ones[:], W['LD'][:], True, True, ['ones', 'LD'], ['pC1'])
                actf(P, W['E1'][:], pC0[:], AF.Exp, ['pC0'], ['E1'])
                actf(P, W['E2'][:], pC0[:], AF.Exp, ['pC0'], ['E2'], scale=-1.0)
                actf(P, W['GT'][:], pC1[:], AF.Exp, ['pC1'], ['GT'])
                actf(P, W['Dinv'][:], W['LD'][:], AF.Exp, ['LD'], ['Dinv'], scale=-1.0)
                tt(P, 'dve', W['E3'][:], W['E1'][:], W['Dinv'][:], ALU.mult, ['E1', 'Dinv'], ['E3'])
                tt(P, 'pool', W['E4'][:], W['GT'][:], W['E2'][:], ALU.mult, ['GT', 'E2'], ['E4'])
                pass
                for pp in range(4):
                    mm(P, pC1[:, pp:pp + 1], W['GT'][:, pp * 128:(pp + 1) * 128], onehot[:, 0:1], True, True, ['GT', 'onehot'], ['pC1'])
                cp(P, 'act', gCs[b][:], pC1[:, 0:4], ['pC1'], [('gCs', b)])
                pass
                tt(P, 'dve', TM[b][:, 1, :], Rb[b][:], W['E1'][:], ALU.mult, [('Rb', b), 'E1'], [('TM', b, 1)])
                tt(P, 'dve', TM[b][:, 2, :], W['KMf'][:], W['E2'][:], ALU.mult, ['KMf', 'E2'], [('TM', b, 2)])
                tt(P, 'pool', TM[b][:, 3, :], W['Bf'][:], W['E2'][:], ALU.mult, ['Bf', 'E2'], [('TM', b, 3)])
                tt(P, 'dve', TM[b][:, 0, :], W['KKf'][:], W['E3'][:], ALU.mult, ['KKf', 'E3'], [('TM', b, 0)])
                tt(P, 'pool', KVb[b][:], W['KMf'][:], W['E4'][:], ALU.mult, ['KMf', 'E4'], [('KVb', b)])
                tt(P, 'pool', BVb[b][:], W['Bf'][:], W['E4'][:], ALU.mult, ['Bf', 'E4'], [('BVb', b)])
                pass
                for hf in range(2):
                    for pq in range(2):
                        pp = hf * 2 + pq
                        for kd in range(4):
                            tr(P, pTP[:, (pq * 4 + kd) * 128:(pq * 4 + kd + 1) * 128], TM[b][:, kd, pp * 128:(pp + 1) * 128], identb[:],
                               [('TM', b, kd), 'identb'], [('pTP', 0)])
                    cp(P, 'act' if hf == 0 else 'dve', FT[b][:, 2 * hf:2 * hf + 2].rearrange("p a k t -> p (a k t)"), pTP[:, 0:1024],
                       [('pTP', 0)], [('FT', b, hf)])
                pass
                for g in range(2):
                    ftk = ('FT', b, g)
                    for i in range(4):
                        pp, h2 = 2 * g + i // 2, i % 2
                        fts = FT[b][h2 * 64:(h2 + 1) * 64, pp]
                        rhs2 = fts[:, 0:2, :].rearrange("p k t -> p (k t)")
                        mm(P, pPA[:, 0:256], fts[:, 2, :], rhs2, True, True, [ftk], ['pPA'])
                        mm(P, pPA[:, 256:512], fts[:, 3, :], rhs2, True, True, [ftk], ['pPA'])
                        tt(P, 'dve', AK[:, i, :], pPA[:], mask4[:], ALU.mult, ['pPA', 'mask4'], [('AK', i)])
                        mm(P, pRX[:, i * 128:(i + 1) * 128], fts[:, 0, :], fts[:, 3, :], True, True, [ftk], ['pRX'])
                    tt(P, 'dve', MT[:], pRX[:].rearrange("p (a t) -> p a t", a=4), negsl[:].unsqueeze(1).to_broadcast([128, 4, 128]), ALU.mult,
                       ['pRX', 'negsl'], ['MT'])
                    tt(P, 'pool', Rm[0][:], AK[:, :, 256:384], identb[:].unsqueeze(1).to_broadcast([128, 4, 128]), ALU.add,
                       [('AK', i) for i in range(4)] + ['identb'], [('Rm', 0)])
                    pass
                    cur = 0
                    for lev in range(1, 7):
                        nxt = lev % 2
                        last = (lev == 6)
                        for i in range(4):
                            if lev == 1:
                                Pc, PTc = AK[:, i, 256:384], MT[:, i, :]
                                rk = [('AK', i), 'MT']
                            else:
                                Pc, PTc = PP[1 - nxt][:, i, 0, :], PP[1 - nxt][:, i, 1, :]
                                rk = [('PP', 1 - nxt)]
                            if not last:
                                mm(P, pPQ[:, i, 0, :], PTc, Pc, True, True, rk, ['pPQ'])
                            mm(P, pPQ[:, i, 1, :], Pc, PTc, True, True, rk, ['pPQ'])
                        if not last:
                            cp(P, 'act', PP[nxt][:], pPQ[:], ['pPQ'], [('PP', nxt)])
                        else:
                            cp(P, 'act', PP[nxt][:, :, 1, :], pPQ[:, :, 1, :], ['pPQ'], [('PP', nxt)])
                        for i in range(4):
                            mm(P, pRX[:, i * 128:(i + 1) * 128], PP[nxt][:, i, 1, :], Rm[cur][:, i, :], True, True,
                               [('PP', nxt), ('Rm', cur)], ['pRX'])
                        tt(P, 'dve', Rm[1 - cur][:], pRX[:].rearrange("p (a t) -> p a t", a=4), Rm[cur][:], ALU.add,
                           ['pRX', ('Rm', cur)], [('Rm', 1 - cur)])
                        cur = 1 - cur
                    pass
                    Rf = Rm[cur]
                    for i in range(4):
                        pp, h2 = 2 * g + i // 2, i % 2
                        gp = cb * 4 + pp
                        hh = 4 * g + i
                        fts = FT[b][h2 * 64:(h2 + 1) * 64, pp]
                        mm(P, pRX[:, i * 64:(i + 1) * 64], fts[:, 0, :], STb[h2 * 64:(h2 + 1) * 64, gp, :], True, False,
                           [ftk, ('STb', gp)], ['pRX'])
                        mm(P, pRX[:, i * 64:(i + 1) * 64], AK[:, i, 0:128], Vb[b][:, hh * 64:(hh + 1) * 64], False, True,
                           [('AK', i), ('Vb', b)], ['pRX'])
                    cp(P, 'act', Xb[:], pRX[:, 0:256], ['pRX'], ['Xb'])
                    for i in range(4):
                        mm(P, pRX[:, 256 + i * 64:256 + (i + 1) * 64], Rf[:, i, :], Xb[:, i * 64:(i + 1) * 64], True, True,
                           [('Rm', cur), 'Xb'], ['pRX'])
                    P.op('act', lambda e: e.mul(out=nSA[:], in_=pRX[:, 256:512], mul=-1.0), ['pRX'], ['nSA'])
                    pass
                    for i in range(4):
                        pp, h2 = 2 * g + i // 2, i % 2
                        gp = cb * 4 + pp
                        hh = 4 * g + i
                        fts = FT[b][h2 * 64:(h2 + 1) * 64, pp]
                        o = pYS[:, i * 64:(i + 1) * 64]
                        mm(P, o, fts[:, 1, :], STb[h2 * 64:(h2 + 1) * 64, gp, :], True, False, [ftk, ('STb', gp)], ['pYS'])
                        mm(P, o, AK[:, i, 128:256], Vb[b][:, hh * 64:(hh + 1) * 64], False, False, [('AK', i), ('Vb', b)], ['pYS'])
                        mm(P, o, AK[:, i, 384:512], nSA[:, i * 64:(i + 1) * 64], False, True, [('AK', i), 'nSA'], ['pYS'])
                    for q in range(2):
                        pp = 2 * g + q
                        o = pYS[:, 256 + q * 128:256 + (q + 1) * 128]
                        mm(P, o, KVb[b][:, pp * 128:(pp + 1) * 128], Vb[b][:, pp * 128:(pp + 1) * 128], True, False,
                           [('KVb', b), ('Vb', b)], ['pYS'])
                        mm(P, o, BVb[b][:, pp * 128:(pp + 1) * 128], nSA[:, q * 128:(q + 1) * 128], False, True,
                           [('BVb', b), 'nSA'], ['pYS'])
                    cp(P, 'act', Ycb[:, g * 256:(g + 1) * 256], pYS[:, 0:256], ['pYS'], [('Ycb', g)])
                    for q in range(2):
                        pp = 2 * g + q
                        gp = cb * 4 + pp
                        for h2 in range(2):
                            sl = slice(h2 * 64, (h2 + 1) * 64)
                            ts(P, 'dve', stmp[sl, :], ST[sl, gp, :], gCs[b][sl, pp:pp + 1], None, ALU.mult, None,
                               [('ST', gp), ('gCs', b)], ['stmp'])
                            tt(P, 'dve', ST[sl, gp, :], pYS[sl, 256 + q * 128 + h2 * 64:256 + q * 128 + (h2 + 1) * 64], stmp[sl, :], ALU.add,
                               ['pYS', 'stmp'], [('ST', gp)])
                        cp(P, 'pool', STb[:, gp, :], ST[:, gp, :], [('ST', gp)], [('STb', gp)])
                pass
                Y3 = Ycb[:].rearrange("p (h c) -> p h c", h=8)
                yk = [('Ycb', 0), ('Ycb', 1)]
                red(P, 'dve', sm[:, 8:16], Y3, yk, [('sm', 1)])
                tt(P, 'pool', EY['Ysq'][:], Ycb[:], Ycb[:], ALU.mult, yk, ['Ysq'])
                red(P, 'dve', sm[:, 16:24], EY['Ysq'][:].rearrange("p (h c) -> p h c", h=8), ['Ysq'], [('sm', 2)])
                ts(P, 'dve', sm[:, 8:16], sm[:, 8:16], 1.0 / 64, None, ALU.mult, None, [('sm', 1)], [('sm', 1)])
                tt(P, 'dve', sm[:, 24:32], sm[:, 8:16], sm[:, 8:16], ALU.mult, [('sm', 1)], [('sm', 3)])
                stt(P, 'dve', sm[:, 16:24], sm[:, 16:24], 1.0 / 64, sm[:, 24:32], ALU.mult, ALU.subtract, [('sm', 2), ('sm', 3)], [('sm', 2)])
                ts(P, 'dve', sm[:, 16:24], sm[:, 16:24], GN_EPS, None, ALU.add, None, [('sm', 2)], [('sm', 2)])
                P.op('act', lambda e: e.sqrt(out=sm[:, 16:24], in_=sm[:, 16:24]), [('sm', 2)], [('sm', 2)])
                recip(P, sm[:, 16:24], sm[:, 16:24], [('sm', 2)], [('sm', 2)])
                Yn3 = EY['Yn'][:].rearrange("p (h c) -> p h c", h=8)
                tt(P, 'dve', Yn3, Y3, sm[:, 8:16].unsqueeze(2).to_broadcast([128, 8, 64]), ALU.subtract, yk + [('sm', 1)], ['Yn'])
                tt(P, 'dve', Yn3, Yn3, sm[:, 16:24].unsqueeze(2).to_broadcast([128, 8, 64]), ALU.mult, ['Yn', ('sm', 2)], ['Yn'])
                tt(P, 'pool', EY['Yn'][:], EY['Yn'][:], prm(5), ALU.mult, ['Yn', ('PRM', b)], ['Yn'])
                tt(P, 'pool', EY['Yn'][:], EY['Yn'][:], prm(6), ALU.add, ['Yn', ('PRM', b)], ['Yn'])
                tt(P, 'dve', EY['Ysq'][:].rearrange("p (h c) -> p h c", h=8), Vb[b][:].rearrange("p (h c) -> p h c", h=8),
                   BON[:, cb * 8:(cb + 1) * 8].unsqueeze(2).to_broadcast([128, 8, 64]), ALU.mult, [('Vb', b), ('BON', cb)], ['Ysq'])
                tt(P, 'pool', EY['Yn'][:], EY['Yn'][:], EY['Ysq'][:], ALU.add, ['Yn', 'Ysq'], ['Yn'])
                actf(P, EY['Sz'][:], Zb[b][:], AF.Silu, [('Zb', b)], ['Sz'])
                tt(P, 'dve', YG[:], EY['Yn'][:], EY['Sz'][:], ALU.mult, ['Yn', 'Sz'], ['YG'])
                for q in range(4):
                    tr(P, pTP[:, q * 128:(q + 1) * 128], YG[:, q * 128:(q + 1) * 128], identb[:], ['YG', 'identb'], [('pTP', 0)])
                cp(P, 'act', ygT[:, cb * 4:(cb + 1) * 4, :].rearrange("p a t -> p (a t)"), pTP[:, 0:512], [('pTP', 0)], ['ygT'])
            pass
            if ch < 17:
                P.dma('sp', d['ygT'][:, :, ch * 128:(ch + 1) * 128], ygT[:], reads=['ygT'], key='ygTst')
            elif ch == 17:
                P.dma('sp', d['ygT'][:, :, SROW0:SROW0 + 128], ygT[:], reads=['ygT'], key='ygTst')
            else:
                s = ch - 17
                P.dma('sp', d['ygT'][:, :, SROW0 + s:SROW0 + s + 1], ygT[:, :, 0:1], reads=['ygT'], key='ygTst',
                      allow_slow_non_contiguous=True)
            if ch == 16:
                state_save(d['o_pwkv'])
            if ch >= 17:
                state_save(d['o_swkv'][ch - 17])
        P.flush()


def stage_outproj(P, nc, d, actn, wsrc, outn, tag):
    for half in range(2):
        with ExitStack() as es:
            tiles = list(range(half * 9, half * 9 + 9))
            actT = mk(nc, es, "actT", [128, 32, 9 * 128], BF16)
            ob = [mk(nc, es, f"obO{i}", [128, 512], F32) for i in range(4)]
            P.dma('sp', actT[:], d[actn][:, :, half * 1152:(half + 1) * 1152], writes=['actT'], key='actT')
            wt = [mk(nc, es, f"wtO{i}", [128, 32, 512], BF16) for i in range(2)]
            ps = [mk(nc, es, f"psO{i}", [128, 512], F32, psum=True) for i in range(4)]
            wv = wsrc.rearrange("(c p) n -> p c n", p=128)
            cnt = 0
            for cb in range(4):
                wb = cb % 2
                P.dma('pool', wt[wb][:], wv[:, :, cb * 512:(cb + 1) * 512], writes=[('wt', wb)], key=('wt', wb))
                for tl, t in enumerate(tiles):
                    pb = cnt % 4
                    cnt += 1
                    for c in range(32):
                        mm(P, ps[pb][:], actT[:, c, tl * 128:(tl + 1) * 128], wt[wb][:, c, :], c == 0, c == 31,
                           ['actT', ('wt', wb)], [('ps', pb)])
                    cp(P, 'act' if cnt % 2 else 'dve', ob[pb][:], ps[pb][:], [('ps', pb)], [('ob', pb)])
                    P.dma('sp', d[outn][t * 128:(t + 1) * 128, cb * 512:(cb + 1) * 512], ob[pb][:], reads=[('ob', pb)], key=('obst', pb))
            P.flush()


def rms_stats(P, x, junk, st, xk, sk):
    P.op('pool', lambda e: e.memset(st[:, 0:1], 0.0), writes=[sk])
    P.op('act', lambda e: e.activation(out=junk[:], in_=x[:], func=AF.Square, accum_out=st[:, 0:1]), reads=[xk], writes=['junk', sk])
    ts(P, 'dve', st[:, 1:2], st[:, 0:1], 1.0 / D, RMS_EPS, ALU.mult, ALU.add, [sk], [sk])
    P.op('act', lambda e: e.sqrt(out=st[:, 1:2], in_=st[:, 1:2]), [sk], [sk])
    recip(P, st[:, 1:2], st[:, 1:2], [sk], [sk])


def stage_F2(P, nc, d):
    with ExitStack() as es:
        gkv = mk(nc, es, "gkv", [128, D], F32)
        gb = mk(nc, es, "gb", [128, D], F32)
        ident = mk(nc, es, "identF2", [128, 128], F32)
        junk = mk(nc, es, "junkF", [128, D], F32)
        xt = [mk(nc, es, f"xF{i}", [128, D], F32) for i in range(2)]
        ot = [mk(nc, es, f"oF{i}", [128, D], F32) for i in range(2)]
        hn = [mk(nc, es, f"hnF{i}", [128, D], F32) for i in range(2)]
        st = [mk(nc, es, f"stF{i}", [128, 2], F32) for i in range(2)]
        hT = [mk(nc, es, f"hTF{i}", [128, 16, 128], BF16) for i in range(2)]
        ps = [mk(nc, es, f"psF{i}", [128, 512], F32, psum=True) for i in range(4)]
        P.dma('sp', gkv[:], d['kv_norm'][0].partition_broadcast(128), writes=['gkv'], key='gkv')
        P.dma('sp', gb[:], d['b_norm'][0].partition_broadcast(128), writes=['gb'], key='gb')
        P.dma('sp', ident[:], d['ident'], writes=['ident'], key='ident')
        ev = 0
        for t in range(NT):
            b = t % 2
            P.dma('sp', xt[b][:], d['xin'][1 + 128 * t:1 + 128 * t + 128, :], writes=[('x', b)], key=('x', b))
            P.dma('sp', ot[b][:], d['o1'][128 * t:128 * t + 128, :], writes=[('o', b)], key=('o', b))
            tt(P, 'dve', xt[b][:], xt[b][:], ot[b][:], ALU.add, [('x', b), ('o', b)], [('x', b)])
            P.dma('sp', d['hp'][128 * t:128 * t + 128, :], xt[b][:], reads=[('x', b)], key=('hpst', b))
            rms_stats(P, xt[b], junk, st[b], ('x', b), ('st', b))
            for vi, (g, gk, dst) in enumerate(((gkv, 'gkv', 'hkvT'), (gb, 'gb', 'hbT'))):
                hb = (2 * t + vi) % 2
                stt(P, 'dve', hn[hb][:], xt[b][:], st[b][:, 1:2], g[:], ALU.mult, ALU.mult,
                    [('x', b), ('st', b), gk], [('hn', hb)])
                for q in range(4):
                    pb = ev % 4
                    for j in range(4):
                        c = q * 4 + j
                        tr(P, ps[pb][:, j * 128:(j + 1) * 128], hn[hb][:, c * 128:(c + 1) * 128], ident[:], [('hn', hb), 'ident'], [('ps', pb)])
                    cp(P, 'act' if ev % 2 == 0 else 'dve', hT[hb][:, q * 4:(q + 1) * 4, :], ps[pb][:].rearrange("p (a n) -> p a n", a=4),
                       [('ps', pb)], [('hT', hb)])
                    ev += 1
                P.dma('sp', d[dst][:, :, t * 128:(t + 1) * 128], hT[hb][:], reads=[('hT', hb)], key=('hTst', hb))
        P.flush()


def rotary(P, eng, Kt, cs, tmp, kk, csk, tk):
    nh = Kt.shape[1]
    cosb = cs[:, 0:8].unsqueeze(1).to_broadcast([128, nh, 8])
    sinb = cs[:, 8:16].unsqueeze(1).to_broadcast([128, nh, 8])
    x1, x2 = Kt[:, :, 0:8], Kt[:, :, 8:16]
    t = [tmp[:, i, 0:nh, :] for i in range(4)]
    tt(P, eng, t[0], x1, cosb, ALU.mult, [kk, csk], [tk])
    tt(P, eng, t[1], x2, sinb, ALU.mult, [kk, csk], [tk])
    tt(P, eng, t[2], x2, cosb, ALU.mult, [kk, csk], [tk])
    tt(P, eng, t[3], x1, sinb, ALU.mult, [kk, csk], [tk])
    tt(P, eng, x1, t[0], t[1], ALU.subtract, [tk], [kk])
    tt(P, eng, x2, t[2], t[3], ALU.add, [tk], [kk])


def stage_G(P, nc, d):
    with ExitStack() as es:
        actT = mk(nc, es, "actT", [128, 16, T], BF16)
        CS = mk(nc, es, "CS", [128, NT, 16], F32)
        ob = [mk(nc, es, f"obG{i}", [128, 512], F32) for i in range(4)]
        obb = [mk(nc, es, f"obbG{i}", [128, 512], BF16) for i in range(4)]
        tmp = [mk(nc, es, f"tmpG{i}", [128, 4, 8, 8], F32) for i in range(4)]
        P.dma('sp', actT[:], d['hkvT'], writes=['actT'], key='actT')
        P.dma('sp', CS[:], d['cs'].rearrange("(t p) c -> p t c", p=128), writes=['CS'], key='CS')
        for s in range(NS):
            P.dma('sp', d['o_sck'][s, 0:127, :], d['ck'][s, 1:128, :], key=('cpk', s))
            P.dma('sp', d['o_scv'][s, 0:127, :], d['cv'][s, 1:128, :], key=('cpv', s))

        def evac(t, cb, pst, pkey, cnt):
            o = cnt % 4
            cp(P, 'act', ob[o][:], pst[:], [pkey], [('ob', o)])
            if cb == 0:
                rotary(P, 'dve', ob[o][:].rearrange("p (h c) -> p h c", h=8), CS[:, t, :], tmp[o], ('ob', o), 'CS', ('tmp', o))
            cp(P, 'pool', obb[o][:], ob[o][:], [('ob', o)], [('obb', o)])
            P.dma('sp', d['KVs'][t * 128:(t + 1) * 128, cb * 512:(cb + 1) * 512], obb[o][:], reads=[('obb', o)], key=('obbst', o))
            if t == 16:
                P.dma('sp', d['o_pck' if cb == 0 else 'o_pcv'], ob[o][:], reads=[('ob', o)], key=('obst', o))
            if t == 17:
                P.dma('sp', d['o_sck' if cb == 0 else 'o_scv'][:, 127, :], ob[o][0:NS, :], reads=[('ob', o)], key=('obst', o))
        gemm_tokmajor(P, nc, es, None, 16, d['w_kv'], 1024, evac, "G", actT=actT)
        P.flush()


def stage_H(P, nc, d):
    with ExitStack() as es:
        actT = mk(nc, es, "actT", [128, 16, T], BF16)
        CS = mk(nc, es, "CS", [128, NT, 16], F32)
        ob = [mk(nc, es, f"obH{i}", [128, 512], F32) for i in range(4)]
        obb = [mk(nc, es, f"obbH{i}", [128, 512], BF16) for i in range(4)]
        tmp = [mk(nc, es, f"tmpH{i}", [128, 4, 8, 8], F32) for i in range(4)]
        P.dma('sp', actT[:], d['hbT'], writes=['actT'], key='actT')
        P.dma('sp', CS[:], d['cs'].rearrange("(t p) c -> p t c", p=128), writes=['CS'], key='CS')

        def evac(t, cb, pst, pkey, cnt):
            o = cnt % 4
            if cb < 8:
                cp(P, 'act', ob[o][:], pst[:], [pkey], [('ob', o)])
                rotary(P, 'dve' if cnt % 2 else 'pool', ob[o][:].rearrange("p (h c) -> p h c", h=8), CS[:, t, :], tmp[o], ('ob', o), 'CS', ('tmp', o))
                cp(P, 'pool' if cnt % 2 else 'dve', obb[o][:], ob[o][:], [('ob', o)], [('obb', o)])
                P.dma('sp', d['qs'][t * 128:(t + 1) * 128, cb * 512:(cb + 1) * 512], obb[o][:], reads=[('obb', o)], key=('obbst', o))
            else:
                cp(P, 'act' if cnt % 2 else 'dve', obb[o][:], pst[:], [pkey], [('obb', o)])
                P.dma('sp', d['zs'][t * 128:(t + 1) * 128, (cb - 8) * 512:(cb - 7) * 512], obb[o][:], reads=[('obb', o)], key=('obbst', o))
        gemm_tokmajor(P, nc, es, None, 16, d['b_w_qz'][0], 8192, evac, "H", actT=actT)
        P.flush()


def stage_I(P, nc, d):
    with ExitStack() as es:
        identb = mk(nc, es, "identBI", [128, 128], BF16)
        MB = mk(nc, es, "MB", [128, 4, 128], BF16)
        esink = mk(nc, es, "esink", [128, 64], F32)
        P.dma('pool', identb[:], d['ident'], writes=['identb'], key='identb')
        P.dma('pool', MB[:], d['mb'], writes=['MB'], key='MB')
        P.dma('sp', esink[:], d['b_sinks'][0].partition_broadcast(128), writes=['esink'], key='esink')
        actf(P, esink[:], esink[:], AF.Exp, ['esink'], ['esink'])
        NSL = 3
        KVt = [mk(nc, es, f"KVt{i}", [128, 1024], BF16) for i in range(NSL)]
        Kd = mk(nc, es, "Kd", [128, 8, 2, 64], BF16)
        KT2 = [mk(nc, es, f"KT2{i}", [128, 8, 128], BF16) for i in range(NSL)]
        VA = [mk(nc, es, f"VA{i}", [128, 8, 65], BF16) for i in range(NSL)]
        Qt = [mk(nc, es, f"Qt{i}", [128, E], BF16) for i in range(2)]
        Zt = [mk(nc, es, f"Zt{i}", [128, E], BF16) for i in range(2)]
        QT = mk(nc, es, "QTt", [128, 32, 128], BF16)
        PTs = [mk(nc, es, f"PTs{i}", [128, 4, 128], BF16) for i in range(2)]
        ATT = mk(nc, es, "ATT", [128, E], F32)
        SZ = mk(nc, es, "SZ", [128, E], F32)
        AG = mk(nc, es, "AG", [128, E], BF16)
        agT = mk(nc, es, "agT", [128, 32, 128], BF16)
        den = mk(nc, es, "den", [128, 8], F32)
        pSC = [mk(nc, es, f"pSC{i}", [128, 512], F32, psum=True) for i in range(2)]
        pAO = [mk(nc, es, f"pAO{i}", [128, 512], F32, psum=True) for i in range(2)]
        pTP = [mk(nc, es, f"pTPI{i}", [128, 1024], BF16, psum=True) for i in range(2)]
        for i in range(NSL):
            P.op('pool', lambda e, i=i: e.memset(VA[i][:, :, 64:65], 1.0), writes=[('VA', i)])

        def prep_kv(slot):
            k3 = KVt[slot][:, 0:512].rearrange("p (g c) -> p g c", g=8)
            cp(P, 'pool', Kd[:, :, 0, :], k3, [('KVt', slot)], ['Kd'])
            cp(P, 'pool', Kd[:, :, 1, :], k3, [('KVt', slot)], ['Kd'])
            for g in range(8):
                tr(P, pTP[0][:, g * 128:(g + 1) * 128], Kd[:, g].rearrange("p a c -> p (a c)"), identb[:], ['Kd', 'identb'], [('pTP', 0)])
            cp(P, 'act', KT2[slot][:].rearrange("p g t -> p (g t)"), pTP[0][:], [('pTP', 0)], [('KT2', slot)])
            cp(P, 'dve', VA[slot][:, :, 0:64], KVt[slot][:, 512:1024].rearrange("p (g c) -> p g c", g=8), [('KVt', slot)], [('VA', slot)])

        slot_of_tile = {}
        nslot = 0
        for ch in range(NCH):
            qb = ch % 2
            load_rows(P, 'sp', Qt[qb], d['qs'], ch, 0, E, [('Qt', qb)], ('Qt', qb))
            load_rows(P, 'sp', Zt[qb], d['zs'], ch, 0, E, [('Zt', qb)], ('Zt', qb))
            if ch < 17:
                cs_ = nslot % NSL
                nslot += 1
                P.dma('sp', KVt[cs_][:], d['KVs'][ch * 128:(ch + 1) * 128, :], writes=[('KVt', cs_)], key=('KVt', cs_))
                prep_kv(cs_)
                slot_of_tile[ch] = cs_
                ps_ = slot_of_tile.get(ch - 1)
                mcur, mprev = (2, None) if ch == 0 else ((0, 3) if ch == 1 else (0, 1))
            else:
                s = ch - 17
                ps_ = nslot % NSL
                nslot += 1
                P.dma('pool', KVt[ps_][:, 0:512], d['ck'][s], writes=[('KVt', ps_)], key=('KVt', ps_))
                P.dma('pool', KVt[ps_][:, 512:1024], d['cv'][s], writes=[('KVt', ps_)], key=('KVt', ps_))
                prep_kv(ps_)
                cs_ = nslot % NSL
                nslot += 1
                load_rows(P, 'sp', KVt[cs_], d['KVs'], ch, 0, 1024, [('KVt', cs_)], ('KVt', cs_))
                prep_kv(cs_)
                mcur, mprev = 0, 1
            for q8 in range(4):
                tp = pTP[q8 % 2]
                for j in range(8):
                    pr = q8 * 8 + j
                    tr(P, tp[:, j * 128:(j + 1) * 128], Qt[qb][:, pr * 128:(pr + 1) * 128], identb[:], [('Qt', qb), 'identb'], [('pTP', q8 % 2)])
                cp(P, 'act' if q8 % 2 else 'dve', QT[:, q8 * 8:(q8 + 1) * 8, :].rearrange("p a t -> p (a t)"), tp[:], [('pTP', q8 % 2)], [('QT', q8)])
            kts = ([(ps_, mprev)] if ps_ is not None and mprev is not None else []) + [(cs_, mcur)]
            for j in range(32):
                g = j // 4
                sc = pSC[j % 2]
                pt = PTs[j % 2]
                for h2 in range(2):
                    sl = slice(h2 * 64, (h2 + 1) * 64)
                    for ki, (slot, mk_) in enumerate(kts):
                        o = sc[:, (h2 * 2 + ki) * 128:(h2 * 2 + ki + 1) * 128]
                        mm(P, o, KT2[slot][sl, g, :], QT[sl, j, :], True, False, [('KT2', slot), ('QT', j // 8)], [('pSC', j % 2)])
                        mm(P, o, identb[:], MB[:, mk_, :], False, True, ['identb', 'MB'], [('pSC', j % 2)])
                nk = len(kts)
                if nk == 2:
                    actf(P, pt[:].rearrange("p a t -> p (a t)"), sc[:], AF.Exp, [('pSC', j % 2)], [('PTs', j % 2)], scale=0.125)
                else:
                    for h2 in range(2):
                        actf(P, pt[:, h2 * 2, :], sc[:, h2 * 256:h2 * 256 + 128], AF.Exp, [('pSC', j % 2)], [('PTs', j % 2)], scale=0.125)
                ao = pAO[(j // 2) % 2]
                for h2 in range(2):
                    col = ((j % 2) * 2 + h2) * 65
                    for ki, (slot, mk_) in enumerate(kts):
                        mm(P, ao[:, col:col + 65], pt[:, h2 * 2 + ki, :], VA[slot][:, g, :], ki == 0, ki == nk - 1,
                           [('PTs', j % 2), ('VA', slot)], [('pAO', (j // 2) % 2)])
                if j % 2 == 1:
                    h0 = (j - 1) * 2
                    ao3 = ao[:, 0:260].rearrange("p (h c) -> p h c", h=4)
                    ak = ('pAO', (j // 2) % 2)
                    tt(P, 'dve', den[:, 0:4], ao3[:, :, 64], esink[:, h0:h0 + 4], ALU.add, [ak, 'esink'], ['den'])
                    recip(P, den[:, 4:8], den[:, 0:4], ['den'], ['den'])
                    tt(P, 'dve', ATT[:, h0 * 64:(h0 + 4) * 64].rearrange("p (h c) -> p h c", h=4), ao3[:, :, 0:64],
                       den[:, 4:8].unsqueeze(2).to_broadcast([128, 4, 64]), ALU.mult, [ak, 'den'], ['ATT'])
            actf(P, SZ[:], Zt[qb][:], AF.Silu, [('Zt', qb)], ['SZ'])
            tt(P, 'pool', AG[:], ATT[:], SZ[:], ALU.mult, ['ATT', 'SZ'], ['AG'])
            for q8 in range(4):
                tp = pTP[q8 % 2]
                for j in range(8):
                    pr = q8 * 8 + j
                    tr(P, tp[:, j * 128:(j + 1) * 128], AG[:, pr * 128:(pr + 1) * 128], identb[:], ['AG', 'identb'], [('pTP', q8 % 2)])
                cp(P, 'act' if q8 % 2 else 'dve', agT[:, q8 * 8:(q8 + 1) * 8, :].rearrange("p a t -> p (a t)"), tp[:], [('pTP', q8 % 2)], ['agT'])
            if ch < 17:
                P.dma('sp', d['agT'][:, :, ch * 128:(ch + 1) * 128], agT[:], reads=['agT'], key='agTst')
            elif ch == 17:
                P.dma('sp', d['agT'][:, :, SROW0:SROW0 + 128], agT[:], reads=['agT'], key='agTst')
            else:
                P.dma('sp', d['agT'][:, :, SROW0 + ch - 17:SROW0 + ch - 16], agT[:, :, 0:1], reads=['agT'], key='agTst',
                      allow_slow_non_contiguous=True)
        P.flush()


def stage_J2(P, nc, d):
    with ExitStack() as es:
        gf = mk(nc, es, "gf", [128, D], F32)
        junk = mk(nc, es, "junkJ", [128, D], F32)
        xt = [mk(nc, es, f"xJ{i}", [128, D], F32) for i in range(2)]
        ot = [mk(nc, es, f"oJ{i}", [128, D], F32) for i in range(2)]
        st = [mk(nc, es, f"stJ{i}", [128, 2], F32) for i in range(2)]
        P.dma('sp', gf[:], d['final_norm'][0].partition_broadcast(128), writes=['gf'], key='gf')
        for t in range(1, NT):
            b = t % 2
            P.dma('sp', xt[b][:], d['hp'][128 * t:128 * t + 128, :], writes=[('x', b)], key=('x', b))
            P.dma('sp', ot[b][:], d['o2'][128 * t:128 * t + 128, :], writes=[('o', b)], key=('o', b))
            tt(P, 'dve', xt[b][:], xt[b][:], ot[b][:], ALU.add, [('x', b), ('o', b)], [('x', b)])
            rms_stats(P, xt[b], junk, st[b], ('x', b), ('st', b))
            stt(P, 'dve', ot[b][:], xt[b][:], st[b][:, 1:2], gf[:], ALU.mult, ALU.mult, [('x', b), ('st', b), 'gf'], [('o', b)])
            if t < 17:
                P.dma('sp', d['o_yp'][(t - 1) * 128:t * 128, :], ot[b][:], reads=[('o', b)], key=('yst', b))
            else:
                P.dma('sp', d['o_ys'], ot[b][0:NS, :], reads=[('o', b)], key=('yst', b))
        P.flush()


class LazyDram(dict):
    def __init__(self, nc, debug_outs, ext_in):
        super().__init__()
        self.nc, self.debug_outs, self.ext_in = nc, debug_outs, ext_in
        self.spec = {}
        self.inputs, self.outputs = [], []

    def __missing__(self, name):
        kind, shape, dt = self.spec[name]
        if kind == 'scr':
            kind = 'ExternalInput' if name in self.ext_in else ('ExternalOutput' if name in self.debug_outs else 'Internal')
        if kind == 'ExternalInput':
            self.inputs.append(name)
        if kind == 'ExternalOutput':
            self.outputs.append(name)
        ap = self.nc.dram_tensor(name, list(shape), dt, kind=kind).ap()
        self[name] = ap
        return ap


def build(debug_outs=(), stages='ABLCFfGHIJj', ext_in=()):
    nc = bass.Bass("TRN2", target_bir_lowering=False)
    d = LazyDram(nc, debug_outs, ext_in)

    def inp(name, shape, dt=F32):
        d.spec[name] = ('ExternalInput', shape, dt)

    def outp(name, shape, dt=F32):
        d.spec[name] = ('ExternalOutput', shape, dt)

    def scr(name, shape, dt):
        d.spec[name] = ('scr', shape, dt)

    inp('xin', [T + 1, D]); inp('sshift', [128, D]); inp('swkv', [NS, 64, 64, 64])
    inp('ck', [NS, 128, 512]); inp('cv', [NS, 128, 512])
    inp('a_norm', [1, D]); inp('muT', [128, 6, 16]); inp('ident', [128, 128]); inp('tri', [128, 128]); inp('ones', [128, 128])
    inp('onehot', [128, 1]); inp('lmask', [128, 2]); inp('mask4', [128, 512]); inp('negsl', [128, 128]); inp('mb', [128, 4, 128])
    inp('cs', [T, 16]); inp('prm', [7, E])
    inp('a_w_rkvz', [1, 4, D, E]); inp('a_w1', [1, D, 96]); inp('a_w2', [1, 96, E]); inp('a_a1', [1, D, 96]); inp('a_a2', [1, 96, E])
    inp('a_w_out', [1, E, D]); inp('kv_norm', [1, D]); inp('w_kv', [D, 1024]); inp('b_norm', [1, D]); inp('b_w_qz', [1, D, 2 * E])
    inp('b_sinks', [1, 64]); inp('b_w_o', [1, E, D]); inp('final_norm', [1, D])
    outp('o_yp', [2048, D]); outp('o_ys', [NS, D]); outp('o_pwkv', [64, 64, 64]); outp('o_pshift', [1, D])
    outp('o_pck', [128, 512]); outp('o_pcv', [128, 512]); outp('o_swkv', [NS, 64, 64, 64]); outp('o_sshift', [NS, D])
    outp('o_sck', [NS, 128, 512]); outp('o_scv', [NS, 128, 512])
    scr('xmT', [6, 128, 16, T], BF16); scr('rkvz', [4, T, E], BF16); scr('wpre', [T, E], F32); scr('apre', [T, E], F32)
    scr('ygT', [128, 32, T], BF16); scr('o1', [T, D], F32); scr('hp', [T, D], F32); scr('hkvT', [128, 16, T], BF16); scr('hbT', [128, 16, T], BF16)
    scr('KVs', [T, 1024], BF16); scr('qs', [T, E], BF16); scr('zs', [T, E], BF16); scr('agT', [128, 32, T], BF16); scr('o2', [T, D], F32)
    with ExitStack() as stack:
        P = Prog(nc, stack)
        if 'A' in stages:
            stage_A(P, nc, d)
        if 'B' in stages:
            stage_B(P, nc, d)
        if 'L' in stages:
            stage_B_lora(P, nc, d)
        if 'C' in stages:
            stage_CDE(P, nc, d)
        if 'F' in stages:
            stage_outproj(P, nc, d, 'ygT', d['a_w_out'][0], 'o1', 'F1')
        if 'f' in stages:
            stage_F2(P, nc, d)
        if 'G' in stages:
            stage_G(P, nc, d)
        if 'H' in stages:
            stage_H(P, nc, d)
        if 'I' in stages:
            stage_I(P, nc, d)
        if 'J' in stages:
            stage_outproj(P, nc, d, 'agT', d['b_w_o'][0], 'o2', 'J1')
        if 'j' in stages:
            stage_J2(P, nc, d)
    nc._lazy = d
    return nc


def host_tables():
    f = np.float32
    j = np.arange(128)
    su = (j[:, None] < j[None, :]).astype(f)
    u = (j[:, None] <= j[None, :]).astype(f)
    tb = {}
    tb['ident'] = np.eye(128, dtype=f)
    tb['tri'] = u.copy()
    tb['ones'] = np.ones((128, 128), f)
    oh = np.zeros((128, 1), f); oh[0, 0] = 1
    tb['onehot'] = oh
    c = f(-np.exp(-0.5))
    lm = np.zeros((128, 2), f); lm[:, 0] = c; lm[0, 1] = c
    tb['lmask'] = lm
    tb['mask4'] = np.concatenate([su, u, -su, u], 1)
    tb['negsl'] = -(su.T).copy()
    NEG = f(-30000.0)
    jj = j[:, None]; ii = j[None, :]
    cur = np.where(jj <= ii, 0, NEG).astype(f)
    prev = np.where(jj >= ii, 0, NEG).astype(f)
    lead = np.where(jj >= 112, 0, NEG).astype(f)
    mb = np.stack([cur, prev, np.minimum(cur, lead), np.minimum(prev, lead)], 1)
    tb['mb'] = np.ascontiguousarray(mb)
    pos = np.zeros(T, f)
    pos[112:2176] = np.arange(2064)
    pos[2176:2176 + NS] = 16384
    inv = (f(500000.0) ** (-np.arange(8, dtype=f) * f(2.0) / f(16))).astype(f)
    ang = (pos[:, None] * inv[None, :]).astype(f)
    tb['cs'] = np.concatenate([np.cos(ang), np.sin(ang)], 1).astype(f)
    return tb


_NC = [None]


def kernel(**inp):
    f = np.float32
    inp = {k: np.asarray(v) for k, v in inp.items()}
    if _NC[0] is None:
        _NC[0] = build()
    nc = _NC[0]
    tb = host_tables()
    mu = inp['a_mu'][0]
    muT = np.ascontiguousarray(mu.reshape(6, 16, 128).transpose(2, 0, 1))
    prm = np.ascontiguousarray(np.stack([inp['a_w0'][0], inp['a_a0'][0], inp['a_k_k'][0], inp['a_k_a'][0], inp['a_r_k'][0].reshape(-1),
                                         inp['a_gn_g'][0], inp['a_gn_b'][0]], 0).astype(f))
    shared = dict(tb)
    shared.update(muT=muT, prm=prm, a_norm=inp['a_norm'], a_w_rkvz=inp['a_w_rkvz'], a_w1=inp['a_w1'], a_w2=inp['a_w2'], a_a1=inp['a_a1'],
                  a_a2=inp['a_a2'], a_w_out=inp['a_w_out'], kv_norm=inp['kv_norm'].reshape(1, D), w_kv=inp['w_kv'], b_norm=inp['b_norm'],
                  b_w_qz=inp['b_w_qz'], b_sinks=inp['b_sinks'], b_w_o=inp['b_w_o'], final_norm=inp['final_norm'].reshape(1, D))
    in_maps = []
    for core in range(8):
        b = core % 4
        ss = slice(core * NS, core * NS + NS)
        xin = np.zeros((T + 1, D), f)
        xin[1 + 112:1 + 128] = inp['meta_tokens']
        xin[1 + 128:1 + 128 + 2048] = inp['x_prompt'][b]
        xin[1 + SROW0:1 + SROW0 + NS] = inp['x_sample'][ss, 0]
        sshift = np.zeros((128, D), f)
        sshift[:NS] = inp['state_shift'][0, ss]
        m = dict(shared)
        m.update(xin=xin, sshift=sshift, swkv=np.ascontiguousarray(inp['state_wkv'][0, ss]),
                 ck=np.ascontiguousarray(inp['cache_k'][ss].reshape(NS, 128, 512)), cv=np.ascontiguousarray(inp['cache_v'][ss].reshape(NS, 128, 512)))
        in_maps.append(m)
    in_maps = [{k: m[k] for k in nc._lazy.inputs if k in m} for m in in_maps]
    res = run_bass_kernel_spmd(nc, in_maps, core_ids=list(range(8)))
    R = res.results
    g = lambda c, n: np.asarray(R[c][n], dtype=f)
    y_prompt = np.stack([g(b, 'o_yp') for b in range(4)], 0)
    y_sample = np.concatenate([g(c, 'o_ys') for c in range(8)], 0)[:, None, :]
    p_wkv = np.stack([g(b, 'o_pwkv') for b in range(4)], 0)[None]
    p_shift = np.concatenate([g(b, 'o_pshift') for b in range(4)], 0)[None]
    p_ck = np.stack([g(b, 'o_pck') for b in range(4)], 0).reshape(4, 128, 8, 64)
    p_cv = np.stack([g(b, 'o_pcv') for b in range(4)], 0).reshape(4, 128, 8, 64)
    s_wkv = np.concatenate([g(c, 'o_swkv') for c in range(8)], 0)[None]
    s_shift = np.concatenate([g(c, 'o_sshift') for c in range(8)], 0)[None]
    s_ck = np.concatenate([g(c, 'o_sck') for c in range(8)], 0).reshape(32, 128, 8, 64)
    s_cv = np.concatenate([g(c, 'o_scv') for c in range(8)], 0).reshape(32, 128, 8, 64)
    return (y_prompt, y_sample, p_wkv, p_shift, p_ck, p_cv, s_wkv, s_shift, s_ck, s_cv)
```

```python
import numpy as np
from contextlib import ExitStack
import concourse.bass as bass
import concourse.mybir as mybir
from concourse.bass_utils import run_bass_kernel_spmd

F32 = mybir.dt.float32
BF16 = mybir.dt.bfloat16
AF = mybir.ActivationFunctionType
ALU = mybir.AluOpType
AX = mybir.AxisListType

COMPUTE = ('pe', 'act', 'dve', 'pool')
ALLENG = ('pe', 'act', 'dve', 'pool', 'sp')
SAME_ENGINE_SYNC = True
PIPELINE = True
PSUM_KEYS = {'pC0', 'pC1', 'pTP', 'pPQ', 'pPA', 'pRX', 'pYS', 'ps', 'psg', 'psL', 'pSC', 'pAO'}


class Prog:
    def __init__(self, nc, stack):
        self.nc = nc
        self.stack = stack
        self.esem = {e: stack.enter_context(nc.semaphore("s_" + e)) for e in COMPUTE}
        self.ecnt = {e: 0 for e in COMPUTE}
        self.dsem = {}
        self.dcnt = {}
        self.dsid = {}
        self.free_dsems = []
        self.nds = 0
        self.waited = {e: {} for e in ALLENG}
        self.reset()

    def reset(self):
        self.ops = []
        self.lastw = {}
        self.readers = {}
        self.chain = {}

    max_ops = None
    cap = None

    def op(self, eng, fn, reads=(), writes=(), key=None):
        if self.cap is not None:
            self.cap.append((eng, fn, list(reads), list(writes), key))
            return -1
        i = len(self.ops)
        if self.max_ops is not None and i >= self.max_ops:
            return -1
        pr = [r for r in reads if (r[0] if isinstance(r, tuple) else r) in PSUM_KEYS]
        if pr:
            reads = [r for r in reads if r not in pr]
            writes = list(writes) + [r for r in pr if r not in writes]
        deps = set()
        for r in reads:
            w = self.lastw.get(r)
            if w is not None:
                deps.add(w)
        for w_ in writes:
            w = self.lastw.get(w_)
            if w is not None:
                deps.add(w)
            deps.update(self.readers.get(w_, ()))
        if key is not None:
            prev = self.chain.get(key)
            if prev is not None:
                deps.add(prev)
            self.chain[key] = i
        self.ops.append(dict(eng=eng, fn=fn, deps=deps, key=key))
        for w_ in writes:
            self.lastw[w_] = i
            self.readers[w_] = []
        ws = set(writes)
        for r in reads:
            if r not in ws:
                self.readers.setdefault(r, []).append(i)
        return i

    def dma(self, q, out, in_, reads=(), writes=(), key=None, **kw):
        assert key is not None
        return self.op(q, lambda e: e.dma_start(out=out, in_=in_, **kw), reads, writes, key=key)

    def flush(self):
        nc = self.nc
        ops = self.ops
        if not ops:
            return
        needed = set()
        for o in ops:
            needed.update(o['deps'])
        lastop = {}
        for i, o in enumerate(ops):
            if o['key'] is None:
                lastop[o['eng']] = i
        needed.update(lastop.values())
        tgt = [None] * len(ops)
        for i, o in enumerate(ops):
            if o['key'] is not None:
                k = o['key']
                if k not in self.dsem:
                    if self.free_dsems:
                        self.dsem[k], self.dcnt[k], self.dsid[k] = self.free_dsems.pop()
                    else:
                        self.nds += 1
                        self.dsem[k] = self.stack.enter_context(nc.semaphore("d_" + str(self.nds)))
                        self.dcnt[k] = 0
                        self.dsid[k] = ('d', self.nds)
                self.dcnt[k] += 16
                tgt[i] = (self.dsem[k], self.dcnt[k], self.dsid[k])
            elif i in needed:
                e = o['eng']
                self.ecnt[e] += 1
                tgt[i] = (self.esem[e], self.ecnt[e], ('e', e))
        per = {e: [] for e in ALLENG}
        for i, o in enumerate(ops):
            per[o['eng']].append(i)
        end_waits = []
        for e in COMPUTE:
            if e in lastop:
                end_waits.append(tgt[lastop[e]])
        for k in self.chain:
            end_waits.append((self.dsem[k], self.dcnt[k], self.dsid[k]))

        def run(ename, eobj):
            waited = self.waited[ename]
            for i in per[ename]:
                o = ops[i]
                need = {}
                for d in o['deps']:
                    od = ops[d]
                    if od['key'] is None and od['eng'] == ename:
                        if ename == 'pe' or not SAME_ENGINE_SYNC:
                            continue
                    sem, val, sid = tgt[d]
                    if need.get(sid, (None, 0))[1] < val:
                        need[sid] = (sem, val)
                for sid, (sem, val) in need.items():
                    if waited.get(sid, 0) < val:
                        eobj.wait_ge(sem, val)
                        waited[sid] = val
                ins = o['fn'](eobj)
                if tgt[i] is not None:
                    if o['key'] is not None:
                        ins.then_inc(tgt[i][0], 16)
                    else:
                        ins.then_inc(tgt[i][0], 1)
            for sem, val, sid in end_waits:
                if waited.get(sid, 0) < val:
                    eobj.wait_ge(sem, val)
                    waited[sid] = val

        with nc.Block() as block:
            @block.tensor
            def _(e):
                run('pe', e)

            @block.scalar
            def _(e):
                run('act', e)

            @block.vector
            def _(e):
                run('dve', e)

            @block.gpsimd
            def _(e):
                run('pool', e)

            @block.sync
            def _(e):
                run('sp', e)
        for k in list(self.dsem):
            self.free_dsems.append((self.dsem[k], self.dcnt[k], self.dsid[k]))
        self.dsem, self.dcnt, self.dsid = {}, {}, {}
        self.reset()


NT = 18
T = NT * 128
D = 2048
E = 4096
NS = 4
RMS_EPS = 1e-6


class Ctx:
    pass


_uid = [0]


def mk(nc, es, name, shape, dt, psum=False):
    _uid[0] += 1
    name = f"{name}_u{_uid[0]}"
    if psum:
        return es.enter_context(nc.psum_tensor(name, shape, dt))
    return es.enter_context(nc.sbuf_tensor(name, shape, dt))


def stage_A(P, nc, d):
    with ExitStack() as es:
        gA = mk(nc, es, "gA", [128, D], F32)
        muT = mk(nc, es, "muT", [128, 6, 16], F32)
        ident = mk(nc, es, "identA", [128, 128], F32)
        xc = [mk(nc, es, f"xc{i}", [128, D], F32) for i in range(2)]
        xp = [mk(nc, es, f"xp{i}", [128, D], F32) for i in range(2)]
        junk = mk(nc, es, "junkA", [128, D], F32)
        st = [mk(nc, es, f"stA{i}", [128, 4], F32) for i in range(2)]
        xnT = [mk(nc, es, f"xnT{i}", [128, 16, 128], F32) for i in range(2)]
        xxT = [mk(nc, es, f"xxT{i}", [128, 16, 128], F32) for i in range(2)]
        tmp = [mk(nc, es, f"tmpA{i}", [128, 16, 128], F32) for i in range(2)]
        xm = [mk(nc, es, f"xmA{i}", [128, 16, 128], BF16) for i in range(3)]
        ps = [mk(nc, es, f"psA{i}", [128, 512], F32, psum=True) for i in range(4)]

        P.dma('sp', gA[:], d['a_norm'][0].partition_broadcast(128), writes=['gA'], key='gA')
        P.dma('sp', muT[:], d['muT'], writes=['muT'], key='muT')
        P.dma('sp', ident[:], d['ident'], writes=['ident'], key='ident')
        ev = 0
        mi = 0
        for t in range(NT):
            b = t % 2
            P.dma('sp', xc[b][:], d['xin'][1 + 128 * t: 1 + 128 * t + 128, :], writes=[('xc', b)], key=('xc', b))
            if t < NT - 1:
                P.dma('sp', xp[b][:], d['xin'][128 * t: 128 * t + 128, :], writes=[('xp', b)], key=('xp', b))
            else:
                P.dma('sp', xp[b][:], d['sshift'], writes=[('xp', b)], key=('xp', b))
            P.op('pool', lambda e, b=b: e.memset(st[b][:, 0:2], 0.0), writes=[('st', b, 0), ('st', b, 1)])
            P.op('act', lambda e, b=b: e.activation(out=junk[:], in_=xc[b][:], func=AF.Square, accum_out=st[b][:, 0:1]),
                 reads=[('xc', b)], writes=['junk', ('st', b, 0)])
            if t < NT - 1:
                P.op('act', lambda e, b=b: e.activation(out=junk[:], in_=xp[b][:], func=AF.Square, accum_out=st[b][:, 1:2]),
                     reads=[('xp', b)], writes=['junk', ('st', b, 1)])
            nst = 2 if t < NT - 1 else 1
            P.op('dve', lambda e, b=b, n=nst: e.tensor_scalar(out=st[b][:, 2:2 + n], in0=st[b][:, 0:n], scalar1=1.0 / D, scalar2=RMS_EPS,
                                                              op0=ALU.mult, op1=ALU.add),
                 reads=[('st', b, 0), ('st', b, 1)], writes=[('st', b, 2)])
            P.op('act', lambda e, b=b, n=nst: e.sqrt(out=st[b][:, 2:2 + n], in_=st[b][:, 2:2 + n]),
                 reads=[('st', b, 2)], writes=[('st', b, 2)])
            P.op('dve', lambda e, b=b, n=nst: e.reciprocal(out=st[b][:, 2:2 + n], in_=st[b][:, 2:2 + n]),
                 reads=[('st', b, 2)], writes=[('st', b, 2)])
            P.op('dve', lambda e, b=b: e.scalar_tensor_tensor(out=xc[b][:], in0=xc[b][:], scalar=st[b][:, 2:3], in1=gA[:],
                                                              op0=ALU.mult, op1=ALU.mult),
                 reads=[('xc', b), ('st', b, 2), 'gA'], writes=[('xc', b)])
            if t < NT - 1:
                P.op('dve', lambda e, b=b: e.scalar_tensor_tensor(out=xp[b][:], in0=xp[b][:], scalar=st[b][:, 3:4], in1=gA[:],
                                                                  op0=ALU.mult, op1=ALU.mult),
                     reads=[('xp', b), ('st', b, 2), 'gA'], writes=[('xp', b)])
            if t == NT - 2:
                P.dma('sp', d['o_pshift'], xc[b][127:128, :], reads=[('xc', b)], key='o_pshift')
            if t == NT - 1:
                P.dma('sp', d['o_sshift'], xc[b][0:NS, :], reads=[('xc', b)], key='o_sshift')
            P.op('pool', lambda e, b=b: e.tensor_tensor(out=xp[b][:], in0=xp[b][:], in1=xc[b][:], op=ALU.subtract),
                 reads=[('xp', b), ('xc', b)], writes=[('xp', b)])
            for (src, srck, dst, dstk) in ((xc, 'xc', xnT, 'xnT'), (xp, 'xp', xxT, 'xxT')):
                for q in range(4):
                    pb = ev % 4
                    for j in range(4):
                        c = q * 4 + j
                        P.op('pe', lambda e, pb=pb, j=j, c=c, src=src, b=b: e.transpose(out=ps[pb][:, j * 128:(j + 1) * 128],
                                                                                        in_=src[b][:, c * 128:(c + 1) * 128], identity=ident[:]),
                             reads=[(srck, b), 'ident'], writes=[('ps', pb)])
                    eng = 'act' if ev % 2 == 0 else 'dve'
                    if eng == 'act':
                        P.op('act', lambda e, pb=pb, q=q, dst=dst, b=b: e.copy(out=dst[b][:, q * 4:(q + 1) * 4, :], in_=ps[pb][:].rearrange("p (a n) -> p a n", a=4)),
                             reads=[('ps', pb)], writes=[(dstk, b)])
                    else:
                        P.op('dve', lambda e, pb=pb, q=q, dst=dst, b=b: e.tensor_copy(out=dst[b][:, q * 4:(q + 1) * 4, :], in_=ps[pb][:].rearrange("p (a n) -> p a n", a=4)),
                             reads=[('ps', pb)], writes=[(dstk, b)])
                    ev += 1
            for p in range(6):
                m = mi % 3
                mi += 1
                e1 = 'dve' if p % 2 == 0 else 'pool'
                P.op(e1, lambda e, b=b, p=p: e.tensor_tensor(out=tmp[p % 2][:], in0=xxT[b][:], in1=muT[:, p, :].unsqueeze(2).to_broadcast([128, 16, 128]), op=ALU.mult),
                     reads=[('xxT', b), 'muT'], writes=[('tmp', p % 2)])
                P.op(e1, lambda e, b=b, p=p, m=m: e.tensor_tensor(out=xm[m][:], in0=tmp[p % 2][:], in1=xnT[b][:], op=ALU.add),
                     reads=[('tmp', p % 2), ('xnT', b)], writes=[('xm', m)])
                P.dma('sp', d['xmT'][p][:, :, t * 128:(t + 1) * 128], xm[m][:], reads=[('xm', m)], writes=[('xmT', p)], key=('xmst', m))
        P.flush()


def gemm_tokmajor(P, nc, es, actT_src, kc, w_src, ncols, evac, wkey, tiles=range(NT), act_res='actT', actT=None):
    wt = [mk(nc, es, f"wt_{wkey}{i}", [128, kc, 512], BF16) for i in range(2)]
    ps = [mk(nc, es, f"psg_{wkey}{i}", [128, 512], F32, psum=True) for i in range(4)]
    wv = w_src.rearrange("(c p) n -> p c n", p=128)
    cnt = 0
    for cb in range(ncols // 512):
        wb = cb % 2
        P.dma('pool', wt[wb][:], wv[:, :, cb * 512:(cb + 1) * 512], writes=[('wt', wkey, wb)], key=('wt', wkey, wb))
        for t in tiles:
            pb = cnt % 4
            cnt += 1
            for c in range(kc):
                P.op('pe', lambda e, pb=pb, c=c, t=t, wb=wb: e.matmul(ps[pb][:], lhsT=actT[:, c, t * 128:(t + 1) * 128], rhs=wt[wb][:, c, :],
                                                                      start=(c == 0), stop=(c == kc - 1)),
                     reads=[act_res, ('wt', wkey, wb)], writes=[('psg', wkey, pb)])
            evac(t, cb, ps[pb], ('psg', wkey, pb), cnt)


def stage_B(P, nc, d, projs=(0, 1, 2, 3)):
    for p in projs:
        with ExitStack() as es:
            actT = mk(nc, es, "actT", [128, 16, T], BF16)
            ob = [mk(nc, es, f"obB{i}", [128, 512], BF16) for i in range(4)]
            P.dma('sp', actT[:], d['xmT'][p], writes=['actT'], key='actT')

            def evac(t, cb, pst, pkey, cnt, p=p):
                o = cnt % 4
                if cnt % 2 == 0:
                    P.op('act', lambda e: e.copy(out=ob[o][:], in_=pst[:]), reads=[pkey], writes=[('ob', o)])
                else:
                    P.op('dve', lambda e: e.tensor_copy(out=ob[o][:], in_=pst[:]), reads=[pkey], writes=[('ob', o)])
                P.dma('sp', d['rkvz'][p][t * 128:(t + 1) * 128, cb * 512:(cb + 1) * 512], ob[o][:], reads=[('ob', o)], key=('obst', o))
            gemm_tokmajor(P, nc, es, None, 16, d['a_w_rkvz'][0, p], E, evac, f"B{p}", actT=actT)
            P.flush()


def tt(P, eng, out, in0, in1, op, reads, writes):
    P.op(eng, lambda e: e.tensor_tensor(out=out, in0=in0, in1=in1, op=op), reads, writes)


def ts(P, eng, out, in0, s1, s2, op0, op1, reads, writes):
    if s2 is None:
        P.op(eng, lambda e: e.tensor_scalar(out=out, in0=in0, scalar1=s1, scalar2=None, op0=op0), reads, writes)
    else:
        P.op(eng, lambda e: e.tensor_scalar(out=out, in0=in0, scalar1=s1, scalar2=s2, op0=op0, op1=op1), reads, writes)


def stt(P, eng, out, in0, scalar, in1, op0, op1, reads, writes):
    P.op(eng, lambda e: e.scalar_tensor_tensor(out=out, in0=in0, scalar=scalar, in1=in1, op0=op0, op1=op1), reads, writes)


def actf(P, out, in_, func, reads, writes, scale=1.0):
    P.op('act', lambda e: e.activation(out=out, in_=in_, func=func, scale=scale), reads, writes)


def cp(P, eng, out, in_, reads, writes):
    if eng == 'act':
        P.op('act', lambda e: e.copy(out=out, in_=in_), reads, writes)
    else:
        P.op(eng, lambda e: e.tensor_copy(out=out, in_=in_), reads, writes)


def mm(P, out, lhsT, rhs, start, stop, reads, writes):
    P.op('pe', lambda e: e.matmul(out, lhsT=lhsT, rhs=rhs, start=start, stop=stop), reads, writes)


def tr(P, out, in_, ident, reads, writes):
    P.op('pe', lambda e: e.transpose(out=out, in_=in_, identity=ident), reads, writes)


def red(P, eng, out, in_, reads, writes):
    P.op(eng, lambda e: e.reduce_sum(out=out, in_=in_, axis=AX.X), reads, writes)


def recip(P, out, in_, reads, writes):
    P.op('dve', lambda e: e.reciprocal(out=out, in_=in_), reads, writes)


GN_EPS = 64e-5
SROW0 = 17 * 128
ZROW0 = SROW0 + NS
NCH = 17 + NS
CH_LIST = list(range(NCH))
CB_LIST = list(range(8))


def load_rows(P, q, dst, src, ch, c0, c1, writes, key):
    if ch < 17:
        P.dma(q, dst[:], src[ch * 128:(ch + 1) * 128, c0:c1], writes=writes, key=key)
    else:
        s = ch - 17
        P.dma(q, dst[0:1], src[SROW0 + s:SROW0 + s + 1, c0:c1], writes=writes, key=key)
        P.dma(q, dst[1:65], src[ZROW0:ZROW0 + 64, c0:c1], writes=writes, key=key)
        P.dma(q, dst[64:128], src[ZROW0:ZROW0 + 64, c0:c1], writes=writes, key=key)


def stage_B_lora(P, nc, d):
    for which, (xi, w1n, w2n, outn, func) in enumerate(((4, 'a_w1', 'a_w2', 'wpre', AF.Tanh), (5, 'a_a1', 'a_a2', 'apre', AF.Copy))):
        with ExitStack() as es:
            actT = mk(nc, es, "actT", [128, 16, T], BF16)
            w1 = mk(nc, es, "w1", [128, 16, 96], BF16)
            w2 = mk(nc, es, "w2", [96, E], BF16)
            hT = mk(nc, es, "hT", [96, T], BF16)
            ob = [mk(nc, es, f"obL{i}", [128, 512], F32) for i in range(4)]
            ps = [mk(nc, es, f"psL{i}", [128, 512], F32, psum=True) for i in range(4)]
            P.dma('sp', actT[:], d['xmT'][xi], writes=['actT'], key='actT')
            P.dma('pool', w1[:], d[w1n][0].rearrange("(c p) n -> p c n", p=128), writes=['w1'], key='w1')
            P.dma('pool', w2[:], d[w2n][0], writes=['w2'], key='w2')
            cnt = 0
            for t in range(NT):
                pb = cnt % 4
                cnt += 1
                for c in range(16):
                    mm(P, ps[pb][0:96, 0:128], w1[:, c, :], actT[:, c, t * 128:(t + 1) * 128], c == 0, c == 15,
                       ['actT', 'w1'], [('psL', pb)])
                actf(P, hT[:, t * 128:(t + 1) * 128], ps[pb][0:96, 0:128], func, [('psL', pb)], [('hT', t)])
            for t in range(NT):
                for cb in range(8):
                    pb = cnt % 4
                    cnt += 1
                    mm(P, ps[pb][:], hT[:, t * 128:(t + 1) * 128], w2[:, cb * 512:(cb + 1) * 512], True, True,
                       [('hT', t), 'w2'], [('psL', pb)])
                    cp(P, 'act' if cnt % 2 else 'dve', ob[pb][:], ps[pb][:], [('psL', pb)], [('obL', pb)])
                    P.dma('sp', d[outn][t * 128:(t + 1) * 128, cb * 512:(cb + 1) * 512], ob[pb][:], reads=[('obL', pb)], key=('obLst', pb))
            P.flush()


def stage_CDE(P, nc, d):
    with ExitStack() as es:
        ident = mk(nc, es, "identF", [128, 128], F32)
        identb = mk(nc, es, "identB", [128, 128], BF16)
        tri = mk(nc, es, "tri", [128, 128], F32)
        ones = mk(nc, es, "ones", [128, 128], F32)
        onehot = mk(nc, es, "onehot", [128, 1], F32)
        lmask = mk(nc, es, "lmask", [128, 2], F32)
        mask4 = mk(nc, es, "mask4", [128, 512], F32)
        negsl = mk(nc, es, "negsl", [128, 128], F32)
        for nm, tl in (('ident', ident), ('tri', tri), ('ones', ones), ('onehot', onehot), ('lmask', lmask), ('mask4', mask4), ('negsl', negsl)):
            P.dma('sp', tl[:], d[nm], writes=[nm], key=nm)
        P.dma('pool', identb[:], d['ident'], writes=['identb'], key='identb')
        CONST = ['ident', 'tri', 'ones', 'onehot', 'lmask', 'mask4', 'negsl', 'identb']
        ST = mk(nc, es, "ST", [128, 32, 64], F32)
        STb = mk(nc, es, "STb", [128, 32, 64], BF16)
        SN = mk(nc, es, "SN", [64, 64, 64], F32)
        BON = mk(nc, es, "BON", [128, 64], F32)
        stmp = mk(nc, es, "stmp", [128, 64], F32)
        ygT = [mk(nc, es, f"ygT{i}", [128, 32, 128], BF16) for i in range(2)]
        NB = 3
        PRM = [mk(nc, es, f"PRM{i}", [128, 7, 512], F32) for i in range(NB)]
        Rb = [mk(nc, es, f"Rb{i}", [128, 512], BF16) for i in range(NB)]
        Kb = [mk(nc, es, f"Kb{i}", [128, 512], BF16) for i in range(NB)]
        Vb = [mk(nc, es, f"Vb{i}", [128, 512], BF16) for i in range(NB)]
        Zb = [mk(nc, es, f"Zb{i}", [128, 512], BF16) for i in range(NB)]
        Wp = [mk(nc, es, f"Wp{i}", [128, 512], F32) for i in range(NB)]
        Ap = [mk(nc, es, f"Ap{i}", [128, 512], F32) for i in range(NB)]
        f32names = ['LD', 'KKf', 'KMf', 'Bf', 'SQ', 'T1', 'E1', 'E2', 'E3', 'E4', 'GT', 'Dinv']
        W = {n: mk(nc, es, n, [128, 512], F32) for n in f32names}
        sm = mk(nc, es, "sm", [128, 64], F32)
        TM = [mk(nc, es, f"TM{i}", [128, 4, 512], BF16) for i in range(NB)]
        KVb = [mk(nc, es, f"KVb{i}", [128, 512], BF16) for i in range(NB)]
        BVb = [mk(nc, es, f"BVb{i}", [128, 512], BF16) for i in range(NB)]
        FT = [mk(nc, es, f"FT{i}", [128, 4, 4, 128], BF16) for i in range(NB)]
        gCs = [mk(nc, es, f"gCs{i}", [128, 4], F32) for i in range(NB)]
        AK = mk(nc, es, "AK", [128, 4, 512], BF16)
        MT = mk(nc, es, "MT", [128, 4, 128], BF16)
        Rm = [mk(nc, es, f"Rm{i}", [128, 4, 128], BF16) for i in range(2)]
        PP = [mk(nc, es, f"PP{i}", [128, 4, 2, 128], BF16) for i in range(2)]
        Xb = mk(nc, es, "Xb", [128, 256], BF16)
        nSA = mk(nc, es, "nSA", [128, 256], BF16)
        Ycb = [mk(nc, es, f"Ycb{i}", [128, 512], F32) for i in range(2)]
        EY = {n: mk(nc, es, n, [128, 512], F32) for n in ('Ysq', 'Yn', 'Sz')}
        YG = mk(nc, es, "YG", [128, 512], BF16)
        pC0 = mk(nc, es, "pC0", [128, 512], F32, psum=True)
        pC1 = mk(nc, es, "pC1", [128, 512], F32, psum=True)
        pTP = mk(nc, es, "pTP", [128, 1024], BF16, psum=True)
        pPQ = mk(nc, es, "pPQ", [128, 4, 2, 128], F32, psum=True)
        pPA = mk(nc, es, "pPA", [128, 512], F32, psum=True)
        pRX = mk(nc, es, "pRX", [128, 512], F32, psum=True)
        pYS = mk(nc, es, "pYS", [128, 512], F32, psum=True)

        def state_load(s):
            P.dma('sp', SN[:], d['swkv'][s].rearrange("h v k -> v h k"), writes=['SN'], key='SN')
            for g8 in range(4):
                for q in range(8):
                    gp = g8 * 8 + q
                    tr(P, pC0[:, q * 64:(q + 1) * 64], SN[:, 2 * gp:2 * gp + 2, :].rearrange("v a k -> v (a k)"), ident[0:64, 0:64],
                       ['SN', 'ident'], ['pC0'])
                cp(P, 'act', ST[:, g8 * 8:(g8 + 1) * 8, :], pC0[:].rearrange("p (a v) -> p a v", a=8), ['pC0'], [('ST', g8 * 8 + q) for q in range(8)])
                cp(P, 'dve', STb[:, g8 * 8:(g8 + 1) * 8, :], pC0[:].rearrange("p (a v) -> p a v", a=8), ['pC0'], [('STb', g8 * 8 + q) for q in range(8)])

        def state_save(dst):
            for g4 in range(8):
                for q in range(4):
                    gp = g4 * 4 + q
                    tr(P, pC0[0:64, q * 128:(q + 1) * 128], ST[:, gp, :], ident[:], [('ST', gp), 'ident'], ['pC0'])
                cp(P, 'act', SN[:, g4 * 8:(g4 + 1) * 8, :].rearrange("v a k -> v (a k)"), pC0[0:64, :], ['pC0'], ['SN'])
            P.dma('sp', dst.rearrange("h v k -> v h k"), SN[:], reads=['SN'], key='SNst')

        P.op('pool', lambda e: e.memset(ST[:], 0.0), writes=[('ST', g) for g in range(32)])
        P.op('pool', lambda e: e.memset(STb[:], 0.0), writes=[('STb', g) for g in range(32)])

        def phase_C(ch, cb, b):
            lcol = 0 if ch < 17 else 1
            c0, c1 = cb * 512, (cb + 1) * 512
            P.dma('sp', PRM[b][:], d['prm'][:, c0:c1].partition_broadcast(128), writes=[('PRM', b)], key=('PRM', b))
            load_rows(P, 'sp', Rb[b], d['rkvz'][0], ch, c0, c1, [('Rb', b)], ('Rb', b))
            load_rows(P, 'sp', Kb[b], d['rkvz'][1], ch, c0, c1, [('Kb', b)], ('Kb', b))
            load_rows(P, 'sp', Vb[b], d['rkvz'][2], ch, c0, c1, [('Vb', b)], ('Vb', b))
            load_rows(P, 'sp', Zb[b], d['rkvz'][3], ch, c0, c1, [('Zb', b)], ('Zb', b))
            load_rows(P, 'sp', Wp[b], d['wpre'], ch, c0, c1, [('Wp', b)], ('Wp', b))
            load_rows(P, 'sp', Ap[b], d['apre'], ch, c0, c1, [('Ap', b)], ('Ap', b))
            prm = lambda i, b=b: PRM[b][:, i, :]
            tt(P, 'dve', Wp[b][:], Wp[b][:], prm(0), ALU.add, [('Wp', b), ('PRM', b)], [('Wp', b)])
            actf(P, Wp[b][:], Wp[b][:], AF.Sigmoid, [('Wp', b)], [('Wp', b)])
            ts(P, 'dve', W['LD'][:], Wp[b][:], lmask[:, lcol:lcol + 1], None, ALU.mult, None, [('Wp', b), 'lmask'], ['LD'])
            tt(P, 'dve', Ap[b][:], Ap[b][:], prm(1), ALU.add, [('Ap', b), ('PRM', b)], [('Ap', b)])
            actf(P, Ap[b][:], Ap[b][:], AF.Sigmoid, [('Ap', b)], [('Ap', b)])
            tt(P, 'dve', W['KKf'][:], Kb[b][:], prm(2), ALU.mult, [('Kb', b), ('PRM', b)], ['KKf'])
            tt(P, 'pool', W['SQ'][:], W['KKf'][:], W['KKf'][:], ALU.mult, ['KKf'], ['SQ'])
            red(P, 'dve', sm[:, 0:8], W['SQ'][:].rearrange("p (h c) -> p h c", h=8), ['SQ'], [('sm', 0)])
            ts(P, 'dve', sm[:, 0:8], sm[:, 0:8], 1e-24, None, ALU.max, None, [('sm', 0)], [('sm', 0)])
            P.op('act', lambda e: e.sqrt(out=sm[:, 0:8], in_=sm[:, 0:8]), [('sm', 0)], [('sm', 0)])
            recip(P, sm[:, 0:8], sm[:, 0:8], [('sm', 0)], [('sm', 0)])
            tt(P, 'dve', W['KKf'][:].rearrange("p (h c) -> p h c", h=8), W['KKf'][:].rearrange("p (h c) -> p h c", h=8),
               sm[:, 0:8].unsqueeze(2).to_broadcast([128, 8, 64]), ALU.mult, ['KKf', ('sm', 0)], ['KKf'])
            stt(P, 'dve', W['T1'][:], Ap[b][:], -1.0, prm(3), ALU.add, ALU.mult, [('Ap', b), ('PRM', b)], ['T1'])
            stt(P, 'dve', W['KMf'][:], W['T1'][:], 1.0, Kb[b][:], ALU.add, ALU.mult, ['T1', ('Kb', b)], ['KMf'])
            tt(P, 'pool', W['Bf'][:], W['KKf'][:], Ap[b][:], ALU.mult, ['KKf', ('Ap', b)], ['Bf'])
            tt(P, 'pool', W['T1'][:], Rb[b][:], W['KMf'][:], ALU.mult, [('Rb', b), 'KMf'], ['T1'])
            tt(P, 'pool', W['T1'][:], W['T1'][:], prm(4), ALU.mult, ['T1', ('PRM', b)], ['T1'])
            red(P, 'dve', BON[:, cb * 8:(cb + 1) * 8], W['T1'][:].rearrange("p (h c) -> p h c", h=8), ['T1'], [('BON', cb)])
            mm(P, pC0[:], tri[:], W['LD'][:], True, True, ['tri', 'LD'], ['pC0'])
            mm(P, pC1[:], ones[:], W['LD'][:], True, True, ['ones', 'LD'], ['pC1'])
            actf(P, W['E1'][:], pC0[:], AF.Exp, ['pC0'], ['E1'])
            actf(P, W['E2'][:], pC0[:], AF.Exp, ['pC0'], ['E2'], scale=-1.0)
            actf(P, W['GT'][:], pC1[:], AF.Exp, ['pC1'], ['GT'])
            actf(P, W['Dinv'][:], W['LD'][:], AF.Exp, ['LD'], ['Dinv'], scale=-1.0)
            tt(P, 'dve', W['E3'][:], W['E1'][:], W['Dinv'][:], ALU.mult, ['E1', 'Dinv'], ['E3'])
            tt(P, 'pool', W['E4'][:], W['GT'][:], W['E2'][:], ALU.mult, ['GT', 'E2'], ['E4'])
            for pp in range(4):
                mm(P, pC1[:, pp:pp + 1], W['GT'][:, pp * 128:(pp + 1) * 128], onehot[:, 0:1], True, True, ['GT', 'onehot'], ['pC1'])
            cp(P, 'act', gCs[b][:], pC1[:, 0:4], ['pC1'], [('gCs', b)])
            tt(P, 'dve', TM[b][:, 1, :], Rb[b][:], W['E1'][:], ALU.mult, [('Rb', b), 'E1'], [('TM', b, 1)])
            tt(P, 'dve', TM[b][:, 2, :], W['KMf'][:], W['E2'][:], ALU.mult, ['KMf', 'E2'], [('TM', b, 2)])
            tt(P, 'pool', TM[b][:, 3, :], W['Bf'][:], W['E2'][:], ALU.mult, ['Bf', 'E2'], [('TM', b, 3)])
            tt(P, 'dve', TM[b][:, 0, :], W['KKf'][:], W['E3'][:], ALU.mult, ['KKf', 'E3'], [('TM', b, 0)])
            tt(P, 'pool', KVb[b][:], W['KMf'][:], W['E4'][:], ALU.mult, ['KMf', 'E4'], [('KVb', b)])
            tt(P, 'pool', BVb[b][:], W['Bf'][:], W['E4'][:], ALU.mult, ['Bf', 'E4'], [('BVb', b)])
            for hf in range(2):
                for pq in range(2):
                    pp = hf * 2 + pq
                    for kd in range(4):
                        tr(P, pTP[:, (pq * 4 + kd) * 128:(pq * 4 + kd + 1) * 128], TM[b][:, kd, pp * 128:(pp + 1) * 128], identb[:],
                           [('TM', b, kd), 'identb'], [('pTP', 0)])
                cp(P, 'act' if hf == 0 else 'dve', FT[b][:, 2 * hf:2 * hf + 2].rearrange("p a k t -> p (a k t)"), pTP[:, 0:1024],
                   [('pTP', 0)], [('FT', b, hf)])

        def phase_D(ch, cb, b, yb):
            for g in range(2):
                ftk = ('FT', b, g)
                for i in range(4):
                    pp, h2 = 2 * g + i // 2, i % 2
                    fts = FT[b][h2 * 64:(h2 + 1) * 64, pp]
                    rhs2 = fts[:, 0:2, :].rearrange("p k t -> p (k t)")
                    mm(P, pPA[:, 0:256], fts[:, 2, :], rhs2, True, True, [ftk], ['pPA'])
                    mm(P, pPA[:, 256:512], fts[:, 3, :], rhs2, True, True, [ftk], ['pPA'])
                    tt(P, 'dve', AK[:, i, :], pPA[:], mask4[:], ALU.mult, ['pPA', 'mask4'], [('AK', i)])
                    mm(P, pRX[:, i * 128:(i + 1) * 128], fts[:, 0, :], fts[:, 3, :], True, True, [ftk], ['pRX'])
                tt(P, 'dve', MT[:], pRX[:].rearrange("p (a t) -> p a t", a=4), negsl[:].unsqueeze(1).to_broadcast([128, 4, 128]), ALU.mult,
                   ['pRX', 'negsl'], ['MT'])
                tt(P, 'pool', Rm[0][:], AK[:, :, 256:384], identb[:].unsqueeze(1).to_broadcast([128, 4, 128]), ALU.add,
                   [('AK', i) for i in range(4)] + ['identb'], [('Rm', 0)])
                cur = 0

                def squares(lev):
                    nxt = lev % 2
                    last = (lev == 6)
                    for i in range(4):
                        if lev == 1:
                            Pc, PTc = AK[:, i, 256:384], MT[:, i, :]
                            rk = [('AK', i), 'MT']
                        else:
                            Pc, PTc = PP[1 - nxt][:, i, 0, :], PP[1 - nxt][:, i, 1, :]
                            rk = [('PP', 1 - nxt)]
                        if not last:
                            mm(P, pPQ[:, i, 0, :], PTc, Pc, True, True, rk, ['pPQ'])
                        mm(P, pPQ[:, i, 1, :], Pc, PTc, True, True, rk, ['pPQ'])

                def evac_pp(lev):
                    nxt = lev % 2
                    if lev < 6:
                        cp(P, 'act', PP[nxt][:], pPQ[:], ['pPQ'], [('PP', nxt)])
                    else:
                        cp(P, 'act', PP[nxt][:, :, 1, :], pPQ[:, :, 1, :], ['pPQ'], [('PP', nxt)])

                squares(1)
                evac_pp(1)
                for lev in range(1, 7):
                    nxt = lev % 2
                    for i in range(4):
                        mm(P, pRX[:, i * 128:(i + 1) * 128], PP[nxt][:, i, 1, :], Rm[cur][:, i, :], True, True,
                           [('PP', nxt), ('Rm', cur)], ['pRX'])
                    if lev < 6:
                        squares(lev + 1)
                    tt(P, 'dve', Rm[1 - cur][:], pRX[:].rearrange("p (a t) -> p a t", a=4), Rm[cur][:], ALU.add,
                       ['pRX', ('Rm', cur)], [('Rm', 1 - cur)])
                    if lev < 6:
                        evac_pp(lev + 1)
                    cur = 1 - cur
                Rf = Rm[cur]
                for i in range(4):
                    pp, h2 = 2 * g + i // 2, i % 2
                    gp = cb * 4 + pp
                    hh = 4 * g + i
                    fts = FT[b][h2 * 64:(h2 + 1) * 64, pp]
                    mm(P, pRX[:, i * 64:(i + 1) * 64], fts[:, 0, :], STb[h2 * 64:(h2 + 1) * 64, gp, :], True, False,
                       [ftk, ('STb', gp)], ['pRX'])
                    mm(P, pRX[:, i * 64:(i + 1) * 64], AK[:, i, 0:128], Vb[b][:, hh * 64:(hh + 1) * 64], False, True,
                       [('AK', i), ('Vb', b)], ['pRX'])
                cp(P, 'act', Xb[:], pRX[:, 0:256], ['pRX'], ['Xb'])
                for i in range(4):
                    mm(P, pRX[:, 256 + i * 64:256 + (i + 1) * 64], Rf[:, i, :], Xb[:, i * 64:(i + 1) * 64], True, True,
                       [('Rm', cur), 'Xb'], ['pRX'])
                P.op('act', lambda e: e.mul(out=nSA[:], in_=pRX[:, 256:512], mul=-1.0), ['pRX'], ['nSA'])
                for i in range(4):
                    pp, h2 = 2 * g + i // 2, i % 2
                    gp = cb * 4 + pp
                    hh = 4 * g + i
                    fts = FT[b][h2 * 64:(h2 + 1) * 64, pp]
                    o = pYS[:, i * 64:(i + 1) * 64]
                    mm(P, o, fts[:, 1, :], STb[h2 * 64:(h2 + 1) * 64, gp, :], True, False, [ftk, ('STb', gp)], ['pYS'])
                    mm(P, o, AK[:, i, 128:256], Vb[b][:, hh * 64:(hh + 1) * 64], False, False, [('AK', i), ('Vb', b)], ['pYS'])
                    mm(P, o, AK[:, i, 384:512], nSA[:, i * 64:(i + 1) * 64], False, True, [('AK', i), 'nSA'], ['pYS'])
                for q in range(2):
                    pp = 2 * g + q
                    o = pYS[:, 256 + q * 128:256 + (q + 1) * 128]
                    mm(P, o, KVb[b][:, pp * 128:(pp + 1) * 128], Vb[b][:, pp * 128:(pp + 1) * 128], True, False,
                       [('KVb', b), ('Vb', b)], ['pYS'])
                    mm(P, o, BVb[b][:, pp * 128:(pp + 1) * 128], nSA[:, q * 128:(q + 1) * 128], False, True,
                       [('BVb', b), 'nSA'], ['pYS'])
                cp(P, 'act', Ycb[yb][:, g * 256:(g + 1) * 256], pYS[:, 0:256], ['pYS'], [('Ycb', yb, g)])
                for q in range(2):
                    pp = 2 * g + q
                    gp = cb * 4 + pp
                    for h2 in range(2):
                        sl = slice(h2 * 64, (h2 + 1) * 64)
                        ts(P, 'dve', stmp[sl, :], ST[sl, gp, :], gCs[b][sl, pp:pp + 1], None, ALU.mult, None,
                           [('ST', gp), ('gCs', b)], ['stmp'])
                        tt(P, 'dve', ST[sl, gp, :], pYS[sl, 256 + q * 128 + h2 * 64:256 + q * 128 + (h2 + 1) * 64], stmp[sl, :], ALU.add,
                           ['pYS', 'stmp'], [('ST', gp)])
                    cp(P, 'pool', STb[:, gp, :], ST[:, gp, :], [('ST', gp)], [('STb', gp)])

        def phase_E(ch, cb, b, yb):
            prm = lambda i, b=b: PRM[b][:, i, :]
            Y3 = Ycb[yb][:].rearrange("p (h c) -> p h c", h=8)
            yk = [('Ycb', yb, 0), ('Ycb', yb, 1)]
            red(P, 'dve', sm[:, 8:16], Y3, yk, [('sm', 1)])
            tt(P, 'pool', EY['Ysq'][:], Ycb[yb][:], Ycb[yb][:], ALU.mult, yk, ['Ysq'])
            red(P, 'dve', sm[:, 16:24], EY['Ysq'][:].rearrange("p (h c) -> p h c", h=8), ['Ysq'], [('sm', 2)])
            ts(P, 'dve', sm[:, 8:16], sm[:, 8:16], 1.0 / 64, None, ALU.mult, None, [('sm', 1)], [('sm', 1)])
            tt(P, 'dve', sm[:, 24:32], sm[:, 8:16], sm[:, 8:16], ALU.mult, [('sm', 1)], [('sm', 3)])
            stt(P, 'dve', sm[:, 16:24], sm[:, 16:24], 1.0 / 64, sm[:, 24:32], ALU.mult, ALU.subtract, [('sm', 2), ('sm', 3)], [('sm', 2)])
            ts(P, 'dve', sm[:, 16:24], sm[:, 16:24], GN_EPS, None, ALU.add, None, [('sm', 2)], [('sm', 2)])
            P.op('act', lambda e: e.sqrt(out=sm[:, 16:24], in_=sm[:, 16:24]), [('sm', 2)], [('sm', 2)])
            recip(P, sm[:, 16:24], sm[:, 16:24], [('sm', 2)], [('sm', 2)])
            Yn3 = EY['Yn'][:].rearrange("p (h c) -> p h c", h=8)
            tt(P, 'dve', Yn3, Y3, sm[:, 8:16].unsqueeze(2).to_broadcast([128, 8, 64]), ALU.subtract, yk + [('sm', 1)], ['Yn'])
            tt(P, 'dve', Yn3, Yn3, sm[:, 16:24].unsqueeze(2).to_broadcast([128, 8, 64]), ALU.mult, ['Yn', ('sm', 2)], ['Yn'])
            tt(P, 'pool', EY['Yn'][:], EY['Yn'][:], prm(5), ALU.mult, ['Yn', ('PRM', b)], ['Yn'])
            tt(P, 'pool', EY['Yn'][:], EY['Yn'][:], prm(6), ALU.add, ['Yn', ('PRM', b)], ['Yn'])
            tt(P, 'dve', EY['Ysq'][:].rearrange("p (h c) -> p h c", h=8), Vb[b][:].rearrange("p (h c) -> p h c", h=8),
               BON[:, cb * 8:(cb + 1) * 8].unsqueeze(2).to_broadcast([128, 8, 64]), ALU.mult, [('Vb', b), ('BON', cb)], ['Ysq'])
            tt(P, 'pool', EY['Yn'][:], EY['Yn'][:], EY['Ysq'][:], ALU.add, ['Yn', 'Ysq'], ['Yn'])
            actf(P, EY['Sz'][:], Zb[b][:], AF.Silu, [('Zb', b)], ['Sz'])
            tt(P, 'dve', YG[:], EY['Yn'][:], EY['Sz'][:], ALU.mult, ['Yn', 'Sz'], ['YG'])
            for q in range(4):
                tr(P, pTP[:, q * 128:(q + 1) * 128], YG[:, q * 128:(q + 1) * 128], identb[:], ['YG', 'identb'], [('pTP', 0)])
            cp(P, 'act', ygT[ch % 2][:, cb * 4:(cb + 1) * 4, :].rearrange("p a t -> p (a t)"), pTP[:, 0:512], [('pTP', 0)], [('ygT', ch % 2)])
            if cb == CB_LIST[-1]:
                yt = ygT[ch % 2]
                if ch < 17:
                    P.dma('sp', d['ygT'][:, :, ch * 128:(ch + 1) * 128], yt[:], reads=[('ygT', ch % 2)], key='ygTst')
                elif ch == 17:
                    P.dma('sp', d['ygT'][:, :, SROW0:SROW0 + 128], yt[:], reads=[('ygT', ch % 2)], key='ygTst')
                else:
                    s_ = ch - 17
                    P.dma('sp', d['ygT'][:, :, SROW0 + s_:SROW0 + s_ + 1], yt[:, :, 0:1], reads=[('ygT', ch % 2)], key='ygTst',
                          allow_slow_non_contiguous=True)

        def capture(fn, *a):
            P.cap = []
            fn(*a)
            l = P.cap
            P.cap = None
            return l

        def replay(l):
            for (eng, fn, reads, writes, key) in l:
                P.op(eng, fn, reads, writes, key)

        def merge(main, first, second):
            out = []
            n = len(main)
            h = n // 2 if (first and second) else (n if first else 0)
            da = db = 0
            for i, o in enumerate(main):
                out.append(o)
                if i < h:
                    t_ = (i + 1) * len(first) // max(h, 1)
                    while da < t_:
                        out.append(first[da])
                        da += 1
                else:
                    if da < len(first):
                        out.extend(first[da:])
                        da = len(first)
                    t_ = (i + 1 - h) * len(second) // max(n - h, 1)
                    while db < t_:
                        out.append(second[db])
                        db += 1
            out.extend(first[da:])
            out.extend(second[db:])
            return out

        units = [(ch, cb) for ch in CH_LIST for cb in CB_LIST]
        replay(capture(phase_C, units[0][0], units[0][1], 0))
        for idx, (ch, cb) in enumerate(units):
            if cb == CB_LIST[0] and ch >= 17:
                state_load(ch - 17)
            Dl = capture(phase_D, ch, cb, idx % NB, idx % 2)
            Cn = capture(phase_C, units[idx + 1][0], units[idx + 1][1], (idx + 1) % NB) if idx + 1 < len(units) else []
            Ep = capture(phase_E, units[idx - 1][0], units[idx - 1][1], (idx - 1) % NB, (idx - 1) % 2) if idx >= 1 else []
            replay(merge(Dl, Ep, Cn) if PIPELINE else Dl + Ep + Cn)
            if cb == CB_LIST[-1]:
                if ch == 16:
                    state_save(d['o_pwkv'])
                if ch >= 17:
                    state_save(d['o_swkv'][ch - 17])
        replay(capture(phase_E, units[-1][0], units[-1][1], (len(units) - 1) % NB, (len(units) - 1) % 2))
        P.flush()


def stage_outproj(P, nc, d, actn, wsrc, outn, tag):
    for half in range(2):
        with ExitStack() as es:
            tiles = list(range(half * 9, half * 9 + 9))
            actT = mk(nc, es, "actT", [128, 32, 9 * 128], BF16)
            ob = [mk(nc, es, f"obO{i}", [128, 512], F32) for i in range(4)]
            P.dma('sp', actT[:], d[actn][:, :, half * 1152:(half + 1) * 1152], writes=['actT'], key='actT')
            wt = [mk(nc, es, f"wtO{i}", [128, 32, 512], BF16) for i in range(2)]
            ps = [mk(nc, es, f"psO{i}", [128, 512], F32, psum=True) for i in range(4)]
            wv = wsrc.rearrange("(c p) n -> p c n", p=128)
            cnt = 0
            for cb in range(4):
                wb = cb % 2
                P.dma('pool', wt[wb][:], wv[:, :, cb * 512:(cb + 1) * 512], writes=[('wt', wb)], key=('wt', wb))
                for tl, t in enumerate(tiles):
                    pb = cnt % 4
                    cnt += 1
                    for c in range(32):
                        mm(P, ps[pb][:], actT[:, c, tl * 128:(tl + 1) * 128], wt[wb][:, c, :], c == 0, c == 31,
                           ['actT', ('wt', wb)], [('ps', pb)])
                    cp(P, 'act' if cnt % 2 else 'dve', ob[pb][:], ps[pb][:], [('ps', pb)], [('ob', pb)])
                    P.dma('sp', d[outn][t * 128:(t + 1) * 128, cb * 512:(cb + 1) * 512], ob[pb][:], reads=[('ob', pb)], key=('obst', pb))
            P.flush()


def rms_stats(P, x, junk, st, xk, sk):
    P.op('pool', lambda e: e.memset(st[:, 0:1], 0.0), writes=[sk])
    P.op('act', lambda e: e.activation(out=junk[:], in_=x[:], func=AF.Square, accum_out=st[:, 0:1]), reads=[xk], writes=['junk', sk])
    ts(P, 'dve', st[:, 1:2], st[:, 0:1], 1.0 / D, RMS_EPS, ALU.mult, ALU.add, [sk], [sk])
    P.op('act', lambda e: e.sqrt(out=st[:, 1:2], in_=st[:, 1:2]), [sk], [sk])
    recip(P, st[:, 1:2], st[:, 1:2], [sk], [sk])


def stage_F2(P, nc, d):
    with ExitStack() as es:
        gkv = mk(nc, es, "gkv", [128, D], F32)
        gb = mk(nc, es, "gb", [128, D], F32)
        ident = mk(nc, es, "identF2", [128, 128], F32)
        junk = mk(nc, es, "junkF", [128, D], F32)
        xt = [mk(nc, es, f"xF{i}", [128, D], F32) for i in range(2)]
        ot = [mk(nc, es, f"oF{i}", [128, D], F32) for i in range(2)]
        hn = [mk(nc, es, f"hnF{i}", [128, D], F32) for i in range(2)]
        st = [mk(nc, es, f"stF{i}", [128, 2], F32) for i in range(2)]
        hT = [mk(nc, es, f"hTF{i}", [128, 16, 128], BF16) for i in range(2)]
        ps = [mk(nc, es, f"psF{i}", [128, 512], F32, psum=True) for i in range(4)]
        P.dma('sp', gkv[:], d['kv_norm'][0].partition_broadcast(128), writes=['gkv'], key='gkv')
        P.dma('sp', gb[:], d['b_norm'][0].partition_broadcast(128), writes=['gb'], key='gb')
        P.dma('sp', ident[:], d['ident'], writes=['ident'], key='ident')
        ev = 0
        for t in range(NT):
            b = t % 2
            P.dma('sp', xt[b][:], d['xin'][1 + 128 * t:1 + 128 * t + 128, :], writes=[('x', b)], key=('x', b))
            P.dma('sp', ot[b][:], d['o1'][128 * t:128 * t + 128, :], writes=[('o', b)], key=('o', b))
            tt(P, 'dve', xt[b][:], xt[b][:], ot[b][:], ALU.add, [('x', b), ('o', b)], [('x', b)])
            P.dma('sp', d['hp'][128 * t:128 * t + 128, :], xt[b][:], reads=[('x', b)], key=('hpst', b))
            rms_stats(P, xt[b], junk, st[b], ('x', b), ('st', b))
            for vi, (g, gk, dst) in enumerate(((gkv, 'gkv', 'hkvT'), (gb, 'gb', 'hbT'))):
                hb = (2 * t + vi) % 2
                stt(P, 'dve', hn[hb][:], xt[b][:], st[b][:, 1:2], g[:], ALU.mult, ALU.mult,
                    [('x', b), ('st', b), gk], [('hn', hb)])
                for q in range(4):
                    pb = ev % 4
                    for j in range(4):
                        c = q * 4 + j
                        tr(P, ps[pb][:, j * 128:(j + 1) * 128], hn[hb][:, c * 128:(c + 1) * 128], ident[:], [('hn', hb), 'ident'], [('ps', pb)])
                    cp(P, 'act' if ev % 2 == 0 else 'dve', hT[hb][:, q * 4:(q + 1) * 4, :], ps[pb][:].rearrange("p (a n) -> p a n", a=4),
                       [('ps', pb)], [('hT', hb)])
                    ev += 1
                P.dma('sp', d[dst][:, :, t * 128:(t + 1) * 128], hT[hb][:], reads=[('hT', hb)], key=('hTst', hb))
        P.flush()


def rotary(P, eng, Kt, cs, tmp, kk, csk, tk):
    nh = Kt.shape[1]
    cosb = cs[:, 0:8].unsqueeze(1).to_broadcast([128, nh, 8])
    sinb = cs[:, 8:16].unsqueeze(1).to_broadcast([128, nh, 8])
    x1, x2 = Kt[:, :, 0:8], Kt[:, :, 8:16]
    t = [tmp[:, i, 0:nh, :] for i in range(4)]
    tt(P, eng, t[0], x1, cosb, ALU.mult, [kk, csk], [tk])
    tt(P, eng, t[1], x2, sinb, ALU.mult, [kk, csk], [tk])
    tt(P, eng, t[2], x2, cosb, ALU.mult, [kk, csk], [tk])
    tt(P, eng, t[3], x1, sinb, ALU.mult, [kk, csk], [tk])
    tt(P, eng, x1, t[0], t[1], ALU.subtract, [tk], [kk])
    tt(P, eng, x2, t[2], t[3], ALU.add, [tk], [kk])


def stage_G(P, nc, d):
    with ExitStack() as es:
        actT = mk(nc, es, "actT", [128, 16, T], BF16)
        CS = mk(nc, es, "CS", [128, NT, 16], F32)
        ob = [mk(nc, es, f"obG{i}", [128, 512], F32) for i in range(4)]
        obb = [mk(nc, es, f"obbG{i}", [128, 512], BF16) for i in range(4)]
        tmp = [mk(nc, es, f"tmpG{i}", [128, 4, 8, 8], F32) for i in range(4)]
        P.dma('sp', actT[:], d['hkvT'], writes=['actT'], key='actT')
        P.dma('sp', CS[:], d['cs'].rearrange("(t p) c -> p t c", p=128), writes=['CS'], key='CS')
        for s in range(NS):
            P.dma('sp', d['o_sck'][s, 0:127, :], d['ck'][s, 1:128, :], key=('cpk', s))
            P.dma('sp', d['o_scv'][s, 0:127, :], d['cv'][s, 1:128, :], key=('cpv', s))

        def evac(t, cb, pst, pkey, cnt):
            o = cnt % 4
            cp(P, 'act', ob[o][:], pst[:], [pkey], [('ob', o)])
            if cb == 0:
                rotary(P, 'dve', ob[o][:].rearrange("p (h c) -> p h c", h=8), CS[:, t, :], tmp[o], ('ob', o), 'CS', ('tmp', o))
            cp(P, 'pool', obb[o][:], ob[o][:], [('ob', o)], [('obb', o)])
            P.dma('sp', d['KVs'][t * 128:(t + 1) * 128, cb * 512:(cb + 1) * 512], obb[o][:], reads=[('obb', o)], key=('obbst', o))
            if t == 16:
                P.dma('sp', d['o_pck' if cb == 0 else 'o_pcv'], ob[o][:], reads=[('ob', o)], key=('obst', o))
            if t == 17:
                P.dma('sp', d['o_sck' if cb == 0 else 'o_scv'][:, 127, :], ob[o][0:NS, :], reads=[('ob', o)], key=('obst', o))
        gemm_tokmajor(P, nc, es, None, 16, d['w_kv'], 1024, evac, "G", actT=actT)
        P.flush()


def stage_H(P, nc, d):
    with ExitStack() as es:
        actT = mk(nc, es, "actT", [128, 16, T], BF16)
        CS = mk(nc, es, "CS", [128, NT, 16], F32)
        ob = [mk(nc, es, f"obH{i}", [128, 512], F32) for i in range(4)]
        obb = [mk(nc, es, f"obbH{i}", [128, 512], BF16) for i in range(4)]
        tmp = [mk(nc, es, f"tmpH{i}", [128, 4, 8, 8], F32) for i in range(4)]
        P.dma('sp', actT[:], d['hbT'], writes=['actT'], key='actT')
        P.dma('sp', CS[:], d['cs'].rearrange("(t p) c -> p t c", p=128), writes=['CS'], key='CS')

        def evac(t, cb, pst, pkey, cnt):
            o = cnt % 4
            if cb < 8:
                cp(P, 'act', ob[o][:], pst[:], [pkey], [('ob', o)])
                rotary(P, 'dve' if cnt % 2 else 'pool', ob[o][:].rearrange("p (h c) -> p h c", h=8), CS[:, t, :], tmp[o], ('ob', o), 'CS', ('tmp', o))
                cp(P, 'pool' if cnt % 2 else 'dve', obb[o][:], ob[o][:], [('ob', o)], [('obb', o)])
                P.dma('sp', d['qs'][t * 128:(t + 1) * 128, cb * 512:(cb + 1) * 512], obb[o][:], reads=[('obb', o)], key=('obbst', o))
            else:
                cp(P, 'act' if cnt % 2 else 'dve', obb[o][:], pst[:], [pkey], [('obb', o)])
                P.dma('sp', d['zs'][t * 128:(t + 1) * 128, (cb - 8) * 512:(cb - 7) * 512], obb[o][:], reads=[('obb', o)], key=('obbst', o))
        gemm_tokmajor(P, nc, es, None, 16, d['b_w_qz'][0], 8192, evac, "H", actT=actT)
        P.flush()


def stage_I(P, nc, d):
    with ExitStack() as es:
        identb = mk(nc, es, "identBI", [128, 128], BF16)
        MB = mk(nc, es, "MB", [128, 4, 128], BF16)
        esink = mk(nc, es, "esink", [128, 64], F32)
        P.dma('pool', identb[:], d['ident'], writes=['identb'], key='identb')
        P.dma('pool', MB[:], d['mb'], writes=['MB'], key='MB')
        P.dma('sp', esink[:], d['b_sinks'][0].partition_broadcast(128), writes=['esink'], key='esink')
        actf(P, esink[:], esink[:], AF.Exp, ['esink'], ['esink'])
        NSL = 3
        KVt = [mk(nc, es, f"KVt{i}", [128, 1024], BF16) for i in range(NSL)]
        Kd = mk(nc, es, "Kd", [128, 8, 2, 64], BF16)
        KT2 = [mk(nc, es, f"KT2{i}", [128, 8, 128], BF16) for i in range(NSL)]
        VA = [mk(nc, es, f"VA{i}", [128, 8, 65], BF16) for i in range(NSL)]
        Qt = [mk(nc, es, f"Qt{i}", [128, E], BF16) for i in range(2)]
        Zt = [mk(nc, es, f"Zt{i}", [128, E], BF16) for i in range(2)]
        QT = mk(nc, es, "QTt", [128, 32, 128], BF16)
        PTs = [mk(nc, es, f"PTs{i}", [128, 4, 128], BF16) for i in range(2)]
        ATT = mk(nc, es, "ATT", [128, E], F32)
        SZ = mk(nc, es, "SZ", [128, E], F32)
        AG = mk(nc, es, "AG", [128, E], BF16)
        agT = mk(nc, es, "agT", [128, 32, 128], BF16)
        den = mk(nc, es, "den", [128, 8], F32)
        pSC = [mk(nc, es, f"pSC{i}", [128, 512], F32, psum=True) for i in range(2)]
        pAO = [mk(nc, es, f"pAO{i}", [128, 512], F32, psum=True) for i in range(2)]
        pTP = [mk(nc, es, f"pTPI{i}", [128, 1024], BF16, psum=True) for i in range(2)]
        for i in range(NSL):
            P.op('pool', lambda e, i=i: e.memset(VA[i][:, :, 64:65], 1.0), writes=[('VA', i)])

        def prep_kv(slot):
            k3 = KVt[slot][:, 0:512].rearrange("p (g c) -> p g c", g=8)
            cp(P, 'pool', Kd[:, :, 0, :], k3, [('KVt', slot)], ['Kd'])
            cp(P, 'pool', Kd[:, :, 1, :], k3, [('KVt', slot)], ['Kd'])
            for g in range(8):
                tr(P, pTP[0][:, g * 128:(g + 1) * 128], Kd[:, g].rearrange("p a c -> p (a c)"), identb[:], ['Kd', 'identb'], [('pTP', 0)])
            cp(P, 'act', KT2[slot][:].rearrange("p g t -> p (g t)"), pTP[0][:], [('pTP', 0)], [('KT2', slot)])
            cp(P, 'dve', VA[slot][:, :, 0:64], KVt[slot][:, 512:1024].rearrange("p (g c) -> p g c", g=8), [('KVt', slot)], [('VA', slot)])

        slot_of_tile = {}
        nslot = 0
        for ch in range(NCH):
            qb = ch % 2
            load_rows(P, 'sp', Qt[qb], d['qs'], ch, 0, E, [('Qt', qb)], ('Qt', qb))
            load_rows(P, 'sp', Zt[qb], d['zs'], ch, 0, E, [('Zt', qb)], ('Zt', qb))
            if ch < 17:
                cs_ = nslot % NSL
                nslot += 1
                P.dma('sp', KVt[cs_][:], d['KVs'][ch * 128:(ch + 1) * 128, :], writes=[('KVt', cs_)], key=('KVt', cs_))
                prep_kv(cs_)
                slot_of_tile[ch] = cs_
                ps_ = slot_of_tile.get(ch - 1)
                mcur, mprev = (2, None) if ch == 0 else ((0, 3) if ch == 1 else (0, 1))
            else:
                s = ch - 17
                ps_ = nslot % NSL
                nslot += 1
                P.dma('pool', KVt[ps_][:, 0:512], d['ck'][s], writes=[('KVt', ps_)], key=('KVt', ps_))
                P.dma('pool', KVt[ps_][:, 512:1024], d['cv'][s], writes=[('KVt', ps_)], key=('KVt', ps_))
                prep_kv(ps_)
                cs_ = nslot % NSL
                nslot += 1
                load_rows(P, 'sp', KVt[cs_], d['KVs'], ch, 0, 1024, [('KVt', cs_)], ('KVt', cs_))
                prep_kv(cs_)
                mcur, mprev = 0, 1
            for q8 in range(4):
                tp = pTP[q8 % 2]
                for j in range(8):
                    pr = q8 * 8 + j
                    tr(P, tp[:, j * 128:(j + 1) * 128], Qt[qb][:, pr * 128:(pr + 1) * 128], identb[:], [('Qt', qb), 'identb'], [('pTP', q8 % 2)])
                cp(P, 'act' if q8 % 2 else 'dve', QT[:, q8 * 8:(q8 + 1) * 8, :].rearrange("p a t -> p (a t)"), tp[:], [('pTP', q8 % 2)], [('QT', q8)])
            kts = ([(ps_, mprev)] if ps_ is not None and mprev is not None else []) + [(cs_, mcur)]
            for j in range(32):
                g = j // 4
                sc = pSC[j % 2]
                pt = PTs[j % 2]
                for h2 in range(2):
                    sl = slice(h2 * 64, (h2 + 1) * 64)
                    for ki, (slot, mk_) in enumerate(kts):
                        o = sc[:, (h2 * 2 + ki) * 128:(h2 * 2 + ki + 1) * 128]
                        mm(P, o, KT2[slot][sl, g, :], QT[sl, j, :], True, False, [('KT2', slot), ('QT', j // 8)], [('pSC', j % 2)])
                        mm(P, o, identb[:], MB[:, mk_, :], False, True, ['identb', 'MB'], [('pSC', j % 2)])
                nk = len(kts)
                if nk == 2:
                    actf(P, pt[:].rearrange("p a t -> p (a t)"), sc[:], AF.Exp, [('pSC', j % 2)], [('PTs', j % 2)], scale=0.125)
                else:
                    for h2 in range(2):
                        actf(P, pt[:, h2 * 2, :], sc[:, h2 * 256:h2 * 256 + 128], AF.Exp, [('pSC', j % 2)], [('PTs', j % 2)], scale=0.125)
                ao = pAO[(j // 2) % 2]
                for h2 in range(2):
                    col = ((j % 2) * 2 + h2) * 65
                    for ki, (slot, mk_) in enumerate(kts):
                        mm(P, ao[:, col:col + 65], pt[:, h2 * 2 + ki, :], VA[slot][:, g, :], ki == 0, ki == nk - 1,
                           [('PTs', j % 2), ('VA', slot)], [('pAO', (j // 2) % 2)])
                if j % 2 == 1:
                    h0 = (j - 1) * 2
                    ao3 = ao[:, 0:260].rearrange("p (h c) -> p h c", h=4)
                    ak = ('pAO', (j // 2) % 2)
                    tt(P, 'dve', den[:, 0:4], ao3[:, :, 64], esink[:, h0:h0 + 4], ALU.add, [ak, 'esink'], ['den'])
                    recip(P, den[:, 4:8], den[:, 0:4], ['den'], ['den'])
                    tt(P, 'dve', ATT[:, h0 * 64:(h0 + 4) * 64].rearrange("p (h c) -> p h c", h=4), ao3[:, :, 0:64],
                       den[:, 4:8].unsqueeze(2).to_broadcast([128, 4, 64]), ALU.mult, [ak, 'den'], ['ATT'])
            actf(P, SZ[:], Zt[qb][:], AF.Silu, [('Zt', qb)], ['SZ'])
            tt(P, 'pool', AG[:], ATT[:], SZ[:], ALU.mult, ['ATT', 'SZ'], ['AG'])
            for q8 in range(4):
                tp = pTP[q8 % 2]
                for j in range(8):
                    pr = q8 * 8 + j
                    tr(P, tp[:, j * 128:(j + 1) * 128], AG[:, pr * 128:(pr + 1) * 128], identb[:], ['AG', 'identb'], [('pTP', q8 % 2)])
                cp(P, 'act' if q8 % 2 else 'dve', agT[:, q8 * 8:(q8 + 1) * 8, :].rearrange("p a t -> p (a t)"), tp[:], [('pTP', q8 % 2)], ['agT'])
            if ch < 17:
                P.dma('sp', d['agT'][:, :, ch * 128:(ch + 1) * 128], agT[:], reads=['agT'], key='agTst')
            elif ch == 17:
                P.dma('sp', d['agT'][:, :, SROW0:SROW0 + 128], agT[:], reads=['agT'], key='agTst')
            else:
                P.dma('sp', d['agT'][:, :, SROW0 + ch - 17:SROW0 + ch - 16], agT[:, :, 0:1], reads=['agT'], key='agTst',
                      allow_slow_non_contiguous=True)
        P.flush()


def stage_J2(P, nc, d):
    with ExitStack() as es:
        gf = mk(nc, es, "gf", [128, D], F32)
        junk = mk(nc, es, "junkJ", [128, D], F32)
        xt = [mk(nc, es, f"xJ{i}", [128, D], F32) for i in range(2)]
        ot = [mk(nc, es, f"oJ{i}", [128, D], F32) for i in range(2)]
        st = [mk(nc, es, f"stJ{i}", [128, 2], F32) for i in range(2)]
        P.dma('sp', gf[:], d['final_norm'][0].partition_broadcast(128), writes=['gf'], key='gf')
        for t in range(1, NT):
            b = t % 2
            P.dma('sp', xt[b][:], d['hp'][128 * t:128 * t + 128, :], writes=[('x', b)], key=('x', b))
            P.dma('sp', ot[b][:], d['o2'][128 * t:128 * t + 128, :], writes=[('o', b)], key=('o', b))
            tt(P, 'dve', xt[b][:], xt[b][:], ot[b][:], ALU.add, [('x', b), ('o', b)], [('x', b)])
            rms_stats(P, xt[b], junk, st[b], ('x', b), ('st', b))
            stt(P, 'dve', ot[b][:], xt[b][:], st[b][:, 1:2], gf[:], ALU.mult, ALU.mult, [('x', b), ('st', b), 'gf'], [('o', b)])
            if t < 17:
                P.dma('sp', d['o_yp'][(t - 1) * 128:t * 128, :], ot[b][:], reads=[('o', b)], key=('yst', b))
            else:
                P.dma('sp', d['o_ys'], ot[b][0:NS, :], reads=[('o', b)], key=('yst', b))
        P.flush()


class LazyDram(dict):
    def __init__(self, nc, debug_outs, ext_in):
        super().__init__()
        self.nc, self.debug_outs, self.ext_in = nc, debug_outs, ext_in
        self.spec = {}
        self.inputs, self.outputs = [], []

    def __missing__(self, name):
        kind, shape, dt = self.spec[name]
        if kind == 'scr':
            kind = 'ExternalInput' if name in self.ext_in else ('ExternalOutput' if name in self.debug_outs else 'Internal')
        if kind == 'ExternalInput':
            self.inputs.append(name)
        if kind == 'ExternalOutput':
            self.outputs.append(name)
        ap = self.nc.dram_tensor(name, list(shape), dt, kind=kind).ap()
        self[name] = ap
        return ap


def build(debug_outs=(), stages='ABLCFfGHIJj', ext_in=()):
    nc = bass.Bass("TRN2", target_bir_lowering=False)
    d = LazyDram(nc, debug_outs, ext_in)

    def inp(name, shape, dt=F32):
        d.spec[name] = ('ExternalInput', shape, dt)

    def outp(name, shape, dt=F32):
        d.spec[name] = ('ExternalOutput', shape, dt)

    def scr(name, shape, dt):
        d.spec[name] = ('scr', shape, dt)

    inp('xin', [T + 1, D]); inp('sshift', [128, D]); inp('swkv', [NS, 64, 64, 64])
    inp('ck', [NS, 128, 512]); inp('cv', [NS, 128, 512])
    inp('a_norm', [1, D]); inp('muT', [128, 6, 16]); inp('ident', [128, 128]); inp('tri', [128, 128]); inp('ones', [128, 128])
    inp('onehot', [128, 1]); inp('lmask', [128, 2]); inp('mask4', [128, 512]); inp('negsl', [128, 128]); inp('mb', [128, 4, 128])
    inp('cs', [T, 16]); inp('prm', [7, E])
    inp('a_w_rkvz', [1, 4, D, E]); inp('a_w1', [1, D, 96]); inp('a_w2', [1, 96, E]); inp('a_a1', [1, D, 96]); inp('a_a2', [1, 96, E])
    inp('a_w_out', [1, E, D]); inp('kv_norm', [1, D]); inp('w_kv', [D, 1024]); inp('b_norm', [1, D]); inp('b_w_qz', [1, D, 2 * E])
    inp('b_sinks', [1, 64]); inp('b_w_o', [1, E, D]); inp('final_norm', [1, D])
    outp('o_yp', [2048, D]); outp('o_ys', [NS, D]); outp('o_pwkv', [64, 64, 64]); outp('o_pshift', [1, D])
    outp('o_pck', [128, 512]); outp('o_pcv', [128, 512]); outp('o_swkv', [NS, 64, 64, 64]); outp('o_sshift', [NS, D])
    outp('o_sck', [NS, 128, 512]); outp('o_scv', [NS, 128, 512])
    scr('xmT', [6, 128, 16, T], BF16); scr('rkvz', [4, T, E], BF16); scr('wpre', [T, E], F32); scr('apre', [T, E], F32)
    scr('ygT', [128, 32, T], BF16); scr('o1', [T, D], F32); scr('hp', [T, D], F32); scr('hkvT', [128, 16, T], BF16); scr('hbT', [128, 16, T], BF16)
    scr('KVs', [T, 1024], BF16); scr('qs', [T, E], BF16); scr('zs', [T, E], BF16); scr('agT', [128, 32, T], BF16); scr('o2', [T, D], F32)
    with ExitStack() as stack:
        P = Prog(nc, stack)
        if 'A' in stages:
            stage_A(P, nc, d)
        if 'B' in stages:
            stage_B(P, nc, d)
        if 'L' in stages:
            stage_B_lora(P, nc, d)
        if 'C' in stages:
            stage_CDE(P, nc, d)
        if 'F' in stages:
            stage_outproj(P, nc, d, 'ygT', d['a_w_out'][0], 'o1', 'F1')
        if 'f' in stages:
            stage_F2(P, nc, d)
        if 'G' in stages:
            stage_G(P, nc, d)
        if 'H' in stages:
            stage_H(P, nc, d)
        if 'I' in stages:
            stage_I(P, nc, d)
        if 'J' in stages:
            stage_outproj(P, nc, d, 'agT', d['b_w_o'][0], 'o2', 'J1')
        if 'j' in stages:
            stage_J2(P, nc, d)
    nc._lazy = d
    return nc


def host_tables():
    f = np.float32
    j = np.arange(128)
    su = (j[:, None] < j[None, :]).astype(f)
    u = (j[:, None] <= j[None, :]).astype(f)
    tb = {}
    tb['ident'] = np.eye(128, dtype=f)
    tb['tri'] = u.copy()
    tb['ones'] = np.ones((128, 128), f)
    oh = np.zeros((128, 1), f); oh[0, 0] = 1
    tb['onehot'] = oh
    c = f(-np.exp(-0.5))
    lm = np.zeros((128, 2), f); lm[:, 0] = c; lm[0, 1] = c
    tb['lmask'] = lm
    tb['mask4'] = np.concatenate([su, u, -su, u], 1)
    tb['negsl'] = -(su.T).copy()
    NEG = f(-30000.0)
    jj = j[:, None]; ii = j[None, :]
    cur = np.where(jj <= ii, 0, NEG).astype(f)
    prev = np.where(jj >= ii, 0, NEG).astype(f)
    lead = np.where(jj >= 112, 0, NEG).astype(f)
    mb = np.stack([cur, prev, np.minimum(cur, lead), np.minimum(prev, lead)], 1)
    tb['mb'] = np.ascontiguousarray(mb)
    pos = np.zeros(T, f)
    pos[112:2176] = np.arange(2064)
    pos[2176:2176 + NS] = 16384
    inv = (f(500000.0) ** (-np.arange(8, dtype=f) * f(2.0) / f(16))).astype(f)
    ang = (pos[:, None] * inv[None, :]).astype(f)
    tb['cs'] = np.concatenate([np.cos(ang), np.sin(ang)], 1).astype(f)
    return tb


_NC = [None]


def kernel(**inp):
    f = np.float32
    inp = {k: np.asarray(v) for k, v in inp.items()}
    if _NC[0] is None:
        _NC[0] = build()
    nc = _NC[0]
    tb = host_tables()
    mu = inp['a_mu'][0]
    muT = np.ascontiguousarray(mu.reshape(6, 16, 128).transpose(2, 0, 1))
    prm = np.ascontiguousarray(np.stack([inp['a_w0'][0], inp['a_a0'][0], inp['a_k_k'][0], inp['a_k_a'][0], inp['a_r_k'][0].reshape(-1),
                                         inp['a_gn_g'][0], inp['a_gn_b'][0]], 0).astype(f))
    shared = dict(tb)
    shared.update(muT=muT, prm=prm, a_norm=inp['a_norm'], a_w_rkvz=inp['a_w_rkvz'], a_w1=inp['a_w1'], a_w2=inp['a_w2'], a_a1=inp['a_a1'],
                  a_a2=inp['a_a2'], a_w_out=inp['a_w_out'], kv_norm=inp['kv_norm'].reshape(1, D), w_kv=inp['w_kv'], b_norm=inp['b_norm'],
                  b_w_qz=inp['b_w_qz'], b_sinks=inp['b_sinks'], b_w_o=inp['b_w_o'], final_norm=inp['final_norm'].reshape(1, D))
    in_maps = []
    for core in range(8):
        b = core % 4
        ss = slice(core * NS, core * NS + NS)
        xin = np.zeros((T + 1, D), f)
        xin[1 + 112:1 + 128] = inp['meta_tokens']
        xin[1 + 128:1 + 128 + 2048] = inp['x_prompt'][b]
        xin[1 + SROW0:1 + SROW0 + NS] = inp['x_sample'][ss, 0]
        sshift = np.zeros((128, D), f)
        sshift[:NS] = inp['state_shift'][0, ss]
        m = dict(shared)
        m.update(xin=xin, sshift=sshift, swkv=np.ascontiguousarray(inp['state_wkv'][0, ss]),
                 ck=np.ascontiguousarray(inp['cache_k'][ss].reshape(NS, 128, 512)), cv=np.ascontiguousarray(inp['cache_v'][ss].reshape(NS, 128, 512)))
        in_maps.append(m)
    in_maps = [{k: m[k] for k in nc._lazy.inputs if k in m} for m in in_maps]
    res = run_bass_kernel_spmd(nc, in_maps, core_ids=list(range(8)))
    R = res.results
    g = lambda c, n: np.asarray(R[c][n], dtype=f)
    y_prompt = np.stack([g(b, 'o_yp') for b in range(4)], 0)
    y_sample = np.concatenate([g(c, 'o_ys') for c in range(8)], 0)[:, None, :]
    p_wkv = np.stack([g(b, 'o_pwkv') for b in range(4)], 0)[None]
    p_shift = np.concatenate([g(b, 'o_pshift') for b in range(4)], 0)[None]
    p_ck = np.stack([g(b, 'o_pck') for b in range(4)], 0).reshape(4, 128, 8, 64)
    p_cv = np.stack([g(b, 'o_pcv') for b in range(4)], 0).reshape(4, 128, 8, 64)
    s_wkv = np.concatenate([g(c, 'o_swkv') for c in range(8)], 0)[None]
    s_shift = np.concatenate([g(c, 'o_sshift') for c in range(8)], 0)[None]
    s_ck = np.concatenate([g(c, 'o_sck') for c in range(8)], 0).reshape(32, 128, 8, 64)
    s_cv = np.concatenate([g(c, 'o_scv') for c in range(8)], 0).reshape(32, 128, 8, 64)
    return (y_prompt, y_sample, p_wkv, p_shift, p_ck, p_cv, s_wkv, s_shift, s_ck, s_cv)
```

```python
import numpy as np
from contextlib import ExitStack
import concourse.bass as bass
import concourse.mybir as mybir
from concourse.bass_utils import run_bass_kernel_spmd

F32 = mybir.dt.float32
BF16 = mybir.dt.bfloat16
AF = mybir.ActivationFunctionType
ALU = mybir.AluOpType
AX = mybir.AxisListType

COMPUTE = ('pe', 'act', 'dve', 'pool')
ALLENG = ('pe', 'act', 'dve', 'pool', 'sp')
SAME_ENGINE_SYNC = True
PIPELINE = True
PSUM_KEYS = {'pC0', 'pC1', 'pTP', 'pPQ', 'pPA', 'pRX', 'pYS', 'ps', 'psg', 'psL', 'pSC', 'pAO'}


class Prog:
    def __init__(self, nc, stack):
        self.nc = nc
        self.stack = stack
        self.esem = {e: stack.enter_context(nc.semaphore("s_" + e)) for e in COMPUTE}
        self.ecnt = {e: 0 for e in COMPUTE}
        self.dsem = {}
        self.dcnt = {}
        self.dsid = {}
        self.free_dsems = []
        self.nds = 0
        self.waited = {e: {} for e in ALLENG}
        self.reset()

    def reset(self):
        self.ops = []
        self.lastw = {}
        self.readers = {}
        self.chain = {}

    max_ops = None
    cap = None

    def op(self, eng, fn, reads=(), writes=(), key=None):
        if self.cap is not None:
            self.cap.append((eng, fn, list(reads), list(writes), key))
            return -1
        i = len(self.ops)
        if self.max_ops is not None and i >= self.max_ops:
            return -1
        pr = [r for r in reads if (r[0] if isinstance(r, tuple) else r) in PSUM_KEYS]
        if pr:
            reads = [r for r in reads if r not in pr]
            writes = list(writes) + [r for r in pr if r not in writes]
        deps = set()
        for r in reads:
            w = self.lastw.get(r)
            if w is not None:
                deps.add(w)
        for w_ in writes:
            w = self.lastw.get(w_)
            if w is not None:
                deps.add(w)
            deps.update(self.readers.get(w_, ()))
        if key is not None:
            prev = self.chain.get(key)
            if prev is not None:
                deps.add(prev)
            self.chain[key] = i
        self.ops.append(dict(eng=eng, fn=fn, deps=deps, key=key))
        for w_ in writes:
            self.lastw[w_] = i
            self.readers[w_] = []
        ws = set(writes)
        for r in reads:
            if r not in ws:
                self.readers.setdefault(r, []).append(i)
        return i

    def dma(self, q, out, in_, reads=(), writes=(), key=None, **kw):
        assert key is not None
        return self.op(q, lambda e: e.dma_start(out=out, in_=in_, **kw), reads, writes, key=key)

    def flush(self):
        nc = self.nc
        ops = self.ops
        if not ops:
            return
        needed = set()
        for o in ops:
            needed.update(o['deps'])
        lastop = {}
        for i, o in enumerate(ops):
            if o['key'] is None:
                lastop[o['eng']] = i
        needed.update(lastop.values())
        tgt = [None] * len(ops)
        for i, o in enumerate(ops):
            if o['key'] is not None:
                k = o['key']
                if k not in self.dsem:
                    if self.free_dsems:
                        self.dsem[k], self.dcnt[k], self.dsid[k] = self.free_dsems.pop()
                    else:
                        self.nds += 1
                        self.dsem[k] = self.stack.enter_context(nc.semaphore("d_" + str(self.nds)))
                        self.dcnt[k] = 0
                        self.dsid[k] = ('d', self.nds)
                self.dcnt[k] += 16
                tgt[i] = (self.dsem[k], self.dcnt[k], self.dsid[k])
            elif i in needed:
                e = o['eng']
                self.ecnt[e] += 1
                tgt[i] = (self.esem[e], self.ecnt[e], ('e', e))
        per = {e: [] for e in ALLENG}
        for i, o in enumerate(ops):
            per[o['eng']].append(i)
        end_waits = []
        for e in COMPUTE:
            if e in lastop:
                end_waits.append(tgt[lastop[e]])
        for k in self.chain:
            end_waits.append((self.dsem[k], self.dcnt[k], self.dsid[k]))

        def run(ename, eobj):
            waited = self.waited[ename]
            for i in per[ename]:
                o = ops[i]
                need = {}
                for d in o['deps']:
                    od = ops[d]
                    if od['key'] is None and od['eng'] == ename:
                        if ename == 'pe' or not SAME_ENGINE_SYNC:
                            continue
                    sem, val, sid = tgt[d]
                    if need.get(sid, (None, 0))[1] < val:
                        need[sid] = (sem, val)
                for sid, (sem, val) in need.items():
                    if waited.get(sid, 0) < val:
                        eobj.wait_ge(sem, val)
                        waited[sid] = val
                ins = o['fn'](eobj)
                if tgt[i] is not None:
                    if o['key'] is not None:
                        ins.then_inc(tgt[i][0], 16)
                    else:
                        ins.then_inc(tgt[i][0], 1)
            for sem, val, sid in end_waits:
                if waited.get(sid, 0) < val:
                    eobj.wait_ge(sem, val)
                    waited[sid] = val

        with nc.Block() as block:
            @block.tensor
            def _(e):
                run('pe', e)

            @block.scalar
            def _(e):
                run('act', e)

            @block.vector
            def _(e):
                run('dve', e)

            @block.gpsimd
            def _(e):
                run('pool', e)

            @block.sync
            def _(e):
                run('sp', e)
        for k in list(self.dsem):
            self.free_dsems.append((self.dsem[k], self.dcnt[k], self.dsid[k]))
        self.dsem, self.dcnt, self.dsid = {}, {}, {}
        self.reset()


NT = 18
T = NT * 128
D = 2048
E = 4096
NS = 4
RMS_EPS = 1e-6


class Ctx:
    pass


_uid = [0]


def mk(nc, es, name, shape, dt, psum=False):
    _uid[0] += 1
    name = f"{name}_u{_uid[0]}"
    if psum:
        return es.enter_context(nc.psum_tensor(name, shape, dt))
    return es.enter_context(nc.sbuf_tensor(name, shape, dt))


def stage_A(P, nc, d):
    with ExitStack() as es:
        gA = mk(nc, es, "gA", [128, D], F32)
        muT = mk(nc, es, "muT", [128, 6, 16], F32)
        ident = mk(nc, es, "identA", [128, 128], F32)
        xc = [mk(nc, es, f"xc{i}", [128, D], F32) for i in range(2)]
        xp = [mk(nc, es, f"xp{i}", [128, D], F32) for i in range(2)]
        junk = mk(nc, es, "junkA", [128, D], F32)
        st = [mk(nc, es, f"stA{i}", [128, 4], F32) for i in range(2)]
        xnT = [mk(nc, es, f"xnT{i}", [128, 16, 128], F32) for i in range(2)]
        xxT = [mk(nc, es, f"xxT{i}", [128, 16, 128], F32) for i in range(2)]
        tmp = [mk(nc, es, f"tmpA{i}", [128, 16, 128], F32) for i in range(2)]
        xm = [mk(nc, es, f"xmA{i}", [128, 16, 128], BF16) for i in range(3)]
        ps = [mk(nc, es, f"psA{i}", [128, 512], F32, psum=True) for i in range(4)]

        P.dma('sp', gA[:], d['a_norm'][0].partition_broadcast(128), writes=['gA'], key='gA')
        P.dma('sp', muT[:], d['muT'], writes=['muT'], key='muT')
        P.dma('sp', ident[:], d['ident'], writes=['ident'], key='ident')
        ev = 0
        mi = 0
        for t in range(NT):
            b = t % 2
            P.dma('sp', xc[b][:], d['xin'][1 + 128 * t: 1 + 128 * t + 128, :], writes=[('xc', b)], key=('xc', b))
            if t < NT - 1:
                P.dma('sp', xp[b][:], d['xin'][128 * t: 128 * t + 128, :], writes=[('xp', b)], key=('xp', b))
            else:
                P.dma('sp', xp[b][:], d['sshift'], writes=[('xp', b)], key=('xp', b))
            P.op('pool', lambda e, b=b: e.memset(st[b][:, 0:2], 0.0), writes=[('st', b, 0), ('st', b, 1)])
            P.op('act', lambda e, b=b: e.activation(out=junk[:], in_=xc[b][:], func=AF.Square, accum_out=st[b][:, 0:1]),
                 reads=[('xc', b)], writes=['junk', ('st', b, 0)])
            if t < NT - 1:
                P.op('act', lambda e, b=b: e.activation(out=junk[:], in_=xp[b][:], func=AF.Square, accum_out=st[b][:, 1:2]),
                     reads=[('xp', b)], writes=['junk', ('st', b, 1)])
            nst = 2 if t < NT - 1 else 1
            P.op('dve', lambda e, b=b, n=nst: e.tensor_scalar(out=st[b][:, 2:2 + n], in0=st[b][:, 0:n], scalar1=1.0 / D, scalar2=RMS_EPS,
                                                              op0=ALU.mult, op1=ALU.add),
                 reads=[('st', b, 0), ('st', b, 1)], writes=[('st', b, 2)])
            P.op('act', lambda e, b=b, n=nst: e.sqrt(out=st[b][:, 2:2 + n], in_=st[b][:, 2:2 + n]),
                 reads=[('st', b, 2)], writes=[('st', b, 2)])
            P.op('dve', lambda e, b=b, n=nst: e.reciprocal(out=st[b][:, 2:2 + n], in_=st[b][:, 2:2 + n]),
                 reads=[('st', b, 2)], writes=[('st', b, 2)])
            P.op('dve', lambda e, b=b: e.scalar_tensor_tensor(out=xc[b][:], in0=xc[b][:], scalar=st[b][:, 2:3], in1=gA[:],
                                                              op0=ALU.mult, op1=ALU.mult),
                 reads=[('xc', b), ('st', b, 2), 'gA'], writes=[('xc', b)])
            if t < NT - 1:
                P.op('dve', lambda e, b=b: e.scalar_tensor_tensor(out=xp[b][:], in0=xp[b][:], scalar=st[b][:, 3:4], in1=gA[:],
                                                                  op0=ALU.mult, op1=ALU.mult),
                     reads=[('xp', b), ('st', b, 2), 'gA'], writes=[('xp', b)])
            if t == NT - 2:
                P.dma('sp', d['o_pshift'], xc[b][127:128, :], reads=[('xc', b)], key='o_pshift')
            if t == NT - 1:
                P.dma('sp', d['o_sshift'], xc[b][0:NS, :], reads=[('xc', b)], key='o_sshift')
            P.op('pool', lambda e, b=b: e.tensor_tensor(out=xp[b][:], in0=xp[b][:], in1=xc[b][:], op=ALU.subtract),
                 reads=[('xp', b), ('xc', b)], writes=[('xp', b)])
            for (src, srck, dst, dstk) in ((xc, 'xc', xnT, 'xnT'), (xp, 'xp', xxT, 'xxT')):
                for q in range(4):
                    pb = ev % 4
                    for j in range(4):
                        c = q * 4 + j
                        P.op('pe', lambda e, pb=pb, j=j, c=c, src=src, b=b: e.transpose(out=ps[pb][:, j * 128:(j + 1) * 128],
                                                                                        in_=src[b][:, c * 128:(c + 1) * 128], identity=ident[:]),
                             reads=[(srck, b), 'ident'], writes=[('ps', pb)])
                    eng = 'act' if ev % 2 == 0 else 'dve'
                    if eng == 'act':
                        P.op('act', lambda e, pb=pb, q=q, dst=dst, b=b: e.copy(out=dst[b][:, q * 4:(q + 1) * 4, :], in_=ps[pb][:].rearrange("p (a n) -> p a n", a=4)),
                             reads=[('ps', pb)], writes=[(dstk, b)])
                    else:
                        P.op('dve', lambda e, pb=pb, q=q, dst=dst, b=b: e.tensor_copy(out=dst[b][:, q * 4:(q + 1) * 4, :], in_=ps[pb][:].rearrange("p (a n) -> p a n", a=4)),
                             reads=[('ps', pb)], writes=[(dstk, b)])
                    ev += 1
            for p in range(6):
                m = mi % 3
                mi += 1
                e1 = 'dve' if p % 2 == 0 else 'pool'
                P.op(e1, lambda e, b=b, p=p: e.tensor_tensor(out=tmp[p % 2][:], in0=xxT[b][:], in1=muT[:, p, :].unsqueeze(2).to_broadcast([128, 16, 128]), op=ALU.mult),
                     reads=[('xxT', b), 'muT'], writes=[('tmp', p % 2)])
                P.op(e1, lambda e, b=b, p=p, m=m: e.tensor_tensor(out=xm[m][:], in0=tmp[p % 2][:], in1=xnT[b][:], op=ALU.add),
                     reads=[('tmp', p % 2), ('xnT', b)], writes=[('xm', m)])
                P.dma('sp', d['xmT'][p][:, :, t * 128:(t + 1) * 128], xm[m][:], reads=[('xm', m)], writes=[('xmT', p)], key=('xmst', m))
        P.flush()


def gemm_tokmajor(P, nc, es, actT_src, kc, w_src, ncols, evac, wkey, tiles=range(NT), act_res='actT', actT=None):
    wt = [mk(nc, es, f"wt_{wkey}{i}", [128, kc, 512], BF16) for i in range(2)]
    ps = [mk(nc, es, f"psg_{wkey}{i}", [128, 512], F32, psum=True) for i in range(4)]
    wv = w_src.rearrange("(c p) n -> p c n", p=128)
    cnt = 0
    for cb in range(ncols // 512):
        wb = cb % 2
        P.dma('pool', wt[wb][:], wv[:, :, cb * 512:(cb + 1) * 512], writes=[('wt', wkey, wb)], key=('wt', wkey, wb))
        for t in tiles:
            pb = cnt % 4
            cnt += 1
            for c in range(kc):
                P.op('pe', lambda e, pb=pb, c=c, t=t, wb=wb: e.matmul(ps[pb][:], lhsT=actT[:, c, t * 128:(t + 1) * 128], rhs=wt[wb][:, c, :],
                                                                      start=(c == 0), stop=(c == kc - 1)),
                     reads=[act_res, ('wt', wkey, wb)], writes=[('psg', wkey, pb)])
            evac(t, cb, ps[pb], ('psg', wkey, pb), cnt)


def stage_B(P, nc, d, projs=(0, 1, 2, 3)):
    for p in projs:
        with ExitStack() as es:
            actT = mk(nc, es, "actT", [128, 16, T], BF16)
            ob = [mk(nc, es, f"obB{i}", [128, 512], BF16) for i in range(4)]
            P.dma('sp', actT[:], d['xmT'][p], writes=['actT'], key='actT')

            def evac(t, cb, pst, pkey, cnt, p=p):
                o = cnt % 4
                if cnt % 2 == 0:
                    P.op('act', lambda e: e.copy(out=ob[o][:], in_=pst[:]), reads=[pkey], writes=[('ob', o)])
                else:
                    P.op('dve', lambda e: e.tensor_copy(out=ob[o][:], in_=pst[:]), reads=[pkey], writes=[('ob', o)])
                P.dma('sp', d['rkvz'][p][t * 128:(t + 1) * 128, cb * 512:(cb + 1) * 512], ob[o][:], reads=[('ob', o)], key=('obst', o))
            gemm_tokmajor(P, nc, es, None, 16, d['a_w_rkvz'][0, p], E, evac, f"B{p}", actT=actT)
            P.flush()


def tt(P, eng, out, in0, in1, op, reads, writes):
    P.op(eng, lambda e: e.tensor_tensor(out=out, in0=in0, in1=in1, op=op), reads, writes)


def ts(P, eng, out, in0, s1, s2, op0, op1, reads, writes):
    if s2 is None:
        P.op(eng, lambda e: e.tensor_scalar(out=out, in0=in0, scalar1=s1, scalar2=None, op0=op0), reads, writes)
    else:
        P.op(eng, lambda e: e.tensor_scalar(out=out, in0=in0, scalar1=s1, scalar2=s2, op0=op0, op1=op1), reads, writes)


def stt(P, eng, out, in0, scalar, in1, op0, op1, reads, writes):
    P.op(eng, lambda e: e.scalar_tensor_tensor(out=out, in0=in0, scalar=scalar, in1=in1, op0=op0, op1=op1), reads, writes)


def actf(P, out, in_, func, reads, writes, scale=1.0):
    P.op('act', lambda e: e.activation(out=out, in_=in_, func=func, scale=scale), reads, writes)


def cp(P, eng, out, in_, reads, writes):
    if eng == 'act':
        P.op('act', lambda e: e.copy(out=out, in_=in_), reads, writes)
    else:
        P.op(eng, lambda e: e.tensor_copy(out=out, in_=in_), reads, writes)


def mm(P, out, lhsT, rhs, start, stop, reads, writes):
    P.op('pe', lambda e: e.matmul(out, lhsT=lhsT, rhs=rhs, start=start, stop=stop), reads, writes)


def tr(P, out, in_, ident, reads, writes):
    P.op('pe', lambda e: e.transpose(out=out, in_=in_, identity=ident), reads, writes)


def red(P, eng, out, in_, reads, writes):
    P.op(eng, lambda e: e.reduce_sum(out=out, in_=in_, axis=AX.X), reads, writes)


def recip(P, out, in_, reads, writes):
    P.op('dve', lambda e: e.reciprocal(out=out, in_=in_), reads, writes)


GN_EPS = 64e-5
SROW0 = 17 * 128
ZROW0 = SROW0 + NS
NCH = 17 + NS
CH_LIST = list(range(NCH))
CB_LIST = list(range(8))


def load_rows(P, q, dst, src, ch, c0, c1, writes, key):
    if ch < 17:
        P.dma(q, dst[:], src[ch * 128:(ch + 1) * 128, c0:c1], writes=writes, key=key)
    else:
        s = ch - 17
        P.dma(q, dst[0:1], src[SROW0 + s:SROW0 + s + 1, c0:c1], writes=writes, key=key)
        P.dma(q, dst[1:65], src[ZROW0:ZROW0 + 64, c0:c1], writes=writes, key=key)
        P.dma(q, dst[64:128], src[ZROW0:ZROW0 + 64, c0:c1], writes=writes, key=key)


def stage_B_lora(P, nc, d):
    for which, (xi, w1n, w2n, outn, func) in enumerate(((4, 'a_w1', 'a_w2', 'wpre', AF.Tanh), (5, 'a_a1', 'a_a2', 'apre', AF.Copy))):
        with ExitStack() as es:
            actT = mk(nc, es, "actT", [128, 16, T], BF16)
            w1 = mk(nc, es, "w1", [128, 16, 96], BF16)
            w2 = mk(nc, es, "w2", [96, E], BF16)
            hT = mk(nc, es, "hT", [96, T], BF16)
            ob = [mk(nc, es, f"obL{i}", [128, 512], F32) for i in range(4)]
            ps = [mk(nc, es, f"psL{i}", [128, 512], F32, psum=True) for i in range(4)]
            P.dma('sp', actT[:], d['xmT'][xi], writes=['actT'], key='actT')
            P.dma('pool', w1[:], d[w1n][0].rearrange("(c p) n -> p c n", p=128), writes=['w1'], key='w1')
            P.dma('pool', w2[:], d[w2n][0], writes=['w2'], key='w2')
            cnt = 0
            for t in range(NT):
                pb = cnt % 4
                cnt += 1
                for c in range(16):
                    mm(P, ps[pb][0:96, 0:128], w1[:, c, :], actT[:, c, t * 128:(t + 1) * 128], c == 0, c == 15,
                       ['actT', 'w1'], [('psL', pb)])
                actf(P, hT[:, t * 128:(t + 1) * 128], ps[pb][0:96, 0:128], func, [('psL', pb)], [('hT', t)])
            for t in range(NT):
                for cb in range(8):
                    pb = cnt % 4
                    cnt += 1
                    mm(P, ps[pb][:], hT[:, t * 128:(t + 1) * 128], w2[:, cb * 512:(cb + 1) * 512], True, True,
                       [('hT', t), 'w2'], [('psL', pb)])
                    cp(P, 'act' if cnt % 2 else 'dve', ob[pb][:], ps[pb][:], [('psL', pb)], [('obL', pb)])
                    P.dma('sp', d[outn][t * 128:(t + 1) * 128, cb * 512:(cb + 1) * 512], ob[pb][:], reads=[('obL', pb)], key=('obLst', pb))
            P.flush()


def stage_CDE(P, nc, d):
    with ExitStack() as es:
        ident = mk(nc, es, "identF", [128, 128], F32)
        identb = mk(nc, es, "identB", [128, 128], BF16)
        tri = mk(nc, es, "tri", [128, 128], F32)
        ones = mk(nc, es, "ones", [128, 128], F32)
        onehot = mk(nc, es, "onehot", [128, 1], F32)
        lmask = mk(nc, es, "lmask", [128, 2], F32)
        mask4 = mk(nc, es, "mask4", [128, 512], F32)
        negsl = mk(nc, es, "negsl", [128, 128], F32)
        for nm, tl in (('ident', ident), ('tri', tri), ('ones', ones), ('onehot', onehot), ('lmask', lmask), ('mask4', mask4), ('negsl', negsl)):
            P.dma('sp', tl[:], d[nm], writes=[nm], key=nm)
        P.dma('pool', identb[:], d['ident'], writes=['identb'], key='identb')
        CONST = ['ident', 'tri', 'ones', 'onehot', 'lmask', 'mask4', 'negsl', 'identb']
        ST = mk(nc, es, "ST", [128, 32, 64], F32)
        STb = mk(nc, es, "STb", [128, 32, 64], BF16)
        SN = mk(nc, es, "SN", [64, 64, 64], F32)
        BON = mk(nc, es, "BON", [128, 64], F32)
        stmp = mk(nc, es, "stmp", [128, 64], F32)
        ygT = [mk(nc, es, f"ygT{i}", [128, 32, 128], BF16) for i in range(2)]
        NB = 3
        PRM = [mk(nc, es, f"PRM{i}", [128, 7, 512], F32) for i in range(NB)]
        Rb = [mk(nc, es, f"Rb{i}", [128, 512], BF16) for i in range(NB)]
        Kb = [mk(nc, es, f"Kb{i}", [128, 512], BF16) for i in range(NB)]
        Vb = [mk(nc, es, f"Vb{i}", [128, 512], BF16) for i in range(NB)]
        Zb = [mk(nc, es, f"Zb{i}", [128, 512], BF16) for i in range(NB)]
        Wp = [mk(nc, es, f"Wp{i}", [128, 512], F32) for i in range(NB)]
        Ap = [mk(nc, es, f"Ap{i}", [128, 512], F32) for i in range(NB)]
        f32names = ['LD', 'KKf', 'KMf', 'Bf', 'SQ', 'T1', 'E1', 'E2', 'E3', 'E4', 'GT', 'Dinv']
        W = {n: mk(nc, es, n, [128, 512], F32) for n in f32names}
        sm = mk(nc, es, "sm", [128, 64], F32)
        TM = [mk(nc, es, f"TM{i}", [128, 4, 512], BF16) for i in range(NB)]
        KVb = [mk(nc, es, f"KVb{i}", [128, 512], BF16) for i in range(NB)]
        BVb = [mk(nc, es, f"BVb{i}", [128, 512], BF16) for i in range(NB)]
        FT = [mk(nc, es, f"FT{i}", [128, 4, 4, 128], BF16) for i in range(NB)]
        gCs = [mk(nc, es, f"gCs{i}", [128, 4], F32) for i in range(NB)]
        AK = mk(nc, es, "AK", [128, 4, 512], BF16)
        MT = mk(nc, es, "MT", [128, 4, 128], BF16)
        Rm = [mk(nc, es, f"Rm{i}", [128, 4, 128], BF16) for i in range(2)]
        PP = [mk(nc, es, f"PP{i}", [128, 4, 2, 128], BF16) for i in range(2)]
        Xb = mk(nc, es, "Xb", [128, 256], BF16)
        nSA = mk(nc, es, "nSA", [128, 256], BF16)
        Ycb = [mk(nc, es, f"Ycb{i}", [128, 512], F32) for i in range(2)]
        EY = {n: mk(nc, es, n, [128, 512], F32) for n in ('Ysq', 'Yn', 'Sz')}
        YG = mk(nc, es, "YG", [128, 512], BF16)
        pC0 = mk(nc, es, "pC0", [128, 512], F32, psum=True)
        pC1 = mk(nc, es, "pC1", [128, 512], F32, psum=True)
        pRX2 = mk(nc, es, "pRX2", [128, 512], F32, psum=True)
        pPQ = mk(nc, es, "pPQ", [128, 4, 2, 128], F32, psum=True)
        pPA = mk(nc, es, "pPA", [128, 512], F32, psum=True)
        pRX = mk(nc, es, "pRX", [128, 512], F32, psum=True)
        pYS = mk(nc, es, "pYS", [128, 512], F32, psum=True)

        def state_load(s):
            P.dma('sp', SN[:], d['swkv'][s].rearrange("h v k -> v h k"), writes=['SN'], key='SN')
            for g8 in range(4):
                for q in range(8):
                    gp = g8 * 8 + q
                    tr(P, pC0[:, q * 64:(q + 1) * 64], SN[:, 2 * gp:2 * gp + 2, :].rearrange("v a k -> v (a k)"), ident[0:64, 0:64],
                       ['SN', 'ident'], ['pC0'])
                cp(P, 'act', ST[:, g8 * 8:(g8 + 1) * 8, :], pC0[:].rearrange("p (a v) -> p a v", a=8), ['pC0'], [('ST', g8 * 8 + q) for q in range(8)])
                cp(P, 'dve', STb[:, g8 * 8:(g8 + 1) * 8, :], pC0[:].rearrange("p (a v) -> p a v", a=8), ['pC0'], [('STb', g8 * 8 + q) for q in range(8)])

        def state_save(dst):
            for g4 in range(8):
                for q in range(4):
                    gp = g4 * 4 + q
                    tr(P, pC0[0:64, q * 128:(q + 1) * 128], ST[:, gp, :], ident[:], [('ST', gp), 'ident'], ['pC0'])
                cp(P, 'act', SN[:, g4 * 8:(g4 + 1) * 8, :].rearrange("v a k -> v (a k)"), pC0[0:64, :], ['pC0'], ['SN'])
            P.dma('sp', dst.rearrange("h v k -> v h k"), SN[:], reads=['SN'], key='SNst')

        P.op('pool', lambda e: e.memset(ST[:], 0.0), writes=[('ST', g) for g in range(32)])
        P.op('pool', lambda e: e.memset(STb[:], 0.0), writes=[('STb', g) for g in range(32)])

        def phase_C(ch, cb, b):
            lcol = 0 if ch < 17 else 1
            c0, c1 = cb * 512, (cb + 1) * 512
            P.dma('sp', PRM[b][:], d['prm'][:, c0:c1].partition_broadcast(128), writes=[('PRM', b)], key=('PRM', b))
            load_rows(P, 'sp', Rb[b], d['rkvz'][0], ch, c0, c1, [('Rb', b)], ('Rb', b))
            load_rows(P, 'sp', Kb[b], d['rkvz'][1], ch, c0, c1, [('Kb', b)], ('Kb', b))
            load_rows(P, 'sp', Vb[b], d['rkvz'][2], ch, c0, c1, [('Vb', b)], ('Vb', b))
            load_rows(P, 'sp', Zb[b], d['rkvz'][3], ch, c0, c1, [('Zb', b)], ('Zb', b))
            load_rows(P, 'sp', Wp[b], d['wpre'], ch, c0, c1, [('Wp', b)], ('Wp', b))
            load_rows(P, 'sp', Ap[b], d['apre'], ch, c0, c1, [('Ap', b)], ('Ap', b))
            prm = lambda i, b=b: PRM[b][:, i, :]
            tt(P, 'dve', Wp[b][:], Wp[b][:], prm(0), ALU.add, [('Wp', b), ('PRM', b)], [('Wp', b)])
            actf(P, Wp[b][:], Wp[b][:], AF.Sigmoid, [('Wp', b)], [('Wp', b)])
            ts(P, 'dve', W['LD'][:], Wp[b][:], lmask[:, lcol:lcol + 1], None, ALU.mult, None, [('Wp', b), 'lmask'], ['LD'])
            tt(P, 'dve', Ap[b][:], Ap[b][:], prm(1), ALU.add, [('Ap', b), ('PRM', b)], [('Ap', b)])
            actf(P, Ap[b][:], Ap[b][:], AF.Sigmoid, [('Ap', b)], [('Ap', b)])
            tt(P, 'dve', W['KKf'][:], Kb[b][:], prm(2), ALU.mult, [('Kb', b), ('PRM', b)], ['KKf'])
            tt(P, 'pool', W['SQ'][:], W['KKf'][:], W['KKf'][:], ALU.mult, ['KKf'], ['SQ'])
            red(P, 'dve', sm[:, 0:8], W['SQ'][:].rearrange("p (h c) -> p h c", h=8), ['SQ'], [('sm', 0)])
            ts(P, 'dve', sm[:, 0:8], sm[:, 0:8], 1e-24, None, ALU.max, None, [('sm', 0)], [('sm', 0)])
            P.op('act', lambda e: e.sqrt(out=sm[:, 0:8], in_=sm[:, 0:8]), [('sm', 0)], [('sm', 0)])
            recip(P, sm[:, 0:8], sm[:, 0:8], [('sm', 0)], [('sm', 0)])
            tt(P, 'dve', W['KKf'][:].rearrange("p (h c) -> p h c", h=8), W['KKf'][:].rearrange("p (h c) -> p h c", h=8),
               sm[:, 0:8].unsqueeze(2).to_broadcast([128, 8, 64]), ALU.mult, ['KKf', ('sm', 0)], ['KKf'])
            stt(P, 'dve', W['T1'][:], Ap[b][:], -1.0, prm(3), ALU.add, ALU.mult, [('Ap', b), ('PRM', b)], ['T1'])
            stt(P, 'dve', W['KMf'][:], W['T1'][:], 1.0, Kb[b][:], ALU.add, ALU.mult, ['T1', ('Kb', b)], ['KMf'])
            tt(P, 'pool', W['Bf'][:], W['KKf'][:], Ap[b][:], ALU.mult, ['KKf', ('Ap', b)], ['Bf'])
            tt(P, 'pool', W['T1'][:], Rb[b][:], W['KMf'][:], ALU.mult, [('Rb', b), 'KMf'], ['T1'])
            tt(P, 'pool', W['T1'][:], W['T1'][:], prm(4), ALU.mult, ['T1', ('PRM', b)], ['T1'])
            red(P, 'dve', BON[:, cb * 8:(cb + 1) * 8], W['T1'][:].rearrange("p (h c) -> p h c", h=8), ['T1'], [('BON', cb)])
            mm(P, pC0[:], tri[:], W['LD'][:], True, True, ['tri', 'LD'], ['pC0'])
            mm(P, pC1[:], ones[:], W['LD'][:], True, True, ['ones', 'LD'], ['pC1'])
            actf(P, W['E1'][:], pC0[:], AF.Exp, ['pC0'], ['E1'])
            actf(P, W['E2'][:], pC0[:], AF.Exp, ['pC0'], ['E2'], scale=-1.0)
            actf(P, W['GT'][:], pC1[:], AF.Exp, ['pC1'], ['GT'])
            actf(P, W['Dinv'][:], W['LD'][:], AF.Exp, ['LD'], ['Dinv'], scale=-1.0)
            tt(P, 'dve', W['E3'][:], W['E1'][:], W['Dinv'][:], ALU.mult, ['E1', 'Dinv'], ['E3'])
            tt(P, 'pool', W['E4'][:], W['GT'][:], W['E2'][:], ALU.mult, ['GT', 'E2'], ['E4'])
            for pp in range(4):
                mm(P, pC1[:, pp:pp + 1], W['GT'][:, pp * 128:(pp + 1) * 128], onehot[:, 0:1], True, True, ['GT', 'onehot'], ['pC1'])
            cp(P, 'act', gCs[b][:], pC1[:, 0:4], ['pC1'], [('gCs', b)])
            tt(P, 'dve', TM[b][:, 1, :], Rb[b][:], W['E1'][:], ALU.mult, [('Rb', b), 'E1'], [('TM', b, 1)])
            tt(P, 'dve', TM[b][:, 2, :], W['KMf'][:], W['E2'][:], ALU.mult, ['KMf', 'E2'], [('TM', b, 2)])
            tt(P, 'pool', TM[b][:, 3, :], W['Bf'][:], W['E2'][:], ALU.mult, ['Bf', 'E2'], [('TM', b, 3)])
            tt(P, 'dve', TM[b][:, 0, :], W['KKf'][:], W['E3'][:], ALU.mult, ['KKf', 'E3'], [('TM', b, 0)])
            tt(P, 'pool', KVb[b][:], W['KMf'][:], W['E4'][:], ALU.mult, ['KMf', 'E4'], [('KVb', b)])
            tt(P, 'pool', BVb[b][:], W['Bf'][:], W['E4'][:], ALU.mult, ['Bf', 'E4'], [('BVb', b)])
            for hf in range(2):
                for pq in range(2):
                    pp = hf * 2 + pq
                    for kd in range(4):
                        tr(P, pC0[:].bitcast(BF16)[:, (pq * 4 + kd) * 128:(pq * 4 + kd + 1) * 128], TM[b][:, kd, pp * 128:(pp + 1) * 128], identb[:],
                           [('TM', b, kd), 'identb'], ['pC0'])
                cp(P, 'act' if hf == 0 else 'dve', FT[b][:, 2 * hf:2 * hf + 2].rearrange("p a k t -> p (a k t)"), pC0[:].bitcast(BF16)[:, 0:1024],
                   ['pC0'], [('FT', b, hf)])

        def phase_D(ch, cb, b, yb):
            for g in range(2):
                ftk = ('FT', b, g)
                abank = [(pPA[:], 'pPA'), (pPQ[:, 0:2].rearrange("p a b t -> p (a b t)"), ('pPQ', 0)),
                         (pPQ[:, 2:4].rearrange("p a b t -> p (a b t)"), ('pPQ', 1)), (pRX2[:], ('pRX', 1))]
                for i in range(4):
                    pp, h2 = 2 * g + i // 2, i % 2
                    fts = FT[b][h2 * 64:(h2 + 1) * 64, pp]
                    rhs2 = fts[:, 0:2, :].rearrange("p k t -> p (k t)")
                    bk, bkey = abank[i]
                    mm(P, bk[:, 0:256], fts[:, 2, :], rhs2, True, True, [ftk], [bkey])
                    mm(P, bk[:, 256:512], fts[:, 3, :], rhs2, True, True, [ftk], [bkey])
                    pnb, pnk = (pYS, 'pYS') if h2 == 0 else (pRX, ('pRX', 0))
                    mm(P, pnb[:, (i // 2) * 128:(i // 2 + 1) * 128], fts[:, 0, :], fts[:, 3, :], True, True, [ftk], [pnk])
                for i in range(4):
                    bk, bkey = abank[i]
                    tt(P, 'dve', AK[:, i, :], bk, mask4[:], ALU.mult, [bkey, 'mask4'], [('AK', i)])
                for i in range(4):
                    pnb, pnk = (pYS, 'pYS') if i % 2 == 0 else (pRX, ('pRX', 0))
                    tt(P, 'dve', MT[:, i, :], pnb[:, (i // 2) * 128:(i // 2 + 1) * 128], negsl[:], ALU.mult, [pnk, 'negsl'], [('MT', i // 2)])
                for hg in range(2):
                    tt(P, 'pool', Rm[0][:, 2 * hg:2 * hg + 2, :], AK[:, 2 * hg:2 * hg + 2, 256:384],
                       identb[:].unsqueeze(1).to_broadcast([128, 2, 128]), ALU.add,
                       [('AK', 2 * hg), ('AK', 2 * hg + 1), 'identb'], [('Rm', 0, hg)])
                cur = 0
                rxb = [pRX, pRX2]

                def squares(lev, hg):
                    nxt = lev % 2
                    last = (lev == 6)
                    for i in (2 * hg, 2 * hg + 1):
                        if lev == 1:
                            Pc, PTc = AK[:, i, 256:384], MT[:, i, :]
                            rk = [('AK', i), ('MT', hg)]
                        else:
                            Pc, PTc = PP[1 - nxt][:, i, 0, :], PP[1 - nxt][:, i, 1, :]
                            rk = [('PP', 1 - nxt, hg)]
                        if not last:
                            mm(P, pPQ[:, i, 0, :], PTc, Pc, True, True, rk, [('pPQ', hg)])
                        mm(P, pPQ[:, i, 1, :], Pc, PTc, True, True, rk, [('pPQ', hg)])

                def evac_pp(lev, hg):
                    nxt = lev % 2
                    hs_ = slice(2 * hg, 2 * hg + 2)
                    if lev < 6:
                        cp(P, 'act', PP[nxt][:, hs_], pPQ[:, hs_], [('pPQ', hg)], [('PP', nxt, hg)])
                    else:
                        cp(P, 'act', PP[nxt][:, hs_, 1, :], pPQ[:, hs_, 1, :], [('pPQ', hg)], [('PP', nxt, hg)])

                for hg in range(2):
                    squares(1, hg)
                for hg in range(2):
                    evac_pp(1, hg)
                for lev in range(1, 7):
                    nxt = lev % 2
                    for hg in range(2):
                        for q, i in enumerate((2 * hg, 2 * hg + 1)):
                            mm(P, rxb[hg][:, q * 128:(q + 1) * 128], PP[nxt][:, i, 1, :], Rm[cur][:, i, :], True, True,
                               [('PP', nxt, hg), ('Rm', cur, hg)], [('pRX', hg)])
                        if lev < 6:
                            squares(lev + 1, hg)
                    for hg in range(2):
                        tt(P, 'dve', Rm[1 - cur][:, 2 * hg:2 * hg + 2, :], rxb[hg][:, 0:256].rearrange("p (a t) -> p a t", a=2),
                           Rm[cur][:, 2 * hg:2 * hg + 2, :], ALU.add, [('pRX', hg), ('Rm', cur, hg)], [('Rm', 1 - cur, hg)])
                        if lev < 6:
                            evac_pp(lev + 1, hg)
                    cur = 1 - cur
                Rf = Rm[cur]
                for i in range(4):
                    pp, h2 = 2 * g + i // 2, i % 2
                    gp = cb * 4 + pp
                    hh = 4 * g + i
                    fts = FT[b][h2 * 64:(h2 + 1) * 64, pp]
                    mm(P, pRX[:, i * 64:(i + 1) * 64], fts[:, 0, :], STb[h2 * 64:(h2 + 1) * 64, gp, :], True, False,
                       [ftk, ('STb', gp)], [('pRX', 0)])
                    mm(P, pRX[:, i * 64:(i + 1) * 64], AK[:, i, 0:128], Vb[b][:, hh * 64:(hh + 1) * 64], False, True,
                       [('AK', i), ('Vb', b)], [('pRX', 0)])
                cp(P, 'act', Xb[:], pRX[:, 0:256], [('pRX', 0)], ['Xb'])
                for i in range(4):
                    mm(P, pRX[:, 256 + i * 64:256 + (i + 1) * 64], Rf[:, i, :], Xb[:, i * 64:(i + 1) * 64], True, True,
                       [('Rm', cur, i // 2), 'Xb'], [('pRX', 0)])
                P.op('act', lambda e: e.mul(out=nSA[:], in_=pRX[:, 256:512], mul=-1.0), [('pRX', 0)], ['nSA'])
                for i in range(4):
                    pp, h2 = 2 * g + i // 2, i % 2
                    gp = cb * 4 + pp
                    hh = 4 * g + i
                    fts = FT[b][h2 * 64:(h2 + 1) * 64, pp]
                    o = pYS[:, i * 64:(i + 1) * 64]
                    mm(P, o, fts[:, 1, :], STb[h2 * 64:(h2 + 1) * 64, gp, :], True, False, [ftk, ('STb', gp)], ['pYS'])
                    mm(P, o, AK[:, i, 128:256], Vb[b][:, hh * 64:(hh + 1) * 64], False, False, [('AK', i), ('Vb', b)], ['pYS'])
                    mm(P, o, AK[:, i, 384:512], nSA[:, i * 64:(i + 1) * 64], False, True, [('AK', i), 'nSA'], ['pYS'])
                for q in range(2):
                    pp = 2 * g + q
                    o = pYS[:, 256 + q * 128:256 + (q + 1) * 128]
                    mm(P, o, KVb[b][:, pp * 128:(pp + 1) * 128], Vb[b][:, pp * 128:(pp + 1) * 128], True, False,
                       [('KVb', b), ('Vb', b)], ['pYS'])
                    mm(P, o, BVb[b][:, pp * 128:(pp + 1) * 128], nSA[:, q * 128:(q + 1) * 128], False, True,
                       [('BVb', b), 'nSA'], ['pYS'])
                cp(P, 'act', Ycb[yb][:, g * 256:(g + 1) * 256], pYS[:, 0:256], ['pYS'], [('Ycb', yb, g)])
                for q in range(2):
                    pp = 2 * g + q
                    gp = cb * 4 + pp
                    for h2 in range(2):
                        sl = slice(h2 * 64, (h2 + 1) * 64)
                        ts(P, 'dve', stmp[sl, :], ST[sl, gp, :], gCs[b][sl, pp:pp + 1], None, ALU.mult, None,
                           [('ST', gp), ('gCs', b)], ['stmp'])
                        tt(P, 'dve', ST[sl, gp, :], pYS[sl, 256 + q * 128 + h2 * 64:256 + q * 128 + (h2 + 1) * 64], stmp[sl, :], ALU.add,
                           ['pYS', 'stmp'], [('ST', gp)])
                    cp(P, 'pool', STb[:, gp, :], ST[:, gp, :], [('ST', gp)], [('STb', gp)])

        def phase_E(ch, cb, b, yb):
            prm = lambda i, b=b: PRM[b][:, i, :]
            Y3 = Ycb[yb][:].rearrange("p (h c) -> p h c", h=8)
            yk = [('Ycb', yb, 0), ('Ycb', yb, 1)]
            red(P, 'dve', sm[:, 8:16], Y3, yk, [('sm', 1)])
            tt(P, 'pool', EY['Ysq'][:], Ycb[yb][:], Ycb[yb][:], ALU.mult, yk, ['Ysq'])
            red(P, 'dve', sm[:, 16:24], EY['Ysq'][:].rearrange("p (h c) -> p h c", h=8), ['Ysq'], [('sm', 2)])
            ts(P, 'dve', sm[:, 8:16], sm[:, 8:16], 1.0 / 64, None, ALU.mult, None, [('sm', 1)], [('sm', 1)])
            tt(P, 'dve', sm[:, 24:32], sm[:, 8:16], sm[:, 8:16], ALU.mult, [('sm', 1)], [('sm', 3)])
            stt(P, 'dve', sm[:, 16:24], sm[:, 16:24], 1.0 / 64, sm[:, 24:32], ALU.mult, ALU.subtract, [('sm', 2), ('sm', 3)], [('sm', 2)])
            ts(P, 'dve', sm[:, 16:24], sm[:, 16:24], GN_EPS, None, ALU.add, None, [('sm', 2)], [('sm', 2)])
            P.op('act', lambda e: e.sqrt(out=sm[:, 16:24], in_=sm[:, 16:24]), [('sm', 2)], [('sm', 2)])
            recip(P, sm[:, 16:24], sm[:, 16:24], [('sm', 2)], [('sm', 2)])
            Yn3 = EY['Yn'][:].rearrange("p (h c) -> p h c", h=8)
            tt(P, 'dve', Yn3, Y3, sm[:, 8:16].unsqueeze(2).to_broadcast([128, 8, 64]), ALU.subtract, yk + [('sm', 1)], ['Yn'])
            tt(P, 'dve', Yn3, Yn3, sm[:, 16:24].unsqueeze(2).to_broadcast([128, 8, 64]), ALU.mult, ['Yn', ('sm', 2)], ['Yn'])
            tt(P, 'pool', EY['Yn'][:], EY['Yn'][:], prm(5), ALU.mult, ['Yn', ('PRM', b)], ['Yn'])
            tt(P, 'pool', EY['Yn'][:], EY['Yn'][:], prm(6), ALU.add, ['Yn', ('PRM', b)], ['Yn'])
            tt(P, 'dve', EY['Ysq'][:].rearrange("p (h c) -> p h c", h=8), Vb[b][:].rearrange("p (h c) -> p h c", h=8),
               BON[:, cb * 8:(cb + 1) * 8].unsqueeze(2).to_broadcast([128, 8, 64]), ALU.mult, [('Vb', b), ('BON', cb)], ['Ysq'])
            tt(P, 'pool', EY['Yn'][:], EY['Yn'][:], EY['Ysq'][:], ALU.add, ['Yn', 'Ysq'], ['Yn'])
            actf(P, EY['Sz'][:], Zb[b][:], AF.Silu, [('Zb', b)], ['Sz'])
            tt(P, 'dve', YG[:], EY['Yn'][:], EY['Sz'][:], ALU.mult, ['Yn', 'Sz'], ['YG'])
            for q in range(4):
                tr(P, pC1[:].bitcast(BF16)[:, q * 128:(q + 1) * 128], YG[:, q * 128:(q + 1) * 128], identb[:], ['YG', 'identb'], ['pC1'])
            cp(P, 'act', ygT[ch % 2][:, cb * 4:(cb + 1) * 4, :].rearrange("p a t -> p (a t)"), pC1[:].bitcast(BF16)[:, 0:512], ['pC1'], [('ygT', ch % 2)])
            if cb == CB_LIST[-1]:
                yt = ygT[ch % 2]
                if ch < 17:
                    P.dma('sp', d['ygT'][:, :, ch * 128:(ch + 1) * 128], yt[:], reads=[('ygT', ch % 2)], key='ygTst')
                elif ch == 17:
                    P.dma('sp', d['ygT'][:, :, SROW0:SROW0 + 128], yt[:], reads=[('ygT', ch % 2)], key='ygTst')
                else:
                    s_ = ch - 17
                    P.dma('sp', d['ygT'][:, :, SROW0 + s_:SROW0 + s_ + 1], yt[:, :, 0:1], reads=[('ygT', ch % 2)], key='ygTst',
                          allow_slow_non_contiguous=True)

        def capture(fn, *a):
            P.cap = []
            fn(*a)
            l = P.cap
            P.cap = None
            return l

        def replay(l):
            for (eng, fn, reads, writes, key) in l:
                P.op(eng, fn, reads, writes, key)

        def merge(main, first, second):
            out = []
            n = len(main)
            h = n // 2 if (first and second) else (n if first else 0)
            da = db = 0
            for i, o in enumerate(main):
                out.append(o)
                if i < h:
                    t_ = (i + 1) * len(first) // max(h, 1)
                    while da < t_:
                        out.append(first[da])
                        da += 1
                else:
                    if da < len(first):
                        out.extend(first[da:])
                        da = len(first)
                    t_ = (i + 1 - h) * len(second) // max(n - h, 1)
                    while db < t_:
                        out.append(second[db])
                        db += 1
            out.extend(first[da:])
            out.extend(second[db:])
            return out

        units = [(ch, cb) for ch in CH_LIST for cb in CB_LIST]
        replay(capture(phase_C, units[0][0], units[0][1], 0))
        for idx, (ch, cb) in enumerate(units):
            if cb == CB_LIST[0] and ch >= 17:
                state_load(ch - 17)
            Dl = capture(phase_D, ch, cb, idx % NB, idx % 2)
            Cn = capture(phase_C, units[idx + 1][0], units[idx + 1][1], (idx + 1) % NB) if idx + 1 < len(units) else []
            Ep = capture(phase_E, units[idx - 1][0], units[idx - 1][1], (idx - 1) % NB, (idx - 1) % 2) if idx >= 1 else []
            replay(merge(Dl, Ep, Cn) if PIPELINE else Dl + Ep + Cn)
            if cb == CB_LIST[-1]:
                if ch == 16:
                    state_save(d['o_pwkv'])
                if ch >= 17:
                    state_save(d['o_swkv'][ch - 17])
        replay(capture(phase_E, units[-1][0], units[-1][1], (len(units) - 1) % NB, (len(units) - 1) % 2))
        P.flush()


def stage_outproj(P, nc, d, actn, wsrc, outn, tag):
    for half in range(2):
        with ExitStack() as es:
            tiles = list(range(half * 9, half * 9 + 9))
            actT = mk(nc, es, "actT", [128, 32, 9 * 128], BF16)
            ob = [mk(nc, es, f"obO{i}", [128, 512], F32) for i in range(4)]
            P.dma('sp', actT[:], d[actn][:, :, half * 1152:(half + 1) * 1152], writes=['actT'], key='actT')
            wt = [mk(nc, es, f"wtO{i}", [128, 32, 512], BF16) for i in range(2)]
            ps = [mk(nc, es, f"psO{i}", [128, 512], F32, psum=True) for i in range(4)]
            wv = wsrc.rearrange("(c p) n -> p c n", p=128)
            cnt = 0
            for cb in range(4):
                wb = cb % 2
                P.dma('pool', wt[wb][:], wv[:, :, cb * 512:(cb + 1) * 512], writes=[('wt', wb)], key=('wt', wb))
                for tl, t in enumerate(tiles):
                    pb = cnt % 4
                    cnt += 1
                    for c in range(32):
                        mm(P, ps[pb][:], actT[:, c, tl * 128:(tl + 1) * 128], wt[wb][:, c, :], c == 0, c == 31,
                           ['actT', ('wt', wb)], [('ps', pb)])
                    cp(P, 'act' if cnt % 2 else 'dve', ob[pb][:], ps[pb][:], [('ps', pb)], [('ob', pb)])
                    P.dma('sp', d[outn][t * 128:(t + 1) * 128, cb * 512:(cb + 1) * 512], ob[pb][:], reads=[('ob', pb)], key=('obst', pb))
            P.flush()


def rms_stats(P, x, junk, st, xk, sk):
    P.op('pool', lambda e: e.memset(st[:, 0:1], 0.0), writes=[sk])
    P.op('act', lambda e: e.activation(out=junk[:], in_=x[:], func=AF.Square, accum_out=st[:, 0:1]), reads=[xk], writes=['junk', sk])
    ts(P, 'dve', st[:, 1:2], st[:, 0:1], 1.0 / D, RMS_EPS, ALU.mult, ALU.add, [sk], [sk])
    P.op('act', lambda e: e.sqrt(out=st[:, 1:2], in_=st[:, 1:2]), [sk], [sk])
    recip(P, st[:, 1:2], st[:, 1:2], [sk], [sk])


def stage_F2(P, nc, d):
    with ExitStack() as es:
        gkv = mk(nc, es, "gkv", [128, D], F32)
        gb = mk(nc, es, "gb", [128, D], F32)
        ident = mk(nc, es, "identF2", [128, 128], F32)
        junk = mk(nc, es, "junkF", [128, D], F32)
        xt = [mk(nc, es, f"xF{i}", [128, D], F32) for i in range(2)]
        ot = [mk(nc, es, f"oF{i}", [128, D], F32) for i in range(2)]
        hn = [mk(nc, es, f"hnF{i}", [128, D], F32) for i in range(2)]
        st = [mk(nc, es, f"stF{i}", [128, 2], F32) for i in range(2)]
        hT = [mk(nc, es, f"hTF{i}", [128, 16, 128], BF16) for i in range(2)]
        ps = [mk(nc, es, f"psF{i}", [128, 512], F32, psum=True) for i in range(4)]
        P.dma('sp', gkv[:], d['kv_norm'][0].partition_broadcast(128), writes=['gkv'], key='gkv')
        P.dma('sp', gb[:], d['b_norm'][0].partition_broadcast(128), writes=['gb'], key='gb')
        P.dma('sp', ident[:], d['ident'], writes=['ident'], key='ident')
        ev = 0
        for t in range(NT):
            b = t % 2
            P.dma('sp', xt[b][:], d['xin'][1 + 128 * t:1 + 128 * t + 128, :], writes=[('x', b)], key=('x', b))
            P.dma('sp', ot[b][:], d['o1'][128 * t:128 * t + 128, :], writes=[('o', b)], key=('o', b))
            tt(P, 'dve', xt[b][:], xt[b][:], ot[b][:], ALU.add, [('x', b), ('o', b)], [('x', b)])
            P.dma('sp', d['hp'][128 * t:128 * t + 128, :], xt[b][:], reads=[('x', b)], key=('hpst', b))
            rms_stats(P, xt[b], junk, st[b], ('x', b), ('st', b))
            for vi, (g, gk, dst) in enumerate(((gkv, 'gkv', 'hkvT'), (gb, 'gb', 'hbT'))):
                hb = (2 * t + vi) % 2
                stt(P, 'dve', hn[hb][:], xt[b][:], st[b][:, 1:2], g[:], ALU.mult, ALU.mult,
                    [('x', b), ('st', b), gk], [('hn', hb)])
                for q in range(4):
                    pb = ev % 4
                    for j in range(4):
                        c = q * 4 + j
                        tr(P, ps[pb][:, j * 128:(j + 1) * 128], hn[hb][:, c * 128:(c + 1) * 128], ident[:], [('hn', hb), 'ident'], [('ps', pb)])
                    cp(P, 'act' if ev % 2 == 0 else 'dve', hT[hb][:, q * 4:(q + 1) * 4, :], ps[pb][:].rearrange("p (a n) -> p a n", a=4),
                       [('ps', pb)], [('hT', hb)])
                    ev += 1
                P.dma('sp', d[dst][:, :, t * 128:(t + 1) * 128], hT[hb][:], reads=[('hT', hb)], key=('hTst', hb))
        P.flush()


def rotary(P, eng, Kt, cs, tmp, kk, csk, tk):
    nh = Kt.shape[1]
    cosb = cs[:, 0:8].unsqueeze(1).to_broadcast([128, nh, 8])
    sinb = cs[:, 8:16].unsqueeze(1).to_broadcast([128, nh, 8])
    x1, x2 = Kt[:, :, 0:8], Kt[:, :, 8:16]
    t = [tmp[:, i, 0:nh, :] for i in range(4)]
    tt(P, eng, t[0], x1, cosb, ALU.mult, [kk, csk], [tk])
    tt(P, eng, t[1], x2, sinb, ALU.mult, [kk, csk], [tk])
    tt(P, eng, t[2], x2, cosb, ALU.mult, [kk, csk], [tk])
    tt(P, eng, t[3], x1, sinb, ALU.mult, [kk, csk], [tk])
    tt(P, eng, x1, t[0], t[1], ALU.subtract, [tk], [kk])
    tt(P, eng, x2, t[2], t[3], ALU.add, [tk], [kk])


def stage_G(P, nc, d):
    with ExitStack() as es:
        actT = mk(nc, es, "actT", [128, 16, T], BF16)
        CS = mk(nc, es, "CS", [128, NT, 16], F32)
        ob = [mk(nc, es, f"obG{i}", [128, 512], F32) for i in range(4)]
        obb = [mk(nc, es, f"obbG{i}", [128, 512], BF16) for i in range(4)]
        tmp = [mk(nc, es, f"tmpG{i}", [128, 4, 8, 8], F32) for i in range(4)]
        P.dma('sp', actT[:], d['hkvT'], writes=['actT'], key='actT')
        P.dma('sp', CS[:], d['cs'].rearrange("(t p) c -> p t c", p=128), writes=['CS'], key='CS')
        for s in range(NS):
            P.dma('sp', d['o_sck'][s, 0:127, :], d['ck'][s, 1:128, :], key=('cpk', s))
            P.dma('sp', d['o_scv'][s, 0:127, :], d['cv'][s, 1:128, :], key=('cpv', s))

        def evac(t, cb, pst, pkey, cnt):
            o = cnt % 4
            cp(P, 'act', ob[o][:], pst[:], [pkey], [('ob', o)])
            if cb == 0:
                rotary(P, 'dve', ob[o][:].rearrange("p (h c) -> p h c", h=8), CS[:, t, :], tmp[o], ('ob', o), 'CS', ('tmp', o))
            cp(P, 'pool', obb[o][:], ob[o][:], [('ob', o)], [('obb', o)])
            P.dma('sp', d['KVs'][t * 128:(t + 1) * 128, cb * 512:(cb + 1) * 512], obb[o][:], reads=[('obb', o)], key=('obbst', o))
            if t == 16:
                P.dma('sp', d['o_pck' if cb == 0 else 'o_pcv'], ob[o][:], reads=[('ob', o)], key=('obst', o))
            if t == 17:
                P.dma('sp', d['o_sck' if cb == 0 else 'o_scv'][:, 127, :], ob[o][0:NS, :], reads=[('ob', o)], key=('obst', o))
        gemm_tokmajor(P, nc, es, None, 16, d['w_kv'], 1024, evac, "G", actT=actT)
        P.flush()


def stage_H(P, nc, d):
    with ExitStack() as es:
        actT = mk(nc, es, "actT", [128, 16, T], BF16)
        CS = mk(nc, es, "CS", [128, NT, 16], F32)
        ob = [mk(nc, es, f"obH{i}", [128, 512], F32) for i in range(4)]
        obb = [mk(nc, es, f"obbH{i}", [128, 512], BF16) for i in range(4)]
        tmp = [mk(nc, es, f"tmpH{i}", [128, 4, 8, 8], F32) for i in range(4)]
        P.dma('sp', actT[:], d['hbT'], writes=['actT'], key='actT')
        P.dma('sp', CS[:], d['cs'].rearrange("(t p) c -> p t c", p=128), writes=['CS'], key='CS')

        def evac(t, cb, pst, pkey, cnt):
            o = cnt % 4
            if cb < 8:
                cp(P, 'act', ob[o][:], pst[:], [pkey], [('ob', o)])
                rotary(P, 'dve' if cnt % 2 else 'pool', ob[o][:].rearrange("p (h c) -> p h c", h=8), CS[:, t, :], tmp[o], ('ob', o), 'CS', ('tmp', o))
                cp(P, 'pool' if cnt % 2 else 'dve', obb[o][:], ob[o][:], [('ob', o)], [('obb', o)])
                P.dma('sp', d['qs'][t * 128:(t + 1) * 128, cb * 512:(cb + 1) * 512], obb[o][:], reads=[('obb', o)], key=('obbst', o))
            else:
                cp(P, 'act' if cnt % 2 else 'dve', obb[o][:], pst[:], [pkey], [('obb', o)])
                P.dma('sp', d['zs'][t * 128:(t + 1) * 128, (cb - 8) * 512:(cb - 7) * 512], obb[o][:], reads=[('obb', o)], key=('obbst', o))
        gemm_tokmajor(P, nc, es, None, 16, d['b_w_qz'][0], 8192, evac, "H", actT=actT)
        P.flush()


def stage_I(P, nc, d):
    with ExitStack() as es:
        identb = mk(nc, es, "identBI", [128, 128], BF16)
        MB = mk(nc, es, "MB", [128, 4, 128], BF16)
        esink = mk(nc, es, "esink", [128, 64], F32)
        P.dma('pool', identb[:], d['ident'], writes=['identb'], key='identb')
        P.dma('pool', MB[:], d['mb'], writes=['MB'], key='MB')
        P.dma('sp', esink[:], d['b_sinks'][0].partition_broadcast(128), writes=['esink'], key='esink')
        actf(P, esink[:], esink[:], AF.Exp, ['esink'], ['esink'])
        NSL = 3
        KVt = [mk(nc, es, f"KVt{i}", [128, 1024], BF16) for i in range(NSL)]
        Kd = mk(nc, es, "Kd", [128, 8, 2, 64], BF16)
        KT2 = [mk(nc, es, f"KT2{i}", [128, 8, 128], BF16) for i in range(NSL)]
        VA = [mk(nc, es, f"VA{i}", [128, 8, 65], BF16) for i in range(NSL)]
        Qt = [mk(nc, es, f"Qt{i}", [128, E], BF16) for i in range(2)]
        Zt = [mk(nc, es, f"Zt{i}", [128, E], BF16) for i in range(2)]
        QT = mk(nc, es, "QTt", [128, 32, 128], BF16)
        PTs = [mk(nc, es, f"PTs{i}", [128, 4, 128], BF16) for i in range(2)]
        ATT = mk(nc, es, "ATT", [128, E], F32)
        SZ = mk(nc, es, "SZ", [128, E], F32)
        AG = mk(nc, es, "AG", [128, E], BF16)
        agT = mk(nc, es, "agT", [128, 32, 128], BF16)
        den = mk(nc, es, "den", [128, 8], F32)
        pSC = [mk(nc, es, f"pSC{i}", [128, 512], F32, psum=True) for i in range(2)]
        pAO = [mk(nc, es, f"pAO{i}", [128, 512], F32, psum=True) for i in range(2)]
        pTP = [mk(nc, es, f"pTPI{i}", [128, 1024], BF16, psum=True) for i in range(2)]
        for i in range(NSL):
            P.op('pool', lambda e, i=i: e.memset(VA[i][:, :, 64:65], 1.0), writes=[('VA', i)])

        def prep_kv(slot):
            k3 = KVt[slot][:, 0:512].rearrange("p (g c) -> p g c", g=8)
            cp(P, 'pool', Kd[:, :, 0, :], k3, [('KVt', slot)], ['Kd'])
            cp(P, 'pool', Kd[:, :, 1, :], k3, [('KVt', slot)], ['Kd'])
            for g in range(8):
                tr(P, pTP[0][:, g * 128:(g + 1) * 128], Kd[:, g].rearrange("p a c -> p (a c)"), identb[:], ['Kd', 'identb'], [('pTP', 0)])
            cp(P, 'act', KT2[slot][:].rearrange("p g t -> p (g t)"), pTP[0][:], [('pTP', 0)], [('KT2', slot)])
            cp(P, 'dve', VA[slot][:, :, 0:64], KVt[slot][:, 512:1024].rearrange("p (g c) -> p g c", g=8), [('KVt', slot)], [('VA', slot)])

        slot_of_tile = {}
        nslot = 0
        for ch in range(NCH):
            qb = ch % 2
            load_rows(P, 'sp', Qt[qb], d['qs'], ch, 0, E, [('Qt', qb)], ('Qt', qb))
            load_rows(P, 'sp', Zt[qb], d['zs'], ch, 0, E, [('Zt', qb)], ('Zt', qb))
            if ch < 17:
                cs_ = nslot % NSL
                nslot += 1
                P.dma('sp', KVt[cs_][:], d['KVs'][ch * 128:(ch + 1) * 128, :], writes=[('KVt', cs_)], key=('KVt', cs_))
                prep_kv(cs_)
                slot_of_tile[ch] = cs_
                ps_ = slot_of_tile.get(ch - 1)
                mcur, mprev = (2, None) if ch == 0 else ((0, 3) if ch == 1 else (0, 1))
            else:
                s = ch - 17
                ps_ = nslot % NSL
                nslot += 1
                P.dma('pool', KVt[ps_][:, 0:512], d['ck'][s], writes=[('KVt', ps_)], key=('KVt', ps_))
                P.dma('pool', KVt[ps_][:, 512:1024], d['cv'][s], writes=[('KVt', ps_)], key=('KVt', ps_))
                prep_kv(ps_)
                cs_ = nslot % NSL
                nslot += 1
                load_rows(P, 'sp', KVt[cs_], d['KVs'], ch, 0, 1024, [('KVt', cs_)], ('KVt', cs_))
                prep_kv(cs_)
                mcur, mprev = 0, 1
            for q8 in range(4):
                tp = pTP[q8 % 2]
                for j in range(8):
                    pr = q8 * 8 + j
                    tr(P, tp[:, j * 128:(j + 1) * 128], Qt[qb][:, pr * 128:(pr + 1) * 128], identb[:], [('Qt', qb), 'identb'], [('pTP', q8 % 2)])
                cp(P, 'act' if q8 % 2 else 'dve', QT[:, q8 * 8:(q8 + 1) * 8, :].rearrange("p a t -> p (a t)"), tp[:], [('pTP', q8 % 2)], [('QT', q8)])
            kts = ([(ps_, mprev)] if ps_ is not None and mprev is not None else []) + [(cs_, mcur)]
            for j in range(32):
                g = j // 4
                sc = pSC[j % 2]
                pt = PTs[j % 2]
                for h2 in range(2):
                    sl = slice(h2 * 64, (h2 + 1) * 64)
                    for ki, (slot, mk_) in enumerate(kts):
                        o = sc[:, (h2 * 2 + ki) * 128:(h2 * 2 + ki + 1) * 128]
                        mm(P, o, KT2[slot][sl, g, :], QT[sl, j, :], True, False, [('KT2', slot), ('QT', j // 8)], [('pSC', j % 2)])
                        mm(P, o, identb[:], MB[:, mk_, :], False, True, ['identb', 'MB'], [('pSC', j % 2)])
                nk = len(kts)
                if nk == 2:
                    actf(P, pt[:].rearrange("p a t -> p (a t)"), sc[:], AF.Exp, [('pSC', j % 2)], [('PTs', j % 2)], scale=0.125)
                else:
                    for h2 in range(2):
                        actf(P, pt[:, h2 * 2, :], sc[:, h2 * 256:h2 * 256 + 128], AF.Exp, [('pSC', j % 2)], [('PTs', j % 2)], scale=0.125)
                ao = pAO[(j // 2) % 2]
                for h2 in range(2):
                    col = ((j % 2) * 2 + h2) * 65
                    for ki, (slot, mk_) in enumerate(kts):
                        mm(P, ao[:, col:col + 65], pt[:, h2 * 2 + ki, :], VA[slot][:, g, :], ki == 0, ki == nk - 1,
                           [('PTs', j % 2), ('VA', slot)], [('pAO', (j // 2) % 2)])
                if j % 2 == 1:
                    h0 = (j - 1) * 2
                    ao3 = ao[:, 0:260].rearrange("p (h c) -> p h c", h=4)
                    ak = ('pAO', (j // 2) % 2)
                    tt(P, 'dve', den[:, 0:4], ao3[:, :, 64], esink[:, h0:h0 + 4], ALU.add, [ak, 'esink'], ['den'])
                    recip(P, den[:, 4:8], den[:, 0:4], ['den'], ['den'])
                    tt(P, 'dve', ATT[:, h0 * 64:(h0 + 4) * 64].rearrange("p (h c) -> p h c", h=4), ao3[:, :, 0:64],
                       den[:, 4:8].unsqueeze(2).to_broadcast([128, 4, 64]), ALU.mult, [ak, 'den'], ['ATT'])
            actf(P, SZ[:], Zt[qb][:], AF.Silu, [('Zt', qb)], ['SZ'])
            tt(P, 'pool', AG[:], ATT[:], SZ[:], ALU.mult, ['ATT', 'SZ'], ['AG'])
            for q8 in range(4):
                tp = pTP[q8 % 2]
                for j in range(8):
                    pr = q8 * 8 + j
                    tr(P, tp[:, j * 128:(j + 1) * 128], AG[:, pr * 128:(pr + 1) * 128], identb[:], ['AG', 'identb'], [('pTP', q8 % 2)])
                cp(P, 'act' if q8 % 2 else 'dve', agT[:, q8 * 8:(q8 + 1) * 8, :].rearrange("p a t -> p (a t)"), tp[:], [('pTP', q8 % 2)], ['agT'])
            if ch < 17:
                P.dma('sp', d['agT'][:, :, ch * 128:(ch + 1) * 128], agT[:], reads=['agT'], key='agTst')
            elif ch == 17:
                P.dma('sp', d['agT'][:, :, SROW0:SROW0 + 128], agT[:], reads=['agT'], key='agTst')
            else:
                P.dma('sp', d['agT'][:, :, SROW0 + ch - 17:SROW0 + ch - 16], agT[:, :, 0:1], reads=['agT'], key='agTst',
                      allow_slow_non_contiguous=True)
        P.flush()


def stage_J2(P, nc, d):
    with ExitStack() as es:
        gf = mk(nc, es, "gf", [128, D], F32)
        junk = mk(nc, es, "junkJ", [128, D], F32)
        xt = [mk(nc, es, f"xJ{i}", [128, D], F32) for i in range(2)]
        ot = [mk(nc, es, f"oJ{i}", [128, D], F32) for i in range(2)]
        st = [mk(nc, es, f"stJ{i}", [128, 2], F32) for i in range(2)]
        P.dma('sp', gf[:], d['final_norm'][0].partition_broadcast(128), writes=['gf'], key='gf')
        for t in range(1, NT):
            b = t % 2
            P.dma('sp', xt[b][:], d['hp'][128 * t:128 * t + 128, :], writes=[('x', b)], key=('x', b))
            P.dma('sp', ot[b][:], d['o2'][128 * t:128 * t + 128, :], writes=[('o', b)], key=('o', b))
            tt(P, 'dve', xt[b][:], xt[b][:], ot[b][:], ALU.add, [('x', b), ('o', b)], [('x', b)])
            rms_stats(P, xt[b], junk, st[b], ('x', b), ('st', b))
            stt(P, 'dve', ot[b][:], xt[b][:], st[b][:, 1:2], gf[:], ALU.mult, ALU.mult, [('x', b), ('st', b), 'gf'], [('o', b)])
            if t < 17:
                P.dma('sp', d['o_yp'][(t - 1) * 128:t * 128, :], ot[b][:], reads=[('o', b)], key=('yst', b))
            else:
                P.dma('sp', d['o_ys'], ot[b][0:NS, :], reads=[('o', b)], key=('yst', b))
        P.flush()


class LazyDram(dict):
    def __init__(self, nc, debug_outs, ext_in):
        super().__init__()
        self.nc, self.debug_outs, self.ext_in = nc, debug_outs, ext_in
        self.spec = {}
        self.inputs, self.outputs = [], []

    def __missing__(self, name):
        kind, shape, dt = self.spec[name]
        if kind == 'scr':
            kind = 'ExternalInput' if name in self.ext_in else ('ExternalOutput' if name in self.debug_outs else 'Internal')
        if kind == 'ExternalInput':
            self.inputs.append(name)
        if kind == 'ExternalOutput':
            self.outputs.append(name)
        ap = self.nc.dram_tensor(name, list(shape), dt, kind=kind).ap()
        self[name] = ap
        return ap


def build(debug_outs=(), stages='ABLCFfGHIJj', ext_in=()):
    nc = bass.Bass("TRN2", target_bir_lowering=False)
    d = LazyDram(nc, debug_outs, ext_in)

    def inp(name, shape, dt=F32):
        d.spec[name] = ('ExternalInput', shape, dt)

    def outp(name, shape, dt=F32):
        d.spec[name] = ('ExternalOutput', shape, dt)

    def scr(name, shape, dt):
        d.spec[name] = ('scr', shape, dt)

    inp('xin', [T + 1, D]); inp('sshift', [128, D]); inp('swkv', [NS, 64, 64, 64])
    inp('ck', [NS, 128, 512]); inp('cv', [NS, 128, 512])
    inp('a_norm', [1, D]); inp('muT', [128, 6, 16]); inp('ident', [128, 128]); inp('tri', [128, 128]); inp('ones', [128, 128])
    inp('onehot', [128, 1]); inp('lmask', [128, 2]); inp('mask4', [128, 512]); inp('negsl', [128, 128]); inp('mb', [128, 4, 128])
    inp('cs', [T, 16]); inp('prm', [7, E])
    inp('a_w_rkvz', [1, 4, D, E]); inp('a_w1', [1, D, 96]); inp('a_w2', [1, 96, E]); inp('a_a1', [1, D, 96]); inp('a_a2', [1, 96, E])
    inp('a_w_out', [1, E, D]); inp('kv_norm', [1, D]); inp('w_kv', [D, 1024]); inp('b_norm', [1, D]); inp('b_w_qz', [1, D, 2 * E])
    inp('b_sinks', [1, 64]); inp('b_w_o', [1, E, D]); inp('final_norm', [1, D])
    outp('o_yp', [2048, D]); outp('o_ys', [NS, D]); outp('o_pwkv', [64, 64, 64]); outp('o_pshift', [1, D])
    outp('o_pck', [128, 512]); outp('o_pcv', [128, 512]); outp('o_swkv', [NS, 64, 64, 64]); outp('o_sshift', [NS, D])
    outp('o_sck', [NS, 128, 512]); outp('o_scv', [NS, 128, 512])
    scr('xmT', [6, 128, 16, T], BF16); scr('rkvz', [4, T, E], BF16); scr('wpre', [T, E], F32); scr('apre', [T, E], F32)
    scr('ygT', [128, 32, T], BF16); scr('o1', [T, D], F32); scr('hp', [T, D], F32); scr('hkvT', [128, 16, T], BF16); scr('hbT', [128, 16, T], BF16)
    scr('KVs', [T, 1024], BF16); scr('qs', [T, E], BF16); scr('zs', [T, E], BF16); scr('agT', [128, 32, T], BF16); scr('o2', [T, D], F32)
    with ExitStack() as stack:
        P = Prog(nc, stack)
        if 'A' in stages:
            stage_A(P, nc, d)
        if 'B' in stages:
            stage_B(P, nc, d)
        if 'L' in stages:
            stage_B_lora(P, nc, d)
        if 'C' in stages:
            stage_CDE(P, nc, d)
        if 'F' in stages:
            stage_outproj(P, nc, d, 'ygT', d['a_w_out'][0], 'o1', 'F1')
        if 'f' in stages:
            stage_F2(P, nc, d)
        if 'G' in stages:
            stage_G(P, nc, d)
        if 'H' in stages:
            stage_H(P, nc, d)
        if 'I' in stages:
            stage_I(P, nc, d)
        if 'J' in stages:
            stage_outproj(P, nc, d, 'agT', d['b_w_o'][0], 'o2', 'J1')
        if 'j' in stages:
            stage_J2(P, nc, d)
    nc._lazy = d
    return nc


def host_tables():
    f = np.float32
    j = np.arange(128)
    su = (j[:, None] < j[None, :]).astype(f)
    u = (j[:, None] <= j[None, :]).astype(f)
    tb = {}
    tb['ident'] = np.eye(128, dtype=f)
    tb['tri'] = u.copy()
    tb['ones'] = np.ones((128, 128), f)
    oh = np.zeros((128, 1), f); oh[0, 0] = 1
    tb['onehot'] = oh
    c = f(-np.exp(-0.5))
    lm = np.zeros((128, 2), f); lm[:, 0] = c; lm[0, 1] = c
    tb['lmask'] = lm
    tb['mask4'] = np.concatenate([su, u, -su, u], 1)
    tb['negsl'] = -(su.T).copy()
    NEG = f(-30000.0)
    jj = j[:, None]; ii = j[None, :]
    cur = np.where(jj <= ii, 0, NEG).astype(f)
    prev = np.where(jj >= ii, 0, NEG).astype(f)
    lead = np.where(jj >= 112, 0, NEG).astype(f)
    mb = np.stack([cur, prev, np.minimum(cur, lead), np.minimum(prev, lead)], 1)
    tb['mb'] = np.ascontiguousarray(mb)
    pos = np.zeros(T, f)
    pos[112:2176] = np.arange(2064)
    pos[2176:2176 + NS] = 16384
    inv = (f(500000.0) ** (-np.arange(8, dtype=f) * f(2.0) / f(16))).astype(f)
    ang = (pos[:, None] * inv[None, :]).astype(f)
    tb['cs'] = np.concatenate([np.cos(ang), np.sin(ang)], 1).astype(f)
    return tb


_NC = [None]


def kernel(**inp):
    f = np.float32
    inp = {k: np.asarray(v) for k, v in inp.items()}
    if _NC[0] is None:
        _NC[0] = build()
    nc = _NC[0]
    tb = host_tables()
    mu = inp['a_mu'][0]
    muT = np.ascontiguousarray(mu.reshape(6, 16, 128).transpose(2, 0, 1))
    prm = np.ascontiguousarray(np.stack([inp['a_w0'][0], inp['a_a0'][0], inp['a_k_k'][0], inp['a_k_a'][0], inp['a_r_k'][0].reshape(-1),
                                         inp['a_gn_g'][0], inp['a_gn_b'][0]], 0).astype(f))
    shared = dict(tb)
    shared.update(muT=muT, prm=prm, a_norm=inp['a_norm'], a_w_rkvz=inp['a_w_rkvz'], a_w1=inp['a_w1'], a_w2=inp['a_w2'], a_a1=inp['a_a1'],
                  a_a2=inp['a_a2'], a_w_out=inp['a_w_out'], kv_norm=inp['kv_norm'].reshape(1, D), w_kv=inp['w_kv'], b_norm=inp['b_norm'],
                  b_w_qz=inp['b_w_qz'], b_sinks=inp['b_sinks'], b_w_o=inp['b_w_o'], final_norm=inp['final_norm'].reshape(1, D))
    in_maps = []
    for core in range(8):
        b = core % 4
        ss = slice(core * NS, core * NS + NS)
        xin = np.zeros((T + 1, D), f)
        xin[1 + 112:1 + 128] = inp['meta_tokens']
        xin[1 + 128:1 + 128 + 2048] = inp['x_prompt'][b]
        xin[1 + SROW0:1 + SROW0 + NS] = inp['x_sample'][ss, 0]
        sshift = np.zeros((128, D), f)
        sshift[:NS] = inp['state_shift'][0, ss]
        m = dict(shared)
        m.update(xin=xin, sshift=sshift, swkv=np.ascontiguousarray(inp['state_wkv'][0, ss]),
                 ck=np.ascontiguousarray(inp['cache_k'][ss].reshape(NS, 128, 512)), cv=np.ascontiguousarray(inp['cache_v'][ss].reshape(NS, 128, 512)))
        in_maps.append(m)
    in_maps = [{k: m[k] for k in nc._lazy.inputs if k in m} for m in in_maps]
    res = run_bass_kernel_spmd(nc, in_maps, core_ids=list(range(8)))
    R = res.results
    g = lambda c, n: np.asarray(R[c][n], dtype=f)
    y_prompt = np.stack([g(b, 'o_yp') for b in range(4)], 0)
    y_sample = np.concatenate([g(c, 'o_ys') for c in range(8)], 0)[:, None, :]
    p_wkv = np.stack([g(b, 'o_pwkv') for b in range(4)], 0)[None]
    p_shift = np.concatenate([g(b, 'o_pshift') for b in range(4)], 0)[None]
    p_ck = np.stack([g(b, 'o_pck') for b in range(4)], 0).reshape(4, 128, 8, 64)
    p_cv = np.stack([g(b, 'o_pcv') for b in range(4)], 0).reshape(4, 128, 8, 64)
    s_wkv = np.concatenate([g(c, 'o_swkv') for c in range(8)], 0)[None]
    s_shift = np.concatenate([g(c, 'o_sshift') for c in range(8)], 0)[None]
    s_ck = np.concatenate([g(c, 'o_sck') for c in range(8)], 0).reshape(32, 128, 8, 64)
    s_cv = np.concatenate([g(c, 'o_scv') for c in range(8)], 0).reshape(32, 128, 8, 64)
    return (y_prompt, y_sample, p_wkv, p_shift, p_ck, p_cv, s_wkv, s_shift, s_ck, s_cv)
```

```python
import numpy as np
from contextlib import ExitStack
import concourse.bass as bass
import concourse.mybir as mybir
from concourse.bass_utils import run_bass_kernel_spmd

F32 = mybir.dt.float32
BF16 = mybir.dt.bfloat16
AF = mybir.ActivationFunctionType
ALU = mybir.AluOpType
AX = mybir.AxisListType

COMPUTE = ('pe', 'act', 'dve', 'pool')
ALLENG = ('pe', 'act', 'dve', 'pool', 'sp')
SAME_ENGINE_SYNC = True
PIPELINE = True
PSUM_KEYS = {'pC0', 'pC1', 'pTP', 'pPQ', 'pPA', 'pRX', 'pYS', 'ps', 'psg', 'psL', 'pSC', 'pAO'}


class Prog:
    def __init__(self, nc, stack):
        self.nc = nc
        self.stack = stack
        self.esem = {e: stack.enter_context(nc.semaphore("s_" + e)) for e in COMPUTE}
        self.ecnt = {e: 0 for e in COMPUTE}
        self.dsem = {}
        self.dcnt = {}
        self.dsid = {}
        self.free_dsems = []
        self.nds = 0
        self.waited = {e: {} for e in ALLENG}
        self.reset()

    def reset(self):
        self.ops = []
        self.lastw = {}
        self.readers = {}
        self.chain = {}

    max_ops = None
    cap = None

    def op(self, eng, fn, reads=(), writes=(), key=None):
        if self.cap is not None:
            self.cap.append((eng, fn, list(reads), list(writes), key))
            return -1
        i = len(self.ops)
        if self.max_ops is not None and i >= self.max_ops:
            return -1
        pr = [r for r in reads if (r[0] if isinstance(r, tuple) else r) in PSUM_KEYS]
        if pr:
            reads = [r for r in reads if r not in pr]
            writes = list(writes) + [r for r in pr if r not in writes]
        deps = set()
        for r in reads:
            w = self.lastw.get(r)
            if w is not None:
                deps.add(w)
        for w_ in writes:
            w = self.lastw.get(w_)
            if w is not None:
                deps.add(w)
            deps.update(self.readers.get(w_, ()))
        if key is not None:
            prev = self.chain.get(key)
            if prev is not None:
                deps.add(prev)
            self.chain[key] = i
        self.ops.append(dict(eng=eng, fn=fn, deps=deps, key=key))
        for w_ in writes:
            self.lastw[w_] = i
            self.readers[w_] = []
        ws = set(writes)
        for r in reads:
            if r not in ws:
                self.readers.setdefault(r, []).append(i)
        return i

    def dma(self, q, out, in_, reads=(), writes=(), key=None, **kw):
        assert key is not None
        return self.op(q, lambda e: e.dma_start(out=out, in_=in_, **kw), reads, writes, key=key)

    def flush(self):
        nc = self.nc
        ops = self.ops
        if not ops:
            return
        needed = set()
        for o in ops:
            needed.update(o['deps'])
        lastop = {}
        for i, o in enumerate(ops):
            if o['key'] is None:
                lastop[o['eng']] = i
        needed.update(lastop.values())
        tgt = [None] * len(ops)
        for i, o in enumerate(ops):
            if o['key'] is not None:
                k = o['key']
                if k not in self.dsem:
                    if self.free_dsems:
                        self.dsem[k], self.dcnt[k], self.dsid[k] = self.free_dsems.pop()
                    else:
                        self.nds += 1
                        self.dsem[k] = self.stack.enter_context(nc.semaphore("d_" + str(self.nds)))
                        self.dcnt[k] = 0
                        self.dsid[k] = ('d', self.nds)
                self.dcnt[k] += 16
                tgt[i] = (self.dsem[k], self.dcnt[k], self.dsid[k])
            elif i in needed:
                e = o['eng']
                self.ecnt[e] += 1
                tgt[i] = (self.esem[e], self.ecnt[e], ('e', e))
        per = {e: [] for e in ALLENG}
        for i, o in enumerate(ops):
            per[o['eng']].append(i)
        end_waits = []
        for e in COMPUTE:
            if e in lastop:
                end_waits.append(tgt[lastop[e]])
        for k in self.chain:
            end_waits.append((self.dsem[k], self.dcnt[k], self.dsid[k]))

        def run(ename, eobj):
            waited = self.waited[ename]
            for i in per[ename]:
                o = ops[i]
                need = {}
                for d in o['deps']:
                    od = ops[d]
                    if od['key'] is None and od['eng'] == ename:
                        if ename == 'pe' or not SAME_ENGINE_SYNC:
                            continue
                    sem, val, sid = tgt[d]
                    if need.get(sid, (None, 0))[1] < val:
                        need[sid] = (sem, val)
                for sid, (sem, val) in need.items():
                    if waited.get(sid, 0) < val:
                        eobj.wait_ge(sem, val)
                        waited[sid] = val
                ins = o['fn'](eobj)
                if tgt[i] is not None:
                    if o['key'] is not None:
                        ins.then_inc(tgt[i][0], 16)
                    else:
                        ins.then_inc(tgt[i][0], 1)
            for sem, val, sid in end_waits:
                if waited.get(sid, 0) < val:
                    eobj.wait_ge(sem, val)
                    waited[sid] = val

        with nc.Block() as block:
            @block.tensor
            def _(e):
                run('pe', e)

            @block.scalar
            def _(e):
                run('act', e)

            @block.vector
            def _(e):
                run('dve', e)

            @block.gpsimd
            def _(e):
                run('pool', e)

            @block.sync
            def _(e):
                run('sp', e)
        for k in list(self.dsem):
            self.free_dsems.append((self.dsem[k], self.dcnt[k], self.dsid[k]))
        self.dsem, self.dcnt, self.dsid = {}, {}, {}
        self.reset()


NT = 18
T = NT * 128
D = 2048
E = 4096
NS = 4
RMS_EPS = 1e-6


class Ctx:
    pass


_uid = [0]


def mk(nc, es, name, shape, dt, psum=False):
    _uid[0] += 1
    name = f"{name}_u{_uid[0]}"
    if psum:
        return es.enter_context(nc.psum_tensor(name, shape, dt))
    return es.enter_context(nc.sbuf_tensor(name, shape, dt))


def stage_A(P, nc, d):
    with ExitStack() as es:
        gA = mk(nc, es, "gA", [128, D], F32)
        muT = mk(nc, es, "muT", [128, 6, 16], F32)
        ident = mk(nc, es, "identA", [128, 128], F32)
        xc = [mk(nc, es, f"xc{i}", [128, D], F32) for i in range(2)]
        xp = [mk(nc, es, f"xp{i}", [128, D], F32) for i in range(2)]
        junk = mk(nc, es, "junkA", [128, D], F32)
        st = [mk(nc, es, f"stA{i}", [128, 4], F32) for i in range(2)]
        xnT = [mk(nc, es, f"xnT{i}", [128, 16, 128], F32) for i in range(2)]
        xxT = [mk(nc, es, f"xxT{i}", [128, 16, 128], F32) for i in range(2)]
        tmp = [mk(nc, es, f"tmpA{i}", [128, 16, 128], F32) for i in range(2)]
        xm = [mk(nc, es, f"xmA{i}", [128, 16, 128], BF16) for i in range(3)]
        ps = [mk(nc, es, f"psA{i}", [128, 512], F32, psum=True) for i in range(4)]

        P.dma('sp', gA[:], d['a_norm'][0].partition_broadcast(128), writes=['gA'], key='gA')
        P.dma('sp', muT[:], d['muT'], writes=['muT'], key='muT')
        P.dma('sp', ident[:], d['ident'], writes=['ident'], key='ident')
        ev = 0
        mi = 0
        for t in range(NT):
            b = t % 2
            P.dma('sp', xc[b][:], d['xin'][1 + 128 * t: 1 + 128 * t + 128, :], writes=[('xc', b)], key=('xc', b))
            if t < NT - 1:
                P.dma('sp', xp[b][:], d['xin'][128 * t: 128 * t + 128, :], writes=[('xp', b)], key=('xp', b))
            else:
                P.dma('sp', xp[b][:], d['sshift'], writes=[('xp', b)], key=('xp', b))
            P.op('pool', lambda e, b=b: e.memset(st[b][:, 0:2], 0.0), writes=[('st', b, 0), ('st', b, 1)])
            P.op('act', lambda e, b=b: e.activation(out=junk[:], in_=xc[b][:], func=AF.Square, accum_out=st[b][:, 0:1]),
                 reads=[('xc', b)], writes=['junk', ('st', b, 0)])
            if t < NT - 1:
                P.op('act', lambda e, b=b: e.activation(out=junk[:], in_=xp[b][:], func=AF.Square, accum_out=st[b][:, 1:2]),
                     reads=[('xp', b)], writes=['junk', ('st', b, 1)])
            nst = 2 if t < NT - 1 else 1
            P.op('dve', lambda e, b=b, n=nst: e.tensor_scalar(out=st[b][:, 2:2 + n], in0=st[b][:, 0:n], scalar1=1.0 / D, scalar2=RMS_EPS,
                                                              op0=ALU.mult, op1=ALU.add),
                 reads=[('st', b, 0), ('st', b, 1)], writes=[('st', b, 2)])
            P.op('act', lambda e, b=b, n=nst: e.sqrt(out=st[b][:, 2:2 + n], in_=st[b][:, 2:2 + n]),
                 reads=[('st', b, 2)], writes=[('st', b, 2)])
            P.op('dve', lambda e, b=b, n=nst: e.reciprocal(out=st[b][:, 2:2 + n], in_=st[b][:, 2:2 + n]),
                 reads=[('st', b, 2)], writes=[('st', b, 2)])
            P.op('dve', lambda e, b=b: e.scalar_tensor_tensor(out=xc[b][:], in0=xc[b][:], scalar=st[b][:, 2:3], in1=gA[:],
                                                              op0=ALU.mult, op1=ALU.mult),
                 reads=[('xc', b), ('st', b, 2), 'gA'], writes=[('xc', b)])
            if t < NT - 1:
                P.op('dve', lambda e, b=b: e.scalar_tensor_tensor(out=xp[b][:], in0=xp[b][:], scalar=st[b][:, 3:4], in1=gA[:],
                                                                  op0=ALU.mult, op1=ALU.mult),
                     reads=[('xp', b), ('st', b, 2), 'gA'], writes=[('xp', b)])
            if t == NT - 2:
                P.dma('sp', d['o_pshift'], xc[b][127:128, :], reads=[('xc', b)], key='o_pshift')
            if t == NT - 1:
                P.dma('sp', d['o_sshift'], xc[b][0:NS, :], reads=[('xc', b)], key='o_sshift')
            P.op('pool', lambda e, b=b: e.tensor_tensor(out=xp[b][:], in0=xp[b][:], in1=xc[b][:], op=ALU.subtract),
                 reads=[('xp', b), ('xc', b)], writes=[('xp', b)])
            for (src, srck, dst, dstk) in ((xc, 'xc', xnT, 'xnT'), (xp, 'xp', xxT, 'xxT')):
                for q in range(4):
                    pb = ev % 4
                    for j in range(4):
                        c = q * 4 + j
                        P.op('pe', lambda e, pb=pb, j=j, c=c, src=src, b=b: e.transpose(out=ps[pb][:, j * 128:(j + 1) * 128],
                                                                                        in_=src[b][:, c * 128:(c + 1) * 128], identity=ident[:]),
                             reads=[(srck, b), 'ident'], writes=[('ps', pb)])
                    eng = 'act' if ev % 2 == 0 else 'dve'
                    if eng == 'act':
                        P.op('act', lambda e, pb=pb, q=q, dst=dst, b=b: e.copy(out=dst[b][:, q * 4:(q + 1) * 4, :], in_=ps[pb][:].rearrange("p (a n) -> p a n", a=4)),
                             reads=[('ps', pb)], writes=[(dstk, b)])
                    else:
                        P.op('dve', lambda e, pb=pb, q=q, dst=dst, b=b: e.tensor_copy(out=dst[b][:, q * 4:(q + 1) * 4, :], in_=ps[pb][:].rearrange("p (a n) -> p a n", a=4)),
                             reads=[('ps', pb)], writes=[(dstk, b)])
                    ev += 1
            for p in range(6):
                m = mi % 3
                mi += 1
                e1 = 'pool' if p % 3 == 2 else 'dve'
                P.op(e1, lambda e, b=b, p=p: e.tensor_tensor(out=tmp[p % 2][:], in0=xxT[b][:], in1=muT[:, p, :].unsqueeze(2).to_broadcast([128, 16, 128]), op=ALU.mult),
                     reads=[('xxT', b), 'muT'], writes=[('tmp', p % 2)])
                P.op(e1, lambda e, b=b, p=p, m=m: e.tensor_tensor(out=xm[m][:], in0=tmp[p % 2][:], in1=xnT[b][:], op=ALU.add),
                     reads=[('tmp', p % 2), ('xnT', b)], writes=[('xm', m)])
                P.dma('sp', d['xmT'][p][:, :, t * 128:(t + 1) * 128], xm[m][:], reads=[('xm', m)], writes=[('xmT', p)], key=('xmst', m))
        P.flush()


def gemm_tokmajor(P, nc, es, actT_src, kc, w_src, ncols, evac, wkey, tiles=range(NT), act_res='actT', actT=None):
    wt = [mk(nc, es, f"wt_{wkey}{i}", [128, kc, 512], BF16) for i in range(2)]
    ps = [mk(nc, es, f"psg_{wkey}{i}", [128, 512], F32, psum=True) for i in range(4)]
    wv = w_src.rearrange("(c p) n -> p c n", p=128)
    cnt = 0
    for cb in range(ncols // 512):
        wb = cb % 2
        P.dma('pool', wt[wb][:], wv[:, :, cb * 512:(cb + 1) * 512], writes=[('wt', wkey, wb)], key=('wt', wkey, wb))
        for t in tiles:
            pb = cnt % 4
            cnt += 1
            for c in range(kc):
                P.op('pe', lambda e, pb=pb, c=c, t=t, wb=wb: e.matmul(ps[pb][:], lhsT=actT[:, c, t * 128:(t + 1) * 128], rhs=wt[wb][:, c, :],
                                                                      start=(c == 0), stop=(c == kc - 1)),
                     reads=[act_res, ('wt', wkey, wb)], writes=[('psg', wkey, pb)])
            evac(t, cb, ps[pb], ('psg', wkey, pb), cnt)


def stage_B(P, nc, d, projs=(0, 1, 2, 3)):
    for p in projs:
        with ExitStack() as es:
            actT = mk(nc, es, "actT", [128, 16, T], BF16)
            ob = [mk(nc, es, f"obB{i}", [128, 512], BF16) for i in range(4)]
            P.dma('sp', actT[:], d['xmT'][p], writes=['actT'], key='actT')

            def evac(t, cb, pst, pkey, cnt, p=p):
                o = cnt % 4
                if cnt % 2 == 0:
                    P.op('act', lambda e: e.copy(out=ob[o][:], in_=pst[:]), reads=[pkey], writes=[('ob', o)])
                else:
                    P.op('dve', lambda e: e.tensor_copy(out=ob[o][:], in_=pst[:]), reads=[pkey], writes=[('ob', o)])
                P.dma('sp', d['rkvz'][p][t * 128:(t + 1) * 128, cb * 512:(cb + 1) * 512], ob[o][:], reads=[('ob', o)], key=('obst', o))
            gemm_tokmajor(P, nc, es, None, 16, d['a_w_rkvz'][0, p], E, evac, f"B{p}", actT=actT)
            P.flush()


def tt(P, eng, out, in0, in1, op, reads, writes):
    P.op(eng, lambda e: e.tensor_tensor(out=out, in0=in0, in1=in1, op=op), reads, writes)


def ts(P, eng, out, in0, s1, s2, op0, op1, reads, writes):
    if s2 is None:
        P.op(eng, lambda e: e.tensor_scalar(out=out, in0=in0, scalar1=s1, scalar2=None, op0=op0), reads, writes)
    else:
        P.op(eng, lambda e: e.tensor_scalar(out=out, in0=in0, scalar1=s1, scalar2=s2, op0=op0, op1=op1), reads, writes)


def stt(P, eng, out, in0, scalar, in1, op0, op1, reads, writes):
    P.op(eng, lambda e: e.scalar_tensor_tensor(out=out, in0=in0, scalar=scalar, in1=in1, op0=op0, op1=op1), reads, writes)


def actf(P, out, in_, func, reads, writes, scale=1.0):
    P.op('act', lambda e: e.activation(out=out, in_=in_, func=func, scale=scale), reads, writes)


def cp(P, eng, out, in_, reads, writes):
    if eng == 'act':
        P.op('act', lambda e: e.copy(out=out, in_=in_), reads, writes)
    else:
        P.op(eng, lambda e: e.tensor_copy(out=out, in_=in_), reads, writes)


def mm(P, out, lhsT, rhs, start, stop, reads, writes):
    P.op('pe', lambda e: e.matmul(out, lhsT=lhsT, rhs=rhs, start=start, stop=stop), reads, writes)


def tr(P, out, in_, ident, reads, writes):
    P.op('pe', lambda e: e.transpose(out=out, in_=in_, identity=ident), reads, writes)


def red(P, eng, out, in_, reads, writes):
    P.op(eng, lambda e: e.reduce_sum(out=out, in_=in_, axis=AX.X), reads, writes)


def recip(P, out, in_, reads, writes):
    P.op('dve', lambda e: e.reciprocal(out=out, in_=in_), reads, writes)


GN_EPS = 64e-5
SROW0 = 17 * 128
ZROW0 = SROW0 + NS
NCH = 17 + NS
CH_LIST = list(range(NCH))
CB_LIST = list(range(8))


def load_rows(P, q, dst, src, ch, c0, c1, writes, key):
    if ch < 17:
        P.dma(q, dst[:], src[ch * 128:(ch + 1) * 128, c0:c1], writes=writes, key=key)
    else:
        s = ch - 17
        P.dma(q, dst[0:1], src[SROW0 + s:SROW0 + s + 1, c0:c1], writes=writes, key=key)
        P.dma(q, dst[1:65], src[ZROW0:ZROW0 + 64, c0:c1], writes=writes, key=key)
        P.dma(q, dst[64:128], src[ZROW0:ZROW0 + 64, c0:c1], writes=writes, key=key)


def stage_B_lora(P, nc, d):
    for which, (xi, w1n, w2n, outn, func) in enumerate(((4, 'a_w1', 'a_w2', 'wpre', AF.Tanh), (5, 'a_a1', 'a_a2', 'apre', AF.Copy))):
        with ExitStack() as es:
            actT = mk(nc, es, "actT", [128, 16, T], BF16)
            w1 = mk(nc, es, "w1", [128, 16, 96], BF16)
            w2 = mk(nc, es, "w2", [96, E], BF16)
            hT = mk(nc, es, "hT", [96, T], BF16)
            ob = [mk(nc, es, f"obL{i}", [128, 512], F32) for i in range(4)]
            ps = [mk(nc, es, f"psL{i}", [128, 512], F32, psum=True) for i in range(4)]
            P.dma('sp', actT[:], d['xmT'][xi], writes=['actT'], key='actT')
            P.dma('pool', w1[:], d[w1n][0].rearrange("(c p) n -> p c n", p=128), writes=['w1'], key='w1')
            P.dma('pool', w2[:], d[w2n][0], writes=['w2'], key='w2')
            cnt = 0
            for t in range(NT):
                pb = cnt % 4
                cnt += 1
                for c in range(16):
                    mm(P, ps[pb][0:96, 0:128], w1[:, c, :], actT[:, c, t * 128:(t + 1) * 128], c == 0, c == 15,
                       ['actT', 'w1'], [('psL', pb)])
                actf(P, hT[:, t * 128:(t + 1) * 128], ps[pb][0:96, 0:128], func, [('psL', pb)], [('hT', t)])
            for t in range(NT):
                for cb in range(8):
                    pb = cnt % 4
                    cnt += 1
                    mm(P, ps[pb][:], hT[:, t * 128:(t + 1) * 128], w2[:, cb * 512:(cb + 1) * 512], True, True,
                       [('hT', t), 'w2'], [('psL', pb)])
                    cp(P, 'act' if cnt % 2 else 'dve', ob[pb][:], ps[pb][:], [('psL', pb)], [('obL', pb)])
                    P.dma('sp', d[outn][t * 128:(t + 1) * 128, cb * 512:(cb + 1) * 512], ob[pb][:], reads=[('obL', pb)], key=('obLst', pb))
            P.flush()


def stage_CDE(P, nc, d):
    with ExitStack() as es:
        ident = mk(nc, es, "identF", [128, 128], F32)
        identb = mk(nc, es, "identB", [128, 128], BF16)
        tri = mk(nc, es, "tri", [128, 128], F32)
        ones = mk(nc, es, "ones", [128, 128], F32)
        onehot = mk(nc, es, "onehot", [128, 1], F32)
        lmask = mk(nc, es, "lmask", [128, 2], F32)
        mask4 = mk(nc, es, "mask4", [128, 512], F32)
        negsl = mk(nc, es, "negsl", [128, 128], F32)
        for nm, tl in (('ident', ident), ('tri', tri), ('ones', ones), ('onehot', onehot), ('lmask', lmask), ('mask4', mask4), ('negsl', negsl)):
            P.dma('sp', tl[:], d[nm], writes=[nm], key=nm)
        P.dma('pool', identb[:], d['ident'], writes=['identb'], key='identb')
        CONST = ['ident', 'tri', 'ones', 'onehot', 'lmask', 'mask4', 'negsl', 'identb']
        ST = mk(nc, es, "ST", [128, 32, 64], F32)
        STb = mk(nc, es, "STb", [128, 32, 64], BF16)
        SN = mk(nc, es, "SN", [64, 64, 64], F32)
        BON = mk(nc, es, "BON", [128, 64], F32)
        stmp = mk(nc, es, "stmp", [128, 64], F32)
        ygT = [mk(nc, es, f"ygT{i}", [128, 32, 128], BF16) for i in range(2)]
        NB = 3
        PRM = [mk(nc, es, f"PRM{i}", [128, 7, 512], F32) for i in range(NB)]
        Rb = [mk(nc, es, f"Rb{i}", [128, 512], BF16) for i in range(NB)]
        Kb = [mk(nc, es, f"Kb{i}", [128, 512], BF16) for i in range(NB)]
        Vb = [mk(nc, es, f"Vb{i}", [128, 512], BF16) for i in range(NB)]
        Zb = [mk(nc, es, f"Zb{i}", [128, 512], BF16) for i in range(NB)]
        Wp = [mk(nc, es, f"Wp{i}", [128, 512], F32) for i in range(NB)]
        Ap = [mk(nc, es, f"Ap{i}", [128, 512], F32) for i in range(NB)]
        f32names = ['LD', 'KKf', 'KMf', 'Bf', 'SQ', 'T1', 'E1', 'E2', 'E3', 'E4', 'GT', 'Dinv']
        W = {n: mk(nc, es, n, [128, 512], F32) for n in f32names}
        sm = mk(nc, es, "sm", [128, 64], F32)
        TM = [mk(nc, es, f"TM{i}", [128, 4, 512], BF16) for i in range(NB)]
        KVb = [mk(nc, es, f"KVb{i}", [128, 512], BF16) for i in range(NB)]
        BVb = [mk(nc, es, f"BVb{i}", [128, 512], BF16) for i in range(NB)]
        FT = [mk(nc, es, f"FT{i}", [128, 4, 4, 128], BF16) for i in range(NB)]
        gCs = [mk(nc, es, f"gCs{i}", [128, 4], F32) for i in range(NB)]
        AK = mk(nc, es, "AK", [128, 4, 512], BF16)
        MT = mk(nc, es, "MT", [128, 4, 128], BF16)
        Rm = [mk(nc, es, f"Rm{i}", [128, 4, 128], BF16) for i in range(2)]
        PP = [mk(nc, es, f"PP{i}", [128, 4, 2, 128], BF16) for i in range(2)]
        Xb = mk(nc, es, "Xb", [128, 256], BF16)
        nSA = mk(nc, es, "nSA", [128, 256], BF16)
        Ycb = [mk(nc, es, f"Ycb{i}", [128, 512], F32) for i in range(2)]
        EY = {n: mk(nc, es, n, [128, 512], F32) for n in ('Ysq', 'Yn', 'Sz')}
        YG = mk(nc, es, "YG", [128, 512], BF16)
        pC0 = mk(nc, es, "pC0", [128, 512], F32, psum=True)
        pC1 = mk(nc, es, "pC1", [128, 512], F32, psum=True)
        pRX2 = mk(nc, es, "pRX2", [128, 512], F32, psum=True)
        pPQ = mk(nc, es, "pPQ", [128, 4, 2, 128], F32, psum=True)
        pPA = mk(nc, es, "pPA", [128, 512], F32, psum=True)
        pRX = mk(nc, es, "pRX", [128, 512], F32, psum=True)
        pYS = mk(nc, es, "pYS", [128, 512], F32, psum=True)

        def state_load(s):
            P.dma('sp', SN[:], d['swkv'][s].rearrange("h v k -> v h k"), writes=['SN'], key='SN')
            for g8 in range(4):
                for q in range(8):
                    gp = g8 * 8 + q
                    tr(P, pC0[:, q * 64:(q + 1) * 64], SN[:, 2 * gp:2 * gp + 2, :].rearrange("v a k -> v (a k)"), ident[0:64, 0:64],
                       ['SN', 'ident'], ['pC0'])
                cp(P, 'act', ST[:, g8 * 8:(g8 + 1) * 8, :], pC0[:].rearrange("p (a v) -> p a v", a=8), ['pC0'], [('ST', g8 * 8 + q) for q in range(8)])
                cp(P, 'dve', STb[:, g8 * 8:(g8 + 1) * 8, :], pC0[:].rearrange("p (a v) -> p a v", a=8), ['pC0'], [('STb', g8 * 8 + q) for q in range(8)])

        def state_save(dst):
            for g4 in range(8):
                for q in range(4):
                    gp = g4 * 4 + q
                    tr(P, pC0[0:64, q * 128:(q + 1) * 128], ST[:, gp, :], ident[:], [('ST', gp), 'ident'], ['pC0'])
                cp(P, 'act', SN[:, g4 * 8:(g4 + 1) * 8, :].rearrange("v a k -> v (a k)"), pC0[0:64, :], ['pC0'], ['SN'])
            P.dma('sp', dst.rearrange("h v k -> v h k"), SN[:], reads=['SN'], key='SNst')

        P.op('pool', lambda e: e.memset(ST[:], 0.0), writes=[('ST', g) for g in range(32)])
        P.op('pool', lambda e: e.memset(STb[:], 0.0), writes=[('STb', g) for g in range(32)])

        def phase_C(ch, cb, b):
            lcol = 0 if ch < 17 else 1
            c0, c1 = cb * 512, (cb + 1) * 512
            P.dma('sp', PRM[b][:], d['prm'][:, c0:c1].partition_broadcast(128), writes=[('PRM', b)], key=('PRM', b))
            load_rows(P, 'sp', Rb[b], d['rkvz'][0], ch, c0, c1, [('Rb', b)], ('Rb', b))
            load_rows(P, 'sp', Kb[b], d['rkvz'][1], ch, c0, c1, [('Kb', b)], ('Kb', b))
            load_rows(P, 'sp', Vb[b], d['rkvz'][2], ch, c0, c1, [('Vb', b)], ('Vb', b))
            load_rows(P, 'sp', Zb[b], d['rkvz'][3], ch, c0, c1, [('Zb', b)], ('Zb', b))
            load_rows(P, 'sp', Wp[b], d['wpre'], ch, c0, c1, [('Wp', b)], ('Wp', b))
            load_rows(P, 'sp', Ap[b], d['apre'], ch, c0, c1, [('Ap', b)], ('Ap', b))
            prm = lambda i, b=b: PRM[b][:, i, :]
            tt(P, 'pool', Wp[b][:], Wp[b][:], prm(0), ALU.add, [('Wp', b), ('PRM', b)], [('Wp', b)])
            actf(P, Wp[b][:], Wp[b][:], AF.Sigmoid, [('Wp', b)], [('Wp', b)])
            ts(P, 'dve', W['LD'][:], Wp[b][:], lmask[:, lcol:lcol + 1], None, ALU.mult, None, [('Wp', b), 'lmask'], ['LD'])
            tt(P, 'pool', Ap[b][:], Ap[b][:], prm(1), ALU.add, [('Ap', b), ('PRM', b)], [('Ap', b)])
            actf(P, Ap[b][:], Ap[b][:], AF.Sigmoid, [('Ap', b)], [('Ap', b)])
            tt(P, 'pool', W['KKf'][:], Kb[b][:], prm(2), ALU.mult, [('Kb', b), ('PRM', b)], ['KKf'])
            tt(P, 'pool', W['SQ'][:], W['KKf'][:], W['KKf'][:], ALU.mult, ['KKf'], ['SQ'])
            red(P, 'dve', sm[:, 0:8], W['SQ'][:].rearrange("p (h c) -> p h c", h=8), ['SQ'], [('sm', 0)])
            ts(P, 'dve', sm[:, 0:8], sm[:, 0:8], 1e-24, None, ALU.max, None, [('sm', 0)], [('sm', 0)])
            P.op('act', lambda e: e.sqrt(out=sm[:, 0:8], in_=sm[:, 0:8]), [('sm', 0)], [('sm', 0)])
            recip(P, sm[:, 0:8], sm[:, 0:8], [('sm', 0)], [('sm', 0)])
            tt(P, 'dve', W['KKf'][:].rearrange("p (h c) -> p h c", h=8), W['KKf'][:].rearrange("p (h c) -> p h c", h=8),
               sm[:, 0:8].unsqueeze(2).to_broadcast([128, 8, 64]), ALU.mult, ['KKf', ('sm', 0)], ['KKf'])
            stt(P, 'dve', W['T1'][:], Ap[b][:], -1.0, prm(3), ALU.add, ALU.mult, [('Ap', b), ('PRM', b)], ['T1'])
            stt(P, 'dve', W['KMf'][:], W['T1'][:], 1.0, Kb[b][:], ALU.add, ALU.mult, ['T1', ('Kb', b)], ['KMf'])
            tt(P, 'pool', W['Bf'][:], W['KKf'][:], Ap[b][:], ALU.mult, ['KKf', ('Ap', b)], ['Bf'])
            tt(P, 'pool', W['T1'][:], Rb[b][:], W['KMf'][:], ALU.mult, [('Rb', b), 'KMf'], ['T1'])
            tt(P, 'pool', W['T1'][:], W['T1'][:], prm(4), ALU.mult, ['T1', ('PRM', b)], ['T1'])
            red(P, 'dve', BON[:, cb * 8:(cb + 1) * 8], W['T1'][:].rearrange("p (h c) -> p h c", h=8), ['T1'], [('BON', cb)])
            mm(P, pC0[:], tri[:], W['LD'][:], True, True, ['tri', 'LD'], ['pC0'])
            mm(P, pC1[:], ones[:], W['LD'][:], True, True, ['ones', 'LD'], ['pC1'])
            actf(P, W['E1'][:], pC0[:], AF.Exp, ['pC0'], ['E1'])
            actf(P, W['E2'][:], pC0[:], AF.Exp, ['pC0'], ['E2'], scale=-1.0)
            actf(P, W['GT'][:], pC1[:], AF.Exp, ['pC1'], ['GT'])
            actf(P, W['Dinv'][:], W['LD'][:], AF.Exp, ['LD'], ['Dinv'], scale=-1.0)
            tt(P, 'dve', W['E3'][:], W['E1'][:], W['Dinv'][:], ALU.mult, ['E1', 'Dinv'], ['E3'])
            tt(P, 'pool', W['E4'][:], W['GT'][:], W['E2'][:], ALU.mult, ['GT', 'E2'], ['E4'])
            for pp in range(4):
                mm(P, pC1[:, pp:pp + 1], W['GT'][:, pp * 128:(pp + 1) * 128], onehot[:, 0:1], True, True, ['GT', 'onehot'], ['pC1'])
            cp(P, 'act', gCs[b][:], pC1[:, 0:4], ['pC1'], [('gCs', b)])
            tt(P, 'dve', TM[b][:, 1, :], Rb[b][:], W['E1'][:], ALU.mult, [('Rb', b), 'E1'], [('TM', b, 1)])
            tt(P, 'dve', TM[b][:, 2, :], W['KMf'][:], W['E2'][:], ALU.mult, ['KMf', 'E2'], [('TM', b, 2)])
            tt(P, 'pool', TM[b][:, 3, :], W['Bf'][:], W['E2'][:], ALU.mult, ['Bf', 'E2'], [('TM', b, 3)])
            tt(P, 'dve', TM[b][:, 0, :], W['KKf'][:], W['E3'][:], ALU.mult, ['KKf', 'E3'], [('TM', b, 0)])
            tt(P, 'pool', KVb[b][:], W['KMf'][:], W['E4'][:], ALU.mult, ['KMf', 'E4'], [('KVb', b)])
            tt(P, 'pool', BVb[b][:], W['Bf'][:], W['E4'][:], ALU.mult, ['Bf', 'E4'], [('BVb', b)])
            for hf in range(2):
                for pq in range(2):
                    pp = hf * 2 + pq
                    for kd in range(4):
                        tr(P, pC0[:].bitcast(BF16)[:, (pq * 4 + kd) * 128:(pq * 4 + kd + 1) * 128], TM[b][:, kd, pp * 128:(pp + 1) * 128], identb[:],
                           [('TM', b, kd), 'identb'], ['pC0'])
                cp(P, 'act' if hf == 0 else 'dve', FT[b][:, 2 * hf:2 * hf + 2].rearrange("p a k t -> p (a k t)"), pC0[:].bitcast(BF16)[:, 0:1024],
                   ['pC0'], [('FT', b, hf)])

        def phase_D(ch, cb, b, yb):
            for g in range(2):
                ftk = ('FT', b, g)
                abank = [(pPA[:], 'pPA'), (pPQ[:, 0:2].rearrange("p a b t -> p (a b t)"), ('pPQ', 0)),
                         (pPQ[:, 2:4].rearrange("p a b t -> p (a b t)"), ('pPQ', 1)), (pRX2[:], ('pRX', 1))]
                for i in range(4):
                    pp, h2 = 2 * g + i // 2, i % 2
                    fts = FT[b][h2 * 64:(h2 + 1) * 64, pp]
                    rhs2 = fts[:, 0:2, :].rearrange("p k t -> p (k t)")
                    bk, bkey = abank[i]
                    mm(P, bk[:, 0:256], fts[:, 2, :], rhs2, True, True, [ftk], [bkey])
                    mm(P, bk[:, 256:512], fts[:, 3, :], rhs2, True, True, [ftk], [bkey])
                    pnb, pnk = (pYS, 'pYS') if h2 == 0 else (pRX, ('pRX', 0))
                    mm(P, pnb[:, (i // 2) * 128:(i // 2 + 1) * 128], fts[:, 0, :], fts[:, 3, :], True, True, [ftk], [pnk])
                for i in range(4):
                    bk, bkey = abank[i]
                    tt(P, 'dve', AK[:, i, :], bk, mask4[:], ALU.mult, [bkey, 'mask4'], [('AK', i)])
                for i in range(4):
                    pnb, pnk = (pYS, 'pYS') if i % 2 == 0 else (pRX, ('pRX', 0))
                    tt(P, 'dve', MT[:, i, :], pnb[:, (i // 2) * 128:(i // 2 + 1) * 128], negsl[:], ALU.mult, [pnk, 'negsl'], [('MT', i // 2)])
                for hg in range(2):
                    tt(P, 'pool', Rm[0][:, 2 * hg:2 * hg + 2, :], AK[:, 2 * hg:2 * hg + 2, 256:384],
                       identb[:].unsqueeze(1).to_broadcast([128, 2, 128]), ALU.add,
                       [('AK', 2 * hg), ('AK', 2 * hg + 1), 'identb'], [('Rm', 0, hg)])
                cur = 0
                rxb = [pRX, pRX2]

                def squares(lev, hg):
                    nxt = lev % 2
                    last = (lev == 6)
                    for i in (2 * hg, 2 * hg + 1):
                        if lev == 1:
                            Pc, PTc = AK[:, i, 256:384], MT[:, i, :]
                            rk = [('AK', i), ('MT', hg)]
                        else:
                            Pc, PTc = PP[1 - nxt][:, i, 0, :], PP[1 - nxt][:, i, 1, :]
                            rk = [('PP', 1 - nxt, hg)]
                        if not last:
                            mm(P, pPQ[:, i, 0, :], PTc, Pc, True, True, rk, [('pPQ', hg)])
                        mm(P, pPQ[:, i, 1, :], Pc, PTc, True, True, rk, [('pPQ', hg)])

                def evac_pp(lev, hg):
                    nxt = lev % 2
                    hs_ = slice(2 * hg, 2 * hg + 2)
                    if lev < 6:
                        cp(P, 'act', PP[nxt][:, hs_], pPQ[:, hs_], [('pPQ', hg)], [('PP', nxt, hg)])
                    else:
                        cp(P, 'act', PP[nxt][:, hs_, 1, :], pPQ[:, hs_, 1, :], [('pPQ', hg)], [('PP', nxt, hg)])

                levels = range(1, 7) if ch < 17 else []
                for hg in (range(2) if ch < 17 else []):
                    squares(1, hg)
                for hg in (range(2) if ch < 17 else []):
                    evac_pp(1, hg)
                for lev in levels:
                    nxt = lev % 2
                    for hg in range(2):
                        for q, i in enumerate((2 * hg, 2 * hg + 1)):
                            mm(P, rxb[hg][:, q * 128:(q + 1) * 128], PP[nxt][:, i, 1, :], Rm[cur][:, i, :], True, True,
                               [('PP', nxt, hg), ('Rm', cur, hg)], [('pRX', hg)])
                        if lev < 6:
                            squares(lev + 1, hg)
                    for hg in range(2):
                        tt(P, 'dve', Rm[1 - cur][:, 2 * hg:2 * hg + 2, :], rxb[hg][:, 0:256].rearrange("p (a t) -> p a t", a=2),
                           Rm[cur][:, 2 * hg:2 * hg + 2, :], ALU.add, [('pRX', hg), ('Rm', cur, hg)], [('Rm', 1 - cur, hg)])
                        if lev < 6:
                            evac_pp(lev + 1, hg)
                    cur = 1 - cur
                Rf = Rm[cur]
                for i in range(4):
                    pp, h2 = 2 * g + i // 2, i % 2
                    gp = cb * 4 + pp
                    hh = 4 * g + i
                    fts = FT[b][h2 * 64:(h2 + 1) * 64, pp]
                    mm(P, pRX[:, i * 64:(i + 1) * 64], fts[:, 0, :], STb[h2 * 64:(h2 + 1) * 64, gp, :], True, False,
                       [ftk, ('STb', gp)], [('pRX', 0)])
                    mm(P, pRX[:, i * 64:(i + 1) * 64], AK[:, i, 0:128], Vb[b][:, hh * 64:(hh + 1) * 64], False, True,
                       [('AK', i), ('Vb', b)], [('pRX', 0)])
                cp(P, 'act', Xb[:], pRX[:, 0:256], [('pRX', 0)], ['Xb'])
                for i in range(4):
                    mm(P, pRX[:, 256 + i * 64:256 + (i + 1) * 64], Rf[:, i, :], Xb[:, i * 64:(i + 1) * 64], True, True,
                       [('Rm', cur, i // 2), 'Xb'], [('pRX', 0)])
                P.op('act', lambda e: e.mul(out=nSA[:], in_=pRX[:, 256:512], mul=-1.0), [('pRX', 0)], ['nSA'])
                for i in range(4):
                    pp, h2 = 2 * g + i // 2, i % 2
                    gp = cb * 4 + pp
                    hh = 4 * g + i
                    fts = FT[b][h2 * 64:(h2 + 1) * 64, pp]
                    o = pYS[:, i * 64:(i + 1) * 64]
                    mm(P, o, fts[:, 1, :], STb[h2 * 64:(h2 + 1) * 64, gp, :], True, False, [ftk, ('STb', gp)], ['pYS'])
                    mm(P, o, AK[:, i, 128:256], Vb[b][:, hh * 64:(hh + 1) * 64], False, False, [('AK', i), ('Vb', b)], ['pYS'])
                    mm(P, o, AK[:, i, 384:512], nSA[:, i * 64:(i + 1) * 64], False, True, [('AK', i), 'nSA'], ['pYS'])
                for q in range(2):
                    pp = 2 * g + q
                    o = pYS[:, 256 + q * 128:256 + (q + 1) * 128]
                    mm(P, o, KVb[b][:, pp * 128:(pp + 1) * 128], Vb[b][:, pp * 128:(pp + 1) * 128], True, False,
                       [('KVb', b), ('Vb', b)], ['pYS'])
                    mm(P, o, BVb[b][:, pp * 128:(pp + 1) * 128], nSA[:, q * 128:(q + 1) * 128], False, True,
                       [('BVb', b), 'nSA'], ['pYS'])
                cp(P, 'act', Ycb[yb][:, g * 256:(g + 1) * 256], pYS[:, 0:256], ['pYS'], [('Ycb', yb, g)])
                for q in range(2):
                    pp = 2 * g + q
                    gp = cb * 4 + pp
                    for h2 in range(2):
                        sl = slice(h2 * 64, (h2 + 1) * 64)
                        ts(P, 'dve', stmp[sl, :], ST[sl, gp, :], gCs[b][sl, pp:pp + 1], None, ALU.mult, None,
                           [('ST', gp), ('gCs', b)], ['stmp'])
                        tt(P, 'dve', ST[sl, gp, :], pYS[sl, 256 + q * 128 + h2 * 64:256 + q * 128 + (h2 + 1) * 64], stmp[sl, :], ALU.add,
                           ['pYS', 'stmp'], [('ST', gp)])
                    cp(P, 'pool', STb[:, gp, :], ST[:, gp, :], [('ST', gp)], [('STb', gp)])

        def phase_E(ch, cb, b, yb):
            prm = lambda i, b=b: PRM[b][:, i, :]
            Y3 = Ycb[yb][:].rearrange("p (h c) -> p h c", h=8)
            yk = [('Ycb', yb, 0), ('Ycb', yb, 1)]
            red(P, 'dve', sm[:, 8:16], Y3, yk, [('sm', 1)])
            tt(P, 'pool', EY['Ysq'][:], Ycb[yb][:], Ycb[yb][:], ALU.mult, yk, ['Ysq'])
            red(P, 'dve', sm[:, 16:24], EY['Ysq'][:].rearrange("p (h c) -> p h c", h=8), ['Ysq'], [('sm', 2)])
            ts(P, 'dve', sm[:, 8:16], sm[:, 8:16], 1.0 / 64, None, ALU.mult, None, [('sm', 1)], [('sm', 1)])
            tt(P, 'dve', sm[:, 24:32], sm[:, 8:16], sm[:, 8:16], ALU.mult, [('sm', 1)], [('sm', 3)])
            stt(P, 'dve', sm[:, 16:24], sm[:, 16:24], 1.0 / 64, sm[:, 24:32], ALU.mult, ALU.subtract, [('sm', 2), ('sm', 3)], [('sm', 2)])
            ts(P, 'dve', sm[:, 16:24], sm[:, 16:24], GN_EPS, None, ALU.add, None, [('sm', 2)], [('sm', 2)])
            P.op('act', lambda e: e.sqrt(out=sm[:, 16:24], in_=sm[:, 16:24]), [('sm', 2)], [('sm', 2)])
            recip(P, sm[:, 16:24], sm[:, 16:24], [('sm', 2)], [('sm', 2)])
            Yn3 = EY['Yn'][:].rearrange("p (h c) -> p h c", h=8)
            tt(P, 'pool', Yn3, Y3, sm[:, 8:16].unsqueeze(2).to_broadcast([128, 8, 64]), ALU.subtract, yk + [('sm', 1)], ['Yn'])
            tt(P, 'dve', Yn3, Yn3, sm[:, 16:24].unsqueeze(2).to_broadcast([128, 8, 64]), ALU.mult, ['Yn', ('sm', 2)], ['Yn'])
            tt(P, 'pool', EY['Yn'][:], EY['Yn'][:], prm(5), ALU.mult, ['Yn', ('PRM', b)], ['Yn'])
            tt(P, 'pool', EY['Yn'][:], EY['Yn'][:], prm(6), ALU.add, ['Yn', ('PRM', b)], ['Yn'])
            tt(P, 'pool', EY['Ysq'][:].rearrange("p (h c) -> p h c", h=8), Vb[b][:].rearrange("p (h c) -> p h c", h=8),
               BON[:, cb * 8:(cb + 1) * 8].unsqueeze(2).to_broadcast([128, 8, 64]), ALU.mult, [('Vb', b), ('BON', cb)], ['Ysq'])
            tt(P, 'pool', EY['Yn'][:], EY['Yn'][:], EY['Ysq'][:], ALU.add, ['Yn', 'Ysq'], ['Yn'])
            actf(P, EY['Sz'][:], Zb[b][:], AF.Silu, [('Zb', b)], ['Sz'])
            tt(P, 'dve', YG[:], EY['Yn'][:], EY['Sz'][:], ALU.mult, ['Yn', 'Sz'], ['YG'])
            for q in range(4):
                tr(P, pC1[:].bitcast(BF16)[:, q * 128:(q + 1) * 128], YG[:, q * 128:(q + 1) * 128], identb[:], ['YG', 'identb'], ['pC1'])
            cp(P, 'act', ygT[ch % 2][:, cb * 4:(cb + 1) * 4, :].rearrange("p a t -> p (a t)"), pC1[:].bitcast(BF16)[:, 0:512], ['pC1'], [('ygT', ch % 2)])
            if cb == CB_LIST[-1]:
                yt = ygT[ch % 2]
                if ch < 17:
                    P.dma('sp', d['ygT'][:, :, ch * 128:(ch + 1) * 128], yt[:], reads=[('ygT', ch % 2)], key='ygTst')
                elif ch == 17:
                    P.dma('sp', d['ygT'][:, :, SROW0:SROW0 + 128], yt[:], reads=[('ygT', ch % 2)], key='ygTst')
                else:
                    s_ = ch - 17
                    P.dma('sp', d['ygT'][:, :, SROW0 + s_:SROW0 + s_ + 1], yt[:, :, 0:1], reads=[('ygT', ch % 2)], key='ygTst',
                          allow_slow_non_contiguous=True)

        def capture(fn, *a):
            P.cap = []
            fn(*a)
            l = P.cap
            P.cap = None
            return l

        def replay(l):
            for (eng, fn, reads, writes, key) in l:
                P.op(eng, fn, reads, writes, key)

        def merge(main, first, second):
            out = []
            n = len(main)
            h = n // 2 if (first and second) else (n if first else 0)
            da = db = 0
            for i, o in enumerate(main):
                out.append(o)
                if i < h:
                    t_ = (i + 1) * len(first) // max(h, 1)
                    while da < t_:
                        out.append(first[da])
                        da += 1
                else:
                    if da < len(first):
                        out.extend(first[da:])
                        da = len(first)
                    t_ = (i + 1 - h) * len(second) // max(n - h, 1)
                    while db < t_:
                        out.append(second[db])
                        db += 1
            out.extend(first[da:])
            out.extend(second[db:])
            return out

        units = [(ch, cb) for ch in CH_LIST for cb in CB_LIST]
        replay(capture(phase_C, units[0][0], units[0][1], 0))
        for idx, (ch, cb) in enumerate(units):
            if cb == CB_LIST[0] and ch >= 17:
                state_load(ch - 17)
            Dl = capture(phase_D, ch, cb, idx % NB, idx % 2)
            Cn = capture(phase_C, units[idx + 1][0], units[idx + 1][1], (idx + 1) % NB) if idx + 1 < len(units) else []
            Ep = capture(phase_E, units[idx - 1][0], units[idx - 1][1], (idx - 1) % NB, (idx - 1) % 2) if idx >= 1 else []
            replay(merge(Dl, Ep, Cn) if PIPELINE else Dl + Ep + Cn)
            if cb == CB_LIST[-1]:
                if ch == 16:
                    state_save(d['o_pwkv'])
                if ch >= 17:
                    state_save(d['o_swkv'][ch - 17])
        replay(capture(phase_E, units[-1][0], units[-1][1], (len(units) - 1) % NB, (len(units) - 1) % 2))
        P.flush()


def stage_outproj(P, nc, d, actn, wsrc, outn, tag):
    for half in range(2):
        with ExitStack() as es:
            tiles = list(range(half * 9, half * 9 + 9))
            actT = mk(nc, es, "actT", [128, 32, 9 * 128], BF16)
            ob = [mk(nc, es, f"obO{i}", [128, 512], F32) for i in range(4)]
            P.dma('sp', actT[:], d[actn][:, :, half * 1152:(half + 1) * 1152], writes=['actT'], key='actT')
            wt = [mk(nc, es, f"wtO{i}", [128, 32, 512], BF16) for i in range(2)]
            ps = [mk(nc, es, f"psO{i}", [128, 512], F32, psum=True) for i in range(4)]
            wv = wsrc.rearrange("(c p) n -> p c n", p=128)
            cnt = 0
            for cb in range(4):
                wb = cb % 2
                P.dma('pool', wt[wb][:], wv[:, :, cb * 512:(cb + 1) * 512], writes=[('wt', wb)], key=('wt', wb))
                for tl, t in enumerate(tiles):
                    pb = cnt % 4
                    cnt += 1
                    for c in range(32):
                        mm(P, ps[pb][:], actT[:, c, tl * 128:(tl + 1) * 128], wt[wb][:, c, :], c == 0, c == 31,
                           ['actT', ('wt', wb)], [('ps', pb)])
                    cp(P, 'act' if cnt % 2 else 'dve', ob[pb][:], ps[pb][:], [('ps', pb)], [('ob', pb)])
                    P.dma('sp', d[outn][t * 128:(t + 1) * 128, cb * 512:(cb + 1) * 512], ob[pb][:], reads=[('ob', pb)], key=('obst', pb))
            P.flush()


def rms_stats(P, x, junk, st, xk, sk):
    P.op('pool', lambda e: e.memset(st[:, 0:1], 0.0), writes=[sk])
    P.op('act', lambda e: e.activation(out=junk[:], in_=x[:], func=AF.Square, accum_out=st[:, 0:1]), reads=[xk], writes=['junk', sk])
    ts(P, 'dve', st[:, 1:2], st[:, 0:1], 1.0 / D, RMS_EPS, ALU.mult, ALU.add, [sk], [sk])
    P.op('act', lambda e: e.sqrt(out=st[:, 1:2], in_=st[:, 1:2]), [sk], [sk])
    recip(P, st[:, 1:2], st[:, 1:2], [sk], [sk])


def stage_F2(P, nc, d):
    with ExitStack() as es:
        gkv = mk(nc, es, "gkv", [128, D], F32)
        gb = mk(nc, es, "gb", [128, D], F32)
        ident = mk(nc, es, "identF2", [128, 128], F32)
        junk = mk(nc, es, "junkF", [128, D], F32)
        xt = [mk(nc, es, f"xF{i}", [128, D], F32) for i in range(2)]
        ot = [mk(nc, es, f"oF{i}", [128, D], F32) for i in range(2)]
        hn = [mk(nc, es, f"hnF{i}", [128, D], F32) for i in range(2)]
        st = [mk(nc, es, f"stF{i}", [128, 2], F32) for i in range(2)]
        hT = [mk(nc, es, f"hTF{i}", [128, 16, 128], BF16) for i in range(2)]
        ps = [mk(nc, es, f"psF{i}", [128, 512], F32, psum=True) for i in range(4)]
        P.dma('sp', gkv[:], d['kv_norm'][0].partition_broadcast(128), writes=['gkv'], key='gkv')
        P.dma('sp', gb[:], d['b_norm'][0].partition_broadcast(128), writes=['gb'], key='gb')
        P.dma('sp', ident[:], d['ident'], writes=['ident'], key='ident')
        ev = 0
        for t in range(NT):
            b = t % 2
            P.dma('sp', xt[b][:], d['xin'][1 + 128 * t:1 + 128 * t + 128, :], writes=[('x', b)], key=('x', b))
            P.dma('sp', ot[b][:], d['o1'][128 * t:128 * t + 128, :], writes=[('o', b)], key=('o', b))
            tt(P, 'dve', xt[b][:], xt[b][:], ot[b][:], ALU.add, [('x', b), ('o', b)], [('x', b)])
            P.dma('sp', d['hp'][128 * t:128 * t + 128, :], xt[b][:], reads=[('x', b)], key=('hpst', b))
            rms_stats(P, xt[b], junk, st[b], ('x', b), ('st', b))
            for vi, (g, gk, dst) in enumerate(((gkv, 'gkv', 'hkvT'), (gb, 'gb', 'hbT'))):
                hb = (2 * t + vi) % 2
                stt(P, 'dve', hn[hb][:], xt[b][:], st[b][:, 1:2], g[:], ALU.mult, ALU.mult,
                    [('x', b), ('st', b), gk], [('hn', hb)])
                for q in range(4):
                    pb = ev % 4
                    for j in range(4):
                        c = q * 4 + j
                        tr(P, ps[pb][:, j * 128:(j + 1) * 128], hn[hb][:, c * 128:(c + 1) * 128], ident[:], [('hn', hb), 'ident'], [('ps', pb)])
                    cp(P, 'act' if ev % 2 == 0 else 'dve', hT[hb][:, q * 4:(q + 1) * 4, :], ps[pb][:].rearrange("p (a n) -> p a n", a=4),
                       [('ps', pb)], [('hT', hb)])
                    ev += 1
                P.dma('sp', d[dst][:, :, t * 128:(t + 1) * 128], hT[hb][:], reads=[('hT', hb)], key=('hTst', hb))
        P.flush()


def rotary(P, eng, Kt, cs, tmp, kk, csk, tk):
    nh = Kt.shape[1]
    cosb = cs[:, 0:8].unsqueeze(1).to_broadcast([128, nh, 8])
    sinb = cs[:, 8:16].unsqueeze(1).to_broadcast([128, nh, 8])
    x1, x2 = Kt[:, :, 0:8], Kt[:, :, 8:16]
    t = [tmp[:, i, 0:nh, :] for i in range(4)]
    tt(P, eng, t[0], x1, cosb, ALU.mult, [kk, csk], [tk])
    tt(P, eng, t[1], x2, sinb, ALU.mult, [kk, csk], [tk])
    tt(P, eng, t[2], x2, cosb, ALU.mult, [kk, csk], [tk])
    tt(P, eng, t[3], x1, sinb, ALU.mult, [kk, csk], [tk])
    tt(P, eng, x1, t[0], t[1], ALU.subtract, [tk], [kk])
    tt(P, eng, x2, t[2], t[3], ALU.add, [tk], [kk])


def stage_G(P, nc, d):
    with ExitStack() as es:
        actT = mk(nc, es, "actT", [128, 16, T], BF16)
        CS = mk(nc, es, "CS", [128, NT, 16], F32)
        ob = [mk(nc, es, f"obG{i}", [128, 512], F32) for i in range(4)]
        obb = [mk(nc, es, f"obbG{i}", [128, 512], BF16) for i in range(4)]
        tmp = [mk(nc, es, f"tmpG{i}", [128, 4, 8, 8], F32) for i in range(4)]
        P.dma('sp', actT[:], d['hkvT'], writes=['actT'], key='actT')
        P.dma('sp', CS[:], d['cs'].rearrange("(t p) c -> p t c", p=128), writes=['CS'], key='CS')
        for s in range(NS):
            P.dma('sp', d['o_sck'][s, 0:127, :], d['ck'][s, 1:128, :], key=('cpk', s))
            P.dma('sp', d['o_scv'][s, 0:127, :], d['cv'][s, 1:128, :], key=('cpv', s))

        def evac(t, cb, pst, pkey, cnt):
            o = cnt % 4
            cp(P, 'act', ob[o][:], pst[:], [pkey], [('ob', o)])
            if cb == 0:
                rotary(P, 'dve', ob[o][:].rearrange("p (h c) -> p h c", h=8), CS[:, t, :], tmp[o], ('ob', o), 'CS', ('tmp', o))
            cp(P, 'pool', obb[o][:], ob[o][:], [('ob', o)], [('obb', o)])
            P.dma('sp', d['KVs'][t * 128:(t + 1) * 128, cb * 512:(cb + 1) * 512], obb[o][:], reads=[('obb', o)], key=('obbst', o))
            if t == 16:
                P.dma('sp', d['o_pck' if cb == 0 else 'o_pcv'], ob[o][:], reads=[('ob', o)], key=('obst', o))
            if t == 17:
                P.dma('sp', d['o_sck' if cb == 0 else 'o_scv'][:, 127, :], ob[o][0:NS, :], reads=[('ob', o)], key=('obst', o))
        gemm_tokmajor(P, nc, es, None, 16, d['w_kv'], 1024, evac, "G", actT=actT)
        P.flush()


def stage_H(P, nc, d):
    with ExitStack() as es:
        actT = mk(nc, es, "actT", [128, 16, T], BF16)
        CS = mk(nc, es, "CS", [128, NT, 16], F32)
        ob = [mk(nc, es, f"obH{i}", [128, 512], F32) for i in range(4)]
        obb = [mk(nc, es, f"obbH{i}", [128, 512], BF16) for i in range(4)]
        tmp = [mk(nc, es, f"tmpH{i}", [128, 4, 8, 8], F32) for i in range(4)]
        P.dma('sp', actT[:], d['hbT'], writes=['actT'], key='actT')
        P.dma('sp', CS[:], d['cs'].rearrange("(t p) c -> p t c", p=128), writes=['CS'], key='CS')

        def evac(t, cb, pst, pkey, cnt):
            o = cnt % 4
            if cb < 8:
                cp(P, 'act', ob[o][:], pst[:], [pkey], [('ob', o)])
                rotary(P, 'dve' if cnt % 2 else 'pool', ob[o][:].rearrange("p (h c) -> p h c", h=8), CS[:, t, :], tmp[o], ('ob', o), 'CS', ('tmp', o))
                cp(P, 'pool' if cnt % 2 else 'dve', obb[o][:], ob[o][:], [('ob', o)], [('obb', o)])
                P.dma('sp', d['qs'][t * 128:(t + 1) * 128, cb * 512:(cb + 1) * 512], obb[o][:], reads=[('obb', o)], key=('obbst', o))
            else:
                cp(P, 'act' if cnt % 2 else 'dve', obb[o][:], pst[:], [pkey], [('obb', o)])
                P.dma('sp', d['zs'][t * 128:(t + 1) * 128, (cb - 8) * 512:(cb - 7) * 512], obb[o][:], reads=[('obb', o)], key=('obbst', o))
        gemm_tokmajor(P, nc, es, None, 16, d['b_w_qz'][0], 8192, evac, "H", actT=actT)
        P.flush()


def stage_I(P, nc, d):
    with ExitStack() as es:
        identb = mk(nc, es, "identBI", [128, 128], BF16)
        MB = mk(nc, es, "MB", [128, 4, 128], BF16)
        esink = mk(nc, es, "esink", [128, 64], F32)
        P.dma('pool', identb[:], d['ident'], writes=['identb'], key='identb')
        P.dma('pool', MB[:], d['mb'], writes=['MB'], key='MB')
        P.dma('sp', esink[:], d['b_sinks'][0].partition_broadcast(128), writes=['esink'], key='esink')
        actf(P, esink[:], esink[:], AF.Exp, ['esink'], ['esink'])
        NSL = 3
        KVt = [mk(nc, es, f"KVt{i}", [128, 1024], BF16) for i in range(NSL)]
        Kd = mk(nc, es, "Kd", [128, 8, 2, 64], BF16)
        KT2 = [mk(nc, es, f"KT2{i}", [128, 8, 128], BF16) for i in range(NSL)]
        VA = [mk(nc, es, f"VA{i}", [128, 8, 65], BF16) for i in range(NSL)]
        Qt = [mk(nc, es, f"Qt{i}", [128, E], BF16) for i in range(2)]
        Zt = [mk(nc, es, f"Zt{i}", [128, E], BF16) for i in range(2)]
        QT = mk(nc, es, "QTt", [128, 32, 128], BF16)
        PTs = [mk(nc, es, f"PTs{i}", [128, 4, 128], BF16) for i in range(2)]
        ATT = mk(nc, es, "ATT", [128, E], F32)
        SZ = mk(nc, es, "SZ", [128, E], F32)
        AG = mk(nc, es, "AG", [128, E], BF16)
        agT = mk(nc, es, "agT", [128, 32, 128], BF16)
        den = mk(nc, es, "den", [128, 8], F32)
        pSC = [mk(nc, es, f"pSC{i}", [128, 512], F32, psum=True) for i in range(2)]
        pAO = [mk(nc, es, f"pAO{i}", [128, 512], F32, psum=True) for i in range(2)]
        pTP = [mk(nc, es, f"pTPI{i}", [128, 1024], BF16, psum=True) for i in range(2)]
        for i in range(NSL):
            P.op('pool', lambda e, i=i: e.memset(VA[i][:, :, 64:65], 1.0), writes=[('VA', i)])

        def prep_kv(slot):
            k3 = KVt[slot][:, 0:512].rearrange("p (g c) -> p g c", g=8)
            cp(P, 'pool', Kd[:, :, 0, :], k3, [('KVt', slot)], ['Kd'])
            cp(P, 'pool', Kd[:, :, 1, :], k3, [('KVt', slot)], ['Kd'])
            for g in range(8):
                tr(P, pTP[0][:, g * 128:(g + 1) * 128], Kd[:, g].rearrange("p a c -> p (a c)"), identb[:], ['Kd', 'identb'], [('pTP', 0)])
            cp(P, 'act', KT2[slot][:].rearrange("p g t -> p (g t)"), pTP[0][:], [('pTP', 0)], [('KT2', slot)])
            cp(P, 'dve', VA[slot][:, :, 0:64], KVt[slot][:, 512:1024].rearrange("p (g c) -> p g c", g=8), [('KVt', slot)], [('VA', slot)])

        slot_of_tile = {}
        nslot = 0
        for ch in range(NCH):
            qb = ch % 2
            load_rows(P, 'sp', Qt[qb], d['qs'], ch, 0, E, [('Qt', qb)], ('Qt', qb))
            load_rows(P, 'sp', Zt[qb], d['zs'], ch, 0, E, [('Zt', qb)], ('Zt', qb))
            if ch < 17:
                cs_ = nslot % NSL
                nslot += 1
                P.dma('sp', KVt[cs_][:], d['KVs'][ch * 128:(ch + 1) * 128, :], writes=[('KVt', cs_)], key=('KVt', cs_))
                prep_kv(cs_)
                slot_of_tile[ch] = cs_
                ps_ = slot_of_tile.get(ch - 1)
                mcur, mprev = (2, None) if ch == 0 else ((0, 3) if ch == 1 else (0, 1))
            else:
                s = ch - 17
                ps_ = nslot % NSL
                nslot += 1
                P.dma('pool', KVt[ps_][:, 0:512], d['ck'][s], writes=[('KVt', ps_)], key=('KVt', ps_))
                P.dma('pool', KVt[ps_][:, 512:1024], d['cv'][s], writes=[('KVt', ps_)], key=('KVt', ps_))
                prep_kv(ps_)
                cs_ = nslot % NSL
                nslot += 1
                load_rows(P, 'sp', KVt[cs_], d['KVs'], ch, 0, 1024, [('KVt', cs_)], ('KVt', cs_))
                prep_kv(cs_)
                mcur, mprev = 0, 1
            for q8 in range(4):
                tp = pTP[q8 % 2]
                for j in range(8):
                    pr = q8 * 8 + j
                    tr(P, tp[:, j * 128:(j + 1) * 128], Qt[qb][:, pr * 128:(pr + 1) * 128], identb[:], [('Qt', qb), 'identb'], [('pTP', q8 % 2)])
                cp(P, 'act' if q8 % 2 else 'dve', QT[:, q8 * 8:(q8 + 1) * 8, :].rearrange("p a t -> p (a t)"), tp[:], [('pTP', q8 % 2)], [('QT', q8)])
            kts = ([(ps_, mprev)] if ps_ is not None and mprev is not None else []) + [(cs_, mcur)]
            for j in range(32):
                g = j // 4
                sc = pSC[j % 2]
                pt = PTs[j % 2]
                for h2 in range(2):
                    sl = slice(h2 * 64, (h2 + 1) * 64)
                    for ki, (slot, mk_) in enumerate(kts):
                        o = sc[:, (h2 * 2 + ki) * 128:(h2 * 2 + ki + 1) * 128]
                        mm(P, o, KT2[slot][sl, g, :], QT[sl, j, :], True, False, [('KT2', slot), ('QT', j // 8)], [('pSC', j % 2)])
                        mm(P, o, identb[:], MB[:, mk_, :], False, True, ['identb', 'MB'], [('pSC', j % 2)])
                nk = len(kts)
                if nk == 2:
                    actf(P, pt[:].rearrange("p a t -> p (a t)"), sc[:], AF.Exp, [('pSC', j % 2)], [('PTs', j % 2)], scale=0.125)
                else:
                    for h2 in range(2):
                        actf(P, pt[:, h2 * 2, :], sc[:, h2 * 256:h2 * 256 + 128], AF.Exp, [('pSC', j % 2)], [('PTs', j % 2)], scale=0.125)
                ao = pAO[(j // 2) % 2]
                for h2 in range(2):
                    col = ((j % 2) * 2 + h2) * 65
                    for ki, (slot, mk_) in enumerate(kts):
                        mm(P, ao[:, col:col + 65], pt[:, h2 * 2 + ki, :], VA[slot][:, g, :], ki == 0, ki == nk - 1,
                           [('PTs', j % 2), ('VA', slot)], [('pAO', (j // 2) % 2)])
                if j % 2 == 1:
                    h0 = (j - 1) * 2
                    ao3 = ao[:, 0:260].rearrange("p (h c) -> p h c", h=4)
                    ak = ('pAO', (j // 2) % 2)
                    tt(P, 'dve', den[:, 0:4], ao3[:, :, 64], esink[:, h0:h0 + 4], ALU.add, [ak, 'esink'], ['den'])
                    recip(P, den[:, 4:8], den[:, 0:4], ['den'], ['den'])
                    tt(P, 'dve', ATT[:, h0 * 64:(h0 + 4) * 64].rearrange("p (h c) -> p h c", h=4), ao3[:, :, 0:64],
                       den[:, 4:8].unsqueeze(2).to_broadcast([128, 4, 64]), ALU.mult, [ak, 'den'], ['ATT'])
            actf(P, SZ[:], Zt[qb][:], AF.Silu, [('Zt', qb)], ['SZ'])
            tt(P, 'dve', AG[:], ATT[:], SZ[:], ALU.mult, ['ATT', 'SZ'], ['AG'])
            for q8 in range(4):
                tp = pTP[q8 % 2]
                for j in range(8):
                    pr = q8 * 8 + j
                    tr(P, tp[:, j * 128:(j + 1) * 128], AG[:, pr * 128:(pr + 1) * 128], identb[:], ['AG', 'identb'], [('pTP', q8 % 2)])
                cp(P, 'act' if q8 % 2 else 'dve', agT[:, q8 * 8:(q8 + 1) * 8, :].rearrange("p a t -> p (a t)"), tp[:], [('pTP', q8 % 2)], ['agT'])
            if ch < 17:
                P.dma('sp', d['agT'][:, :, ch * 128:(ch + 1) * 128], agT[:], reads=['agT'], key='agTst')
            elif ch == 17:
                P.dma('sp', d['agT'][:, :, SROW0:SROW0 + 128], agT[:], reads=['agT'], key='agTst')
            else:
                P.dma('sp', d['agT'][:, :, SROW0 + ch - 17:SROW0 + ch - 16], agT[:, :, 0:1], reads=['agT'], key='agTst',
                      allow_slow_non_contiguous=True)
        P.flush()


def stage_J2(P, nc, d):
    with ExitStack() as es:
        gf = mk(nc, es, "gf", [128, D], F32)
        junk = mk(nc, es, "junkJ", [128, D], F32)
        xt = [mk(nc, es, f"xJ{i}", [128, D], F32) for i in range(2)]
        ot = [mk(nc, es, f"oJ{i}", [128, D], F32) for i in range(2)]
        st = [mk(nc, es, f"stJ{i}", [128, 2], F32) for i in range(2)]
        P.dma('sp', gf[:], d['final_norm'][0].partition_broadcast(128), writes=['gf'], key='gf')
        for t in range(1, NT):
            b = t % 2
            P.dma('sp', xt[b][:], d['hp'][128 * t:128 * t + 128, :], writes=[('x', b)], key=('x', b))
            P.dma('sp', ot[b][:], d['o2'][128 * t:128 * t + 128, :], writes=[('o', b)], key=('o', b))
            tt(P, 'dve', xt[b][:], xt[b][:], ot[b][:], ALU.add, [('x', b), ('o', b)], [('x', b)])
            rms_stats(P, xt[b], junk, st[b], ('x', b), ('st', b))
            stt(P, 'dve', ot[b][:], xt[b][:], st[b][:, 1:2], gf[:], ALU.mult, ALU.mult, [('x', b), ('st', b), 'gf'], [('o', b)])
            if t < 17:
                P.dma('sp', d['o_yp'][(t - 1) * 128:t * 128, :], ot[b][:], reads=[('o', b)], key=('yst', b))
            else:
                P.dma('sp', d['o_ys'], ot[b][0:NS, :], reads=[('o', b)], key=('yst', b))
        P.flush()


class LazyDram(dict):
    def __init__(self, nc, debug_outs, ext_in):
        super().__init__()
        self.nc, self.debug_outs, self.ext_in = nc, debug_outs, ext_in
        self.spec = {}
        self.inputs, self.outputs = [], []

    def __missing__(self, name):
        kind, shape, dt = self.spec[name]
        if kind == 'scr':
            kind = 'ExternalInput' if name in self.ext_in else ('ExternalOutput' if name in self.debug_outs else 'Internal')
        if kind == 'ExternalInput':
            self.inputs.append(name)
        if kind == 'ExternalOutput':
            self.outputs.append(name)
        ap = self.nc.dram_tensor(name, list(shape), dt, kind=kind).ap()
        self[name] = ap
        return ap


def build(debug_outs=(), stages='ABLCFfGHIJj', ext_in=()):
    nc = bass.Bass("TRN2", target_bir_lowering=False)
    d = LazyDram(nc, debug_outs, ext_in)

    def inp(name, shape, dt=F32):
        d.spec[name] = ('ExternalInput', shape, dt)

    def outp(name, shape, dt=F32):
        d.spec[name] = ('ExternalOutput', shape, dt)

    def scr(name, shape, dt):
        d.spec[name] = ('scr', shape, dt)

    inp('xin', [T + 1, D]); inp('sshift', [128, D]); inp('swkv', [NS, 64, 64, 64])
    inp('ck', [NS, 128, 512]); inp('cv', [NS, 128, 512])
    inp('a_norm', [1, D]); inp('muT', [128, 6, 16]); inp('ident', [128, 128]); inp('tri', [128, 128]); inp('ones', [128, 128])
    inp('onehot', [128, 1]); inp('lmask', [128, 2]); inp('mask4', [128, 512]); inp('negsl', [128, 128]); inp('mb', [128, 4, 128])
    inp('cs', [T, 16]); inp('prm', [7, E])
    inp('a_w_rkvz', [1, 4, D, E]); inp('a_w1', [1, D, 96]); inp('a_w2', [1, 96, E]); inp('a_a1', [1, D, 96]); inp('a_a2', [1, 96, E])
    inp('a_w_out', [1, E, D]); inp('kv_norm', [1, D]); inp('w_kv', [D, 1024]); inp('b_norm', [1, D]); inp('b_w_qz', [1, D, 2 * E])
    inp('b_sinks', [1, 64]); inp('b_w_o', [1, E, D]); inp('final_norm', [1, D])
    outp('o_yp', [2048, D]); outp('o_ys', [NS, D]); outp('o_pwkv', [64, 64, 64]); outp('o_pshift', [1, D])
    outp('o_pck', [128, 512]); outp('o_pcv', [128, 512]); outp('o_swkv', [NS, 64, 64, 64]); outp('o_sshift', [NS, D])
    outp('o_sck', [NS, 128, 512]); outp('o_scv', [NS, 128, 512])
    scr('xmT', [6, 128, 16, T], BF16); scr('rkvz', [4, T, E], BF16); scr('wpre', [T, E], F32); scr('apre', [T, E], F32)
    scr('ygT', [128, 32, T], BF16); scr('o1', [T, D], F32); scr('hp', [T, D], F32); scr('hkvT', [128, 16, T], BF16); scr('hbT', [128, 16, T], BF16)
    scr('KVs', [T, 1024], BF16); scr('qs', [T, E], BF16); scr('zs', [T, E], BF16); scr('agT', [128, 32, T], BF16); scr('o2', [T, D], F32)
    with ExitStack() as stack:
        P = Prog(nc, stack)
        if 'A' in stages:
            stage_A(P, nc, d)
        if 'B' in stages:
            stage_B(P, nc, d)
        if 'L' in stages:
            stage_B_lora(P, nc, d)
        if 'C' in stages:
            stage_CDE(P, nc, d)
        if 'F' in stages:
            stage_outproj(P, nc, d, 'ygT', d['a_w_out'][0], 'o1', 'F1')
        if 'f' in stages:
            stage_F2(P, nc, d)
        if 'G' in stages:
            stage_G(P, nc, d)
        if 'H' in stages:
            stage_H(P, nc, d)
        if 'I' in stages:
            stage_I(P, nc, d)
        if 'J' in stages:
            stage_outproj(P, nc, d, 'agT', d['b_w_o'][0], 'o2', 'J1')
        if 'j' in stages:
            stage_J2(P, nc, d)
    nc._lazy = d
    return nc


def host_tables():
    f = np.float32
    j = np.arange(128)
    su = (j[:, None] < j[None, :]).astype(f)
    u = (j[:, None] <= j[None, :]).astype(f)
    tb = {}
    tb['ident'] = np.eye(128, dtype=f)
    tb['tri'] = u.copy()
    tb['ones'] = np.ones((128, 128), f)
    oh = np.zeros((128, 1), f); oh[0, 0] = 1
    tb['onehot'] = oh
    c = f(-np.exp(-0.5))
    lm = np.zeros((128, 2), f); lm[:, 0] = c; lm[0, 1] = c
    tb['lmask'] = lm
    tb['mask4'] = np.concatenate([su, u, -su, u], 1)
    tb['negsl'] = -(su.T).copy()
    NEG = f(-30000.0)
    jj = j[:, None]; ii = j[None, :]
    cur = np.where(jj <= ii, 0, NEG).astype(f)
    prev = np.where(jj >= ii, 0, NEG).astype(f)
    lead = np.where(jj >= 112, 0, NEG).astype(f)
    mb = np.stack([cur, prev, np.minimum(cur, lead), np.minimum(prev, lead)], 1)
    tb['mb'] = np.ascontiguousarray(mb)
    pos = np.zeros(T, f)
    pos[112:2176] = np.arange(2064)
    pos[2176:2176 + NS] = 16384
    inv = (f(500000.0) ** (-np.arange(8, dtype=f) * f(2.0) / f(16))).astype(f)
    ang = (pos[:, None] * inv[None, :]).astype(f)
    tb['cs'] = np.concatenate([np.cos(ang), np.sin(ang)], 1).astype(f)
    return tb


_NC = [None]


def kernel(**inp):
    f = np.float32
    inp = {k: np.asarray(v) for k, v in inp.items()}
    if _NC[0] is None:
        _NC[0] = build()
    nc = _NC[0]
    tb = host_tables()
    mu = inp['a_mu'][0]
    muT = np.ascontiguousarray(mu.reshape(6, 16, 128).transpose(2, 0, 1))
    prm = np.ascontiguousarray(np.stack([inp['a_w0'][0], inp['a_a0'][0], inp['a_k_k'][0], inp['a_k_a'][0], inp['a_r_k'][0].reshape(-1),
                                         inp['a_gn_g'][0], inp['a_gn_b'][0]], 0).astype(f))
    shared = dict(tb)
    shared.update(muT=muT, prm=prm, a_norm=inp['a_norm'], a_w_rkvz=inp['a_w_rkvz'], a_w1=inp['a_w1'], a_w2=inp['a_w2'], a_a1=inp['a_a1'],
                  a_a2=inp['a_a2'], a_w_out=inp['a_w_out'], kv_norm=inp['kv_norm'].reshape(1, D), w_kv=inp['w_kv'], b_norm=inp['b_norm'],
                  b_w_qz=inp['b_w_qz'], b_sinks=inp['b_sinks'], b_w_o=inp['b_w_o'], final_norm=inp['final_norm'].reshape(1, D))
    in_maps = []
    for core in range(8):
        b = core % 4
        ss = slice(core * NS, core * NS + NS)
        xin = np.zeros((T + 1, D), f)
        xin[1 + 112:1 + 128] = inp['meta_tokens']
        xin[1 + 128:1 + 128 + 2048] = inp['x_prompt'][b]
        xin[1 + SROW0:1 + SROW0 + NS] = inp['x_sample'][ss, 0]
        sshift = np.zeros((128, D), f)
        sshift[:NS] = inp['state_shift'][0, ss]
        m = dict(shared)
        m.update(xin=xin, sshift=sshift, swkv=np.ascontiguousarray(inp['state_wkv'][0, ss]),
                 ck=np.ascontiguousarray(inp['cache_k'][ss].reshape(NS, 128, 512)), cv=np.ascontiguousarray(inp['cache_v'][ss].reshape(NS, 128, 512)))
        in_maps.append(m)
    in_maps = [{k: m[k] for k in nc._lazy.inputs if k in m} for m in in_maps]
    res = run_bass_kernel_spmd(nc, in_maps, core_ids=list(range(8)))
    R = res.results
    g = lambda c, n: np.asarray(R[c][n], dtype=f)
    y_prompt = np.stack([g(b, 'o_yp') for b in range(4)], 0)
    y_sample = np.concatenate([g(c, 'o_ys') for c in range(8)], 0)[:, None, :]
    p_wkv = np.stack([g(b, 'o_pwkv') for b in range(4)], 0)[None]
    p_shift = np.concatenate([g(b, 'o_pshift') for b in range(4)], 0)[None]
    p_ck = np.stack([g(b, 'o_pck') for b in range(4)], 0).reshape(4, 128, 8, 64)
    p_cv = np.stack([g(b, 'o_pcv') for b in range(4)], 0).reshape(4, 128, 8, 64)
    s_wkv = np.concatenate([g(c, 'o_swkv') for c in range(8)], 0)[None]
    s_shift = np.concatenate([g(c, 'o_sshift') for c in range(8)], 0)[None]
    s_ck = np.concatenate([g(c, 'o_sck') for c in range(8)], 0).reshape(32, 128, 8, 64)
    s_cv = np.concatenate([g(c, 'o_scv') for c in range(8)], 0).reshape(32, 128, 8, 64)
    return (y_prompt, y_sample, p_wkv, p_shift, p_ck, p_cv, s_wkv, s_shift, s_ck, s_cv)
```

```python
import numpy as np
from contextlib import ExitStack
import concourse.bass as bass
import concourse.mybir as mybir
from concourse.bass_utils import run_bass_kernel_spmd

F32 = mybir.dt.float32
BF16 = mybir.dt.bfloat16
AF = mybir.ActivationFunctionType
ALU = mybir.AluOpType
AX = mybir.AxisListType

COMPUTE = ('pe', 'act', 'dve', 'pool')
ALLENG = ('pe', 'act', 'dve', 'pool', 'sp')
SAME_ENGINE_SYNC = True
PIPELINE = True
PSUM_KEYS = {'pC0', 'pC1', 'pTP', 'pPQ', 'pPA', 'pRX', 'pYS', 'ps', 'psg', 'psL', 'pSC', 'pAO'}


class Prog:
    def __init__(self, nc, stack):
        self.nc = nc
        self.stack = stack
        self.esem = {e: stack.enter_context(nc.semaphore("s_" + e)) for e in COMPUTE}
        self.ecnt = {e: 0 for e in COMPUTE}
        self.dsem = {}
        self.dcnt = {}
        self.dsid = {}
        self.free_dsems = []
        self.nds = 0
        self.waited = {e: {} for e in ALLENG}
        self.reset()

    def reset(self):
        self.ops = []
        self.lastw = {}
        self.readers = {}
        self.chain = {}

    max_ops = None
    cap = None

    def op(self, eng, fn, reads=(), writes=(), key=None):
        if self.cap is not None:
            self.cap.append((eng, fn, list(reads), list(writes), key))
            return -1
        i = len(self.ops)
        if self.max_ops is not None and i >= self.max_ops:
            return -1
        pr = [r for r in reads if (r[0] if isinstance(r, tuple) else r) in PSUM_KEYS]
        if pr:
            reads = [r for r in reads if r not in pr]
            writes = list(writes) + [r for r in pr if r not in writes]
        deps = set()
        for r in reads:
            w = self.lastw.get(r)
            if w is not None:
                deps.add(w)
        for w_ in writes:
            w = self.lastw.get(w_)
            if w is not None:
                deps.add(w)
            deps.update(self.readers.get(w_, ()))
        if key is not None:
            prev = self.chain.get(key)
            if prev is not None:
                deps.add(prev)
            self.chain[key] = i
        self.ops.append(dict(eng=eng, fn=fn, deps=deps, key=key))
        for w_ in writes:
            self.lastw[w_] = i
            self.readers[w_] = []
        ws = set(writes)
        for r in reads:
            if r not in ws:
                self.readers.setdefault(r, []).append(i)
        return i

    def dma(self, q, out, in_, reads=(), writes=(), key=None, **kw):
        assert key is not None
        return self.op(q, lambda e: e.dma_start(out=out, in_=in_, **kw), reads, writes, key=key)

    def flush(self):
        nc = self.nc
        ops = self.ops
        if not ops:
            return
        needed = set()
        for o in ops:
            needed.update(o['deps'])
        lastop = {}
        for i, o in enumerate(ops):
            if o['key'] is None:
                lastop[o['eng']] = i
        needed.update(lastop.values())
        tgt = [None] * len(ops)
        for i, o in enumerate(ops):
            if o['key'] is not None:
                k = o['key']
                if k not in self.dsem:
                    if self.free_dsems:
                        self.dsem[k], self.dcnt[k], self.dsid[k] = self.free_dsems.pop()
                    else:
                        self.nds += 1
                        self.dsem[k] = self.stack.enter_context(nc.semaphore("d_" + str(self.nds)))
                        self.dcnt[k] = 0
                        self.dsid[k] = ('d', self.nds)
                self.dcnt[k] += 16
                tgt[i] = (self.dsem[k], self.dcnt[k], self.dsid[k])
            elif i in needed:
                e = o['eng']
                self.ecnt[e] += 1
                tgt[i] = (self.esem[e], self.ecnt[e], ('e', e))
        per = {e: [] for e in ALLENG}
        for i, o in enumerate(ops):
            per[o['eng']].append(i)
        end_waits = []
        for e in COMPUTE:
            if e in lastop:
                end_waits.append(tgt[lastop[e]])
        for k in self.chain:
            end_waits.append((self.dsem[k], self.dcnt[k], self.dsid[k]))

        def run(ename, eobj):
            waited = self.waited[ename]
            for i in per[ename]:
                o = ops[i]
                need = {}
                for d in o['deps']:
                    od = ops[d]
                    if od['key'] is None and od['eng'] == ename:
                        if ename == 'pe' or not SAME_ENGINE_SYNC:
                            continue
                    sem, val, sid = tgt[d]
                    if need.get(sid, (None, 0))[1] < val:
                        need[sid] = (sem, val)
                for sid, (sem, val) in need.items():
                    if waited.get(sid, 0) < val:
                        eobj.wait_ge(sem, val)
                        waited[sid] = val
                ins = o['fn'](eobj)
                if tgt[i] is not None:
                    if o['key'] is not None:
                        ins.then_inc(tgt[i][0], 16)
                    else:
                        ins.then_inc(tgt[i][0], 1)
            for sem, val, sid in end_waits:
                if waited.get(sid, 0) < val:
                    eobj.wait_ge(sem, val)
                    waited[sid] = val

        with nc.Block() as block:
            @block.tensor
            def _(e):
                run('pe', e)

            @block.scalar
            def _(e):
                run('act', e)

            @block.vector
            def _(e):
                run('dve', e)

            @block.gpsimd
            def _(e):
                run('pool', e)

            @block.sync
            def _(e):
                run('sp', e)
        for k in list(self.dsem):
            self.free_dsems.append((self.dsem[k], self.dcnt[k], self.dsid[k]))
        self.dsem, self.dcnt, self.dsid = {}, {}, {}
        self.reset()


NT = 18
T = NT * 128
D = 2048
E = 4096
NS = 4
RMS_EPS = 1e-6


class Ctx:
    pass


_uid = [0]


def mk(nc, es, name, shape, dt, psum=False):
    _uid[0] += 1
    name = f"{name}_u{_uid[0]}"
    if psum:
        return es.enter_context(nc.psum_tensor(name, shape, dt))
    return es.enter_context(nc.sbuf_tensor(name, shape, dt))


def stage_A(P, nc, d):
    with ExitStack() as es:
        gA = mk(nc, es, "gA", [128, D], F32)
        muT = mk(nc, es, "muT", [128, 6, 16], F32)
        ident = mk(nc, es, "identA", [128, 128], F32)
        xc = [mk(nc, es, f"xc{i}", [128, D], F32) for i in range(2)]
        xp = [mk(nc, es, f"xp{i}", [128, D], F32) for i in range(2)]
        junk = mk(nc, es, "junkA", [128, D], F32)
        st = [mk(nc, es, f"stA{i}", [128, 4], F32) for i in range(2)]
        xnT = [mk(nc, es, f"xnT{i}", [128, 16, 128], F32) for i in range(2)]
        xxT = [mk(nc, es, f"xxT{i}", [128, 16, 128], F32) for i in range(2)]
        tmp = [mk(nc, es, f"tmpA{i}", [128, 16, 128], F32) for i in range(2)]
        xm = [mk(nc, es, f"xmA{i}", [128, 16, 128], BF16) for i in range(3)]
        ps = [mk(nc, es, f"psA{i}", [128, 512], F32, psum=True) for i in range(4)]

        P.dma('sp', gA[:], d['a_norm'][0].partition_broadcast(128), writes=['gA'], key='gA')
        P.dma('sp', muT[:], d['muT'], writes=['muT'], key='muT')
        P.dma('sp', ident[:], d['ident'], writes=['ident'], key='ident')
        ev = 0
        mi = 0
        for t in range(NT):
            b = t % 2
            P.dma('sp', xc[b][:], d['xin'][1 + 128 * t: 1 + 128 * t + 128, :], writes=[('xc', b)], key=('xc', b))
            if t < NT - 1:
                P.dma('sp', xp[b][:], d['xin'][128 * t: 128 * t + 128, :], writes=[('xp', b)], key=('xp', b))
            else:
                P.dma('sp', xp[b][:], d['sshift'], writes=[('xp', b)], key=('xp', b))
            P.op('pool', lambda e, b=b: e.memset(st[b][:, 0:2], 0.0), writes=[('st', b, 0), ('st', b, 1)])
            P.op('act', lambda e, b=b: e.activation(out=junk[:], in_=xc[b][:], func=AF.Square, accum_out=st[b][:, 0:1]),
                 reads=[('xc', b)], writes=['junk', ('st', b, 0)])
            if t < NT - 1:
                P.op('act', lambda e, b=b: e.activation(out=junk[:], in_=xp[b][:], func=AF.Square, accum_out=st[b][:, 1:2]),
                     reads=[('xp', b)], writes=['junk', ('st', b, 1)])
            nst = 2 if t < NT - 1 else 1
            P.op('dve', lambda e, b=b, n=nst: e.tensor_scalar(out=st[b][:, 2:2 + n], in0=st[b][:, 0:n], scalar1=1.0 / D, scalar2=RMS_EPS,
                                                              op0=ALU.mult, op1=ALU.add),
                 reads=[('st', b, 0), ('st', b, 1)], writes=[('st', b, 2)])
            P.op('act', lambda e, b=b, n=nst: e.sqrt(out=st[b][:, 2:2 + n], in_=st[b][:, 2:2 + n]),
                 reads=[('st', b, 2)], writes=[('st', b, 2)])
            P.op('dve', lambda e, b=b, n=nst: e.reciprocal(out=st[b][:, 2:2 + n], in_=st[b][:, 2:2 + n]),
                 reads=[('st', b, 2)], writes=[('st', b, 2)])
            P.op('dve', lambda e, b=b: e.scalar_tensor_tensor(out=xc[b][:], in0=xc[b][:], scalar=st[b][:, 2:3], in1=gA[:],
                                                              op0=ALU.mult, op1=ALU.mult),
                 reads=[('xc', b), ('st', b, 2), 'gA'], writes=[('xc', b)])
            if t < NT - 1:
                P.op('dve', lambda e, b=b: e.scalar_tensor_tensor(out=xp[b][:], in0=xp[b][:], scalar=st[b][:, 3:4], in1=gA[:],
                                                                  op0=ALU.mult, op1=ALU.mult),
                     reads=[('xp', b), ('st', b, 2), 'gA'], writes=[('xp', b)])
            if t == NT - 2:
                P.dma('sp', d['o_pshift'], xc[b][127:128, :], reads=[('xc', b)], key='o_pshift')
            if t == NT - 1:
                P.dma('sp', d['o_sshift'], xc[b][0:NS, :], reads=[('xc', b)], key='o_sshift')
            P.op('pool', lambda e, b=b: e.tensor_tensor(out=xp[b][:], in0=xp[b][:], in1=xc[b][:], op=ALU.subtract),
                 reads=[('xp', b), ('xc', b)], writes=[('xp', b)])
            for (src, srck, dst, dstk) in ((xc, 'xc', xnT, 'xnT'), (xp, 'xp', xxT, 'xxT')):
                for q in range(4):
                    pb = ev % 4
                    for j in range(4):
                        c = q * 4 + j
                        P.op('pe', lambda e, pb=pb, j=j, c=c, src=src, b=b: e.transpose(out=ps[pb][:, j * 128:(j + 1) * 128],
                                                                                        in_=src[b][:, c * 128:(c + 1) * 128], identity=ident[:]),
                             reads=[(srck, b), 'ident'], writes=[('ps', pb)])
                    eng = 'act' if ev % 2 == 0 else 'dve'
                    if eng == 'act':
                        P.op('act', lambda e, pb=pb, q=q, dst=dst, b=b: e.copy(out=dst[b][:, q * 4:(q + 1) * 4, :], in_=ps[pb][:].rearrange("p (a n) -> p a n", a=4)),
                             reads=[('ps', pb)], writes=[(dstk, b)])
                    else:
                        P.op('dve', lambda e, pb=pb, q=q, dst=dst, b=b: e.tensor_copy(out=dst[b][:, q * 4:(q + 1) * 4, :], in_=ps[pb][:].rearrange("p (a n) -> p a n", a=4)),
                             reads=[('ps', pb)], writes=[(dstk, b)])
                    ev += 1
            for p in range(6):
                m = mi % 3
                mi += 1
                e1 = 'pool' if p % 3 == 2 else 'dve'
                P.op(e1, lambda e, b=b, p=p: e.tensor_tensor(out=tmp[p % 2][:], in0=xxT[b][:], in1=muT[:, p, :].unsqueeze(2).to_broadcast([128, 16, 128]), op=ALU.mult),
                     reads=[('xxT', b), 'muT'], writes=[('tmp', p % 2)])
                P.op(e1, lambda e, b=b, p=p, m=m: e.tensor_tensor(out=xm[m][:], in0=tmp[p % 2][:], in1=xnT[b][:], op=ALU.add),
                     reads=[('tmp', p % 2), ('xnT', b)], writes=[('xm', m)])
                P.dma('sp', d['xmT'][p][:, :, t * 128:(t + 1) * 128], xm[m][:], reads=[('xm', m)], writes=[('xmT', p)], key=('xmst', m))
        P.flush()


def gemm_tokmajor(P, nc, es, actT_src, kc, w_src, ncols, evac, wkey, tiles=range(NT), act_res='actT', actT=None):
    wt = [mk(nc, es, f"wt_{wkey}{i}", [128, kc, 512], BF16) for i in range(2)]
    ps = [mk(nc, es, f"psg_{wkey}{i}", [128, 512], F32, psum=True) for i in range(4)]
    wv = w_src.rearrange("(c p) n -> p c n", p=128)
    cnt = 0
    for cb in range(ncols // 512):
        wb = cb % 2
        P.dma('pool', wt[wb][:], wv[:, :, cb * 512:(cb + 1) * 512], writes=[('wt', wkey, wb)], key=('wt', wkey, wb))
        for t in tiles:
            pb = cnt % 4
            cnt += 1
            for c in range(kc):
                P.op('pe', lambda e, pb=pb, c=c, t=t, wb=wb: e.matmul(ps[pb][:], lhsT=actT[:, c, t * 128:(t + 1) * 128], rhs=wt[wb][:, c, :],
                                                                      start=(c == 0), stop=(c == kc - 1)),
                     reads=[act_res, ('wt', wkey, wb)], writes=[('psg', wkey, pb)])
            evac(t, cb, ps[pb], ('psg', wkey, pb), cnt)


def stage_B(P, nc, d, projs=(0, 1, 2, 3)):
    for p in projs:
        with ExitStack() as es:
            actT = mk(nc, es, "actT", [128, 16, T], BF16)
            ob = [mk(nc, es, f"obB{i}", [128, 512], BF16) for i in range(4)]
            P.dma('sp', actT[:], d['xmT'][p], writes=['actT'], key='actT')

            def evac(t, cb, pst, pkey, cnt, p=p):
                o = cnt % 4
                if cnt % 2 == 0:
                    P.op('act', lambda e: e.copy(out=ob[o][:], in_=pst[:]), reads=[pkey], writes=[('ob', o)])
                else:
                    P.op('dve', lambda e: e.tensor_copy(out=ob[o][:], in_=pst[:]), reads=[pkey], writes=[('ob', o)])
                P.dma('sp', d['rkvz'][p][t * 128:(t + 1) * 128, cb * 512:(cb + 1) * 512], ob[o][:], reads=[('ob', o)], key=('obst', o))
            gemm_tokmajor(P, nc, es, None, 16, d['a_w_rkvz'][0, p], E, evac, f"B{p}", actT=actT)
            P.flush()


def tt(P, eng, out, in0, in1, op, reads, writes):
    P.op(eng, lambda e: e.tensor_tensor(out=out, in0=in0, in1=in1, op=op), reads, writes)


def ts(P, eng, out, in0, s1, s2, op0, op1, reads, writes):
    if s2 is None:
        P.op(eng, lambda e: e.tensor_scalar(out=out, in0=in0, scalar1=s1, scalar2=None, op0=op0), reads, writes)
    else:
        P.op(eng, lambda e: e.tensor_scalar(out=out, in0=in0, scalar1=s1, scalar2=s2, op0=op0, op1=op1), reads, writes)


def stt(P, eng, out, in0, scalar, in1, op0, op1, reads, writes):
    P.op(eng, lambda e: e.scalar_tensor_tensor(out=out, in0=in0, scalar=scalar, in1=in1, op0=op0, op1=op1), reads, writes)


def actf(P, out, in_, func, reads, writes, scale=1.0):
    P.op('act', lambda e: e.activation(out=out, in_=in_, func=func, scale=scale), reads, writes)


def cp(P, eng, out, in_, reads, writes):
    if eng == 'act':
        P.op('act', lambda e: e.copy(out=out, in_=in_), reads, writes)
    else:
        P.op(eng, lambda e: e.tensor_copy(out=out, in_=in_), reads, writes)


def mm(P, out, lhsT, rhs, start, stop, reads, writes):
    P.op('pe', lambda e: e.matmul(out, lhsT=lhsT, rhs=rhs, start=start, stop=stop), reads, writes)


def tr(P, out, in_, ident, reads, writes):
    P.op('pe', lambda e: e.transpose(out=out, in_=in_, identity=ident), reads, writes)


def red(P, eng, out, in_, reads, writes):
    P.op(eng, lambda e: e.reduce_sum(out=out, in_=in_, axis=AX.X), reads, writes)


def recip(P, out, in_, reads, writes):
    P.op('dve', lambda e: e.reciprocal(out=out, in_=in_), reads, writes)


GN_EPS = 64e-5
SROW0 = 17 * 128
ZROW0 = SROW0 + NS
NCH = 17 + NS
CH_LIST = list(range(NCH))
CB_LIST = list(range(8))


def load_rows(P, q, dst, src, ch, c0, c1, writes, key):
    if ch < 17:
        P.dma(q, dst[:], src[ch * 128:(ch + 1) * 128, c0:c1], writes=writes, key=key)
    else:
        s = ch - 17
        P.dma(q, dst[0:1], src[SROW0 + s:SROW0 + s + 1, c0:c1], writes=writes, key=key)
        P.dma(q, dst[1:65], src[ZROW0:ZROW0 + 64, c0:c1], writes=writes, key=key)
        P.dma(q, dst[64:128], src[ZROW0:ZROW0 + 64, c0:c1], writes=writes, key=key)


def stage_B_lora(P, nc, d):
    for which, (xi, w1n, w2n, outn, func) in enumerate(((4, 'a_w1', 'a_w2', 'wpre', AF.Tanh), (5, 'a_a1', 'a_a2', 'apre', AF.Copy))):
        with ExitStack() as es:
            actT = mk(nc, es, "actT", [128, 16, T], BF16)
            w1 = mk(nc, es, "w1", [128, 16, 96], BF16)
            w2 = mk(nc, es, "w2", [96, E], BF16)
            hT = mk(nc, es, "hT", [96, T], BF16)
            ob = [mk(nc, es, f"obL{i}", [128, 512], F32) for i in range(4)]
            ps = [mk(nc, es, f"psL{i}", [128, 512], F32, psum=True) for i in range(4)]
            P.dma('sp', actT[:], d['xmT'][xi], writes=['actT'], key='actT')
            P.dma('pool', w1[:], d[w1n][0].rearrange("(c p) n -> p c n", p=128), writes=['w1'], key='w1')
            P.dma('pool', w2[:], d[w2n][0], writes=['w2'], key='w2')
            cnt = 0
            for t in range(NT):
                pb = cnt % 4
                cnt += 1
                for c in range(16):
                    mm(P, ps[pb][0:96, 0:128], w1[:, c, :], actT[:, c, t * 128:(t + 1) * 128], c == 0, c == 15,
                       ['actT', 'w1'], [('psL', pb)])
                actf(P, hT[:, t * 128:(t + 1) * 128], ps[pb][0:96, 0:128], func, [('psL', pb)], [('hT', t)])
            for t in range(NT):
                for cb in range(8):
                    pb = cnt % 4
                    cnt += 1
                    mm(P, ps[pb][:], hT[:, t * 128:(t + 1) * 128], w2[:, cb * 512:(cb + 1) * 512], True, True,
                       [('hT', t), 'w2'], [('psL', pb)])
                    cp(P, 'act' if cnt % 2 else 'dve', ob[pb][:], ps[pb][:], [('psL', pb)], [('obL', pb)])
                    P.dma('sp', d[outn][t * 128:(t + 1) * 128, cb * 512:(cb + 1) * 512], ob[pb][:], reads=[('obL', pb)], key=('obLst', pb))
            P.flush()


def stage_CDE(P, nc, d):
    with ExitStack() as es:
        ident = mk(nc, es, "identF", [128, 128], F32)
        identb = mk(nc, es, "identB", [128, 128], BF16)
        tri = mk(nc, es, "tri", [128, 128], F32)
        ones = mk(nc, es, "ones", [128, 128], F32)
        onehot = mk(nc, es, "onehot", [128, 1], F32)
        lmask = mk(nc, es, "lmask", [128, 2], F32)
        mask4 = mk(nc, es, "mask4", [128, 512], F32)
        negsl = mk(nc, es, "negsl", [128, 128], F32)
        for nm, tl in (('ident', ident), ('tri', tri), ('ones', ones), ('onehot', onehot), ('lmask', lmask), ('mask4', mask4), ('negsl', negsl)):
            P.dma('sp', tl[:], d[nm], writes=[nm], key=nm)
        P.dma('pool', identb[:], d['ident'], writes=['identb'], key='identb')
        CONST = ['ident', 'tri', 'ones', 'onehot', 'lmask', 'mask4', 'negsl', 'identb']
        ST = mk(nc, es, "ST", [128, 32, 64], F32)
        STb = mk(nc, es, "STb", [128, 32, 64], BF16)
        SN = mk(nc, es, "SN", [64, 64, 64], F32)
        BON = mk(nc, es, "BON", [128, 64], F32)
        stmp = mk(nc, es, "stmp", [128, 64], F32)
        ygT = [mk(nc, es, f"ygT{i}", [128, 32, 128], BF16) for i in range(2)]
        NB = 3
        PRM = [mk(nc, es, f"PRM{i}", [128, 7, 512], F32) for i in range(NB)]
        Rb = [mk(nc, es, f"Rb{i}", [128, 512], BF16) for i in range(NB)]
        Kb = [mk(nc, es, f"Kb{i}", [128, 512], BF16) for i in range(NB)]
        Vb = [mk(nc, es, f"Vb{i}", [128, 512], BF16) for i in range(NB)]
        Zb = [mk(nc, es, f"Zb{i}", [128, 512], BF16) for i in range(NB)]
        Wp = [mk(nc, es, f"Wp{i}", [128, 512], F32) for i in range(NB)]
        Ap = [mk(nc, es, f"Ap{i}", [128, 512], F32) for i in range(NB)]
        f32names = ['LD', 'KKf', 'KMf', 'Bf', 'SQ', 'T1', 'E1', 'E2', 'E3', 'E4', 'GT', 'Dinv']
        W = {n: mk(nc, es, n, [128, 512], F32) for n in f32names}
        sm = mk(nc, es, "sm", [128, 64], F32)
        TM = [mk(nc, es, f"TM{i}", [128, 4, 512], BF16) for i in range(NB)]
        KVb = [mk(nc, es, f"KVb{i}", [128, 512], BF16) for i in range(NB)]
        BVb = [mk(nc, es, f"BVb{i}", [128, 512], BF16) for i in range(NB)]
        FT = [mk(nc, es, f"FT{i}", [128, 4, 4, 128], BF16) for i in range(NB)]
        gCs = [mk(nc, es, f"gCs{i}", [128, 4], F32) for i in range(NB)]
        AK = mk(nc, es, "AK", [128, 4, 512], BF16)
        MT = mk(nc, es, "MT", [128, 4, 128], BF16)
        Rm = [mk(nc, es, f"Rm{i}", [128, 4, 128], BF16) for i in range(2)]
        PP = [mk(nc, es, f"PP{i}", [128, 4, 2, 128], BF16) for i in range(2)]
        Xb = mk(nc, es, "Xb", [128, 256], BF16)
        nSA = mk(nc, es, "nSA", [128, 256], BF16)
        Ycb = [mk(nc, es, f"Ycb{i}", [128, 512], F32) for i in range(2)]
        EY = {n: mk(nc, es, n, [128, 512], F32) for n in ('Ysq', 'Yn', 'Sz')}
        YG = mk(nc, es, "YG", [128, 512], BF16)
        pC0 = mk(nc, es, "pC0", [128, 512], F32, psum=True)
        pC1 = mk(nc, es, "pC1", [128, 512], F32, psum=True)
        pRX2 = mk(nc, es, "pRX2", [128, 512], F32, psum=True)
        pPQ = mk(nc, es, "pPQ", [128, 4, 2, 128], F32, psum=True)
        pPA = mk(nc, es, "pPA", [128, 512], F32, psum=True)
        pRX = mk(nc, es, "pRX", [128, 512], F32, psum=True)
        pYS = mk(nc, es, "pYS", [128, 512], F32, psum=True)

        def state_load(s):
            P.dma('sp', SN[:], d['swkv'][s].rearrange("h v k -> v h k"), writes=['SN'], key='SN')
            for g8 in range(4):
                for q in range(8):
                    gp = g8 * 8 + q
                    tr(P, pC0[:, q * 64:(q + 1) * 64], SN[:, 2 * gp:2 * gp + 2, :].rearrange("v a k -> v (a k)"), ident[0:64, 0:64],
                       ['SN', 'ident'], ['pC0'])
                cp(P, 'act', ST[:, g8 * 8:(g8 + 1) * 8, :], pC0[:].rearrange("p (a v) -> p a v", a=8), ['pC0'], [('ST', g8 * 8 + q) for q in range(8)])
                cp(P, 'dve', STb[:, g8 * 8:(g8 + 1) * 8, :], pC0[:].rearrange("p (a v) -> p a v", a=8), ['pC0'], [('STb', g8 * 8 + q) for q in range(8)])

        def state_save(dst):
            for g4 in range(8):
                for q in range(4):
                    gp = g4 * 4 + q
                    tr(P, pC0[0:64, q * 128:(q + 1) * 128], ST[:, gp, :], ident[:], [('ST', gp), 'ident'], ['pC0'])
                cp(P, 'act', SN[:, g4 * 8:(g4 + 1) * 8, :].rearrange("v a k -> v (a k)"), pC0[0:64, :], ['pC0'], ['SN'])
            P.dma('sp', dst.rearrange("h v k -> v h k"), SN[:], reads=['SN'], key='SNst')

        P.op('pool', lambda e: e.memset(ST[:], 0.0), writes=[('ST', g) for g in range(32)])
        P.op('pool', lambda e: e.memset(STb[:], 0.0), writes=[('STb', g) for g in range(32)])

        def phase_C(ch, cb, b):
            lcol = 0 if ch < 17 else 1
            c0, c1 = cb * 512, (cb + 1) * 512
            P.dma('sp', PRM[b][:], d['prm'][:, c0:c1].partition_broadcast(128), writes=[('PRM', b)], key=('PRM', b))
            load_rows(P, 'sp', Rb[b], d['rkvz'][0], ch, c0, c1, [('Rb', b)], ('Rb', b))
            load_rows(P, 'sp', Kb[b], d['rkvz'][1], ch, c0, c1, [('Kb', b)], ('Kb', b))
            load_rows(P, 'sp', Vb[b], d['rkvz'][2], ch, c0, c1, [('Vb', b)], ('Vb', b))
            load_rows(P, 'sp', Zb[b], d['rkvz'][3], ch, c0, c1, [('Zb', b)], ('Zb', b))
            load_rows(P, 'sp', Wp[b], d['wpre'], ch, c0, c1, [('Wp', b)], ('Wp', b))
            load_rows(P, 'sp', Ap[b], d['apre'], ch, c0, c1, [('Ap', b)], ('Ap', b))
            prm = lambda i, b=b: PRM[b][:, i, :]
            tt(P, 'pool', Wp[b][:], Wp[b][:], prm(0), ALU.add, [('Wp', b), ('PRM', b)], [('Wp', b)])
            actf(P, Wp[b][:], Wp[b][:], AF.Sigmoid, [('Wp', b)], [('Wp', b)])
            ts(P, 'dve', W['LD'][:], Wp[b][:], lmask[:, lcol:lcol + 1], None, ALU.mult, None, [('Wp', b), 'lmask'], ['LD'])
            tt(P, 'pool', Ap[b][:], Ap[b][:], prm(1), ALU.add, [('Ap', b), ('PRM', b)], [('Ap', b)])
            actf(P, Ap[b][:], Ap[b][:], AF.Sigmoid, [('Ap', b)], [('Ap', b)])
            tt(P, 'pool', W['KKf'][:], Kb[b][:], prm(2), ALU.mult, [('Kb', b), ('PRM', b)], ['KKf'])
            tt(P, 'pool', W['SQ'][:], W['KKf'][:], W['KKf'][:], ALU.mult, ['KKf'], ['SQ'])
            red(P, 'dve', sm[:, 0:8], W['SQ'][:].rearrange("p (h c) -> p h c", h=8), ['SQ'], [('sm', 0)])
            ts(P, 'dve', sm[:, 0:8], sm[:, 0:8], 1e-24, None, ALU.max, None, [('sm', 0)], [('sm', 0)])
            P.op('act', lambda e: e.sqrt(out=sm[:, 0:8], in_=sm[:, 0:8]), [('sm', 0)], [('sm', 0)])
            recip(P, sm[:, 0:8], sm[:, 0:8], [('sm', 0)], [('sm', 0)])
            tt(P, 'dve', W['KKf'][:].rearrange("p (h c) -> p h c", h=8), W['KKf'][:].rearrange("p (h c) -> p h c", h=8),
               sm[:, 0:8].unsqueeze(2).to_broadcast([128, 8, 64]), ALU.mult, ['KKf', ('sm', 0)], ['KKf'])
            stt(P, 'dve', W['T1'][:], Ap[b][:], -1.0, prm(3), ALU.add, ALU.mult, [('Ap', b), ('PRM', b)], ['T1'])
            stt(P, 'dve', W['KMf'][:], W['T1'][:], 1.0, Kb[b][:], ALU.add, ALU.mult, ['T1', ('Kb', b)], ['KMf'])
            tt(P, 'pool', W['Bf'][:], W['KKf'][:], Ap[b][:], ALU.mult, ['KKf', ('Ap', b)], ['Bf'])
            tt(P, 'pool', W['T1'][:], Rb[b][:], W['KMf'][:], ALU.mult, [('Rb', b), 'KMf'], ['T1'])
            tt(P, 'pool', W['T1'][:], W['T1'][:], prm(4), ALU.mult, ['T1', ('PRM', b)], ['T1'])
            red(P, 'dve', BON[:, cb * 8:(cb + 1) * 8], W['T1'][:].rearrange("p (h c) -> p h c", h=8), ['T1'], [('BON', cb)])
            mm(P, pC0[:], tri[:], W['LD'][:], True, True, ['tri', 'LD'], ['pC0'])
            mm(P, pC1[:], ones[:], W['LD'][:], True, True, ['ones', 'LD'], ['pC1'])
            actf(P, W['E1'][:], pC0[:], AF.Exp, ['pC0'], ['E1'])
            actf(P, W['E2'][:], pC0[:], AF.Exp, ['pC0'], ['E2'], scale=-1.0)
            actf(P, W['GT'][:], pC1[:], AF.Exp, ['pC1'], ['GT'])
            actf(P, W['Dinv'][:], W['LD'][:], AF.Exp, ['LD'], ['Dinv'], scale=-1.0)
            tt(P, 'dve', W['E3'][:], W['E1'][:], W['Dinv'][:], ALU.mult, ['E1', 'Dinv'], ['E3'])
            tt(P, 'pool', W['E4'][:], W['GT'][:], W['E2'][:], ALU.mult, ['GT', 'E2'], ['E4'])
            for pp in range(4):
                mm(P, pC1[:, pp:pp + 1], W['GT'][:, pp * 128:(pp + 1) * 128], onehot[:, 0:1], True, True, ['GT', 'onehot'], ['pC1'])
            cp(P, 'act', gCs[b][:], pC1[:, 0:4], ['pC1'], [('gCs', b)])
            tt(P, 'dve', TM[b][:, 1, :], Rb[b][:], W['E1'][:], ALU.mult, [('Rb', b), 'E1'], [('TM', b, 1)])
            tt(P, 'dve', TM[b][:, 2, :], W['KMf'][:], W['E2'][:], ALU.mult, ['KMf', 'E2'], [('TM', b, 2)])
            tt(P, 'pool', TM[b][:, 3, :], W['Bf'][:], W['E2'][:], ALU.mult, ['Bf', 'E2'], [('TM', b, 3)])
            tt(P, 'dve', TM[b][:, 0, :], W['KKf'][:], W['E3'][:], ALU.mult, ['KKf', 'E3'], [('TM', b, 0)])
            tt(P, 'pool', KVb[b][:], W['KMf'][:], W['E4'][:], ALU.mult, ['KMf', 'E4'], [('KVb', b)])
            tt(P, 'pool', BVb[b][:], W['Bf'][:], W['E4'][:], ALU.mult, ['Bf', 'E4'], [('BVb', b)])
            for hf in range(2):
                for pq in range(2):
                    pp = hf * 2 + pq
                    for kd in range(4):
                        tr(P, pC0[:].bitcast(BF16)[:, (pq * 4 + kd) * 128:(pq * 4 + kd + 1) * 128], TM[b][:, kd, pp * 128:(pp + 1) * 128], identb[:],
                           [('TM', b, kd), 'identb'], ['pC0'])
                cp(P, 'act' if hf == 0 else 'dve', FT[b][:, 2 * hf:2 * hf + 2].rearrange("p a k t -> p (a k t)"), pC0[:].bitcast(BF16)[:, 0:1024],
                   ['pC0'], [('FT', b, hf)])

        def phase_D(ch, cb, b, yb):
            for g in range(2):
                ftk = ('FT', b, g)
                abank = [(pPA[:], 'pPA'), (pPQ[:, 0:2].rearrange("p a b t -> p (a b t)"), ('pPQ', 0)),
                         (pPQ[:, 2:4].rearrange("p a b t -> p (a b t)"), ('pPQ', 1)), (pRX2[:], ('pRX', 1))]
                for i in range(4):
                    pp, h2 = 2 * g + i // 2, i % 2
                    fts = FT[b][h2 * 64:(h2 + 1) * 64, pp]
                    rhs2 = fts[:, 0:2, :].rearrange("p k t -> p (k t)")
                    bk, bkey = abank[i]
                    mm(P, bk[:, 0:256], fts[:, 2, :], rhs2, True, True, [ftk], [bkey])
                    mm(P, bk[:, 256:512], fts[:, 3, :], rhs2, True, True, [ftk], [bkey])
                    pnb, pnk = (pYS, 'pYS') if h2 == 0 else (pRX, ('pRX', 0))
                    mm(P, pnb[:, (i // 2) * 128:(i // 2 + 1) * 128], fts[:, 0, :], fts[:, 3, :], True, True, [ftk], [pnk])
                for i in range(4):
                    bk, bkey = abank[i]
                    tt(P, 'dve', AK[:, i, :], bk, mask4[:], ALU.mult, [bkey, 'mask4'], [('AK', i)])
                for i in range(4):
                    pnb, pnk = (pYS, 'pYS') if i % 2 == 0 else (pRX, ('pRX', 0))
                    tt(P, 'dve', MT[:, i, :], pnb[:, (i // 2) * 128:(i // 2 + 1) * 128], negsl[:], ALU.mult, [pnk, 'negsl'], [('MT', i // 2)])
                for hg in range(2):
                    tt(P, 'pool', Rm[0][:, 2 * hg:2 * hg + 2, :], AK[:, 2 * hg:2 * hg + 2, 256:384],
                       identb[:].unsqueeze(1).to_broadcast([128, 2, 128]), ALU.add,
                       [('AK', 2 * hg), ('AK', 2 * hg + 1), 'identb'], [('Rm', 0, hg)])
                cur = 0
                rxb = [pRX, pRX2]

                def squares(lev, hg):
                    nxt = lev % 2
                    last = (lev == 6)
                    for i in (2 * hg, 2 * hg + 1):
                        if lev == 1:
                            Pc, PTc = AK[:, i, 256:384], MT[:, i, :]
                            rk = [('AK', i), ('MT', hg)]
                        else:
                            Pc, PTc = PP[1 - nxt][:, i, 0, :], PP[1 - nxt][:, i, 1, :]
                            rk = [('PP', 1 - nxt, hg)]
                        if not last:
                            mm(P, pPQ[:, i, 0, :], PTc, Pc, True, True, rk, [('pPQ', hg)])
                        mm(P, pPQ[:, i, 1, :], Pc, PTc, True, True, rk, [('pPQ', hg)])

                def evac_pp(lev, hg):
                    nxt = lev % 2
                    hs_ = slice(2 * hg, 2 * hg + 2)
                    if lev < 6:
                        cp(P, 'act', PP[nxt][:, hs_], pPQ[:, hs_], [('pPQ', hg)], [('PP', nxt, hg)])
                    else:
                        cp(P, 'act', PP[nxt][:, hs_, 1, :], pPQ[:, hs_, 1, :], [('pPQ', hg)], [('PP', nxt, hg)])

                levels = range(1, 7) if ch < 17 else []
                for hg in (range(2) if ch < 17 else []):
                    squares(1, hg)
                for hg in (range(2) if ch < 17 else []):
                    evac_pp(1, hg)
                for lev in levels:
                    nxt = lev % 2
                    for hg in range(2):
                        for q, i in enumerate((2 * hg, 2 * hg + 1)):
                            mm(P, rxb[hg][:, q * 128:(q + 1) * 128], PP[nxt][:, i, 1, :], Rm[cur][:, i, :], True, True,
                               [('PP', nxt, hg), ('Rm', cur, hg)], [('pRX', hg)])
                        if lev < 6:
                            squares(lev + 1, hg)
                    for hg in range(2):
                        tt(P, 'dve', Rm[1 - cur][:, 2 * hg:2 * hg + 2, :], rxb[hg][:, 0:256].rearrange("p (a t) -> p a t", a=2),
                           Rm[cur][:, 2 * hg:2 * hg + 2, :], ALU.add, [('pRX', hg), ('Rm', cur, hg)], [('Rm', 1 - cur, hg)])
                        if lev < 6:
                            evac_pp(lev + 1, hg)
                    cur = 1 - cur
                Rf = Rm[cur]
                for i in range(4):
                    pp, h2 = 2 * g + i // 2, i % 2
                    gp = cb * 4 + pp
                    hh = 4 * g + i
                    fts = FT[b][h2 * 64:(h2 + 1) * 64, pp]
                    mm(P, pRX[:, i * 64:(i + 1) * 64], fts[:, 0, :], STb[h2 * 64:(h2 + 1) * 64, gp, :], True, False,
                       [ftk, ('STb', gp)], [('pRX', 0)])
                    mm(P, pRX[:, i * 64:(i + 1) * 64], AK[:, i, 0:128], Vb[b][:, hh * 64:(hh + 1) * 64], False, True,
                       [('AK', i), ('Vb', b)], [('pRX', 0)])
                cp(P, 'act', Xb[:], pRX[:, 0:256], [('pRX', 0)], ['Xb'])
                for i in range(4):
                    mm(P, pRX[:, 256 + i * 64:256 + (i + 1) * 64], Rf[:, i, :], Xb[:, i * 64:(i + 1) * 64], True, True,
                       [('Rm', cur, i // 2), 'Xb'], [('pRX', 0)])
                P.op('act', lambda e: e.mul(out=nSA[:], in_=pRX[:, 256:512], mul=-1.0), [('pRX', 0)], ['nSA'])
                for i in range(4):
                    pp, h2 = 2 * g + i // 2, i % 2
                    gp = cb * 4 + pp
                    hh = 4 * g + i
                    fts = FT[b][h2 * 64:(h2 + 1) * 64, pp]
                    o = pYS[:, i * 64:(i + 1) * 64]
                    mm(P, o, fts[:, 1, :], STb[h2 * 64:(h2 + 1) * 64, gp, :], True, False, [ftk, ('STb', gp)], ['pYS'])
                    mm(P, o, AK[:, i, 128:256], Vb[b][:, hh * 64:(hh + 1) * 64], False, False, [('AK', i), ('Vb', b)], ['pYS'])
                    mm(P, o, AK[:, i, 384:512], nSA[:, i * 64:(i + 1) * 64], False, True, [('AK', i), 'nSA'], ['pYS'])
                for q in range(2):
                    pp = 2 * g + q
                    o = pYS[:, 256 + q * 128:256 + (q + 1) * 128]
                    mm(P, o, KVb[b][:, pp * 128:(pp + 1) * 128], Vb[b][:, pp * 128:(pp + 1) * 128], True, False,
                       [('KVb', b), ('Vb', b)], ['pYS'])
                    mm(P, o, BVb[b][:, pp * 128:(pp + 1) * 128], nSA[:, q * 128:(q + 1) * 128], False, True,
                       [('BVb', b), 'nSA'], ['pYS'])
                cp(P, 'act', Ycb[yb][:, g * 256:(g + 1) * 256], pYS[:, 0:256], ['pYS'], [('Ycb', yb, g)])
                for q in range(2):
                    pp = 2 * g + q
                    gp = cb * 4 + pp
                    for h2 in range(2):
                        sl = slice(h2 * 64, (h2 + 1) * 64)
                        ts(P, 'dve', stmp[sl, :], ST[sl, gp, :], gCs[b][sl, pp:pp + 1], None, ALU.mult, None,
                           [('ST', gp), ('gCs', b)], ['stmp'])
                        tt(P, 'dve', ST[sl, gp, :], pYS[sl, 256 + q * 128 + h2 * 64:256 + q * 128 + (h2 + 1) * 64], stmp[sl, :], ALU.add,
                           ['pYS', 'stmp'], [('ST', gp)])
                    cp(P, 'pool', STb[:, gp, :], ST[:, gp, :], [('ST', gp)], [('STb', gp)])

        def phase_E(ch, cb, b, yb):
            prm = lambda i, b=b: PRM[b][:, i, :]
            Y3 = Ycb[yb][:].rearrange("p (h c) -> p h c", h=8)
            yk = [('Ycb', yb, 0), ('Ycb', yb, 1)]
            red(P, 'dve', sm[:, 8:16], Y3, yk, [('sm', 1)])
            tt(P, 'pool', EY['Ysq'][:], Ycb[yb][:], Ycb[yb][:], ALU.mult, yk, ['Ysq'])
            red(P, 'dve', sm[:, 16:24], EY['Ysq'][:].rearrange("p (h c) -> p h c", h=8), ['Ysq'], [('sm', 2)])
            ts(P, 'dve', sm[:, 8:16], sm[:, 8:16], 1.0 / 64, None, ALU.mult, None, [('sm', 1)], [('sm', 1)])
            tt(P, 'dve', sm[:, 24:32], sm[:, 8:16], sm[:, 8:16], ALU.mult, [('sm', 1)], [('sm', 3)])
            stt(P, 'dve', sm[:, 16:24], sm[:, 16:24], 1.0 / 64, sm[:, 24:32], ALU.mult, ALU.subtract, [('sm', 2), ('sm', 3)], [('sm', 2)])
            ts(P, 'dve', sm[:, 16:24], sm[:, 16:24], GN_EPS, None, ALU.add, None, [('sm', 2)], [('sm', 2)])
            P.op('act', lambda e: e.sqrt(out=sm[:, 16:24], in_=sm[:, 16:24]), [('sm', 2)], [('sm', 2)])
            recip(P, sm[:, 16:24], sm[:, 16:24], [('sm', 2)], [('sm', 2)])
            Yn3 = EY['Yn'][:].rearrange("p (h c) -> p h c", h=8)
            tt(P, 'pool', Yn3, Y3, sm[:, 8:16].unsqueeze(2).to_broadcast([128, 8, 64]), ALU.subtract, yk + [('sm', 1)], ['Yn'])
            tt(P, 'dve', Yn3, Yn3, sm[:, 16:24].unsqueeze(2).to_broadcast([128, 8, 64]), ALU.mult, ['Yn', ('sm', 2)], ['Yn'])
            tt(P, 'pool', EY['Yn'][:], EY['Yn'][:], prm(5), ALU.mult, ['Yn', ('PRM', b)], ['Yn'])
            tt(P, 'pool', EY['Yn'][:], EY['Yn'][:], prm(6), ALU.add, ['Yn', ('PRM', b)], ['Yn'])
            tt(P, 'pool', EY['Ysq'][:].rearrange("p (h c) -> p h c", h=8), Vb[b][:].rearrange("p (h c) -> p h c", h=8),
               BON[:, cb * 8:(cb + 1) * 8].unsqueeze(2).to_broadcast([128, 8, 64]), ALU.mult, [('Vb', b), ('BON', cb)], ['Ysq'])
            tt(P, 'pool', EY['Yn'][:], EY['Yn'][:], EY['Ysq'][:], ALU.add, ['Yn', 'Ysq'], ['Yn'])
            actf(P, EY['Sz'][:], Zb[b][:], AF.Silu, [('Zb', b)], ['Sz'])
            tt(P, 'dve', YG[:], EY['Yn'][:], EY['Sz'][:], ALU.mult, ['Yn', 'Sz'], ['YG'])
            for q in range(4):
                tr(P, pC1[:].bitcast(BF16)[:, q * 128:(q + 1) * 128], YG[:, q * 128:(q + 1) * 128], identb[:], ['YG', 'identb'], ['pC1'])
            cp(P, 'act', ygT[ch % 2][:, cb * 4:(cb + 1) * 4, :].rearrange("p a t -> p (a t)"), pC1[:].bitcast(BF16)[:, 0:512], ['pC1'], [('ygT', ch % 2)])
            if cb == CB_LIST[-1]:
                yt = ygT[ch % 2]
                if ch < 17:
                    P.dma('sp', d['ygT'][:, :, ch * 128:(ch + 1) * 128], yt[:], reads=[('ygT', ch % 2)], key='ygTst')
                elif ch == 17:
                    P.dma('sp', d['ygT'][:, :, SROW0:SROW0 + 128], yt[:], reads=[('ygT', ch % 2)], key='ygTst')
                else:
                    s_ = ch - 17
                    P.dma('sp', d['ygT'][:, :, SROW0 + s_:SROW0 + s_ + 1], yt[:, :, 0:1], reads=[('ygT', ch % 2)], key='ygTst',
                          allow_slow_non_contiguous=True)

        def capture(fn, *a):
            P.cap = []
            fn(*a)
            l = P.cap
            P.cap = None
            return l

        def replay(l):
            for (eng, fn, reads, writes, key) in l:
                P.op(eng, fn, reads, writes, key)

        def merge(main, first, second):
            out = []
            n = len(main)
            h = n // 2 if (first and second) else (n if first else 0)
            da = db = 0
            for i, o in enumerate(main):
                out.append(o)
                if i < h:
                    t_ = (i + 1) * len(first) // max(h, 1)
                    while da < t_:
                        out.append(first[da])
                        da += 1
                else:
                    if da < len(first):
                        out.extend(first[da:])
                        da = len(first)
                    t_ = (i + 1 - h) * len(second) // max(n - h, 1)
                    while db < t_:
                        out.append(second[db])
                        db += 1
            out.extend(first[da:])
            out.extend(second[db:])
            return out

        units = [(ch, cb) for ch in CH_LIST for cb in CB_LIST]
        replay(capture(phase_C, units[0][0], units[0][1], 0))
        for idx, (ch, cb) in enumerate(units):
            if cb == CB_LIST[0] and ch >= 17:
                state_load(ch - 17)
            Dl = capture(phase_D, ch, cb, idx % NB, idx % 2)
            Cn = capture(phase_C, units[idx + 1][0], units[idx + 1][1], (idx + 1) % NB) if idx + 1 < len(units) else []
            Ep = capture(phase_E, units[idx - 1][0], units[idx - 1][1], (idx - 1) % NB, (idx - 1) % 2) if idx >= 1 else []
            replay(merge(Dl, Ep, Cn) if PIPELINE else Dl + Ep + Cn)
            if cb == CB_LIST[-1]:
                if ch == 16:
                    state_save(d['o_pwkv'])
                if ch >= 17:
                    state_save(d['o_swkv'][ch - 17])
        replay(capture(phase_E, units[-1][0], units[-1][1], (len(units) - 1) % NB, (len(units) - 1) % 2))
        P.flush()


def stage_outproj(P, nc, d, actn, wsrc, outn, tag):
    for half in range(2):
        with ExitStack() as es:
            tiles = list(range(half * 9, half * 9 + 9))
            actT = mk(nc, es, "actT", [128, 32, 9 * 128], BF16)
            ob = [mk(nc, es, f"obO{i}", [128, 512], F32) for i in range(4)]
            P.dma('sp', actT[:], d[actn][:, :, half * 1152:(half + 1) * 1152], writes=['actT'], key='actT')
            wt = [mk(nc, es, f"wtO{i}", [128, 32, 512], BF16) for i in range(2)]
            ps = [mk(nc, es, f"psO{i}", [128, 512], F32, psum=True) for i in range(4)]
            wv = wsrc.rearrange("(c p) n -> p c n", p=128)
            cnt = 0
            for cb in range(4):
                wb = cb % 2
                P.dma('pool', wt[wb][:], wv[:, :, cb * 512:(cb + 1) * 512], writes=[('wt', wb)], key=('wt', wb))
                for tl, t in enumerate(tiles):
                    pb = cnt % 4
                    cnt += 1
                    for c in range(32):
                        mm(P, ps[pb][:], actT[:, c, tl * 128:(tl + 1) * 128], wt[wb][:, c, :], c == 0, c == 31,
                           ['actT', ('wt', wb)], [('ps', pb)])
                    cp(P, 'act' if cnt % 2 else 'dve', ob[pb][:], ps[pb][:], [('ps', pb)], [('ob', pb)])
                    P.dma('sp', d[outn][t * 128:(t + 1) * 128, cb * 512:(cb + 1) * 512], ob[pb][:], reads=[('ob', pb)], key=('obst', pb))
            P.flush()


def rms_stats(P, x, junk, st, xk, sk):
    P.op('pool', lambda e: e.memset(st[:, 0:1], 0.0), writes=[sk])
    P.op('act', lambda e: e.activation(out=junk[:], in_=x[:], func=AF.Square, accum_out=st[:, 0:1]), reads=[xk], writes=['junk', sk])
    ts(P, 'dve', st[:, 1:2], st[:, 0:1], 1.0 / D, RMS_EPS, ALU.mult, ALU.add, [sk], [sk])
    P.op('act', lambda e: e.sqrt(out=st[:, 1:2], in_=st[:, 1:2]), [sk], [sk])
    recip(P, st[:, 1:2], st[:, 1:2], [sk], [sk])


def stage_F2(P, nc, d):
    with ExitStack() as es:
        gkv = mk(nc, es, "gkv", [128, D], F32)
        gb = mk(nc, es, "gb", [128, D], F32)
        ident = mk(nc, es, "identF2", [128, 128], F32)
        junk = mk(nc, es, "junkF", [128, D], F32)
        xt = [mk(nc, es, f"xF{i}", [128, D], F32) for i in range(2)]
        ot = [mk(nc, es, f"oF{i}", [128, D], F32) for i in range(2)]
        hn = [mk(nc, es, f"hnF{i}", [128, D], F32) for i in range(2)]
        st = [mk(nc, es, f"stF{i}", [128, 2], F32) for i in range(2)]
        hT = [mk(nc, es, f"hTF{i}", [128, 16, 128], BF16) for i in range(2)]
        ps = [mk(nc, es, f"psF{i}", [128, 512], F32, psum=True) for i in range(4)]
        P.dma('sp', gkv[:], d['kv_norm'][0].partition_broadcast(128), writes=['gkv'], key='gkv')
        P.dma('sp', gb[:], d['b_norm'][0].partition_broadcast(128), writes=['gb'], key='gb')
        P.dma('sp', ident[:], d['ident'], writes=['ident'], key='ident')
        ev = 0
        for t in range(NT):
            b = t % 2
            P.dma('sp', xt[b][:], d['xin'][1 + 128 * t:1 + 128 * t + 128, :], writes=[('x', b)], key=('x', b))
            P.dma('sp', ot[b][:], d['o1'][128 * t:128 * t + 128, :], writes=[('o', b)], key=('o', b))
            tt(P, 'dve', xt[b][:], xt[b][:], ot[b][:], ALU.add, [('x', b), ('o', b)], [('x', b)])
            P.dma('sp', d['hp'][128 * t:128 * t + 128, :], xt[b][:], reads=[('x', b)], key=('hpst', b))
            rms_stats(P, xt[b], junk, st[b], ('x', b), ('st', b))
            for vi, (g, gk, dst) in enumerate(((gkv, 'gkv', 'hkvT'), (gb, 'gb', 'hbT'))):
                hb = (2 * t + vi) % 2
                stt(P, 'dve', hn[hb][:], xt[b][:], st[b][:, 1:2], g[:], ALU.mult, ALU.mult,
                    [('x', b), ('st', b), gk], [('hn', hb)])
                for q in range(4):
                    pb = ev % 4
                    for j in range(4):
                        c = q * 4 + j
                        tr(P, ps[pb][:, j * 128:(j + 1) * 128], hn[hb][:, c * 128:(c + 1) * 128], ident[:], [('hn', hb), 'ident'], [('ps', pb)])
                    cp(P, 'act' if ev % 2 == 0 else 'dve', hT[hb][:, q * 4:(q + 1) * 4, :], ps[pb][:].rearrange("p (a n) -> p a n", a=4),
                       [('ps', pb)], [('hT', hb)])
                    ev += 1
                P.dma('sp', d[dst][:, :, t * 128:(t + 1) * 128], hT[hb][:], reads=[('hT', hb)], key=('hTst', hb))
        P.flush()


def rotary(P, eng, Kt, cs, tmp, kk, csk, tk):
    nh = Kt.shape[1]
    cosb = cs[:, 0:8].unsqueeze(1).to_broadcast([128, nh, 8])
    sinb = cs[:, 8:16].unsqueeze(1).to_broadcast([128, nh, 8])
    x1, x2 = Kt[:, :, 0:8], Kt[:, :, 8:16]
    t = [tmp[:, i, 0:nh, :] for i in range(4)]
    tt(P, eng, t[0], x1, cosb, ALU.mult, [kk, csk], [tk])
    tt(P, eng, t[1], x2, sinb, ALU.mult, [kk, csk], [tk])
    tt(P, eng, t[2], x2, cosb, ALU.mult, [kk, csk], [tk])
    tt(P, eng, t[3], x1, sinb, ALU.mult, [kk, csk], [tk])
    tt(P, eng, x1, t[0], t[1], ALU.subtract, [tk], [kk])
    tt(P, eng, x2, t[2], t[3], ALU.add, [tk], [kk])


def stage_G(P, nc, d):
    with ExitStack() as es:
        actT = mk(nc, es, "actT", [128, 16, T], BF16)
        CS = mk(nc, es, "CS", [128, NT, 16], F32)
        ob = [mk(nc, es, f"obG{i}", [128, 512], F32) for i in range(4)]
        obb = [mk(nc, es, f"obbG{i}", [128, 512], BF16) for i in range(4)]
        tmp = [mk(nc, es, f"tmpG{i}", [128, 4, 8, 8], F32) for i in range(4)]
        P.dma('sp', actT[:], d['hkvT'], writes=['actT'], key='actT')
        P.dma('sp', CS[:], d['cs'].rearrange("(t p) c -> p t c", p=128), writes=['CS'], key='CS')
        for s in range(NS):
            P.dma('sp', d['o_sck'][s, 0:127, :], d['ck'][s, 1:128, :], key=('cpk', s))
            P.dma('sp', d['o_scv'][s, 0:127, :], d['cv'][s, 1:128, :], key=('cpv', s))

        def evac(t, cb, pst, pkey, cnt):
            o = cnt % 4
            cp(P, 'act', ob[o][:], pst[:], [pkey], [('ob', o)])
            if cb == 0:
                rotary(P, 'dve', ob[o][:].rearrange("p (h c) -> p h c", h=8), CS[:, t, :], tmp[o], ('ob', o), 'CS', ('tmp', o))
            cp(P, 'pool', obb[o][:], ob[o][:], [('ob', o)], [('obb', o)])
            P.dma('sp', d['KVs'][t * 128:(t + 1) * 128, cb * 512:(cb + 1) * 512], obb[o][:], reads=[('obb', o)], key=('obbst', o))
            if t == 16:
                P.dma('sp', d['o_pck' if cb == 0 else 'o_pcv'], ob[o][:], reads=[('ob', o)], key=('obst', o))
            if t == 17:
                P.dma('sp', d['o_sck' if cb == 0 else 'o_scv'][:, 127, :], ob[o][0:NS, :], reads=[('ob', o)], key=('obst', o))
        gemm_tokmajor(P, nc, es, None, 16, d['w_kv'], 1024, evac, "G", actT=actT)
        P.flush()


def stage_H(P, nc, d):
    with ExitStack() as es:
        actT = mk(nc, es, "actT", [128, 16, T], BF16)
        CS = mk(nc, es, "CS", [128, NT, 16], F32)
        ob = [mk(nc, es, f"obH{i}", [128, 512], F32) for i in range(4)]
        obb = [mk(nc, es, f"obbH{i}", [128, 512], BF16) for i in range(4)]
        tmp = [mk(nc, es, f"tmpH{i}", [128, 4, 8, 8], F32) for i in range(4)]
        P.dma('sp', actT[:], d['hbT'], writes=['actT'], key='actT')
        P.dma('sp', CS[:], d['cs'].rearrange("(t p) c -> p t c", p=128), writes=['CS'], key='CS')

        def evac(t, cb, pst, pkey, cnt):
            o = cnt % 4
            if cb < 8:
                cp(P, 'act', ob[o][:], pst[:], [pkey], [('ob', o)])
                rotary(P, 'dve' if cnt % 2 else 'pool', ob[o][:].rearrange("p (h c) -> p h c", h=8), CS[:, t, :], tmp[o], ('ob', o), 'CS', ('tmp', o))
                cp(P, 'pool' if cnt % 2 else 'dve', obb[o][:], ob[o][:], [('ob', o)], [('obb', o)])
                P.dma('sp', d['qs'][t * 128:(t + 1) * 128, cb * 512:(cb + 1) * 512], obb[o][:], reads=[('obb', o)], key=('obbst', o))
            else:
                cp(P, 'act' if cnt % 2 else 'dve', obb[o][:], pst[:], [pkey], [('obb', o)])
                P.dma('sp', d['zs'][t * 128:(t + 1) * 128, (cb - 8) * 512:(cb - 7) * 512], obb[o][:], reads=[('obb', o)], key=('obbst', o))
        gemm_tokmajor(P, nc, es, None, 16, d['b_w_qz'][0], 8192, evac, "H", actT=actT)
        P.flush()


def stage_I(P, nc, d):
    with ExitStack() as es:
        identb = mk(nc, es, "identBI", [128, 128], BF16)
        MB = mk(nc, es, "MB", [128, 4, 128], BF16)
        esink = mk(nc, es, "esink", [128, 64], F32)
        P.dma('pool', identb[:], d['ident'], writes=['identb'], key='identb')
        P.dma('pool', MB[:], d['mb'], writes=['MB'], key='MB')
        P.dma('sp', esink[:], d['b_sinks'][0].partition_broadcast(128), writes=['esink'], key='esink')
        actf(P, esink[:], esink[:], AF.Exp, ['esink'], ['esink'])
        NSL = 3
        KVt = [mk(nc, es, f"KVt{i}", [128, 1024], BF16) for i in range(NSL)]
        Kd = mk(nc, es, "Kd", [128, 8, 2, 64], BF16)
        KT2 = [mk(nc, es, f"KT2{i}", [128, 8, 128], BF16) for i in range(NSL)]
        VA = [mk(nc, es, f"VA{i}", [128, 8, 65], BF16) for i in range(NSL)]
        Qt = [mk(nc, es, f"Qt{i}", [128, E], BF16) for i in range(2)]
        Zt = [mk(nc, es, f"Zt{i}", [128, E], BF16) for i in range(2)]
        QT = mk(nc, es, "QTt", [128, 32, 128], BF16)
        NSC = 3
        PTs = [mk(nc, es, f"PTs{i}", [128, 4, 128], BF16) for i in range(NSC)]
        ATT = mk(nc, es, "ATT", [128, E], F32)
        SZ = mk(nc, es, "SZ", [128, E], F32)
        AG = mk(nc, es, "AG", [128, E], BF16)
        agT = mk(nc, es, "agT", [128, 32, 128], BF16)
        den = mk(nc, es, "den", [128, 8], F32)
        pSC = [mk(nc, es, f"pSC{i}", [128, 512], F32, psum=True) for i in range(NSC)]
        pAO = [mk(nc, es, f"pAO{i}", [128, 512], F32, psum=True) for i in range(2)]
        pTP = [mk(nc, es, f"pTPI{i}", [128, 1024], BF16, psum=True) for i in range(2)]
        for i in range(NSL):
            P.op('pool', lambda e, i=i: e.memset(VA[i][:, :, 64:65], 1.0), writes=[('VA', i)])

        def prep_kv(slot):
            k3 = KVt[slot][:, 0:512].rearrange("p (g c) -> p g c", g=8)
            cp(P, 'pool', Kd[:, :, 0, :], k3, [('KVt', slot)], ['Kd'])
            cp(P, 'pool', Kd[:, :, 1, :], k3, [('KVt', slot)], ['Kd'])
            for g in range(8):
                tr(P, pTP[0][:, g * 128:(g + 1) * 128], Kd[:, g].rearrange("p a c -> p (a c)"), identb[:], ['Kd', 'identb'], [('pTP', 0)])
            cp(P, 'act', KT2[slot][:].rearrange("p g t -> p (g t)"), pTP[0][:], [('pTP', 0)], [('KT2', slot)])
            cp(P, 'dve', VA[slot][:, :, 0:64], KVt[slot][:, 512:1024].rearrange("p (g c) -> p g c", g=8), [('KVt', slot)], [('VA', slot)])

        slot_of_tile = {}
        nslot = 0
        for ch in range(NCH):
            qb = ch % 2
            load_rows(P, 'sp', Qt[qb], d['qs'], ch, 0, E, [('Qt', qb)], ('Qt', qb))
            load_rows(P, 'sp', Zt[qb], d['zs'], ch, 0, E, [('Zt', qb)], ('Zt', qb))
            if ch < 17:
                cs_ = nslot % NSL
                nslot += 1
                P.dma('sp', KVt[cs_][:], d['KVs'][ch * 128:(ch + 1) * 128, :], writes=[('KVt', cs_)], key=('KVt', cs_))
                prep_kv(cs_)
                slot_of_tile[ch] = cs_
                ps_ = slot_of_tile.get(ch - 1)
                mcur, mprev = (2, None) if ch == 0 else ((0, 3) if ch == 1 else (0, 1))
            else:
                s = ch - 17
                ps_ = nslot % NSL
                nslot += 1
                P.dma('pool', KVt[ps_][:, 0:512], d['ck'][s], writes=[('KVt', ps_)], key=('KVt', ps_))
                P.dma('pool', KVt[ps_][:, 512:1024], d['cv'][s], writes=[('KVt', ps_)], key=('KVt', ps_))
                prep_kv(ps_)
                cs_ = nslot % NSL
                nslot += 1
                load_rows(P, 'sp', KVt[cs_], d['KVs'], ch, 0, 1024, [('KVt', cs_)], ('KVt', cs_))
                prep_kv(cs_)
                mcur, mprev = 0, 1
            for q8 in range(4):
                tp = pTP[q8 % 2]
                for j in range(8):
                    pr = q8 * 8 + j
                    tr(P, tp[:, j * 128:(j + 1) * 128], Qt[qb][:, pr * 128:(pr + 1) * 128], identb[:], [('Qt', qb), 'identb'], [('pTP', q8 % 2)])
                cp(P, 'act' if q8 % 2 else 'dve', QT[:, q8 * 8:(q8 + 1) * 8, :].rearrange("p a t -> p (a t)"), tp[:], [('pTP', q8 % 2)], [('QT', q8)])
            kts = ([(ps_, mprev)] if ps_ is not None and mprev is not None else []) + [(cs_, mcur)]
            nk = len(kts)

            def scores(j):
                g = j // 4
                sc = pSC[j % NSC]
                for h2 in range(2):
                    sl = slice(h2 * 64, (h2 + 1) * 64)
                    for ki, (slot, mk_) in enumerate(kts):
                        o = sc[:, (h2 * 2 + ki) * 128:(h2 * 2 + ki + 1) * 128]
                        mm(P, o, KT2[slot][sl, g, :], QT[sl, j, :], True, False, [('KT2', slot), ('QT', j // 8)], [('pSC', j % NSC)])
                        mm(P, o, identb[:], MB[:, mk_, :], False, True, ['identb', 'MB'], [('pSC', j % NSC)])

            def expv(j):
                g = j // 4
                sc = pSC[j % NSC]
                pt = PTs[j % NSC]
                if nk == 2:
                    actf(P, pt[:].rearrange("p a t -> p (a t)"), sc[:], AF.Exp, [('pSC', j % NSC)], [('PTs', j % NSC)], scale=0.125)
                else:
                    for h2 in range(2):
                        actf(P, pt[:, h2 * 2, :], sc[:, h2 * 256:h2 * 256 + 128], AF.Exp, [('pSC', j % NSC)], [('PTs', j % NSC)], scale=0.125)
                ao = pAO[(j // 2) % 2]
                for h2 in range(2):
                    col = ((j % 2) * 2 + h2) * 65
                    for ki, (slot, mk_) in enumerate(kts):
                        mm(P, ao[:, col:col + 65], pt[:, h2 * 2 + ki, :], VA[slot][:, g, :], ki == 0, ki == nk - 1,
                           [('PTs', j % NSC), ('VA', slot)], [('pAO', (j // 2) % 2)])
                if j % 2 == 1:
                    h0 = (j - 1) * 2
                    ao3 = ao[:, 0:260].rearrange("p (h c) -> p h c", h=4)
                    ak = ('pAO', (j // 2) % 2)
                    tt(P, 'dve', den[:, 0:4], ao3[:, :, 64], esink[:, h0:h0 + 4], ALU.add, [ak, 'esink'], ['den'])
                    recip(P, den[:, 4:8], den[:, 0:4], ['den'], ['den'])
                    tt(P, 'dve', ATT[:, h0 * 64:(h0 + 4) * 64].rearrange("p (h c) -> p h c", h=4), ao3[:, :, 0:64],
                       den[:, 4:8].unsqueeze(2).to_broadcast([128, 4, 64]), ALU.mult, [ak, 'den'], ['ATT'])

            for j in range(min(NSC - 1, 32)):
                scores(j)
            for j in range(32):
                if j + NSC - 1 < 32:
                    scores(j + NSC - 1)
                expv(j)
            actf(P, SZ[:], Zt[qb][:], AF.Silu, [('Zt', qb)], ['SZ'])
            tt(P, 'dve', AG[:], ATT[:], SZ[:], ALU.mult, ['ATT', 'SZ'], ['AG'])
            for q8 in range(4):
                tp = pTP[q8 % 2]
                for j in range(8):
                    pr = q8 * 8 + j
                    tr(P, tp[:, j * 128:(j + 1) * 128], AG[:, pr * 128:(pr + 1) * 128], identb[:], ['AG', 'identb'], [('pTP', q8 % 2)])
                cp(P, 'act' if q8 % 2 else 'dve', agT[:, q8 * 8:(q8 + 1) * 8, :].rearrange("p a t -> p (a t)"), tp[:], [('pTP', q8 % 2)], ['agT'])
            if ch < 17:
                P.dma('sp', d['agT'][:, :, ch * 128:(ch + 1) * 128], agT[:], reads=['agT'], key='agTst')
            elif ch == 17:
                P.dma('sp', d['agT'][:, :, SROW0:SROW0 + 128], agT[:], reads=['agT'], key='agTst')
            else:
                P.dma('sp', d['agT'][:, :, SROW0 + ch - 17:SROW0 + ch - 16], agT[:, :, 0:1], reads=['agT'], key='agTst',
                      allow_slow_non_contiguous=True)
        P.flush()


def stage_J2(P, nc, d):
    with ExitStack() as es:
        gf = mk(nc, es, "gf", [128, D], F32)
        junk = mk(nc, es, "junkJ", [128, D], F32)
        xt = [mk(nc, es, f"xJ{i}", [128, D], F32) for i in range(2)]
        ot = [mk(nc, es, f"oJ{i}", [128, D], F32) for i in range(2)]
        st = [mk(nc, es, f"stJ{i}", [128, 2], F32) for i in range(2)]
        P.dma('sp', gf[:], d['final_norm'][0].partition_broadcast(128), writes=['gf'], key='gf')
        for t in range(1, NT):
            b = t % 2
            P.dma('sp', xt[b][:], d['hp'][128 * t:128 * t + 128, :], writes=[('x', b)], key=('x', b))
            P.dma('sp', ot[b][:], d['o2'][128 * t:128 * t + 128, :], writes=[('o', b)], key=('o', b))
            tt(P, 'dve', xt[b][:], xt[b][:], ot[b][:], ALU.add, [('x', b), ('o', b)], [('x', b)])
            rms_stats(P, xt[b], junk, st[b], ('x', b), ('st', b))
            stt(P, 'dve', ot[b][:], xt[b][:], st[b][:, 1:2], gf[:], ALU.mult, ALU.mult, [('x', b), ('st', b), 'gf'], [('o', b)])
            if t < 17:
                P.dma('sp', d['o_yp'][(t - 1) * 128:t * 128, :], ot[b][:], reads=[('o', b)], key=('yst', b))
            else:
                P.dma('sp', d['o_ys'], ot[b][0:NS, :], reads=[('o', b)], key=('yst', b))
        P.flush()


class LazyDram(dict):
    def __init__(self, nc, debug_outs, ext_in):
        super().__init__()
        self.nc, self.debug_outs, self.ext_in = nc, debug_outs, ext_in
        self.spec = {}
        self.inputs, self.outputs = [], []

    def __missing__(self, name):
        kind, shape, dt = self.spec[name]
        if kind == 'scr':
            kind = 'ExternalInput' if name in self.ext_in else ('ExternalOutput' if name in self.debug_outs else 'Internal')
        if kind == 'ExternalInput':
            self.inputs.append(name)
        if kind == 'ExternalOutput':
            self.outputs.append(name)
        ap = self.nc.dram_tensor(name, list(shape), dt, kind=kind).ap()
        self[name] = ap
        return ap


def build(debug_outs=(), stages='ABLCFfGHIJj', ext_in=()):
    nc = bass.Bass("TRN2", target_bir_lowering=False)
    d = LazyDram(nc, debug_outs, ext_in)

    def inp(name, shape, dt=F32):
        d.spec[name] = ('ExternalInput', shape, dt)

    def outp(name, shape, dt=F32):
        d.spec[name] = ('ExternalOutput', shape, dt)

    def scr(name, shape, dt):
        d.spec[name] = ('scr', shape, dt)

    inp('xin', [T + 1, D]); inp('sshift', [128, D]); inp('swkv', [NS, 64, 64, 64])
    inp('ck', [NS, 128, 512]); inp('cv', [NS, 128, 512])
    inp('a_norm', [1, D]); inp('muT', [128, 6, 16]); inp('ident', [128, 128]); inp('tri', [128, 128]); inp('ones', [128, 128])
    inp('onehot', [128, 1]); inp('lmask', [128, 2]); inp('mask4', [128, 512]); inp('negsl', [128, 128]); inp('mb', [128, 4, 128])
    inp('cs', [T, 16]); inp('prm', [7, E])
    inp('a_w_rkvz', [1, 4, D, E]); inp('a_w1', [1, D, 96]); inp('a_w2', [1, 96, E]); inp('a_a1', [1, D, 96]); inp('a_a2', [1, 96, E])
    inp('a_w_out', [1, E, D]); inp('kv_norm', [1, D]); inp('w_kv', [D, 1024]); inp('b_norm', [1, D]); inp('b_w_qz', [1, D, 2 * E])
    inp('b_sinks', [1, 64]); inp('b_w_o', [1, E, D]); inp('final_norm', [1, D])
    outp('o_yp', [2048, D]); outp('o_ys', [NS, D]); outp('o_pwkv', [64, 64, 64]); outp('o_pshift', [1, D])
    outp('o_pck', [128, 512]); outp('o_pcv', [128, 512]); outp('o_swkv', [NS, 64, 64, 64]); outp('o_sshift', [NS, D])
    outp('o_sck', [NS, 128, 512]); outp('o_scv', [NS, 128, 512])
    scr('xmT', [6, 128, 16, T], BF16); scr('rkvz', [4, T, E], BF16); scr('wpre', [T, E], F32); scr('apre', [T, E], F32)
    scr('ygT', [128, 32, T], BF16); scr('o1', [T, D], F32); scr('hp', [T, D], F32); scr('hkvT', [128, 16, T], BF16); scr('hbT', [128, 16, T], BF16)
    scr('KVs', [T, 1024], BF16); scr('qs', [T, E], BF16); scr('zs', [T, E], BF16); scr('agT', [128, 32, T], BF16); scr('o2', [T, D], F32)
    with ExitStack() as stack:
        P = Prog(nc, stack)
        if 'A' in stages:
            stage_A(P, nc, d)
        if 'B' in stages:
            stage_B(P, nc, d)
        if 'L' in stages:
            stage_B_lora(P, nc, d)
        if 'C' in stages:
            stage_CDE(P, nc, d)
        if 'F' in stages:
            stage_outproj(P, nc, d, 'ygT', d['a_w_out'][0], 'o1', 'F1')
        if 'f' in stages:
            stage_F2(P, nc, d)
        if 'G' in stages:
            stage_G(P, nc, d)
        if 'H' in stages:
            stage_H(P, nc, d)
        if 'I' in stages:
            stage_I(P, nc, d)
        if 'J' in stages:
            stage_outproj(P, nc, d, 'agT', d['b_w_o'][0], 'o2', 'J1')
        if 'j' in stages:
            stage_J2(P, nc, d)
    nc._lazy = d
    return nc


def host_tables():
    f = np.float32
    j = np.arange(128)
    su = (j[:, None] < j[None, :]).astype(f)
    u = (j[:, None] <= j[None, :]).astype(f)
    tb = {}
    tb['ident'] = np.eye(128, dtype=f)
    tb['tri'] = u.copy()
    tb['ones'] = np.ones((128, 128), f)
    oh = np.zeros((128, 1), f); oh[0, 0] = 1
    tb['onehot'] = oh
    c = f(-np.exp(-0.5))
    lm = np.zeros((128, 2), f); lm[:, 0] = c; lm[0, 1] = c
    tb['lmask'] = lm
    tb['mask4'] = np.concatenate([su, u, -su, u], 1)
    tb['negsl'] = -(su.T).copy()
    NEG = f(-30000.0)
    jj = j[:, None]; ii = j[None, :]
    cur = np.where(jj <= ii, 0, NEG).astype(f)
    prev = np.where(jj >= ii, 0, NEG).astype(f)
    lead = np.where(jj >= 112, 0, NEG).astype(f)
    mb = np.stack([cur, prev, np.minimum(cur, lead), np.minimum(prev, lead)], 1)
    tb['mb'] = np.ascontiguousarray(mb)
    pos = np.zeros(T, f)
    pos[112:2176] = np.arange(2064)
    pos[2176:2176 + NS] = 16384
    inv = (f(500000.0) ** (-np.arange(8, dtype=f) * f(2.0) / f(16))).astype(f)
    ang = (pos[:, None] * inv[None, :]).astype(f)
    tb['cs'] = np.concatenate([np.cos(ang), np.sin(ang)], 1).astype(f)
    return tb


_NC = [None]


def kernel(**inp):
    f = np.float32
    inp = {k: np.asarray(v) for k, v in inp.items()}
    if _NC[0] is None:
        _NC[0] = build()
    nc = _NC[0]
    tb = host_tables()
    mu = inp['a_mu'][0]
    muT = np.ascontiguousarray(mu.reshape(6, 16, 128).transpose(2, 0, 1))
    prm = np.ascontiguousarray(np.stack([inp['a_w0'][0], inp['a_a0'][0], inp['a_k_k'][0], inp['a_k_a'][0], inp['a_r_k'][0].reshape(-1),
                                         inp['a_gn_g'][0], inp['a_gn_b'][0]], 0).astype(f))
    shared = dict(tb)
    shared.update(muT=muT, prm=prm, a_norm=inp['a_norm'], a_w_rkvz=inp['a_w_rkvz'], a_w1=inp['a_w1'], a_w2=inp['a_w2'], a_a1=inp['a_a1'],
                  a_a2=inp['a_a2'], a_w_out=inp['a_w_out'], kv_norm=inp['kv_norm'].reshape(1, D), w_kv=inp['w_kv'], b_norm=inp['b_norm'],
                  b_w_qz=inp['b_w_qz'], b_sinks=inp['b_sinks'], b_w_o=inp['b_w_o'], final_norm=inp['final_norm'].reshape(1, D))
    in_maps = []
    for core in range(8):
        b = core % 4
        ss = slice(core * NS, core * NS + NS)
        xin = np.zeros((T + 1, D), f)
        xin[1 + 112:1 + 128] = inp['meta_tokens']
        xin[1 + 128:1 + 128 + 2048] = inp['x_prompt'][b]
        xin[1 + SROW0:1 + SROW0 + NS] = inp['x_sample'][ss, 0]
        sshift = np.zeros((128, D), f)
        sshift[:NS] = inp['state_shift'][0, ss]
        m = dict(shared)
        m.update(xin=xin, sshift=sshift, swkv=np.ascontiguousarray(inp['state_wkv'][0, ss]),
                 ck=np.ascontiguousarray(inp['cache_k'][ss].reshape(NS, 128, 512)), cv=np.ascontiguousarray(inp['cache_v'][ss].reshape(NS, 128, 512)))
        in_maps.append(m)
    in_maps = [{k: m[k] for k in nc._lazy.inputs if k in m} for m in in_maps]
    res = run_bass_kernel_spmd(nc, in_maps, core_ids=list(range(8)))
    R = res.results
    g = lambda c, n: np.asarray(R[c][n], dtype=f)
    y_prompt = np.stack([g(b, 'o_yp') for b in range(4)], 0)
    y_sample = np.concatenate([g(c, 'o_ys') for c in range(8)], 0)[:, None, :]
    p_wkv = np.stack([g(b, 'o_pwkv') for b in range(4)], 0)[None]
    p_shift = np.concatenate([g(b, 'o_pshift') for b in range(4)], 0)[None]
    p_ck = np.stack([g(b, 'o_pck') for b in range(4)], 0).reshape(4, 128, 8, 64)
    p_cv = np.stack([g(b, 'o_pcv') for b in range(4)], 0).reshape(4, 128, 8, 64)
    s_wkv = np.concatenate([g(c, 'o_swkv') for c in range(8)], 0)[None]
    s_shift = np.concatenate([g(c, 'o_sshift') for c in range(8)], 0)[None]
    s_ck = np.concatenate([g(c, 'o_sck') for c in range(8)], 0).reshape(32, 128, 8, 64)
    s_cv = np.concatenate([g(c, 'o_scv') for c in range(8)], 0).reshape(32, 128, 8, 64)
    return (y_prompt, y_sample, p_wkv, p_shift, p_ck, p_cv, s_wkv, s_shift, s_ck, s_cv)
```

```python
import numpy as np
from contextlib import ExitStack
import concourse.bass as bass
import concourse.mybir as mybir
from concourse.bass_utils import run_bass_kernel_spmd

F32 = mybir.dt.float32
BF16 = mybir.dt.bfloat16
AF = mybir.ActivationFunctionType
ALU = mybir.AluOpType
AX = mybir.AxisListType

COMPUTE = ('pe', 'act', 'dve', 'pool')
ALLENG = ('pe', 'act', 'dve', 'pool', 'sp')
SAME_ENGINE_SYNC = True
PIPELINE = True
PSUM_KEYS = {'pC0', 'pC1', 'pTP', 'pPQ', 'pPA', 'pRX', 'pYS', 'ps', 'psg', 'psL', 'pSC', 'pAO'}


class Prog:
    def __init__(self, nc, stack):
        self.nc = nc
        self.stack = stack
        self.esem = {e: stack.enter_context(nc.semaphore("s_" + e)) for e in COMPUTE}
        self.ecnt = {e: 0 for e in COMPUTE}
        self.dsem = {}
        self.dcnt = {}
        self.dsid = {}
        self.free_dsems = []
        self.nds = 0
        self.waited = {e: {} for e in ALLENG}
        self.reset()

    def reset(self):
        self.ops = []
        self.lastw = {}
        self.readers = {}
        self.chain = {}

    max_ops = None
    cap = None

    def op(self, eng, fn, reads=(), writes=(), key=None):
        if self.cap is not None:
            self.cap.append((eng, fn, list(reads), list(writes), key))
            return -1
        i = len(self.ops)
        if self.max_ops is not None and i >= self.max_ops:
            return -1
        pr = [r for r in reads if (r[0] if isinstance(r, tuple) else r) in PSUM_KEYS]
        if pr:
            reads = [r for r in reads if r not in pr]
            writes = list(writes) + [r for r in pr if r not in writes]
        deps = set()
        for r in reads:
            w = self.lastw.get(r)
            if w is not None:
                deps.add(w)
        for w_ in writes:
            w = self.lastw.get(w_)
            if w is not None:
                deps.add(w)
            deps.update(self.readers.get(w_, ()))
        if key is not None:
            prev = self.chain.get(key)
            if prev is not None:
                deps.add(prev)
            self.chain[key] = i
        self.ops.append(dict(eng=eng, fn=fn, deps=deps, key=key))
        for w_ in writes:
            self.lastw[w_] = i
            self.readers[w_] = []
        ws = set(writes)
        for r in reads:
            if r not in ws:
                self.readers.setdefault(r, []).append(i)
        return i

    def dma(self, q, out, in_, reads=(), writes=(), key=None, **kw):
        assert key is not None
        return self.op(q, lambda e: e.dma_start(out=out, in_=in_, **kw), reads, writes, key=key)

    def flush(self):
        nc = self.nc
        ops = self.ops
        if not ops:
            return
        needed = set()
        for o in ops:
            needed.update(o['deps'])
        lastop = {}
        for i, o in enumerate(ops):
            if o['key'] is None:
                lastop[o['eng']] = i
        needed.update(lastop.values())
        tgt = [None] * len(ops)
        for i, o in enumerate(ops):
            if o['key'] is not None:
                k = o['key']
                if k not in self.dsem:
                    if self.free_dsems:
                        self.dsem[k], self.dcnt[k], self.dsid[k] = self.free_dsems.pop()
                    else:
                        self.nds += 1
                        self.dsem[k] = self.stack.enter_context(nc.semaphore("d_" + str(self.nds)))
                        self.dcnt[k] = 0
                        self.dsid[k] = ('d', self.nds)
                self.dcnt[k] += 16
                tgt[i] = (self.dsem[k], self.dcnt[k], self.dsid[k])
            elif i in needed:
                e = o['eng']
                self.ecnt[e] += 1
                tgt[i] = (self.esem[e], self.ecnt[e], ('e', e))
        per = {e: [] for e in ALLENG}
        for i, o in enumerate(ops):
            per[o['eng']].append(i)
        end_waits = []
        for e in COMPUTE:
            if e in lastop:
                end_waits.append(tgt[lastop[e]])
        for k in self.chain:
            end_waits.append((self.dsem[k], self.dcnt[k], self.dsid[k]))

        def run(ename, eobj):
            waited = self.waited[ename]
            for i in per[ename]:
                o = ops[i]
                need = {}
                for d in o['deps']:
                    od = ops[d]
                    if od['key'] is None and od['eng'] == ename:
                        if ename == 'pe' or not SAME_ENGINE_SYNC:
                            continue
                    sem, val, sid = tgt[d]
                    if need.get(sid, (None, 0))[1] < val:
                        need[sid] = (sem, val)
                for sid, (sem, val) in need.items():
                    if waited.get(sid, 0) < val:
                        eobj.wait_ge(sem, val)
                        waited[sid] = val
                ins = o['fn'](eobj)
                if tgt[i] is not None:
                    if o['key'] is not None:
                        ins.then_inc(tgt[i][0], 16)
                    else:
                        ins.then_inc(tgt[i][0], 1)
            for sem, val, sid in end_waits:
                if waited.get(sid, 0) < val:
                    eobj.wait_ge(sem, val)
                    waited[sid] = val

        with nc.Block() as block:
            @block.tensor
            def _(e):
                run('pe', e)

            @block.scalar
            def _(e):
                run('act', e)

            @block.vector
            def _(e):
                run('dve', e)

            @block.gpsimd
            def _(e):
                run('pool', e)

            @block.sync
            def _(e):
                run('sp', e)
        for k in list(self.dsem):
            self.free_dsems.append((self.dsem[k], self.dcnt[k], self.dsid[k]))
        self.dsem, self.dcnt, self.dsid = {}, {}, {}
        self.reset()


NT = 18
T = NT * 128
D = 2048
E = 4096
NS = 4
RMS_EPS = 1e-6


class Ctx:
    pass


_uid = [0]


def mk(nc, es, name, shape, dt, psum=False):
    _uid[0] += 1
    name = f"{name}_u{_uid[0]}"
    if psum:
        return es.enter_context(nc.psum_tensor(name, shape, dt))
    return es.enter_context(nc.sbuf_tensor(name, shape, dt))


def stage_A(P, nc, d):
    with ExitStack() as es:
        gA = mk(nc, es, "gA", [128, D], F32)
        muT = mk(nc, es, "muT", [128, 6, 16], F32)
        ident = mk(nc, es, "identA", [128, 128], F32)
        xc = [mk(nc, es, f"xc{i}", [128, D], F32) for i in range(2)]
        xp = [mk(nc, es, f"xp{i}", [128, D], F32) for i in range(2)]
        junk = mk(nc, es, "junkA", [128, D], F32)
        st = [mk(nc, es, f"stA{i}", [128, 4], F32) for i in range(2)]
        xnT = [mk(nc, es, f"xnT{i}", [128, 16, 128], F32) for i in range(2)]
        xxT = [mk(nc, es, f"xxT{i}", [128, 16, 128], F32) for i in range(2)]
        tmp = [mk(nc, es, f"tmpA{i}", [128, 16, 128], F32) for i in range(2)]
        xm = [mk(nc, es, f"xmA{i}", [128, 16, 128], BF16) for i in range(3)]
        ps = [mk(nc, es, f"psA{i}", [128, 512], F32, psum=True) for i in range(4)]

        P.dma('sp', gA[:], d['a_norm'][0].partition_broadcast(128), writes=['gA'], key='gA')
        P.dma('sp', muT[:], d['muT'], writes=['muT'], key='muT')
        P.dma('sp', ident[:], d['ident'], writes=['ident'], key='ident')
        ev = 0
        mi = 0
        def loads_A(t):
            b = t % 2
            P.dma('sp', xc[b][:], d['xin'][1 + 128 * t: 1 + 128 * t + 128, :], writes=[('xc', b)], key=('xc', b))
            if t < NT - 1:
                P.dma('sp', xp[b][:], d['xin'][128 * t: 128 * t + 128, :], writes=[('xp', b)], key=('xp', b))
            else:
                P.dma('sp', xp[b][:], d['sshift'], writes=[('xp', b)], key=('xp', b))

        loads_A(0)
        for t in range(NT):
            b = t % 2
            if t + 1 < NT:
                loads_A(t + 1)
            P.op('pool', lambda e, b=b: e.memset(st[b][:, 0:2], 0.0), writes=[('st', b, 0), ('st', b, 1)])
            P.op('act', lambda e, b=b: e.activation(out=junk[:], in_=xc[b][:], func=AF.Square, accum_out=st[b][:, 0:1]),
                 reads=[('xc', b)], writes=['junk', ('st', b, 0)])
            if t < NT - 1:
                P.op('act', lambda e, b=b: e.activation(out=junk[:], in_=xp[b][:], func=AF.Square, accum_out=st[b][:, 1:2]),
                     reads=[('xp', b)], writes=['junk', ('st', b, 1)])
            nst = 2 if t < NT - 1 else 1
            P.op('dve', lambda e, b=b, n=nst: e.tensor_scalar(out=st[b][:, 2:2 + n], in0=st[b][:, 0:n], scalar1=1.0 / D, scalar2=RMS_EPS,
                                                              op0=ALU.mult, op1=ALU.add),
                 reads=[('st', b, 0), ('st', b, 1)], writes=[('st', b, 2)])
            P.op('act', lambda e, b=b, n=nst: e.sqrt(out=st[b][:, 2:2 + n], in_=st[b][:, 2:2 + n]),
                 reads=[('st', b, 2)], writes=[('st', b, 2)])
            P.op('dve', lambda e, b=b, n=nst: e.reciprocal(out=st[b][:, 2:2 + n], in_=st[b][:, 2:2 + n]),
                 reads=[('st', b, 2)], writes=[('st', b, 2)])
            P.op('dve', lambda e, b=b: e.scalar_tensor_tensor(out=xc[b][:], in0=xc[b][:], scalar=st[b][:, 2:3], in1=gA[:],
                                                              op0=ALU.mult, op1=ALU.mult),
                 reads=[('xc', b), ('st', b, 2), 'gA'], writes=[('xc', b)])
            if t < NT - 1:
                P.op('dve', lambda e, b=b: e.scalar_tensor_tensor(out=xp[b][:], in0=xp[b][:], scalar=st[b][:, 3:4], in1=gA[:],
                                                                  op0=ALU.mult, op1=ALU.mult),
                     reads=[('xp', b), ('st', b, 2), 'gA'], writes=[('xp', b)])
            if t == NT - 2:
                P.dma('sp', d['o_pshift'], xc[b][127:128, :], reads=[('xc', b)], key='o_pshift')
            if t == NT - 1:
                P.dma('sp', d['o_sshift'], xc[b][0:NS, :], reads=[('xc', b)], key='o_sshift')
            P.op('pool', lambda e, b=b: e.tensor_tensor(out=xp[b][:], in0=xp[b][:], in1=xc[b][:], op=ALU.subtract),
                 reads=[('xp', b), ('xc', b)], writes=[('xp', b)])
            for (src, srck, dst, dstk) in ((xc, 'xc', xnT, 'xnT'), (xp, 'xp', xxT, 'xxT')):
                for q in range(4):
                    pb = ev % 4
                    for j in range(4):
                        c = q * 4 + j
                        P.op('pe', lambda e, pb=pb, j=j, c=c, src=src, b=b: e.transpose(out=ps[pb][:, j * 128:(j + 1) * 128],
                                                                                        in_=src[b][:, c * 128:(c + 1) * 128], identity=ident[:]),
                             reads=[(srck, b), 'ident'], writes=[('ps', pb)])
                    eng = 'act' if ev % 2 == 0 else 'dve'
                    if eng == 'act':
                        P.op('act', lambda e, pb=pb, q=q, dst=dst, b=b: e.copy(out=dst[b][:, q * 4:(q + 1) * 4, :], in_=ps[pb][:].rearrange("p (a n) -> p a n", a=4)),
                             reads=[('ps', pb)], writes=[(dstk, b)])
                    else:
                        P.op('dve', lambda e, pb=pb, q=q, dst=dst, b=b: e.tensor_copy(out=dst[b][:, q * 4:(q + 1) * 4, :], in_=ps[pb][:].rearrange("p (a n) -> p a n", a=4)),
                             reads=[('ps', pb)], writes=[(dstk, b)])
                    ev += 1
            for p in range(6):
                m = mi % 3
                mi += 1
                e1 = 'pool' if p % 3 == 2 else 'dve'
                P.op(e1, lambda e, b=b, p=p: e.tensor_tensor(out=tmp[p % 2][:], in0=xxT[b][:], in1=muT[:, p, :].unsqueeze(2).to_broadcast([128, 16, 128]), op=ALU.mult),
                     reads=[('xxT', b), 'muT'], writes=[('tmp', p % 2)])
                P.op(e1, lambda e, b=b, p=p, m=m: e.tensor_tensor(out=xm[m][:], in0=tmp[p % 2][:], in1=xnT[b][:], op=ALU.add),
                     reads=[('tmp', p % 2), ('xnT', b)], writes=[('xm', m)])
                P.dma('sp', d['xmT'][p][:, :, t * 128:(t + 1) * 128], xm[m][:], reads=[('xm', m)], writes=[('xmT', p)], key=('xmst', m))
        P.flush()


def gemm_tokmajor(P, nc, es, actT_src, kc, w_src, ncols, evac, wkey, tiles=range(NT), act_res='actT', actT=None):
    wt = [mk(nc, es, f"wt_{wkey}{i}", [128, kc, 512], BF16) for i in range(2)]
    ps = [mk(nc, es, f"psg_{wkey}{i}", [128, 512], F32, psum=True) for i in range(4)]
    wv = w_src.rearrange("(c p) n -> p c n", p=128)
    cnt = 0
    for cb in range(ncols // 512):
        wb = cb % 2
        P.dma('pool', wt[wb][:], wv[:, :, cb * 512:(cb + 1) * 512], writes=[('wt', wkey, wb)], key=('wt', wkey, wb))
        for t in tiles:
            pb = cnt % 4
            cnt += 1
            for c in range(kc):
                P.op('pe', lambda e, pb=pb, c=c, t=t, wb=wb: e.matmul(ps[pb][:], lhsT=actT[:, c, t * 128:(t + 1) * 128], rhs=wt[wb][:, c, :],
                                                                      start=(c == 0), stop=(c == kc - 1)),
                     reads=[act_res, ('wt', wkey, wb)], writes=[('psg', wkey, pb)])
            evac(t, cb, ps[pb], ('psg', wkey, pb), cnt)


def stage_B(P, nc, d, projs=(0, 1, 2, 3)):
    for p in projs:
        with ExitStack() as es:
            actT = mk(nc, es, "actT", [128, 16, T], BF16)
            ob = [mk(nc, es, f"obB{i}", [128, 512], BF16) for i in range(4)]
            P.dma('sp', actT[:], d['xmT'][p], writes=['actT'], key='actT')

            def evac(t, cb, pst, pkey, cnt, p=p):
                o = cnt % 4
                if cnt % 2 == 0:
                    P.op('act', lambda e: e.copy(out=ob[o][:], in_=pst[:]), reads=[pkey], writes=[('ob', o)])
                else:
                    P.op('dve', lambda e: e.tensor_copy(out=ob[o][:], in_=pst[:]), reads=[pkey], writes=[('ob', o)])
                P.dma('sp', d['rkvz'][p][t * 128:(t + 1) * 128, cb * 512:(cb + 1) * 512], ob[o][:], reads=[('ob', o)], key=('obst', o))
            gemm_tokmajor(P, nc, es, None, 16, d['a_w_rkvz'][0, p], E, evac, f"B{p}", actT=actT)
            P.flush()


def tt(P, eng, out, in0, in1, op, reads, writes):
    P.op(eng, lambda e: e.tensor_tensor(out=out, in0=in0, in1=in1, op=op), reads, writes)


def ts(P, eng, out, in0, s1, s2, op0, op1, reads, writes):
    if s2 is None:
        P.op(eng, lambda e: e.tensor_scalar(out=out, in0=in0, scalar1=s1, scalar2=None, op0=op0), reads, writes)
    else:
        P.op(eng, lambda e: e.tensor_scalar(out=out, in0=in0, scalar1=s1, scalar2=s2, op0=op0, op1=op1), reads, writes)


def stt(P, eng, out, in0, scalar, in1, op0, op1, reads, writes):
    P.op(eng, lambda e: e.scalar_tensor_tensor(out=out, in0=in0, scalar=scalar, in1=in1, op0=op0, op1=op1), reads, writes)


def actf(P, out, in_, func, reads, writes, scale=1.0):
    P.op('act', lambda e: e.activation(out=out, in_=in_, func=func, scale=scale), reads, writes)


def cp(P, eng, out, in_, reads, writes):
    if eng == 'act':
        P.op('act', lambda e: e.copy(out=out, in_=in_), reads, writes)
    else:
        P.op(eng, lambda e: e.tensor_copy(out=out, in_=in_), reads, writes)


def mm(P, out, lhsT, rhs, start, stop, reads, writes):
    P.op('pe', lambda e: e.matmul(out, lhsT=lhsT, rhs=rhs, start=start, stop=stop), reads, writes)


def tr(P, out, in_, ident, reads, writes):
    P.op('pe', lambda e: e.transpose(out=out, in_=in_, identity=ident), reads, writes)


def red(P, eng, out, in_, reads, writes):
    P.op(eng, lambda e: e.reduce_sum(out=out, in_=in_, axis=AX.X), reads, writes)


def recip(P, out, in_, reads, writes):
    P.op('dve', lambda e: e.reciprocal(out=out, in_=in_), reads, writes)


GN_EPS = 64e-5
SROW0 = 17 * 128
ZROW0 = SROW0 + NS
NCH = 17 + NS
CH_LIST = list(range(NCH))
CB_LIST = list(range(8))


def load_rows(P, q, dst, src, ch, c0, c1, writes, key):
    if ch < 17:
        P.dma(q, dst[:], src[ch * 128:(ch + 1) * 128, c0:c1], writes=writes, key=key)
    else:
        s = ch - 17
        P.dma(q, dst[0:1], src[SROW0 + s:SROW0 + s + 1, c0:c1], writes=writes, key=key)
        P.dma(q, dst[1:65], src[ZROW0:ZROW0 + 64, c0:c1], writes=writes, key=key)
        P.dma(q, dst[64:128], src[ZROW0:ZROW0 + 64, c0:c1], writes=writes, key=key)


def stage_B_lora(P, nc, d):
    for which, (xi, w1n, w2n, outn, func) in enumerate(((4, 'a_w1', 'a_w2', 'wpre', AF.Tanh), (5, 'a_a1', 'a_a2', 'apre', AF.Copy))):
        with ExitStack() as es:
            actT = mk(nc, es, "actT", [128, 16, T], BF16)
            w1 = mk(nc, es, "w1", [128, 16, 96], BF16)
            w2 = mk(nc, es, "w2", [96, E], BF16)
            hT = mk(nc, es, "hT", [96, T], BF16)
            ob = [mk(nc, es, f"obL{i}", [128, 512], F32) for i in range(4)]
            ps = [mk(nc, es, f"psL{i}", [128, 512], F32, psum=True) for i in range(4)]
            P.dma('sp', actT[:], d['xmT'][xi], writes=['actT'], key='actT')
            P.dma('pool', w1[:], d[w1n][0].rearrange("(c p) n -> p c n", p=128), writes=['w1'], key='w1')
            P.dma('pool', w2[:], d[w2n][0], writes=['w2'], key='w2')
            cnt = 0
            for t in range(NT):
                pb = cnt % 4
                cnt += 1
                for c in range(16):
                    mm(P, ps[pb][0:96, 0:128], w1[:, c, :], actT[:, c, t * 128:(t + 1) * 128], c == 0, c == 15,
                       ['actT', 'w1'], [('psL', pb)])
                actf(P, hT[:, t * 128:(t + 1) * 128], ps[pb][0:96, 0:128], func, [('psL', pb)], [('hT', t)])
            for t in range(NT):
                for cb in range(8):
                    pb = cnt % 4
                    cnt += 1
                    mm(P, ps[pb][:], hT[:, t * 128:(t + 1) * 128], w2[:, cb * 512:(cb + 1) * 512], True, True,
                       [('hT', t), 'w2'], [('psL', pb)])
                    cp(P, 'act' if cnt % 2 else 'dve', ob[pb][:], ps[pb][:], [('psL', pb)], [('obL', pb)])
                    P.dma('sp', d[outn][t * 128:(t + 1) * 128, cb * 512:(cb + 1) * 512], ob[pb][:], reads=[('obL', pb)], key=('obLst', pb))
            P.flush()


def stage_CDE(P, nc, d):
    with ExitStack() as es:
        ident = mk(nc, es, "identF", [128, 128], F32)
        identb = mk(nc, es, "identB", [128, 128], BF16)
        tri = mk(nc, es, "tri", [128, 128], F32)
        ones = mk(nc, es, "ones", [128, 128], F32)
        onehot = mk(nc, es, "onehot", [128, 1], F32)
        lmask = mk(nc, es, "lmask", [128, 2], F32)
        mask4 = mk(nc, es, "mask4", [128, 512], F32)
        negsl = mk(nc, es, "negsl", [128, 128], F32)
        for nm, tl in (('ident', ident), ('tri', tri), ('ones', ones), ('onehot', onehot), ('lmask', lmask), ('mask4', mask4), ('negsl', negsl)):
            P.dma('sp', tl[:], d[nm], writes=[nm], key=nm)
        P.dma('pool', identb[:], d['ident'], writes=['identb'], key='identb')
        CONST = ['ident', 'tri', 'ones', 'onehot', 'lmask', 'mask4', 'negsl', 'identb']
        ST = mk(nc, es, "ST", [128, 32, 64], F32)
        STb = mk(nc, es, "STb", [128, 32, 64], BF16)
        SN = mk(nc, es, "SN", [64, 64, 64], F32)
        BON = mk(nc, es, "BON", [128, 64], F32)
        stmp = mk(nc, es, "stmp", [128, 64], F32)
        ygT = [mk(nc, es, f"ygT{i}", [128, 32, 128], BF16) for i in range(2)]
        NB = 3
        PRM = [mk(nc, es, f"PRM{i}", [128, 7, 512], F32) for i in range(NB)]
        Rb = [mk(nc, es, f"Rb{i}", [128, 512], BF16) for i in range(NB)]
        Kb = [mk(nc, es, f"Kb{i}", [128, 512], BF16) for i in range(NB)]
        Vb = [mk(nc, es, f"Vb{i}", [128, 512], BF16) for i in range(NB)]
        Zb = [mk(nc, es, f"Zb{i}", [128, 512], BF16) for i in range(NB)]
        Wp = [mk(nc, es, f"Wp{i}", [128, 512], F32) for i in range(NB)]
        Ap = [mk(nc, es, f"Ap{i}", [128, 512], F32) for i in range(NB)]
        f32names = ['LD', 'KKf', 'KMf', 'Bf', 'SQ', 'T1', 'E1', 'E2', 'E3', 'E4', 'GT', 'Dinv']
        W = {n: mk(nc, es, n, [128, 512], F32) for n in f32names}
        sm = mk(nc, es, "sm", [128, 64], F32)
        TM = [mk(nc, es, f"TM{i}", [128, 4, 512], BF16) for i in range(NB)]
        KVb = [mk(nc, es, f"KVb{i}", [128, 512], BF16) for i in range(NB)]
        BVb = [mk(nc, es, f"BVb{i}", [128, 512], BF16) for i in range(NB)]
        FT = [mk(nc, es, f"FT{i}", [128, 4, 4, 128], BF16) for i in range(NB)]
        gCs = [mk(nc, es, f"gCs{i}", [128, 4], F32) for i in range(NB)]
        AK = mk(nc, es, "AK", [128, 4, 512], BF16)
        MT = mk(nc, es, "MT", [128, 4, 128], BF16)
        Rm = [mk(nc, es, f"Rm{i}", [128, 4, 128], BF16) for i in range(2)]
        PP = [mk(nc, es, f"PP{i}", [128, 4, 2, 128], BF16) for i in range(2)]
        Xb = mk(nc, es, "Xb", [128, 256], BF16)
        nSA = mk(nc, es, "nSA", [128, 256], BF16)
        Ycb = [mk(nc, es, f"Ycb{i}", [128, 512], F32) for i in range(2)]
        EY = {n: mk(nc, es, n, [128, 512], F32) for n in ('Ysq', 'Yn', 'Sz')}
        YG = mk(nc, es, "YG", [128, 512], BF16)
        pC0 = mk(nc, es, "pC0", [128, 512], F32, psum=True)
        pC1 = mk(nc, es, "pC1", [128, 512], F32, psum=True)
        pRX2 = mk(nc, es, "pRX2", [128, 512], F32, psum=True)
        pPQ = mk(nc, es, "pPQ", [128, 4, 2, 128], F32, psum=True)
        pPA = mk(nc, es, "pPA", [128, 512], F32, psum=True)
        pRX = mk(nc, es, "pRX", [128, 512], F32, psum=True)
        pYS = mk(nc, es, "pYS", [128, 512], F32, psum=True)

        def state_load(s):
            P.dma('sp', SN[:], d['swkv'][s].rearrange("h v k -> v h k"), writes=['SN'], key='SN')
            for g8 in range(4):
                for q in range(8):
                    gp = g8 * 8 + q
                    tr(P, pC0[:, q * 64:(q + 1) * 64], SN[:, 2 * gp:2 * gp + 2, :].rearrange("v a k -> v (a k)"), ident[0:64, 0:64],
                       ['SN', 'ident'], ['pC0'])
                cp(P, 'act', ST[:, g8 * 8:(g8 + 1) * 8, :], pC0[:].rearrange("p (a v) -> p a v", a=8), ['pC0'], [('ST', g8 * 8 + q) for q in range(8)])
                cp(P, 'dve', STb[:, g8 * 8:(g8 + 1) * 8, :], pC0[:].rearrange("p (a v) -> p a v", a=8), ['pC0'], [('STb', g8 * 8 + q) for q in range(8)])

        def state_save(dst):
            for g4 in range(8):
                for q in range(4):
                    gp = g4 * 4 + q
                    tr(P, pC0[0:64, q * 128:(q + 1) * 128], ST[:, gp, :], ident[:], [('ST', gp), 'ident'], ['pC0'])
                cp(P, 'act', SN[:, g4 * 8:(g4 + 1) * 8, :].rearrange("v a k -> v (a k)"), pC0[0:64, :], ['pC0'], ['SN'])
            P.dma('sp', dst.rearrange("h v k -> v h k"), SN[:], reads=['SN'], key='SNst')

        P.op('pool', lambda e: e.memset(ST[:], 0.0), writes=[('ST', g) for g in range(32)])
        P.op('pool', lambda e: e.memset(STb[:], 0.0), writes=[('STb', g) for g in range(32)])

        def phase_C(ch, cb, b):
            lcol = 0 if ch < 17 else 1
            c0, c1 = cb * 512, (cb + 1) * 512
            P.dma('sp', PRM[b][:], d['prm'][:, c0:c1].partition_broadcast(128), writes=[('PRM', b)], key=('PRM', b))
            load_rows(P, 'sp', Rb[b], d['rkvz'][0], ch, c0, c1, [('Rb', b)], ('Rb', b))
            load_rows(P, 'sp', Kb[b], d['rkvz'][1], ch, c0, c1, [('Kb', b)], ('Kb', b))
            load_rows(P, 'sp', Vb[b], d['rkvz'][2], ch, c0, c1, [('Vb', b)], ('Vb', b))
            load_rows(P, 'sp', Zb[b], d['rkvz'][3], ch, c0, c1, [('Zb', b)], ('Zb', b))
            load_rows(P, 'sp', Wp[b], d['wpre'], ch, c0, c1, [('Wp', b)], ('Wp', b))
            load_rows(P, 'sp', Ap[b], d['apre'], ch, c0, c1, [('Ap', b)], ('Ap', b))
            prm = lambda i, b=b: PRM[b][:, i, :]
            tt(P, 'pool', Wp[b][:], Wp[b][:], prm(0), ALU.add, [('Wp', b), ('PRM', b)], [('Wp', b)])
            actf(P, Wp[b][:], Wp[b][:], AF.Sigmoid, [('Wp', b)], [('Wp', b)])
            ts(P, 'dve', W['LD'][:], Wp[b][:], lmask[:, lcol:lcol + 1], None, ALU.mult, None, [('Wp', b), 'lmask'], ['LD'])
            tt(P, 'pool', Ap[b][:], Ap[b][:], prm(1), ALU.add, [('Ap', b), ('PRM', b)], [('Ap', b)])
            actf(P, Ap[b][:], Ap[b][:], AF.Sigmoid, [('Ap', b)], [('Ap', b)])
            tt(P, 'pool', W['KKf'][:], Kb[b][:], prm(2), ALU.mult, [('Kb', b), ('PRM', b)], ['KKf'])
            tt(P, 'pool', W['SQ'][:], W['KKf'][:], W['KKf'][:], ALU.mult, ['KKf'], ['SQ'])
            red(P, 'dve', sm[:, 0:8], W['SQ'][:].rearrange("p (h c) -> p h c", h=8), ['SQ'], [('sm', 0)])
            ts(P, 'dve', sm[:, 0:8], sm[:, 0:8], 1e-24, None, ALU.max, None, [('sm', 0)], [('sm', 0)])
            P.op('act', lambda e: e.sqrt(out=sm[:, 0:8], in_=sm[:, 0:8]), [('sm', 0)], [('sm', 0)])
            recip(P, sm[:, 0:8], sm[:, 0:8], [('sm', 0)], [('sm', 0)])
            tt(P, 'dve', W['KKf'][:].rearrange("p (h c) -> p h c", h=8), W['KKf'][:].rearrange("p (h c) -> p h c", h=8),
               sm[:, 0:8].unsqueeze(2).to_broadcast([128, 8, 64]), ALU.mult, ['KKf', ('sm', 0)], ['KKf'])
            stt(P, 'dve', W['T1'][:], Ap[b][:], -1.0, prm(3), ALU.add, ALU.mult, [('Ap', b), ('PRM', b)], ['T1'])
            stt(P, 'dve', W['KMf'][:], W['T1'][:], 1.0, Kb[b][:], ALU.add, ALU.mult, ['T1', ('Kb', b)], ['KMf'])
            tt(P, 'pool', W['Bf'][:], W['KKf'][:], Ap[b][:], ALU.mult, ['KKf', ('Ap', b)], ['Bf'])
            tt(P, 'pool', W['T1'][:], Rb[b][:], W['KMf'][:], ALU.mult, [('Rb', b), 'KMf'], ['T1'])
            tt(P, 'pool', W['T1'][:], W['T1'][:], prm(4), ALU.mult, ['T1', ('PRM', b)], ['T1'])
            red(P, 'dve', BON[:, cb * 8:(cb + 1) * 8], W['T1'][:].rearrange("p (h c) -> p h c", h=8), ['T1'], [('BON', cb)])
            mm(P, pC0[:], tri[:], W['LD'][:], True, True, ['tri', 'LD'], ['pC0'])
            mm(P, pC1[:], ones[:], W['LD'][:], True, True, ['ones', 'LD'], ['pC1'])
            actf(P, W['E1'][:], pC0[:], AF.Exp, ['pC0'], ['E1'])
            actf(P, W['E2'][:], pC0[:], AF.Exp, ['pC0'], ['E2'], scale=-1.0)
            actf(P, W['GT'][:], pC1[:], AF.Exp, ['pC1'], ['GT'])
            actf(P, W['Dinv'][:], W['LD'][:], AF.Exp, ['LD'], ['Dinv'], scale=-1.0)
            tt(P, 'dve', W['E3'][:], W['E1'][:], W['Dinv'][:], ALU.mult, ['E1', 'Dinv'], ['E3'])
            tt(P, 'pool', W['E4'][:], W['GT'][:], W['E2'][:], ALU.mult, ['GT', 'E2'], ['E4'])
            for pp in range(4):
                mm(P, pC1[:, pp:pp + 1], W['GT'][:, pp * 128:(pp + 1) * 128], onehot[:, 0:1], True, True, ['GT', 'onehot'], ['pC1'])
            cp(P, 'act', gCs[b][:], pC1[:, 0:4], ['pC1'], [('gCs', b)])
            tt(P, 'dve', TM[b][:, 1, :], Rb[b][:], W['E1'][:], ALU.mult, [('Rb', b), 'E1'], [('TM', b, 1)])
            tt(P, 'dve', TM[b][:, 2, :], W['KMf'][:], W['E2'][:], ALU.mult, ['KMf', 'E2'], [('TM', b, 2)])
            tt(P, 'pool', TM[b][:, 3, :], W['Bf'][:], W['E2'][:], ALU.mult, ['Bf', 'E2'], [('TM', b, 3)])
            tt(P, 'dve', TM[b][:, 0, :], W['KKf'][:], W['E3'][:], ALU.mult, ['KKf', 'E3'], [('TM', b, 0)])
            tt(P, 'pool', KVb[b][:], W['KMf'][:], W['E4'][:], ALU.mult, ['KMf', 'E4'], [('KVb', b)])
            tt(P, 'pool', BVb[b][:], W['Bf'][:], W['E4'][:], ALU.mult, ['Bf', 'E4'], [('BVb', b)])
            for hf in range(2):
                for pq in range(2):
                    pp = hf * 2 + pq
                    for kd in range(4):
                        tr(P, pC0[:].bitcast(BF16)[:, (pq * 4 + kd) * 128:(pq * 4 + kd + 1) * 128], TM[b][:, kd, pp * 128:(pp + 1) * 128], identb[:],
                           [('TM', b, kd), 'identb'], ['pC0'])
                cp(P, 'act' if hf == 0 else 'dve', FT[b][:, 2 * hf:2 * hf + 2].rearrange("p a k t -> p (a k t)"), pC0[:].bitcast(BF16)[:, 0:1024],
                   ['pC0'], [('FT', b, hf)])

        def phase_D(ch, cb, b, yb):
            for g in range(2):
                ftk = ('FT', b, g)
                abank = [(pPA[:], 'pPA'), (pPQ[:, 0:2].rearrange("p a b t -> p (a b t)"), ('pPQ', 0)),
                         (pPQ[:, 2:4].rearrange("p a b t -> p (a b t)"), ('pPQ', 1)), (pRX2[:], ('pRX', 1))]
                for i in range(4):
                    pp, h2 = 2 * g + i // 2, i % 2
                    fts = FT[b][h2 * 64:(h2 + 1) * 64, pp]
                    rhs2 = fts[:, 0:2, :].rearrange("p k t -> p (k t)")
                    bk, bkey = abank[i]
                    mm(P, bk[:, 0:256], fts[:, 2, :], rhs2, True, True, [ftk], [bkey])
                    mm(P, bk[:, 256:512], fts[:, 3, :], rhs2, True, True, [ftk], [bkey])
                    pnb, pnk = (pYS, 'pYS') if h2 == 0 else (pRX, ('pRX', 0))
                    mm(P, pnb[:, (i // 2) * 128:(i // 2 + 1) * 128], fts[:, 0, :], fts[:, 3, :], True, True, [ftk], [pnk])
                for i in range(4):
                    bk, bkey = abank[i]
                    tt(P, 'dve', AK[:, i, :], bk, mask4[:], ALU.mult, [bkey, 'mask4'], [('AK', i)])
                for i in range(4):
                    pnb, pnk = (pYS, 'pYS') if i % 2 == 0 else (pRX, ('pRX', 0))
                    tt(P, 'dve', MT[:, i, :], pnb[:, (i // 2) * 128:(i // 2 + 1) * 128], negsl[:], ALU.mult, [pnk, 'negsl'], [('MT', i // 2)])
                for hg in range(2):
                    tt(P, 'pool', Rm[0][:, 2 * hg:2 * hg + 2, :], AK[:, 2 * hg:2 * hg + 2, 256:384],
                       identb[:].unsqueeze(1).to_broadcast([128, 2, 128]), ALU.add,
                       [('AK', 2 * hg), ('AK', 2 * hg + 1), 'identb'], [('Rm', 0, hg)])
                cur = 0
                rxb = [pRX, pRX2]

                def squares(lev, hg):
                    nxt = lev % 2
                    last = (lev == 6)
                    for i in (2 * hg, 2 * hg + 1):
                        if lev == 1:
                            Pc, PTc = AK[:, i, 256:384], MT[:, i, :]
                            rk = [('AK', i), ('MT', hg)]
                        else:
                            Pc, PTc = PP[1 - nxt][:, i, 0, :], PP[1 - nxt][:, i, 1, :]
                            rk = [('PP', 1 - nxt, hg)]
                        if not last:
                            mm(P, pPQ[:, i, 0, :], PTc, Pc, True, True, rk, [('pPQ', hg)])
                        mm(P, pPQ[:, i, 1, :], Pc, PTc, True, True, rk, [('pPQ', hg)])

                def evac_pp(lev, hg):
                    nxt = lev % 2
                    hs_ = slice(2 * hg, 2 * hg + 2)
                    if lev < 6:
                        cp(P, 'act', PP[nxt][:, hs_], pPQ[:, hs_], [('pPQ', hg)], [('PP', nxt, hg)])
                    else:
                        cp(P, 'act', PP[nxt][:, hs_, 1, :], pPQ[:, hs_, 1, :], [('pPQ', hg)], [('PP', nxt, hg)])

                levels = range(1, 7) if ch < 17 else []
                for hg in (range(2) if ch < 17 else []):
                    squares(1, hg)
                for hg in (range(2) if ch < 17 else []):
                    evac_pp(1, hg)
                for lev in levels:
                    nxt = lev % 2
                    for hg in range(2):
                        for q, i in enumerate((2 * hg, 2 * hg + 1)):
                            mm(P, rxb[hg][:, q * 128:(q + 1) * 128], PP[nxt][:, i, 1, :], Rm[cur][:, i, :], True, True,
                               [('PP', nxt, hg), ('Rm', cur, hg)], [('pRX', hg)])
                        if lev < 6:
                            squares(lev + 1, hg)
                    for hg in range(2):
                        tt(P, 'dve', Rm[1 - cur][:, 2 * hg:2 * hg + 2, :], rxb[hg][:, 0:256].rearrange("p (a t) -> p a t", a=2),
                           Rm[cur][:, 2 * hg:2 * hg + 2, :], ALU.add, [('pRX', hg), ('Rm', cur, hg)], [('Rm', 1 - cur, hg)])
                        if lev < 6:
                            evac_pp(lev + 1, hg)
                    cur = 1 - cur
                Rf = Rm[cur]
                for i in range(4):
                    pp, h2 = 2 * g + i // 2, i % 2
                    gp = cb * 4 + pp
                    hh = 4 * g + i
                    fts = FT[b][h2 * 64:(h2 + 1) * 64, pp]
                    mm(P, pRX[:, i * 64:(i + 1) * 64], fts[:, 0, :], STb[h2 * 64:(h2 + 1) * 64, gp, :], True, False,
                       [ftk, ('STb', gp)], [('pRX', 0)])
                    mm(P, pRX[:, i * 64:(i + 1) * 64], AK[:, i, 0:128], Vb[b][:, hh * 64:(hh + 1) * 64], False, True,
                       [('AK', i), ('Vb', b)], [('pRX', 0)])
                cp(P, 'act', Xb[:], pRX[:, 0:256], [('pRX', 0)], ['Xb'])
                for i in range(4):
                    mm(P, pRX[:, 256 + i * 64:256 + (i + 1) * 64], Rf[:, i, :], Xb[:, i * 64:(i + 1) * 64], True, True,
                       [('Rm', cur, i // 2), 'Xb'], [('pRX', 0)])
                P.op('act', lambda e: e.mul(out=nSA[:], in_=pRX[:, 256:512], mul=-1.0), [('pRX', 0)], ['nSA'])
                for i in range(4):
                    pp, h2 = 2 * g + i // 2, i % 2
                    gp = cb * 4 + pp
                    hh = 4 * g + i
                    fts = FT[b][h2 * 64:(h2 + 1) * 64, pp]
                    o = pYS[:, i * 64:(i + 1) * 64]
                    mm(P, o, fts[:, 1, :], STb[h2 * 64:(h2 + 1) * 64, gp, :], True, False, [ftk, ('STb', gp)], ['pYS'])
                    mm(P, o, AK[:, i, 128:256], Vb[b][:, hh * 64:(hh + 1) * 64], False, False, [('AK', i), ('Vb', b)], ['pYS'])
                    mm(P, o, AK[:, i, 384:512], nSA[:, i * 64:(i + 1) * 64], False, True, [('AK', i), 'nSA'], ['pYS'])
                for q in range(2):
                    pp = 2 * g + q
                    o = pYS[:, 256 + q * 128:256 + (q + 1) * 128]
                    mm(P, o, KVb[b][:, pp * 128:(pp + 1) * 128], Vb[b][:, pp * 128:(pp + 1) * 128], True, False,
                       [('KVb', b), ('Vb', b)], ['pYS'])
                    mm(P, o, BVb[b][:, pp * 128:(pp + 1) * 128], nSA[:, q * 128:(q + 1) * 128], False, True,
                       [('BVb', b), 'nSA'], ['pYS'])
                cp(P, 'act', Ycb[yb][:, g * 256:(g + 1) * 256], pYS[:, 0:256], ['pYS'], [('Ycb', yb, g)])
                for q in range(2):
                    pp = 2 * g + q
                    gp = cb * 4 + pp
                    for h2 in range(2):
                        sl = slice(h2 * 64, (h2 + 1) * 64)
                        ts(P, 'dve', stmp[sl, :], ST[sl, gp, :], gCs[b][sl, pp:pp + 1], None, ALU.mult, None,
                           [('ST', gp), ('gCs', b)], ['stmp'])
                        tt(P, 'dve', ST[sl, gp, :], pYS[sl, 256 + q * 128 + h2 * 64:256 + q * 128 + (h2 + 1) * 64], stmp[sl, :], ALU.add,
                           ['pYS', 'stmp'], [('ST', gp)])
                    cp(P, 'pool', STb[:, gp, :], ST[:, gp, :], [('ST', gp)], [('STb', gp)])

        def phase_E(ch, cb, b, yb):
            prm = lambda i, b=b: PRM[b][:, i, :]
            Y3 = Ycb[yb][:].rearrange("p (h c) -> p h c", h=8)
            yk = [('Ycb', yb, 0), ('Ycb', yb, 1)]
            red(P, 'dve', sm[:, 8:16], Y3, yk, [('sm', 1)])
            tt(P, 'pool', EY['Ysq'][:], Ycb[yb][:], Ycb[yb][:], ALU.mult, yk, ['Ysq'])
            red(P, 'dve', sm[:, 16:24], EY['Ysq'][:].rearrange("p (h c) -> p h c", h=8), ['Ysq'], [('sm', 2)])
            ts(P, 'dve', sm[:, 8:16], sm[:, 8:16], 1.0 / 64, None, ALU.mult, None, [('sm', 1)], [('sm', 1)])
            tt(P, 'dve', sm[:, 24:32], sm[:, 8:16], sm[:, 8:16], ALU.mult, [('sm', 1)], [('sm', 3)])
            stt(P, 'dve', sm[:, 16:24], sm[:, 16:24], 1.0 / 64, sm[:, 24:32], ALU.mult, ALU.subtract, [('sm', 2), ('sm', 3)], [('sm', 2)])
            ts(P, 'dve', sm[:, 16:24], sm[:, 16:24], GN_EPS, None, ALU.add, None, [('sm', 2)], [('sm', 2)])
            P.op('act', lambda e: e.sqrt(out=sm[:, 16:24], in_=sm[:, 16:24]), [('sm', 2)], [('sm', 2)])
            recip(P, sm[:, 16:24], sm[:, 16:24], [('sm', 2)], [('sm', 2)])
            Yn3 = EY['Yn'][:].rearrange("p (h c) -> p h c", h=8)
            tt(P, 'pool', Yn3, Y3, sm[:, 8:16].unsqueeze(2).to_broadcast([128, 8, 64]), ALU.subtract, yk + [('sm', 1)], ['Yn'])
            tt(P, 'dve', Yn3, Yn3, sm[:, 16:24].unsqueeze(2).to_broadcast([128, 8, 64]), ALU.mult, ['Yn', ('sm', 2)], ['Yn'])
            tt(P, 'pool', EY['Yn'][:], EY['Yn'][:], prm(5), ALU.mult, ['Yn', ('PRM', b)], ['Yn'])
            tt(P, 'pool', EY['Yn'][:], EY['Yn'][:], prm(6), ALU.add, ['Yn', ('PRM', b)], ['Yn'])
            tt(P, 'pool', EY['Ysq'][:].rearrange("p (h c) -> p h c", h=8), Vb[b][:].rearrange("p (h c) -> p h c", h=8),
               BON[:, cb * 8:(cb + 1) * 8].unsqueeze(2).to_broadcast([128, 8, 64]), ALU.mult, [('Vb', b), ('BON', cb)], ['Ysq'])
            tt(P, 'pool', EY['Yn'][:], EY['Yn'][:], EY['Ysq'][:], ALU.add, ['Yn', 'Ysq'], ['Yn'])
            actf(P, EY['Sz'][:], Zb[b][:], AF.Silu, [('Zb', b)], ['Sz'])
            tt(P, 'dve', YG[:], EY['Yn'][:], EY['Sz'][:], ALU.mult, ['Yn', 'Sz'], ['YG'])
            for q in range(4):
                tr(P, pC1[:].bitcast(BF16)[:, q * 128:(q + 1) * 128], YG[:, q * 128:(q + 1) * 128], identb[:], ['YG', 'identb'], ['pC1'])
            cp(P, 'act', ygT[ch % 2][:, cb * 4:(cb + 1) * 4, :].rearrange("p a t -> p (a t)"), pC1[:].bitcast(BF16)[:, 0:512], ['pC1'], [('ygT', ch % 2)])
            if cb == CB_LIST[-1]:
                yt = ygT[ch % 2]
                if ch < 17:
                    P.dma('sp', d['ygT'][:, :, ch * 128:(ch + 1) * 128], yt[:], reads=[('ygT', ch % 2)], key='ygTst')
                elif ch == 17:
                    P.dma('sp', d['ygT'][:, :, SROW0:SROW0 + 128], yt[:], reads=[('ygT', ch % 2)], key='ygTst')
                else:
                    s_ = ch - 17
                    P.dma('sp', d['ygT'][:, :, SROW0 + s_:SROW0 + s_ + 1], yt[:, :, 0:1], reads=[('ygT', ch % 2)], key='ygTst',
                          allow_slow_non_contiguous=True)

        def capture(fn, *a):
            P.cap = []
            fn(*a)
            l = P.cap
            P.cap = None
            return l

        def replay(l):
            for (eng, fn, reads, writes, key) in l:
                P.op(eng, fn, reads, writes, key)

        def merge(main, first, second):
            out = []
            n = len(main)
            h = n // 2 if (first and second) else (n if first else 0)
            da = db = 0
            for i, o in enumerate(main):
                out.append(o)
                if i < h:
                    t_ = (i + 1) * len(first) // max(h, 1)
                    while da < t_:
                        out.append(first[da])
                        da += 1
                else:
                    if da < len(first):
                        out.extend(first[da:])
                        da = len(first)
                    t_ = (i + 1 - h) * len(second) // max(n - h, 1)
                    while db < t_:
                        out.append(second[db])
                        db += 1
            out.extend(first[da:])
            out.extend(second[db:])
            return out

        units = [(ch, cb) for ch in CH_LIST for cb in CB_LIST]
        replay(capture(phase_C, units[0][0], units[0][1], 0))
        for idx, (ch, cb) in enumerate(units):
            if cb == CB_LIST[0] and ch >= 17:
                state_load(ch - 17)
            Dl = capture(phase_D, ch, cb, idx % NB, idx % 2)
            Cn = capture(phase_C, units[idx + 1][0], units[idx + 1][1], (idx + 1) % NB) if idx + 1 < len(units) else []
            Ep = capture(phase_E, units[idx - 1][0], units[idx - 1][1], (idx - 1) % NB, (idx - 1) % 2) if idx >= 1 else []
            replay(merge(Dl, Ep, Cn) if PIPELINE else Dl + Ep + Cn)
            if cb == CB_LIST[-1]:
                if ch == 16:
                    state_save(d['o_pwkv'])
                if ch >= 17:
                    state_save(d['o_swkv'][ch - 17])
        replay(capture(phase_E, units[-1][0], units[-1][1], (len(units) - 1) % NB, (len(units) - 1) % 2))
        P.flush()


def stage_outproj(P, nc, d, actn, wsrc, outn, tag):
    for half in range(2):
        with ExitStack() as es:
            tiles = list(range(half * 9, half * 9 + 9))
            actT = mk(nc, es, "actT", [128, 32, 9 * 128], BF16)
            ob = [mk(nc, es, f"obO{i}", [128, 512], F32) for i in range(4)]
            P.dma('sp', actT[:], d[actn][:, :, half * 1152:(half + 1) * 1152], writes=['actT'], key='actT')
            wt = [mk(nc, es, f"wtO{i}", [128, 32, 512], BF16) for i in range(2)]
            ps = [mk(nc, es, f"psO{i}", [128, 512], F32, psum=True) for i in range(4)]
            wv = wsrc.rearrange("(c p) n -> p c n", p=128)
            cnt = 0
            for cb in range(4):
                wb = cb % 2
                P.dma('pool', wt[wb][:], wv[:, :, cb * 512:(cb + 1) * 512], writes=[('wt', wb)], key=('wt', wb))
                for tl, t in enumerate(tiles):
                    pb = cnt % 4
                    cnt += 1
                    for c in range(32):
                        mm(P, ps[pb][:], actT[:, c, tl * 128:(tl + 1) * 128], wt[wb][:, c, :], c == 0, c == 31,
                           ['actT', ('wt', wb)], [('ps', pb)])
                    cp(P, 'act' if cnt % 2 else 'dve', ob[pb][:], ps[pb][:], [('ps', pb)], [('ob', pb)])
                    P.dma('sp', d[outn][t * 128:(t + 1) * 128, cb * 512:(cb + 1) * 512], ob[pb][:], reads=[('ob', pb)], key=('obst', pb))
            P.flush()


def rms_stats(P, x, junk, st, xk, sk):
    P.op('pool', lambda e: e.memset(st[:, 0:1], 0.0), writes=[sk])
    P.op('act', lambda e: e.activation(out=junk[:], in_=x[:], func=AF.Square, accum_out=st[:, 0:1]), reads=[xk], writes=['junk', sk])
    ts(P, 'dve', st[:, 1:2], st[:, 0:1], 1.0 / D, RMS_EPS, ALU.mult, ALU.add, [sk], [sk])
    P.op('act', lambda e: e.sqrt(out=st[:, 1:2], in_=st[:, 1:2]), [sk], [sk])
    recip(P, st[:, 1:2], st[:, 1:2], [sk], [sk])


def stage_F2(P, nc, d):
    with ExitStack() as es:
        gkv = mk(nc, es, "gkv", [128, D], F32)
        gb = mk(nc, es, "gb", [128, D], F32)
        ident = mk(nc, es, "identF2", [128, 128], F32)
        junk = mk(nc, es, "junkF", [128, D], F32)
        xt = [mk(nc, es, f"xF{i}", [128, D], F32) for i in range(2)]
        ot = [mk(nc, es, f"oF{i}", [128, D], F32) for i in range(2)]
        hn = [mk(nc, es, f"hnF{i}", [128, D], F32) for i in range(2)]
        st = [mk(nc, es, f"stF{i}", [128, 2], F32) for i in range(2)]
        hT = [mk(nc, es, f"hTF{i}", [128, 16, 128], BF16) for i in range(2)]
        ps = [mk(nc, es, f"psF{i}", [128, 512], F32, psum=True) for i in range(4)]
        P.dma('sp', gkv[:], d['kv_norm'][0].partition_broadcast(128), writes=['gkv'], key='gkv')
        P.dma('sp', gb[:], d['b_norm'][0].partition_broadcast(128), writes=['gb'], key='gb')
        P.dma('sp', ident[:], d['ident'], writes=['ident'], key='ident')
        ev = 0
        def loads_F(t):
            b = t % 2
            P.dma('sp', xt[b][:], d['xin'][1 + 128 * t:1 + 128 * t + 128, :], writes=[('x', b)], key=('x', b))
            P.dma('sp', ot[b][:], d['o1'][128 * t:128 * t + 128, :], writes=[('o', b)], key=('o', b))

        loads_F(0)
        for t in range(NT):
            b = t % 2
            if t + 1 < NT:
                loads_F(t + 1)
            tt(P, 'dve', xt[b][:], xt[b][:], ot[b][:], ALU.add, [('x', b), ('o', b)], [('x', b)])
            P.dma('sp', d['hp'][128 * t:128 * t + 128, :], xt[b][:], reads=[('x', b)], key=('hpst', b))
            rms_stats(P, xt[b], junk, st[b], ('x', b), ('st', b))
            for vi, (g, gk, dst) in enumerate(((gkv, 'gkv', 'hkvT'), (gb, 'gb', 'hbT'))):
                hb = (2 * t + vi) % 2
                stt(P, 'dve', hn[hb][:], xt[b][:], st[b][:, 1:2], g[:], ALU.mult, ALU.mult,
                    [('x', b), ('st', b), gk], [('hn', hb)])
                for q in range(4):
                    pb = ev % 4
                    for j in range(4):
                        c = q * 4 + j
                        tr(P, ps[pb][:, j * 128:(j + 1) * 128], hn[hb][:, c * 128:(c + 1) * 128], ident[:], [('hn', hb), 'ident'], [('ps', pb)])
                    cp(P, 'act' if ev % 2 == 0 else 'dve', hT[hb][:, q * 4:(q + 1) * 4, :], ps[pb][:].rearrange("p (a n) -> p a n", a=4),
                       [('ps', pb)], [('hT', hb)])
                    ev += 1
                P.dma('sp', d[dst][:, :, t * 128:(t + 1) * 128], hT[hb][:], reads=[('hT', hb)], key=('hTst', hb))
        P.flush()


def rotary(P, eng, Kt, cs, tmp, kk, csk, tk):
    nh = Kt.shape[1]
    cosb = cs[:, 0:8].unsqueeze(1).to_broadcast([128, nh, 8])
    sinb = cs[:, 8:16].unsqueeze(1).to_broadcast([128, nh, 8])
    x1, x2 = Kt[:, :, 0:8], Kt[:, :, 8:16]
    t = [tmp[:, i, 0:nh, :] for i in range(4)]
    tt(P, eng, t[0], x1, cosb, ALU.mult, [kk, csk], [tk])
    tt(P, eng, t[1], x2, sinb, ALU.mult, [kk, csk], [tk])
    tt(P, eng, t[2], x2, cosb, ALU.mult, [kk, csk], [tk])
    tt(P, eng, t[3], x1, sinb, ALU.mult, [kk, csk], [tk])
    tt(P, eng, x1, t[0], t[1], ALU.subtract, [tk], [kk])
    tt(P, eng, x2, t[2], t[3], ALU.add, [tk], [kk])


def stage_G(P, nc, d):
    with ExitStack() as es:
        actT = mk(nc, es, "actT", [128, 16, T], BF16)
        CS = mk(nc, es, "CS", [128, NT, 16], F32)
        ob = [mk(nc, es, f"obG{i}", [128, 512], F32) for i in range(4)]
        obb = [mk(nc, es, f"obbG{i}", [128, 512], BF16) for i in range(4)]
        tmp = [mk(nc, es, f"tmpG{i}", [128, 4, 8, 8], F32) for i in range(4)]
        P.dma('sp', actT[:], d['hkvT'], writes=['actT'], key='actT')
        P.dma('sp', CS[:], d['cs'].rearrange("(t p) c -> p t c", p=128), writes=['CS'], key='CS')
        for s in range(NS):
            P.dma('sp', d['o_sck'][s, 0:127, :], d['ck'][s, 1:128, :], key=('cpk', s))
            P.dma('sp', d['o_scv'][s, 0:127, :], d['cv'][s, 1:128, :], key=('cpv', s))

        def evac(t, cb, pst, pkey, cnt):
            o = cnt % 4
            cp(P, 'act', ob[o][:], pst[:], [pkey], [('ob', o)])
            if cb == 0:
                rotary(P, 'dve', ob[o][:].rearrange("p (h c) -> p h c", h=8), CS[:, t, :], tmp[o], ('ob', o), 'CS', ('tmp', o))
            cp(P, 'pool', obb[o][:], ob[o][:], [('ob', o)], [('obb', o)])
            P.dma('sp', d['KVs'][t * 128:(t + 1) * 128, cb * 512:(cb + 1) * 512], obb[o][:], reads=[('obb', o)], key=('obbst', o))
            if t == 16:
                P.dma('sp', d['o_pck' if cb == 0 else 'o_pcv'], ob[o][:], reads=[('ob', o)], key=('obst', o))
            if t == 17:
                P.dma('sp', d['o_sck' if cb == 0 else 'o_scv'][:, 127, :], ob[o][0:NS, :], reads=[('ob', o)], key=('obst', o))
        gemm_tokmajor(P, nc, es, None, 16, d['w_kv'], 1024, evac, "G", actT=actT)
        P.flush()


def stage_H(P, nc, d):
    with ExitStack() as es:
        actT = mk(nc, es, "actT", [128, 16, T], BF16)
        CS = mk(nc, es, "CS", [128, NT, 16], F32)
        ob = [mk(nc, es, f"obH{i}", [128, 512], F32) for i in range(4)]
        obb = [mk(nc, es, f"obbH{i}", [128, 512], BF16) for i in range(4)]
        tmp = [mk(nc, es, f"tmpH{i}", [128, 4, 8, 8], F32) for i in range(4)]
        P.dma('sp', actT[:], d['hbT'], writes=['actT'], key='actT')
        P.dma('sp', CS[:], d['cs'].rearrange("(t p) c -> p t c", p=128), writes=['CS'], key='CS')

        def evac(t, cb, pst, pkey, cnt):
            o = cnt % 4
            if cb < 8:
                cp(P, 'act', ob[o][:], pst[:], [pkey], [('ob', o)])
                rotary(P, 'dve' if cnt % 2 else 'pool', ob[o][:].rearrange("p (h c) -> p h c", h=8), CS[:, t, :], tmp[o], ('ob', o), 'CS', ('tmp', o))
                cp(P, 'pool' if cnt % 2 else 'dve', obb[o][:], ob[o][:], [('ob', o)], [('obb', o)])
                P.dma('sp', d['qs'][t * 128:(t + 1) * 128, cb * 512:(cb + 1) * 512], obb[o][:], reads=[('obb', o)], key=('obbst', o))
            else:
                cp(P, 'act' if cnt % 2 else 'dve', obb[o][:], pst[:], [pkey], [('obb', o)])
                P.dma('sp', d['zs'][t * 128:(t + 1) * 128, (cb - 8) * 512:(cb - 7) * 512], obb[o][:], reads=[('obb', o)], key=('obbst', o))
        gemm_tokmajor(P, nc, es, None, 16, d['b_w_qz'][0], 8192, evac, "H", actT=actT)
        P.flush()


def stage_I(P, nc, d):
    with ExitStack() as es:
        identb = mk(nc, es, "identBI", [128, 128], BF16)
        MB = mk(nc, es, "MB", [128, 4, 128], BF16)
        esink = mk(nc, es, "esink", [128, 64], F32)
        P.dma('pool', identb[:], d['ident'], writes=['identb'], key='identb')
        P.dma('pool', MB[:], d['mb'], writes=['MB'], key='MB')
        P.dma('sp', esink[:], d['b_sinks'][0].partition_broadcast(128), writes=['esink'], key='esink')
        actf(P, esink[:], esink[:], AF.Exp, ['esink'], ['esink'])
        NSL = 3
        KVt = [mk(nc, es, f"KVt{i}", [128, 1024], BF16) for i in range(NSL)]
        Kd = mk(nc, es, "Kd", [128, 8, 2, 64], BF16)
        KT2 = [mk(nc, es, f"KT2{i}", [128, 8, 128], BF16) for i in range(NSL)]
        VA = [mk(nc, es, f"VA{i}", [128, 8, 65], BF16) for i in range(NSL)]
        Qt = [mk(nc, es, f"Qt{i}", [128, E], BF16) for i in range(2)]
        Zt = [mk(nc, es, f"Zt{i}", [128, E], BF16) for i in range(2)]
        QT = mk(nc, es, "QTt", [128, 32, 128], BF16)
        NSC = 3
        PTs = [mk(nc, es, f"PTs{i}", [128, 4, 128], BF16) for i in range(NSC)]
        ATT = mk(nc, es, "ATT", [128, E], F32)
        SZ = mk(nc, es, "SZ", [128, E], F32)
        AG = mk(nc, es, "AG", [128, E], BF16)
        agT = mk(nc, es, "agT", [128, 32, 128], BF16)
        den = mk(nc, es, "den", [128, 8], F32)
        pSC = [mk(nc, es, f"pSC{i}", [128, 512], F32, psum=True) for i in range(NSC)]
        pAO = [mk(nc, es, f"pAO{i}", [128, 512], F32, psum=True) for i in range(2)]
        pTP = [mk(nc, es, f"pTPI{i}", [128, 1024], BF16, psum=True) for i in range(2)]
        for i in range(NSL):
            P.op('pool', lambda e, i=i: e.memset(VA[i][:, :, 64:65], 1.0), writes=[('VA', i)])

        def prep_kv(slot):
            k3 = KVt[slot][:, 0:512].rearrange("p (g c) -> p g c", g=8)
            cp(P, 'pool', Kd[:, :, 0, :], k3, [('KVt', slot)], ['Kd'])
            cp(P, 'pool', Kd[:, :, 1, :], k3, [('KVt', slot)], ['Kd'])
            for g in range(8):
                tr(P, pTP[0][:, g * 128:(g + 1) * 128], Kd[:, g].rearrange("p a c -> p (a c)"), identb[:], ['Kd', 'identb'], [('pTP', 0)])
            cp(P, 'act', KT2[slot][:].rearrange("p g t -> p (g t)"), pTP[0][:], [('pTP', 0)], [('KT2', slot)])
            cp(P, 'dve', VA[slot][:, :, 0:64], KVt[slot][:, 512:1024].rearrange("p (g c) -> p g c", g=8), [('KVt', slot)], [('VA', slot)])

        slot_of_tile = {}
        nslot = 0
        def loads_QZ(ch):
            qb = ch % 2
            load_rows(P, 'sp', Qt[qb], d['qs'], ch, 0, E, [('Qt', qb)], ('Qt', qb))
            load_rows(P, 'sp', Zt[qb], d['zs'], ch, 0, E, [('Zt', qb)], ('Zt', qb))

        loads_QZ(0)
        for ch in range(NCH):
            qb = ch % 2
            if ch + 1 < NCH:
                loads_QZ(ch + 1)
            if ch < 17:
                cs_ = nslot % NSL
                nslot += 1
                P.dma('sp', KVt[cs_][:], d['KVs'][ch * 128:(ch + 1) * 128, :], writes=[('KVt', cs_)], key=('KVt', cs_))
                prep_kv(cs_)
                slot_of_tile[ch] = cs_
                ps_ = slot_of_tile.get(ch - 1)
                mcur, mprev = (2, None) if ch == 0 else ((0, 3) if ch == 1 else (0, 1))
            else:
                s = ch - 17
                ps_ = nslot % NSL
                nslot += 1
                P.dma('pool', KVt[ps_][:, 0:512], d['ck'][s], writes=[('KVt', ps_)], key=('KVt', ps_))
                P.dma('pool', KVt[ps_][:, 512:1024], d['cv'][s], writes=[('KVt', ps_)], key=('KVt', ps_))
                prep_kv(ps_)
                cs_ = nslot % NSL
                nslot += 1
                load_rows(P, 'sp', KVt[cs_], d['KVs'], ch, 0, 1024, [('KVt', cs_)], ('KVt', cs_))
                prep_kv(cs_)
                mcur, mprev = 0, 1
            for q8 in range(4):
                tp = pTP[q8 % 2]
                for j in range(8):
                    pr = q8 * 8 + j
                    tr(P, tp[:, j * 128:(j + 1) * 128], Qt[qb][:, pr * 128:(pr + 1) * 128], identb[:], [('Qt', qb), 'identb'], [('pTP', q8 % 2)])
                cp(P, 'act' if q8 % 2 else 'dve', QT[:, q8 * 8:(q8 + 1) * 8, :].rearrange("p a t -> p (a t)"), tp[:], [('pTP', q8 % 2)], [('QT', q8)])
            kts = ([(ps_, mprev)] if ps_ is not None and mprev is not None else []) + [(cs_, mcur)]
            nk = len(kts)

            def scores(j):
                g = j // 4
                sc = pSC[j % NSC]
                for h2 in range(2):
                    sl = slice(h2 * 64, (h2 + 1) * 64)
                    for ki, (slot, mk_) in enumerate(kts):
                        o = sc[:, (h2 * 2 + ki) * 128:(h2 * 2 + ki + 1) * 128]
                        mm(P, o, KT2[slot][sl, g, :], QT[sl, j, :], True, False, [('KT2', slot), ('QT', j // 8)], [('pSC', j % NSC)])
                        mm(P, o, identb[:], MB[:, mk_, :], False, True, ['identb', 'MB'], [('pSC', j % NSC)])

            def expv(j):
                g = j // 4
                sc = pSC[j % NSC]
                pt = PTs[j % NSC]
                if nk == 2:
                    actf(P, pt[:].rearrange("p a t -> p (a t)"), sc[:], AF.Exp, [('pSC', j % NSC)], [('PTs', j % NSC)], scale=0.125)
                else:
                    for h2 in range(2):
                        actf(P, pt[:, h2 * 2, :], sc[:, h2 * 256:h2 * 256 + 128], AF.Exp, [('pSC', j % NSC)], [('PTs', j % NSC)], scale=0.125)
                ao = pAO[(j // 2) % 2]
                for h2 in range(2):
                    col = ((j % 2) * 2 + h2) * 65
                    for ki, (slot, mk_) in enumerate(kts):
                        mm(P, ao[:, col:col + 65], pt[:, h2 * 2 + ki, :], VA[slot][:, g, :], ki == 0, ki == nk - 1,
                           [('PTs', j % NSC), ('VA', slot)], [('pAO', (j // 2) % 2)])
                if j % 2 == 1:
                    h0 = (j - 1) * 2
                    ao3 = ao[:, 0:260].rearrange("p (h c) -> p h c", h=4)
                    ak = ('pAO', (j // 2) % 2)
                    tt(P, 'dve', den[:, 0:4], ao3[:, :, 64], esink[:, h0:h0 + 4], ALU.add, [ak, 'esink'], ['den'])
                    recip(P, den[:, 4:8], den[:, 0:4], ['den'], ['den'])
                    tt(P, 'dve', ATT[:, h0 * 64:(h0 + 4) * 64].rearrange("p (h c) -> p h c", h=4), ao3[:, :, 0:64],
                       den[:, 4:8].unsqueeze(2).to_broadcast([128, 4, 64]), ALU.mult, [ak, 'den'], ['ATT'])

            for j in range(min(NSC - 1, 32)):
                scores(j)
            for j in range(32):
                if j + NSC - 1 < 32:
                    scores(j + NSC - 1)
                expv(j)
            actf(P, SZ[:], Zt[qb][:], AF.Silu, [('Zt', qb)], ['SZ'])
            tt(P, 'dve', AG[:], ATT[:], SZ[:], ALU.mult, ['ATT', 'SZ'], ['AG'])
            for q8 in range(4):
                tp = pTP[q8 % 2]
                for j in range(8):
                    pr = q8 * 8 + j
                    tr(P, tp[:, j * 128:(j + 1) * 128], AG[:, pr * 128:(pr + 1) * 128], identb[:], ['AG', 'identb'], [('pTP', q8 % 2)])
                cp(P, 'act' if q8 % 2 else 'dve', agT[:, q8 * 8:(q8 + 1) * 8, :].rearrange("p a t -> p (a t)"), tp[:], [('pTP', q8 % 2)], ['agT'])
            if ch < 17:
                P.dma('sp', d['agT'][:, :, ch * 128:(ch + 1) * 128], agT[:], reads=['agT'], key='agTst')
            elif ch == 17:
                P.dma('sp', d['agT'][:, :, SROW0:SROW0 + 128], agT[:], reads=['agT'], key='agTst')
            else:
                P.dma('sp', d['agT'][:, :, SROW0 + ch - 17:SROW0 + ch - 16], agT[:, :, 0:1], reads=['agT'], key='agTst',
                      allow_slow_non_contiguous=True)
        P.flush()


def stage_J2(P, nc, d):
    with ExitStack() as es:
        gf = mk(nc, es, "gf", [128, D], F32)
        junk = mk(nc, es, "junkJ", [128, D], F32)
        xt = [mk(nc, es, f"xJ{i}", [128, D], F32) for i in range(2)]
        ot = [mk(nc, es, f"oJ{i}", [128, D], F32) for i in range(2)]
        st = [mk(nc, es, f"stJ{i}", [128, 2], F32) for i in range(2)]
        P.dma('sp', gf[:], d['final_norm'][0].partition_broadcast(128), writes=['gf'], key='gf')
        def loads_J(t):
            b = t % 2
            P.dma('sp', xt[b][:], d['hp'][128 * t:128 * t + 128, :], writes=[('x', b)], key=('x', b))
            P.dma('sp', ot[b][:], d['o2'][128 * t:128 * t + 128, :], writes=[('o', b)], key=('o', b))

        loads_J(1)
        for t in range(1, NT):
            b = t % 2
            if t + 1 < NT:
                loads_J(t + 1)
            tt(P, 'dve', xt[b][:], xt[b][:], ot[b][:], ALU.add, [('x', b), ('o', b)], [('x', b)])
            rms_stats(P, xt[b], junk, st[b], ('x', b), ('st', b))
            stt(P, 'dve', ot[b][:], xt[b][:], st[b][:, 1:2], gf[:], ALU.mult, ALU.mult, [('x', b), ('st', b), 'gf'], [('o', b)])
            if t < 17:
                P.dma('sp', d['o_yp'][(t - 1) * 128:t * 128, :], ot[b][:], reads=[('o', b)], key=('yst', b))
            else:
                P.dma('sp', d['o_ys'], ot[b][0:NS, :], reads=[('o', b)], key=('yst', b))
        P.flush()


class LazyDram(dict):
    def __init__(self, nc, debug_outs, ext_in):
        super().__init__()
        self.nc, self.debug_outs, self.ext_in = nc, debug_outs, ext_in
        self.spec = {}
        self.inputs, self.outputs = [], []

    def __missing__(self, name):
        kind, shape, dt = self.spec[name]
        if kind == 'scr':
            kind = 'ExternalInput' if name in self.ext_in else ('ExternalOutput' if name in self.debug_outs else 'Internal')
        if kind == 'ExternalInput':
            self.inputs.append(name)
        if kind == 'ExternalOutput':
            self.outputs.append(name)
        ap = self.nc.dram_tensor(name, list(shape), dt, kind=kind).ap()
        self[name] = ap
        return ap


def build(debug_outs=(), stages='ABLCFfGHIJj', ext_in=()):
    nc = bass.Bass("TRN2", target_bir_lowering=False)
    d = LazyDram(nc, debug_outs, ext_in)

    def inp(name, shape, dt=F32):
        d.spec[name] = ('ExternalInput', shape, dt)

    def outp(name, shape, dt=F32):
        d.spec[name] = ('ExternalOutput', shape, dt)

    def scr(name, shape, dt):
        d.spec[name] = ('scr', shape, dt)

    inp('xin', [T + 1, D]); inp('sshift', [128, D]); inp('swkv', [NS, 64, 64, 64])
    inp('ck', [NS, 128, 512]); inp('cv', [NS, 128, 512])
    inp('a_norm', [1, D]); inp('muT', [128, 6, 16]); inp('ident', [128, 128]); inp('tri', [128, 128]); inp('ones', [128, 128])
    inp('onehot', [128, 1]); inp('lmask', [128, 2]); inp('mask4', [128, 512]); inp('negsl', [128, 128]); inp('mb', [128, 4, 128])
    inp('cs', [T, 16]); inp('prm', [7, E])
    inp('a_w_rkvz', [1, 4, D, E]); inp('a_w1', [1, D, 96]); inp('a_w2', [1, 96, E]); inp('a_a1', [1, D, 96]); inp('a_a2', [1, 96, E])
    inp('a_w_out', [1, E, D]); inp('kv_norm', [1, D]); inp('w_kv', [D, 1024]); inp('b_norm', [1, D]); inp('b_w_qz', [1, D, 2 * E])
    inp('b_sinks', [1, 64]); inp('b_w_o', [1, E, D]); inp('final_norm', [1, D])
    outp('o_yp', [2048, D]); outp('o_ys', [NS, D]); outp('o_pwkv', [64, 64, 64]); outp('o_pshift', [1, D])
    outp('o_pck', [128, 512]); outp('o_pcv', [128, 512]); outp('o_swkv', [NS, 64, 64, 64]); outp('o_sshift', [NS, D])
    outp('o_sck', [NS, 128, 512]); outp('o_scv', [NS, 128, 512])
    scr('xmT', [6, 128, 16, T], BF16); scr('rkvz', [4, T, E], BF16); scr('wpre', [T, E], F32); scr('apre', [T, E], F32)
    scr('ygT', [128, 32, T], BF16); scr('o1', [T, D], F32); scr('hp', [T, D], F32); scr('hkvT', [128, 16, T], BF16); scr('hbT', [128, 16, T], BF16)
    scr('KVs', [T, 1024], BF16); scr('qs', [T, E], BF16); scr('zs', [T, E], BF16); scr('agT', [128, 32, T], BF16); scr('o2', [T, D], F32)
    with ExitStack() as stack:
        P = Prog(nc, stack)
        if 'A' in stages:
            stage_A(P, nc, d)
        if 'B' in stages:
            stage_B(P, nc, d)
        if 'L' in stages:
            stage_B_lora(P, nc, d)
        if 'C' in stages:
            stage_CDE(P, nc, d)
        if 'F' in stages:
            stage_outproj(P, nc, d, 'ygT', d['a_w_out'][0], 'o1', 'F1')
        if 'f' in stages:
            stage_F2(P, nc, d)
        if 'G' in stages:
            stage_G(P, nc, d)
        if 'H' in stages:
            stage_H(P, nc, d)
        if 'I' in stages:
            stage_I(P, nc, d)
        if 'J' in stages:
            stage_outproj(P, nc, d, 'agT', d['b_w_o'][0], 'o2', 'J1')
        if 'j' in stages:
            stage_J2(P, nc, d)
    nc._lazy = d
    return nc


def host_tables():
    f = np.float32
    j = np.arange(128)
    su = (j[:, None] < j[None, :]).astype(f)
    u = (j[:, None] <= j[None, :]).astype(f)
    tb = {}
    tb['ident'] = np.eye(128, dtype=f)
    tb['tri'] = u.copy()
    tb['ones'] = np.ones((128, 128), f)
    oh = np.zeros((128, 1), f); oh[0, 0] = 1
    tb['onehot'] = oh
    c = f(-np.exp(-0.5))
    lm = np.zeros((128, 2), f); lm[:, 0] = c; lm[0, 1] = c
    tb['lmask'] = lm
    tb['mask4'] = np.concatenate([su, u, -su, u], 1)
    tb['negsl'] = -(su.T).copy()
    NEG = f(-30000.0)
    jj = j[:, None]; ii = j[None, :]
    cur = np.where(jj <= ii, 0, NEG).astype(f)
    prev = np.where(jj >= ii, 0, NEG).astype(f)
    lead = np.where(jj >= 112, 0, NEG).astype(f)
    mb = np.stack([cur, prev, np.minimum(cur, lead), np.minimum(prev, lead)], 1)
    tb['mb'] = np.ascontiguousarray(mb)
    pos = np.zeros(T, f)
    pos[112:2176] = np.arange(2064)
    pos[2176:2176 + NS] = 16384
    inv = (f(500000.0) ** (-np.arange(8, dtype=f) * f(2.0) / f(16))).astype(f)
    ang = (pos[:, None] * inv[None, :]).astype(f)
    tb['cs'] = np.concatenate([np.cos(ang), np.sin(ang)], 1).astype(f)
    return tb


_NC = [None]


def kernel(**inp):
    f = np.float32
    inp = {k: np.asarray(v) for k, v in inp.items()}
    if _NC[0] is None:
        _NC[0] = build()
    nc = _NC[0]
    tb = host_tables()
    mu = inp['a_mu'][0]
    muT = np.ascontiguousarray(mu.reshape(6, 16, 128).transpose(2, 0, 1))
    prm = np.ascontiguousarray(np.stack([inp['a_w0'][0], inp['a_a0'][0], inp['a_k_k'][0], inp['a_k_a'][0], inp['a_r_k'][0].reshape(-1),
                                         inp['a_gn_g'][0], inp['a_gn_b'][0]], 0).astype(f))
    shared = dict(tb)
    shared.update(muT=muT, prm=prm, a_norm=inp['a_norm'], a_w_rkvz=inp['a_w_rkvz'], a_w1=inp['a_w1'], a_w2=inp['a_w2'], a_a1=inp['a_a1'],
                  a_a2=inp['a_a2'], a_w_out=inp['a_w_out'], kv_norm=inp['kv_norm'].reshape(1, D), w_kv=inp['w_kv'], b_norm=inp['b_norm'],
                  b_w_qz=inp['b_w_qz'], b_sinks=inp['b_sinks'], b_w_o=inp['b_w_o'], final_norm=inp['final_norm'].reshape(1, D))
    in_maps = []
    for core in range(8):
        b = core % 4
        ss = slice(core * NS, core * NS + NS)
        xin = np.zeros((T + 1, D), f)
        xin[1 + 112:1 + 128] = inp['meta_tokens']
        xin[1 + 128:1 + 128 + 2048] = inp['x_prompt'][b]
        xin[1 + SROW0:1 + SROW0 + NS] = inp['x_sample'][ss, 0]
        sshift = np.zeros((128, D), f)
        sshift[:NS] = inp['state_shift'][0, ss]
        m = dict(shared)
        m.update(xin=xin, sshift=sshift, swkv=np.ascontiguousarray(inp['state_wkv'][0, ss]),
                 ck=np.ascontiguousarray(inp['cache_k'][ss].reshape(NS, 128, 512)), cv=np.ascontiguousarray(inp['cache_v'][ss].reshape(NS, 128, 512)))
        in_maps.append(m)
    in_maps = [{k: m[k] for k in nc._lazy.inputs if k in m} for m in in_maps]
    res = run_bass_kernel_spmd(nc, in_maps, core_ids=list(range(8)))
    R = res.results
    g = lambda c, n: np.asarray(R[c][n], dtype=f)
    y_prompt = np.stack([g(b, 'o_yp') for b in range(4)], 0)
    y_sample = np.concatenate([g(c, 'o_ys') for c in range(8)], 0)[:, None, :]
    p_wkv = np.stack([g(b, 'o_pwkv') for b in range(4)], 0)[None]
    p_shift = np.concatenate([g(b, 'o_pshift') for b in range(4)], 0)[None]
    p_ck = np.stack([g(b, 'o_pck') for b in range(4)], 0).reshape(4, 128, 8, 64)
    p_cv = np.stack([g(b, 'o_pcv') for b in range(4)], 0).reshape(4, 128, 8, 64)
    s_wkv = np.concatenate([g(c, 'o_swkv') for c in range(8)], 0)[None]
    s_shift = np.concatenate([g(c, 'o_sshift') for c in range(8)], 0)[None]
    s_ck = np.concatenate([g(c, 'o_sck') for c in range(8)], 0).reshape(32, 128, 8, 64)
    s_cv = np.concatenate([g(c, 'o_scv') for c in range(8)], 0).reshape(32, 128, 8, 64)
    return (y_prompt, y_sample, p_wkv, p_shift, p_ck, p_cv, s_wkv, s_shift, s_ck, s_cv)
```
